# Optimizing a Trainium2 kernel written in Bass

```python
import math
import jax
import jax.numpy as jnp
from jax import lax
import numpy as np

D_MODEL = 1024
BATCH = 8
SEQ = 2048
DEPTH = 2

GRID_W = 64
CTX_LEN = 256
HEAD_DIM = 64
A_HEADS = 6
A_KV_HEADS = 2
WINDOW = 128
WIN_BLK = 128
B_HEADS = 4
B_QK_DIM = 32
C_HEADS = 6
NA_KH = 8
NA_KW = 16
Q_BLK = 128
N_MOD = 6
D_FF_RAW = -(-8 * D_MODEL // 3)
D_FF = -(-D_FF_RAW // 256) * 256
MIX_WIDTH = (A_HEADS + B_HEADS + C_HEADS) * HEAD_DIM
A_Q_W = A_HEADS * HEAD_DIM
A_KV_W = A_KV_HEADS * HEAD_DIM
B_QK_W = B_HEADS * 2 * B_QK_DIM
B_V_W = B_HEADS * HEAD_DIM
C_W = C_HEADS * HEAD_DIM
PROJ_WIDTH = A_Q_W + 2 * A_KV_W + 2 * B_QK_W + B_V_W + 3 * C_W
ROPE_BASE = 10000.0
LN_EPS = 1e-5
NEG_INF = -1e30

kernel_name = "hybrid_dit_parallel_head_groups"


def layer_norm(x, g, b):
    xf = x.astype(jnp.float32)
    mu = jnp.mean(xf, axis=-1, keepdims=True)
    var = jnp.mean(jnp.square(xf - mu), axis=-1, keepdims=True)
    return ((xf - mu) * lax.rsqrt(var + LN_EPS) * g + b).astype(x.dtype)


def rms_norm(x, g):
    xf = x.astype(jnp.float32)
    return (xf * lax.rsqrt(jnp.mean(jnp.square(xf), axis=-1, keepdims=True) + LN_EPS) * g).astype(x.dtype)


def modulate(t, shift, scale):
    return t * (1.0 + scale) + shift


def post_norm(res, y, gate, g, b, alpha):
    return layer_norm(alpha * res + gate * y, g, b)


def swiglu(h, w_in_l, w_out_l):
    gate, up = jnp.split(h @ w_in_l, 2, axis=-1)
    return (jax.nn.silu(gate) * up) @ w_out_l


def rope_1d(x, pos):
    half = x.shape[-1] // 2
    inv = ROPE_BASE ** (-jnp.arange(half, dtype=jnp.float32) / half)
    ang = pos.astype(jnp.float32)[:, None] * inv[None, :]
    cos = jnp.cos(ang)[:, None, :]
    sin = jnp.sin(ang)[:, None, :]
    x1 = x[..., :half].astype(jnp.float32)
    x2 = x[..., half:].astype(jnp.float32)
    return jnp.concatenate([x1 * cos - x2 * sin, x2 * cos + x1 * sin], axis=-1).astype(x.dtype)


def rope_2d(x, rows, cols):
    d = x.shape[-1]
    return jnp.concatenate([rope_1d(x[..., : d // 2], rows), rope_1d(x[..., d // 2:], cols)], axis=-1)


def split_proj(p):
    B, L, _ = p.shape
    widths = (A_Q_W, A_KV_W, A_KV_W, B_QK_W, B_QK_W, B_V_W, C_W, C_W, C_W)
    offs = [sum(widths[:i]) for i in range(1, len(widths))]
    aq, ak, av, bq, bk, bv, cq, ck, cv = jnp.split(p, offs, axis=-1)
    return (aq.reshape(B, L, A_HEADS, HEAD_DIM),
            ak.reshape(B, L, A_KV_HEADS, HEAD_DIM),
            av.reshape(B, L, A_KV_HEADS, HEAD_DIM),
            bq.reshape(B, L, B_HEADS, 2, B_QK_DIM),
            bk.reshape(B, L, B_HEADS, 2, B_QK_DIM),
            bv.reshape(B, L, B_HEADS, HEAD_DIM),
            cq.reshape(B, L, C_HEADS, HEAD_DIM),
            ck.reshape(B, L, C_HEADS, HEAD_DIM),
            cv.reshape(B, L, C_HEADS, HEAD_DIM))


def context_attention(q, k, v, sink=None):
    B, C, HQ, d = q.shape
    G = k.shape[2]
    R = HQ // G
    qg = q.reshape(B, C, G, R, d)
    s = jnp.einsum('bqgrd,bkgd->bgrqk', qg, k).astype(jnp.float32) * (d ** -0.5)
    if sink is not None:
        sk = jnp.broadcast_to(sink.reshape(G, R)[None, :, :, None, None].astype(jnp.float32), (B, G, R, C, 1))
        s = jnp.concatenate([s, sk], axis=-1)
    p = jax.nn.softmax(s, axis=-1)[..., :C]
    return jnp.einsum('bgrqk,bkgd->bqgrd', p, v).reshape(B, C, HQ * d).astype(q.dtype)


def window_gqa(q, k, v, kc, vc, sink):
    B, L, HQ, d = q.shape
    G = A_KV_HEADS
    R = HQ // G
    nb = L // WIN_BLK
    C = kc.shape[1]
    qb = q.reshape(B, nb, WIN_BLK, G, R, d)
    pad = ((0, 0), (WIN_BLK, WIN_BLK), (0, 0), (0, 0))
    kp = jnp.pad(k, pad)
    vp = jnp.pad(v, pad)

    def bands(t):
        return jnp.concatenate([t[:, j * WIN_BLK: j * WIN_BLK + L].reshape(B, nb, WIN_BLK, G, d) for j in range(3)], axis=2)

    kb, vb = bands(kp), bands(vp)
    scale = d ** -0.5
    s_win = jnp.einsum('bnqgrd,bnkgd->bgrnqk', qb, kb).astype(jnp.float32) * scale
    s_ctx = jnp.einsum('bnqgrd,bcgd->bgrnqc', qb, kc).astype(jnp.float32) * scale
    blk = jnp.arange(nb, dtype=jnp.int32)[:, None] * WIN_BLK
    qpos = blk + jnp.arange(WIN_BLK, dtype=jnp.int32)[None, :]
    kpos = blk - WIN_BLK + jnp.arange(3 * WIN_BLK, dtype=jnp.int32)[None, :]
    valid = ((jnp.abs(qpos[:, :, None] - kpos[:, None, :]) <= WINDOW)
             & (kpos >= 0)[:, None, :] & (kpos < L)[:, None, :])
    s_win = jnp.where(valid, s_win, NEG_INF)
    sk = jnp.broadcast_to(sink.reshape(G, R)[None, :, :, None, None, None].astype(jnp.float32), (B, G, R, nb, WIN_BLK, 1))
    p = jax.nn.softmax(jnp.concatenate([s_win, s_ctx, sk], axis=-1), axis=-1)
    pw = p[..., : 3 * WIN_BLK]
    pc = p[..., 3 * WIN_BLK: 3 * WIN_BLK + C]
    out = jnp.einsum('bgrnqk,bnkgd->bnqgrd', pw, vb) + jnp.einsum('bgrnqc,bcgd->bnqgrd', pc, vc)
    return out.reshape(B, L, HQ * d).astype(q.dtype)


def diff_core(q, k, v, lam):
    s = jnp.einsum('bqhmd,bkhmd->bhmqk', q, k).astype(jnp.float32) * (q.shape[-1] ** -0.5)
    p = jax.nn.softmax(s, axis=-1)
    a = p[:, :, 0] - lam * p[:, :, 1]
    return jnp.einsum('bhqk,bkhd->bqhd', a, v)


def diff_head_norm(o, g, lam_init):
    B, L = o.shape[:2]
    return (rms_norm(o, g) * (1.0 - lam_init)).reshape(B, L, B_HEADS * HEAD_DIM)


def diff_attention_latent(q, k, v, kc, vc, lam, g, lam_init):
    B, L = q.shape[:2]
    nb = L // Q_BLK
    k_all = jnp.concatenate([kc, k], axis=1)
    v_all = jnp.concatenate([vc, v], axis=1)
    qb = jnp.swapaxes(q.reshape(B, nb, Q_BLK, B_HEADS, 2, B_QK_DIM), 0, 1)
    ob = lax.map(lambda qblk: diff_core(qblk, k_all, v_all, lam), qb)
    o = jnp.swapaxes(ob, 0, 1).reshape(B, L, B_HEADS, HEAD_DIM)
    return diff_head_norm(o, g, lam_init)


def neighbourhood_attention(q, k, v, kc, vc, rel_bias):
    B, L, H, d = q.shape
    W = GRID_W
    R = L // W
    KH = min(NA_KH, R)
    qg, kg, vg = (t.reshape(B, R, W, H, d) for t in (q, k, v))
    r = jnp.arange(R, dtype=jnp.int32)
    rs = jnp.clip(r - KH // 2, 0, R - KH)
    row_idx = rs[:, None] + jnp.arange(KH, dtype=jnp.int32)[None, :]
    kr = kg[:, row_idx]
    vr = vg[:, row_idx]
    cq = jnp.arange(W, dtype=jnp.int32)
    cs = jnp.clip(cq - NA_KW // 2, 0, W - NA_KW)
    col_valid = (cq[None, :] >= cs[:, None]) & (cq[None, :] < cs[:, None] + NA_KW)
    dr = row_idx - r[:, None] + (NA_KH - 1)
    dc = jnp.clip(cq[None, :] - cq[:, None], -(NA_KW - 1), NA_KW - 1) + (NA_KW - 1)
    bias = rel_bias[:, dr[:, None, :, None], dc[None, :, None, :]].astype(jnp.float32)
    scale = d ** -0.5
    s_win = jnp.einsum('brwhd,brkvhd->bhrwkv', qg, kr).astype(jnp.float32) * scale + bias
    s_win = jnp.where(col_valid[:, None, :], s_win, NEG_INF).reshape(B, H, R, W, KH * W)
    s_ctx = jnp.einsum('brwhd,bchd->bhrwc', qg, kc).astype(jnp.float32) * scale
    p = jax.nn.softmax(jnp.concatenate([s_win, s_ctx], axis=-1), axis=-1)
    pw = p[..., : KH * W].reshape(B, H, R, W, KH, W)
    pc = p[..., KH * W:]
    out = jnp.einsum('bhrwkv,brkvhd->brwhd', pw, vr) + jnp.einsum('bhrwc,bchd->brwhd', pc, vc)
    return out.reshape(B, L, H * d).astype(q.dtype)


def setup_inputs(seed: int = 0) -> dict:
    key = jax.random.key(seed)
    ks = jax.random.split(key, 24)
    f32 = jnp.float32
    beta = (8.0 * DEPTH) ** -0.25

    def nrm(k, shape, scale):
        return jax.random.normal(k, shape, f32) * scale

    return {
        "x": nrm(ks[0], (BATCH, SEQ, D_MODEL), 1.0),
        "c": nrm(ks[1], (BATCH, D_MODEL), 1.0),
        "ctx": nrm(ks[2], (BATCH, CTX_LEN, D_MODEL), 1.0),
        "c_ctx": nrm(ks[3], (D_MODEL,), 1.0),
        "w_ada": nrm(ks[4], (DEPTH, D_MODEL, N_MOD * D_MODEL), 0.5 * D_MODEL ** -0.5),
        "b_ada": nrm(ks[5], (DEPTH, N_MOD * D_MODEL), 0.02),
        "w_in": nrm(ks[6], (DEPTH, D_MODEL, PROJ_WIDTH), D_MODEL ** -0.5),
        "w_o": nrm(ks[7], (DEPTH, MIX_WIDTH, D_MODEL), beta * MIX_WIDTH ** -0.5),
        "sink": nrm(ks[8], (DEPTH, A_HEADS), 0.5),
        "lam_q1": nrm(ks[9], (DEPTH, B_QK_DIM), 0.1),
        "lam_k1": nrm(ks[10], (DEPTH, B_QK_DIM), 0.1),
        "lam_q2": nrm(ks[11], (DEPTH, B_QK_DIM), 0.1),
        "lam_k2": nrm(ks[12], (DEPTH, B_QK_DIM), 0.1),
        "subln_g": 1.0 + nrm(ks[13], (DEPTH, HEAD_DIM), 0.02),
        "na_bias": nrm(ks[14], (DEPTH, C_HEADS, 2 * NA_KH - 1, 2 * NA_KW - 1), 0.1),
        "ln1_g": 1.0 + nrm(ks[15], (DEPTH, D_MODEL), 0.02),
        "ln1_b": nrm(ks[16], (DEPTH, D_MODEL), 0.02),
        "w_ffn_in": nrm(ks[17], (DEPTH, D_MODEL, 2 * D_FF), D_MODEL ** -0.5),
        "w_ffn_out": nrm(ks[18], (DEPTH, D_FF, D_MODEL), beta * D_FF ** -0.5),
        "ln2_g": 1.0 + nrm(ks[19], (DEPTH, D_MODEL), 0.02),
        "ln2_b": nrm(ks[20], (DEPTH, D_MODEL), 0.02),
    }


def reference(x, c, ctx, c_ctx, w_ada, b_ada, w_in, w_o, sink, lam_q1, lam_k1, lam_q2, lam_k2,
              subln_g, na_bias, ln1_g, ln1_b, w_ffn_in, w_ffn_out, ln2_g, ln2_b):
    B, L, _ = x.shape
    tpos = jnp.arange(L, dtype=jnp.int32)
    rows = tpos // GRID_W
    cols = tpos % GRID_W
    alpha = (2.0 * DEPTH) ** 0.25
    sc = jax.nn.silu(c)
    scc = jax.nn.silu(c_ctx)
    xs, cs = x, ctx
    for l in range(DEPTH):
        last = l == DEPTH - 1
        mx = (sc @ w_ada[l] + b_ada[l])[:, None, :]
        mc = scc @ w_ada[l] + b_ada[l]
        x_sh1, x_sc1, x_g1, x_sh2, x_sc2, x_g2 = jnp.split(mx, N_MOD, axis=-1)
        c_sh1, c_sc1, c_g1, c_sh2, c_sc2, c_g2 = jnp.split(mc, N_MOD, axis=-1)
        hx = modulate(xs, x_sh1, x_sc1)
        hc = modulate(cs, c_sh1, c_sc1)
        aq, ak, av, bq, bk, bv, cq, ck, cv = split_proj(hx @ w_in[l])
        aqc, akc, avc, bqc, bkc, bvc, cqc, ckc, cvc = split_proj(hc @ w_in[l])
        aq = rope_2d(aq, rows, cols)
        ak = rope_2d(ak, rows, cols)
        bq = rope_2d(bq.reshape(B, L, 2 * B_HEADS, B_QK_DIM), rows, cols).reshape(B, L, B_HEADS, 2, B_QK_DIM)
        bk = rope_2d(bk.reshape(B, L, 2 * B_HEADS, B_QK_DIM), rows, cols).reshape(B, L, B_HEADS, 2, B_QK_DIM)
        lam_init = 0.8 - 0.6 * math.exp(-0.3 * l)
        lam = (jnp.exp(jnp.sum(lam_q1[l] * lam_k1[l]).astype(jnp.float32))
               - jnp.exp(jnp.sum(lam_q2[l] * lam_k2[l]).astype(jnp.float32)) + lam_init)
        ya = window_gqa(aq, ak, av, akc, avc, sink[l])
        yb = diff_attention_latent(bq, bk, bv, bkc, bvc, lam, subln_g[l], lam_init)
        yc = neighbourhood_attention(cq, ck, cv, ckc, cvc, na_bias[l])
        y = jnp.concatenate([ya, yb, yc], axis=-1).astype(xs.dtype) @ w_o[l]
        xn = post_norm(xs, y, x_g1, ln1_g[l], ln1_b[l], alpha)
        xn = post_norm(xn, swiglu(modulate(xn, x_sh2, x_sc2), w_ffn_in[l], w_ffn_out[l]), x_g2, ln2_g[l], ln2_b[l], alpha)
        if not last:
            yac = context_attention(aqc, akc, avc, sink[l])
            ybc = diff_head_norm(diff_core(bqc, bkc, bvc, lam), subln_g[l], lam_init)
            ycc = context_attention(cqc, ckc, cvc)
            yctx = jnp.concatenate([yac, ybc, ycc], axis=-1).astype(cs.dtype) @ w_o[l]
            cs = post_norm(cs, yctx, c_g1, ln1_g[l], ln1_b[l], alpha)
            cs = post_norm(cs, swiglu(modulate(cs, c_sh2, c_sc2), w_ffn_in[l], w_ffn_out[l]), c_g2, ln2_g[l], ln2_b[l], alpha)
        xs = xn
    return xs
```

```python
import math
from contextlib import ExitStack

import numpy as np
import concourse.bass as bass
import concourse.mybir as mybir
from concourse.bass_utils import run_bass_kernel_spmd

F32 = mybir.dt.float32
BF16 = mybir.dt.bfloat16
AF = mybir.ActivationFunctionType
ALU = mybir.AluOpType

D = 1024
L = 2048
C = 256
T = L + C
NT = T // 128
DFF = 2816
NCH = 22
NCOL = NCH * 128 + 768
DEPTH = 2
ALPHA = (2.0 * DEPTH) ** 0.25
EPS = 1e-5
NV = 14


class Buf:
    __slots__ = ("name", "w", "r")

    def __init__(self, name):
        self.name = name
        self.w = None
        self.r = {}


class FW:
    NDMA = 24

    def __init__(self, nc, stack):
        self.nc = nc
        self.eng = {"pe": nc.tensor, "act": nc.scalar, "dve": nc.vector, "pool": nc.gpsimd, "sp": nc.sync}
        self.sems = {}
        self.cnt = {}
        self.known = {e: {} for e in self.eng}
        for e in ("pe", "act", "dve", "pool"):
            self.sems[e] = stack.enter_context(nc.semaphore("s_" + e))
            self.cnt[e] = 0
        self.dsems = []
        for i in range(self.NDMA):
            k = "dma%d" % i
            self.sems[k] = stack.enter_context(nc.semaphore(k))
            self.cnt[k] = 0
            self.dsems.append(k)
        self.rr = 0
        self.bufs = {}

    def B(self, name):
        b = self.bufs.get(name)
        if b is None:
            b = self.bufs[name] = Buf(name)
        return b

    def _wait(self, e, toks):
        need = {}
        for t in toks:
            if t is None:
                continue
            k, v = t
            if e == "pe" and k == "pe":
                continue
            if self.known[e].get(k, 0) >= v:
                continue
            if need.get(k, 0) < v:
                need[k] = v
        for k, v in need.items():
            self.eng[e].wait_ge(self.sems[k], v)
            self.known[e][k] = v

    @staticmethod
    def _deps(reads, writes):
        toks = []
        for b in reads:
            toks.append(b.w)
        for b in writes:
            toks.append(b.w)
            for k, v in b.r.items():
                toks.append((k, v))
        return toks

    @staticmethod
    def _mark(tok, reads, writes):
        k, v = tok
        for b in reads:
            if b.r.get(k, 0) < v:
                b.r[k] = v
        for b in writes:
            b.w = tok
            b.r = {}

    def op(self, e, fn, reads=(), writes=(), inc=True):
        self._wait(e, self._deps(reads, writes))
        ins = fn()
        if inc:
            self.cnt[e] += 1
            ins.then_inc(self.sems[e], 1)
            tok = (e, self.cnt[e])
        else:
            tok = (e, self.cnt[e] + 1)
        self._mark(tok, reads, writes)
        return ins

    def dma(self, q, out, in_, reads=(), writes=()):
        sem = self.dsems[self.rr]
        self.rr = (self.rr + 1) % self.NDMA
        toks = self._deps(reads, writes)
        if self.cnt[sem] > 0:
            toks.append((sem, self.cnt[sem]))
        self._wait(q, toks)
        ins = self.eng[q].dma_start(out=out, in_=in_)
        self.cnt[sem] += 16
        ins.then_inc(self.sems[sem], 16)
        self._mark((sem, self.cnt[sem]), reads, writes)
        return ins

    def barrier(self):
        toks = [(k, v) for k, v in self.cnt.items() if v > 0]
        for e in self.eng:
            self._wait(e, toks)


def _win_cols():
    aq0, ak0, av0, bq0, bk0, bv0, cq0, ck0, cv0 = 0, 384, 512, 640, 896, 1152, 1408, 1792, 2176
    pA = np.array([d + 16 if d % 32 < 16 else d - 16 for d in range(64)])
    pB32 = np.array([d + 8 if d % 16 < 8 else d - 8 for d in range(32)])
    pB = np.concatenate([pB32, 32 + pB32])
    ar = np.arange(64)

    def hc(base, h, perm=None):
        return base + h * 64 + (ar if perm is None else perm)

    ch = []
    for j in range(3):
        ch.append(np.concatenate([hc(aq0, j), hc(aq0, 3 + j)]))
    ch.append(np.concatenate([hc(ak0, 0), hc(ak0, 1)]))
    for j in range(3):
        ch.append(np.concatenate([hc(aq0, j, pA), hc(aq0, 3 + j, pA)]))
    ch.append(np.concatenate([hc(ak0, 0, pA), hc(ak0, 1, pA)]))
    for base, perm in ((bq0, None), (bk0, None), (bq0, pB), (bk0, pB)):
        for c in range(2):
            ch.append(np.concatenate([hc(base, 2 * c, perm), hc(base, 2 * c + 1, perm)]))
    for base in (cq0, ck0):
        for c in range(3):
            ch.append(np.concatenate([hc(base, 2 * c), hc(base, 2 * c + 1)]))
    ch.append(np.arange(av0, av0 + 128))
    ch.append(np.arange(bv0, bv0 + 256))
    ch.append(np.arange(cv0, cv0 + 384))
    cols = np.concatenate(ch)
    assert cols.shape[0] == NCOL
    return cols


def _rope_tables():
    f = np.float32
    pos = np.arange(L)
    rows = (pos // 64).astype(f)
    cols = (pos % 64).astype(f)
    tabs = np.zeros((8, 128, T), f)
    tabs[0::2, :, L:] = 1.0
    p = np.arange(128)
    inv16 = np.power(f(10000.0), -np.arange(16, dtype=f) / f(16)).astype(f)
    d = p % 64
    ang = np.where((d < 32)[:, None], rows[None, :], cols[None, :]).astype(f) * inv16[(d % 32) % 16][:, None]
    ang = ang.astype(f)
    sgn = np.where((d % 32) < 16, -1.0, 1.0).astype(f)
    tabs[0, :, :L] = np.cos(ang)
    tabs[1, :, :L] = np.sin(ang) * sgn[:, None]
    inv8 = np.power(f(10000.0), -np.arange(8, dtype=f) / f(8)).astype(f)
    d = p % 32
    ang = np.where((d < 16)[:, None], rows[None, :], cols[None, :]).astype(f) * inv8[(d % 16) % 8][:, None]
    ang = ang.astype(f)
    sgn = np.where((d % 16) < 8, -1.0, 1.0).astype(f)
    tabs[2, :, :L] = np.cos(ang)
    tabs[3, :, :L] = np.sin(ang) * sgn[:, None]
    m0 = ((p % 64) < 32).astype(f)[:, None]
    tabs[4] = tabs[2] * m0
    tabs[5] = tabs[3] * m0
    tabs[6] = tabs[2] * (1 - m0)
    tabs[7] = tabs[3] * (1 - m0)
    return tabs


def _na_variants():
    R, W, KH, KW = 32, 64, 8, 16
    var = []
    for i in range(6):
        var.append(([12, 13, 14, 15], [8 + 2 * i, 9 + 2 * i]))
    for u in range(4):
        var.append(([0, 1, 2, 3], [2 * u, 2 * u + 1]))
    for u in range(12, 16):
        var.append(([28, 29, 30, 31], [2 * u, 2 * u + 1]))
    dr = np.zeros((NV, 128, 256), np.int64)
    dc = np.zeros((NV, 128, 256), np.int64)
    ok = np.zeros((NV, 128, 256), np.float32)
    kc = np.arange(64)[:, None]
    qc = np.arange(64)[None, :]
    cs = np.clip(qc - KW // 2, 0, W - KW)
    colv = (kc >= cs) & (kc < cs + KW)
    dcm = np.clip(kc - qc, -(KW - 1), KW - 1) + (KW - 1)
    for v, (qrows, krows) in enumerate(var):
        for kk, kr in enumerate(krows):
            for qq, r in enumerate(qrows):
                rs = min(max(r - KH // 2, 0), R - KH)
                rowv = (rs <= kr <= rs + KH - 1)
                dr[v, kk * 64:(kk + 1) * 64, qq * 64:(qq + 1) * 64] = min(max(kr - r + 7, 0), 14)
                dc[v, kk * 64:(kk + 1) * 64, qq * 64:(qq + 1) * 64] = dcm
                ok[v, kk * 64:(kk + 1) * 64, qq * 64:(qq + 1) * 64] = (colv & rowv).astype(np.float32)
    return dr, dc, ok


_CONST = {}


def _consts():
    if not _CONST:
        _CONST["cols"] = _win_cols()
        _CONST["rope"] = _rope_tables()
        dr, dc, ok = _na_variants()
        _CONST["dr"], _CONST["dc"] = dr, dc
        _CONST["nmask"] = np.ascontiguousarray(ok.transpose(1, 0, 2).reshape(128, NV * 256))
        j = np.arange(128)[:, None]
        i = np.arange(128)[None, :]
        prev = (j >= i).astype(np.float32)
        nxt = (j <= i).astype(np.float32)
        _CONST["amask"] = np.concatenate([np.tile(prev, (1, 3)), np.tile(nxt, (1, 3))], axis=1)
        ffperm = np.concatenate([np.concatenate([np.arange(c * 128, (c + 1) * 128), DFF + np.arange(c * 128, (c + 1) * 128)])
                                 for c in range(22)])
        _CONST["ffperm"] = ffperm
    return _CONST


def build(nlayers=DEPTH, dbg=False):
    nc = bass.Bass("TRN2", target_bir_lowering=False)

    def din(name, shape, dt=F32):
        return nc.dram_tensor(name, list(shape), dt, kind="ExternalInput").ap()

    def dscr(name, shape, dt):
        return nc.dram_tensor(name, list(shape), dt, kind="ExternalOutput" if (dbg and name in ("s_q", "s_o", "s_x1", "s_g")) else "Internal").ap()

    xin = din("xin", [T, D])
    cT = din("cT", [128, 16])
    w_ada = din("w_ada", [DEPTH, D, 6 * D])
    b_ada = din("b_ada", [DEPTH, 6 * D])
    b_adaT = din("b_adaT", [DEPTH, 128, 48])
    win = din("win", [DEPTH, D, NCOL])
    w_o = din("w_o", [DEPTH, D, D])
    sink = din("sink", [DEPTH, 6])
    lamv = din("lamv", [DEPTH, 4, 32])
    subg = din("subg", [DEPTH, 64])
    nag = din("nag", [DEPTH, 6, 128, NV * 256])
    nmask = din("nmask", [128, NV * 256])
    amask = din("amask", [128, 768])
    lnp = din("lnp", [DEPTH, 4, D])
    wfi = din("wfi", [DEPTH, D, 2 * DFF])
    wfo = din("wfo", [DEPTH, DFF, D])
    rope = din("rope", [8, 128, T])
    out = nc.dram_tensor("out", [L, D], F32, kind="ExternalOutput").ap()

    s_ada = dscr("s_ada", [DEPTH, 128, 8, 6 * D], BF16)
    s_win = dscr("s_win", [DEPTH, 128, 8, NCOL], BF16)
    s_wo = dscr("s_wo", [DEPTH, 64, 16, D], BF16)
    s_wfi = dscr("s_wfi", [DEPTH, 128, 8, 2 * DFF], BF16)
    s_wfo = dscr("s_wfo", [DEPTH, 128, 22, D], BF16)
    s_q = dscr("s_q", [8, 128, T], BF16)
    s_o = dscr("s_o", [16, 64, T], BF16)
    s_x1 = dscr("s_x1", [T, D], F32)
    s_g = dscr("s_g", [2, 2, 128, D], F32)

    with ExitStack() as top:
        top.enter_context(nc.allow_low_precision(reason="bf16 matmul operands by design; fp32 accumulation"))
        fw = FW(nc, top)
        B = fw.B

        uid = [0]

        def sb(stack, name, shape, dt):
            uid[0] += 1
            return stack.enter_context(nc.sbuf_tensor("%s_u%d" % (name, uid[0]), list(shape), dt))

        psF = []
        psB = []
        pcnt = {"f": 0, "b": 0, "a": 0, "s": 0}

        def alloc_psum(stack, nf, nb):
            del psF[:]
            del psB[:]
            for i in range(nf):
                uid[0] += 1
                psF.append(stack.enter_context(nc.psum_tensor("psf%d_u%d" % (i, uid[0]), [128, 512], F32)))
            for i in range(nb):
                uid[0] += 1
                psB.append(stack.enter_context(nc.psum_tensor("psb%d_u%d" % (i, uid[0]), [128, 1024], BF16)))

        def next_ps(pool=None):
            if pool is None:
                i = pcnt["f"] % len(psF)
                pcnt["f"] += 1
            elif pool == "a":
                i = pcnt["a"] % 4
                pcnt["a"] += 1
            else:
                i = 4 + pcnt["s"] % 4
                pcnt["s"] += 1
            return psF[i], B("psf%d" % i)

        def next_psb():
            i = pcnt["b"] % 2
            pcnt["b"] += 1
            return psB[i], B("psb%d" % i)

        def mm(o, lhsT, rhs, start, stop, rd, wr):
            fw.op("pe", lambda: nc.tensor.matmul(o, lhsT=lhsT, rhs=rhs, start=start, stop=stop), reads=rd, writes=wr, inc=stop)

        ident = sb(top, "ident", [128, 128], BF16)
        onesb = sb(top, "onesb", [128, 128], BF16)
        sel = sb(top, "sel", [65, 64], BF16)
        o64 = sb(top, "o64", [64, 64], BF16)
        modT = sb(top, "modT", [128, 32, 2], F32)
        sinkE = sb(top, "sinkE", [65, 8], F32)
        lamt = sb(top, "lamt", [65, 4], F32)
        lamw = sb(top, "lamw", [65, 4, 32], F32)
        gs = sb(top, "gs", [64, 2], F32)

        fw.op("pool", lambda: nc.gpsimd.memset(ident[:], 1.0), writes=[B("ident")])
        fw.op("pool", lambda: nc.gpsimd.affine_select(out=ident[:], in_=ident[:], pattern=[[-1, 128]], compare_op=ALU.is_equal,
                                                     fill=0.0, base=0, channel_multiplier=1), reads=[B("ident")], writes=[B("ident")])
        fw.op("pool", lambda: nc.gpsimd.memset(onesb[:], 1.0), writes=[B("onesb")])
        fw.op("pool", lambda: nc.gpsimd.memset(sel[:], 0.0), writes=[B("sel")])
        fw.op("pool", lambda: nc.gpsimd.memset(sel[64:65, :], 1.0), reads=[B("sel")], writes=[B("sel")])
        fw.op("pool", lambda: nc.gpsimd.memset(o64[:], 1.0 / 64.0), writes=[B("o64")])

        def cast_weights(l):
            for kc in range(8):
                fw.dma("pool", s_ada[l, :, kc, :], w_ada[l, kc * 128:(kc + 1) * 128, :], writes=[B("s_ada%d_%d" % (l, kc))])
            for kc in range(8):
                fw.dma("pool", s_win[l, :, kc, :], win[l, kc * 128:(kc + 1) * 128, :], writes=[B("s_win%d_%d" % (l, kc))])
            for h in range(16):
                fw.dma("pool", s_wo[l, :, h, :], w_o[l, h * 64:(h + 1) * 64, :], writes=[B("s_wo%d_%d" % (l, h))])
            for kc in range(8):
                fw.dma("pool", s_wfi[l, :, kc, :], wfi[l, kc * 128:(kc + 1) * 128, :], writes=[B("s_wfi%d_%d" % (l, kc))])
            for c in range(22):
                fw.dma("pool", s_wfo[l, :, c, :], wfo[l, c * 128:(c + 1) * 128, :], writes=[B("s_wfo%d_%d" % (l, c))])

        cast_weights(0)

        def layer_norm(src, dst, lng, lnb, slot, rd, wr, ph):
            st_, mv, rs, nb = ph["lnst"][slot], ph["lnmv"][slot], ph["lnrs"][slot], ph["lnnb"][slot]
            bs = B("lnsm%d" % slot)
            fw.op("dve", lambda: nc.vector.bn_stats(out=st_[:, 0:6], in_=src[:, 0:512]), reads=rd, writes=[bs])
            fw.op("dve", lambda: nc.vector.bn_stats(out=st_[:, 6:12], in_=src[:, 512:1024]), reads=rd + [bs], writes=[bs])
            fw.op("dve", lambda: nc.vector.bn_aggr(out=mv[:, 0:2], in_=st_[:, 0:12]), reads=[bs], writes=[bs])
            fw.op("act", lambda: nc.scalar.activation(out=rs[:, 0:1], in_=mv[:, 1:2], func=AF.Sqrt, bias=EPS, scale=1.0), reads=[bs], writes=[bs])
            fw.op("dve", lambda: nc.vector.reciprocal(out=rs[:, 0:1], in_=rs[:, 0:1]), reads=[bs], writes=[bs])
            fw.op("dve", lambda: nc.vector.scalar_tensor_tensor(out=nb[:, 0:1], in0=mv[:, 0:1], scalar=-1.0, in1=rs[:, 0:1],
                                                               op0=ALU.mult, op1=ALU.mult), reads=[bs], writes=[bs])
            fw.op("act", lambda: nc.scalar.activation(out=src, in_=src, func=AF.Identity, bias=nb[:, 0:1], scale=rs[:, 0:1]),
                  reads=rd + [bs], writes=rd)
            fw.op("dve", lambda: nc.vector.tensor_tensor(out=dst, in0=src, in1=lng, op=ALU.mult), reads=rd + [B("lnp")], writes=wr)
            fw.op("dve", lambda: nc.vector.tensor_tensor(out=dst, in0=dst, in1=lnb, op=ALU.add), reads=wr + [B("lnp")], writes=wr)

        def transpose_mod(xb, nt, hT, mi, j, tag):
            N = nt * 128
            for c in range(8):
                pb, pbb = next_psb()
                for i in range(nt):
                    fw.op("pe", lambda i=i, c=c, pb=pb: nc.tensor.transpose(out=pb[:, i * 128:(i + 1) * 128],
                                                                          in_=xb[:, i, c * 128:(c + 1) * 128], identity=ident[:]),
                          reads=[B("xb%d" % i), B("ident")], writes=[pbb], inc=(i == nt - 1))
                fw.op("act", lambda c=c, pb=pb: nc.scalar.activation(out=hT[:, c, 0:N], in_=pb[:, 0:N], func=AF.Identity,
                                                                    bias=modT[:, mi * 8 + c, j:j + 1],
                                                                    scale=modT[:, (mi + 1) * 8 + c, j:j + 1]),
                      reads=[pbb, B("modT")], writes=[B("%s%d" % (tag, c))])

        cur_in = xin
        for l in range(nlayers):
            last = (l == nlayers - 1) and (nlayers == DEPTH)
            lam_init = 0.8 - 0.6 * math.exp(-0.3 * l)
            dst = out if last else s_x1
            nblk = 4 if last else 5

            with ExitStack() as ph:
                alloc_psum(ph, 6, 2)
                wada = sb(ph, "wada", [128, 8, 6 * D], BF16)
                cTt = sb(ph, "cTt", [128, 16], F32)
                scf = sb(ph, "scf", [128, 16], F32)
                scb = sb(ph, "scb", [128, 8, 2], BF16)
                rep = [sb(ph, "rep%d" % j, [128, 8, 128], BF16) for j in range(2)]
                bT = sb(ph, "bT", [128, 48], F32)
                bg = sb(ph, "bg", [128, 2, D], F32)
                Gt = [sb(ph, "Gt%d" % i, [128, D], F32) for i in range(2)]
                for kc in range(8):
                    fw.dma("sp", wada[:, kc, :], s_ada[l, :, kc, :], reads=[B("s_ada%d_%d" % (l, kc))], writes=[B("wada%d" % kc)])
                fw.dma("sp", cTt[:], cT, writes=[B("cTt")])
                fw.dma("sp", bT[:], b_adaT[l], writes=[B("bT")])
                for gi, m in enumerate((2, 5)):
                    fw.dma("sp", bg[:, gi, :], b_ada[l, m * D:(m + 1) * D].partition_broadcast(128), writes=[B("bg")])
                fw.dma("sp", sinkE[64:65, 0:6], sink[l:l + 1, :], writes=[B("sinkE")])
                fw.dma("sp", lamw[64:65, :, :], lamv[l:l + 1, :, :], writes=[B("lamw")])
                fw.dma("sp", gs[:, 0:1], subg[l].rearrange("(d o) -> d o", o=1), writes=[B("gs")])
                fw.op("act", lambda: nc.scalar.activation(out=sinkE[64:65, 0:6], in_=sinkE[64:65, 0:6], func=AF.Exp),
                      reads=[B("sinkE")], writes=[B("sinkE")])
                fw.op("dve", lambda: nc.vector.tensor_tensor(out=lamw[64:65, 0, :], in0=lamw[64:65, 0, :], in1=lamw[64:65, 1, :], op=ALU.mult),
                      reads=[B("lamw")], writes=[B("lamw")])
                fw.op("dve", lambda: nc.vector.tensor_tensor(out=lamw[64:65, 2, :], in0=lamw[64:65, 2, :], in1=lamw[64:65, 3, :], op=ALU.mult),
                      reads=[B("lamw")], writes=[B("lamw")])
                fw.op("dve", lambda: nc.vector.reduce_sum(out=lamt[64:65, 0:1], in_=lamw[64:65, 0, :], axis=mybir.AxisListType.X),
                      reads=[B("lamw")], writes=[B("lamt")])
                fw.op("dve", lambda: nc.vector.reduce_sum(out=lamt[64:65, 1:2], in_=lamw[64:65, 2, :], axis=mybir.AxisListType.X),
                      reads=[B("lamw"), B("lamt")], writes=[B("lamt")])
                fw.op("act", lambda: nc.scalar.activation(out=lamt[64:65, 0:2], in_=lamt[64:65, 0:2], func=AF.Exp),
                      reads=[B("lamt")], writes=[B("lamt")])
                fw.op("dve", lambda: nc.vector.tensor_tensor(out=lamt[64:65, 2:3], in0=lamt[64:65, 0:1], in1=lamt[64:65, 1:2], op=ALU.subtract),
                      reads=[B("lamt")], writes=[B("lamt")])
                fw.op("dve", lambda: nc.vector.tensor_scalar(out=lamt[64:65, 3:4], in0=lamt[64:65, 2:3], scalar1=float(lam_init), scalar2=None,
                                                            op0=ALU.add), reads=[B("lamt")], writes=[B("lamt")])
                fw.op("dve", lambda: nc.vector.tensor_scalar(out=gs[:, 1:2], in0=gs[:, 0:1], scalar1=float(1.0 - lam_init), scalar2=None,
                                                            op0=ALU.mult), reads=[B("gs")], writes=[B("gs")])
                fw.op("act", lambda: nc.scalar.activation(out=scf[:], in_=cTt[:], func=AF.Silu), reads=[B("cTt")], writes=[B("scf")])
                fw.op("dve", lambda: nc.vector.tensor_copy(out=scb[:].rearrange("p k j -> p (k j)"), in_=scf[:]), reads=[B("scf")], writes=[B("scb")])
                for j in range(2):
                    for kc in range(8):
                        fw.op("dve", lambda j=j, kc=kc: nc.vector.tensor_scalar(out=rep[j][:, kc, :], in0=onesb[:, :],
                                                                               scalar1=scf[:, kc * 2 + j:kc * 2 + j + 1], scalar2=None, op0=ALU.mult),
                              reads=[B("scf"), B("onesb")], writes=[B("rep%d" % j)])
                ps, psb_ = next_ps()
                for mi, m in enumerate((0, 1, 3, 4)):
                    for c in range(8):
                        col0 = m * D + c * 128
                        o = (mi * 8 + c) * 2
                        for kc in range(8):
                            mm(ps[:, o:o + 2], wada[:, kc, col0:col0 + 128], scb[:, kc, 0:2], kc == 0, kc == 7,
                               [B("wada%d" % kc), B("scb")], [psb_])
                psv = ps[:, 0:64].rearrange("p (a j) -> p a j", j=2)
                for j in range(2):
                    for (a0, b0) in ((0, 0), (16, 24)):
                        fw.op("dve", lambda j=j, a0=a0, b0=b0: nc.vector.tensor_tensor(out=modT[:, a0:a0 + 16, j], in0=psv[:, a0:a0 + 16, j],
                                                                                       in1=bT[:, b0:b0 + 16], op=ALU.add),
                              reads=[psb_, B("bT")], writes=[B("modT")])
                for a0 in (8, 24):
                    fw.op("dve", lambda a0=a0: nc.vector.tensor_scalar(out=modT[:, a0:a0 + 8, :], in0=modT[:, a0:a0 + 8, :], scalar1=1.0,
                                                                      scalar2=None, op0=ALU.add), reads=[B("modT")], writes=[B("modT")])
                for j in range(2 if not last else 1):
                    for gi, m in enumerate((2, 5)):
                        for half in range(2):
                            ps, psb_ = next_ps()
                            for kc in range(8):
                                mm(ps[:, :], rep[j][:, kc, :], wada[:, kc, m * D + half * 512:m * D + (half + 1) * 512], kc == 0, kc == 7,
                                   [B("rep%d" % j), B("wada%d" % kc)], [psb_])
                            fw.op("dve", lambda ps=ps, gi=gi, half=half: nc.vector.tensor_tensor(
                                out=Gt[gi][:, half * 512:(half + 1) * 512], in0=ps[:, :], in1=bg[:, gi, half * 512:(half + 1) * 512], op=ALU.add),
                                reads=[psb_, B("bg")], writes=[B("Gt%d" % gi)])
                        fw.dma("pool", s_g[j, gi], Gt[gi][:], reads=[B("Gt%d" % gi)], writes=[B("s_g%d%d" % (j, gi))])
                fw.barrier()

            kvs = ExitStack()
            KT = sb(kvs, "KT", [128, 8, T], BF16)
            Vx = sb(kvs, "Vx", [128, NT, 12, 66], BF16)
            fw.op("pool", lambda: nc.gpsimd.memset(Vx[:], 0.0), writes=[B("Vx")])
            fw.op("pool", lambda: nc.gpsimd.memset(Vx[:, :, :, 64:65], 1.0), reads=[B("Vx")], writes=[B("Vx")])
            with ExitStack() as ph:
                alloc_psum(ph, 6, 2)
                Win = sb(ph, "Win", [128, 8, NCOL], BF16)
                xsb = [sb(ph, "xsb%d" % s, [128, 4, D], F32) for s in range(2)]
                xb = sb(ph, "xb", [128, 4, D], BF16)
                hT = sb(ph, "hT", [128, 8, 512], BF16)
                rt = sb(ph, "rt", [128, 8, 512], F32)
                t1 = [sb(ph, "t1_%d" % s, [128, 512], F32) for s in range(2)]
                t2 = [sb(ph, "t2_%d" % s, [128, 512], F32) for s in range(2)]
                qst = [sb(ph, "qst%d" % s, [128, 512], BF16) for s in range(2)]
                for kc in range(8):
                    fw.dma("sp", Win[:, kc, :], s_win[l, :, kc, :], reads=[B("s_win%d_%d" % (l, kc))], writes=[B("Win%d" % kc)])
                WinB = [B("Win%d" % kc) for kc in range(8)]
                tcnt = [0]

                def load_x(bi):
                    nt = 4 if bi < 4 else 2
                    s = bi % 2
                    fw.dma("sp", xsb[s][:, 0:nt, :], cur_in[bi * 512:bi * 512 + nt * 128, :].rearrange("(i p) d -> p i d", p=128),
                           reads=[B("xs_%d_%d" % (l, bi))], writes=[B("xsb%d" % s)])

                load_x(0)
                for bi in range(5):
                    nt = 4 if bi < 4 else 2
                    N = nt * 128
                    tok0 = bi * 512
                    j = 0 if bi < 4 else 1
                    s = bi % 2
                    if bi + 1 < 5:
                        load_x(bi + 1)
                    for k in range(8):
                        fw.dma("sp", rt[:, k, 0:N], rope[k, :, tok0:tok0 + N], writes=[B("rt%d" % k)])
                    for i in range(nt):
                        fw.op("act" if i % 2 else "dve",
                              (lambda i=i: nc.scalar.copy(out=xb[:, i, :], in_=xsb[s][:, i, :])) if i % 2 else
                              (lambda i=i: nc.vector.tensor_copy(out=xb[:, i, :], in_=xsb[s][:, i, :])),
                              reads=[B("xsb%d" % s)], writes=[B("xb%d" % i)])
                    transpose_mod(xb, nt, hT, 0, j, "hT")
                    hTB = [B("hT%d" % c) for c in range(8)]

                    def proj(ch):
                        ps, pb_ = next_ps()
                        for kc in range(8):
                            mm(ps[:, 0:N], Win[:, kc, ch * 128:(ch + 1) * 128], hT[:, kc, 0:N], kc == 0, kc == 7, [WinB[kc], hTB[kc]], [pb_])
                        return ps, pb_

                    def roped(ch, chp, tc, ts, dst_ap, dst_b):
                        pa, pab = proj(ch)
                        pp, ppb = proj(chp)
                        rope_comb(pa, pab, pp, ppb, tc, ts, dst_ap, dst_b)

                    def rope_comb(pa, pab, pp, ppb, tc, ts, dst_ap, dst_b):
                        k = tcnt[0] % 2
                        tcnt[0] += 1
                        fw.op("dve", lambda: nc.vector.tensor_tensor(out=t1[k][:, 0:N], in0=pa[:, 0:N], in1=rt[:, tc, 0:N], op=ALU.mult),
                              reads=[pab, B("rt%d" % tc)], writes=[B("t1_%d" % k)])
                        fw.op("dve", lambda: nc.vector.tensor_tensor(out=t2[k][:, 0:N], in0=pp[:, 0:N], in1=rt[:, ts, 0:N], op=ALU.mult),
                              reads=[ppb, B("rt%d" % ts)], writes=[B("t2_%d" % k)])
                        fw.op("pool", lambda: nc.gpsimd.tensor_tensor(out=dst_ap, in0=t1[k][:, 0:N], in1=t2[k][:, 0:N], op=ALU.add),
                              reads=[B("t1_%d" % k), B("t2_%d" % k)], writes=dst_b)

                    def q_out(qi, fn):
                        k = tcnt[0] % 2
                        fn(qst[k][:, 0:N], [B("qst%d" % k)])
                        fw.dma("sp", s_q[qi, :, tok0:tok0 + N], qst[k][:, 0:N], reads=[B("qst%d" % k)], writes=[B("s_q%d" % bi)])

                    need_q = (bi < 4) or (not last)
                    if need_q:
                        for jq in range(3):
                            q_out(jq, lambda ap, bb, jq=jq: roped(jq, 4 + jq, 0, 1, ap, bb))
                    roped(3, 7, 0, 1, KT[:, 0, tok0:tok0 + N], [B("KT0_%d" % bi)])
                    if need_q:
                        for c in range(2):
                            q_out(3 + c, lambda ap, bb, c=c: roped(8 + c, 12 + c, 2, 3, ap, bb))
                    for c in range(2):
                        pa, pab = proj(10 + c)
                        pp, ppb = proj(14 + c)
                        rope_comb(pa, pab, pp, ppb, 4, 5, KT[:, 1 + c, tok0:tok0 + N], [B("KT%d_%d" % (1 + c, bi))])
                        rope_comb(pa, pab, pp, ppb, 6, 7, KT[:, 3 + c, tok0:tok0 + N], [B("KT%d_%d" % (3 + c, bi))])
                    if need_q:
                        for c in range(3):
                            def cq(ap, bb, c=c):
                                ps, pb_ = proj(16 + c)
                                fw.op("act", lambda: nc.scalar.copy(out=ap, in_=ps[:, 0:N]), reads=[pb_], writes=bb)
                            q_out(5 + c, cq)
                            tcnt[0] += 1
                    for c in range(3):
                        ps, pb_ = proj(19 + c)
                        fw.op("act", lambda ps=ps, c=c: nc.scalar.copy(out=KT[:, 5 + c, tok0:tok0 + N], in_=ps[:, 0:N]),
                              reads=[pb_], writes=[B("KT%d_%d" % (5 + c, bi))])
                    for i in range(nt):
                        gt = bi * 4 + i
                        ps, pb_ = next_ps()
                        for kc in range(8):
                            mm(ps[:, :], hT[:, kc, i * 128:(i + 1) * 128], Win[:, kc, NCH * 128:NCH * 128 + 512], kc == 0, kc == 7,
                               [WinB[kc], hTB[kc]], [pb_])
                        fw.op("act", lambda ps=ps, gt=gt: nc.scalar.copy(out=Vx[:, gt, 0:8, 0:64], in_=ps[:, :].rearrange("p (h d) -> p h d", d=64)),
                              reads=[pb_], writes=[B("Vx")])
                        ps, pb_ = next_ps()
                        for kc in range(8):
                            mm(ps[:, 0:256], hT[:, kc, i * 128:(i + 1) * 128], Win[:, kc, NCH * 128 + 512:NCH * 128 + 768], kc == 0, kc == 7,
                               [WinB[kc], hTB[kc]], [pb_])
                        fw.op("dve", lambda ps=ps, gt=gt: nc.vector.tensor_copy(out=Vx[:, gt, 8:12, 0:64],
                                                                                in_=ps[:, 0:256].rearrange("p (h d) -> p h d", d=64)),
                              reads=[pb_], writes=[B("Vx")])
                fw.barrier()

            with ExitStack() as ph:
                alloc_psum(ph, 8, 0)
                Et = sb(ph, "Et", [128, 6, NV * 256], BF16)
                nmk = sb(ph, "nmk", [128, NV * 256], BF16)
                MA = sb(ph, "MA", [128, 768], BF16)
                QTb = [sb(ph, "QTb%d" % s, [128, 8, 512], BF16) for s in range(2)]
                OTb = sb(ph, "OTb", [64, 16, 512], BF16)
                NPT = 6
                PT = [sb(ph, "PT%d" % s, [128, 512], BF16) for s in range(NPT)]
                rsf = [sb(ph, "rsf%d" % s, [65, 512], F32) for s in range(2)]
                rb = [sb(ph, "rb%d" % s, [65, 512], BF16) for s in range(3)]
                bcs = [sb(ph, "bcs%d" % s, [64, 512], F32) for s in range(3)]
                tA = [sb(ph, "tA%d" % s, [64, 512], F32) for s in range(3)]
                sq = sb(ph, "sq", [64, 512], BF16)
                cnt = {"pt": 0, "rb": 0, "rsf": 0, "bc": 0}
                if l + 1 < nlayers:
                    cast_weights(l + 1)
                with ExitStack() as sub:
                    gstg = sb(sub, "gstg", [128, NV * 256], F32)
                    fw.dma("pool", nmk[:], nmask, writes=[B("nmk")])
                    fw.dma("pool", MA[:], amask, writes=[B("MA")])
                    for h in range(6):
                        fw.dma("sp", gstg[:], nag[l, h], writes=[B("gstg")])
                        fw.op("act", lambda h=h: nc.scalar.activation(out=Et[:, h, :], in_=gstg[:], func=AF.Exp), reads=[B("gstg")], writes=[B("Et%d" % h)])
                        fw.op("dve", lambda h=h: nc.vector.tensor_tensor(out=Et[:, h, :], in0=Et[:, h, :], in1=nmk[:], op=ALU.mult),
                              reads=[B("Et%d" % h), B("nmk")], writes=[B("Et%d" % h)])
                    for s in range(3):
                        fw.op("pool", lambda s=s: nc.gpsimd.memset(rb[s][:], 0.0), writes=[B("rb%d" % s)])
                    fw.barrier()

                def load_q(bi):
                    nt = 4 if bi < 4 else 2
                    s = bi % 2
                    fw.dma("sp", QTb[s][:, :, 0:nt * 128], s_q[:, :, bi * 512:bi * 512 + nt * 128].rearrange("c p t -> p c t"),
                           reads=[B("s_q%d" % bi)], writes=[B("QTb%d" % s)])

                def new_pt():
                    k = cnt["pt"] % NPT
                    cnt["pt"] += 1
                    return PT[k], B("PT%d" % k)

                def normalise(po, pob, N, recips, outs):
                    res = []
                    for (k,) in recips:
                        pbc, pbcb = next_ps("s")
                        mm(pbc[0:64, 0:N], sel[0:65, 0:64], rb[k][0:65, 0:N], True, True, [B("sel"), B("rb%d" % k)], [pbcb])
                        kk = cnt["bc"] % 3
                        cnt["bc"] += 1
                        fw.op("act", lambda pbc=pbc, kk=kk: nc.scalar.copy(out=bcs[kk][:, 0:N], in_=pbc[0:64, 0:N]), reads=[pbcb], writes=[B("bcs%d" % kk)])
                        res.append(kk)
                    return res

                nqb = 4 if last else 5
                load_q(0)
                for bi in range(nqb):
                    ctxq = bi == 4
                    nt = 4 if bi < 4 else 2
                    N = nt * 128
                    tok0 = bi * 512
                    s = bi % 2
                    if bi + 1 < nqb:
                        load_q(bi + 1)
                    Q = QTb[s]
                    Qb = B("QTb%d" % s)
                    for n in range(nt):
                        gq = bi * 4 + n
                        for g in range(2):
                            if ctxq:
                                tiles = [(16, None), (17, None)]
                            else:
                                tiles = []
                                if gq - 1 >= 0:
                                    tiles.append((gq - 1, 0))
                                tiles.append((gq, None))
                                if gq + 1 < 16:
                                    tiles.append((gq + 1, 1))
                                tiles += [(16, None), (17, None)]
                            po, pob = next_ps("a")
                            for ti, (t, mk) in enumerate(tiles):
                                ps, psb_ = next_ps("s")
                                mm(ps[:, 0:384].rearrange("p (a b) -> p a b", a=3), KT[g * 64:(g + 1) * 64, 0, t * 128:(t + 1) * 128],
                                   Q[g * 64:(g + 1) * 64, 0:3, n * 128:(n + 1) * 128], True, True, [B("KT0_%d" % (t // 4)), Qb], [psb_])
                                pt, ptb = new_pt()
                                fw.op("act", lambda ps=ps, pt=pt: nc.scalar.activation(out=pt[:, 0:384], in_=ps[:, 0:384], func=AF.Exp, scale=0.125),
                                      reads=[psb_], writes=[ptb])
                                if mk is not None:
                                    fw.op("dve", lambda pt=pt, mk=mk: nc.vector.tensor_tensor(out=pt[:, 0:384], in0=pt[:, 0:384],
                                                                                             in1=MA[:, mk * 384:(mk + 1) * 384], op=ALU.mult),
                                          reads=[ptb, B("MA")], writes=[ptb])
                                mm(po[0:65, 0:384], Vx[:, t, g, 0:65], pt[:, 0:384], ti == 0, ti == len(tiles) - 1, [B("Vx"), ptb], [pob])
                            kr = cnt["rsf"] % 2
                            cnt["rsf"] += 1
                            for sg in range(3):
                                h = 3 * g + sg
                                fw.op("dve", lambda po=po, sg=sg, h=h, kr=kr: nc.vector.tensor_scalar(
                                    out=rsf[kr][64:65, sg * 128:(sg + 1) * 128], in0=po[64:65, sg * 128:(sg + 1) * 128],
                                    scalar1=sinkE[64:65, h:h + 1], scalar2=None, op0=ALU.add),
                                    reads=[pob, B("sinkE")], writes=[B("rsf%d" % kr)])
                            k = cnt["rb"] % 3
                            cnt["rb"] += 1
                            fw.op("dve", lambda kr=kr, k=k: nc.vector.reciprocal(out=rb[k][64:65, 0:384], in_=rsf[kr][64:65, 0:384]),
                                  reads=[B("rsf%d" % kr)], writes=[B("rb%d" % k)])
                            (kk,) = normalise(None, None, 384, [(k,)], None)
                            fw.op("dve", lambda po=po, kk=kk, g=g, n=n: nc.vector.tensor_tensor(
                                out=OTb[0:64, 3 * g:3 * g + 3, n * 128:(n + 1) * 128], in0=po[0:64, 0:384].rearrange("p (a b) -> p a b", a=3),
                                in1=bcs[kk][:, 0:384].rearrange("p (a b) -> p a b", a=3), op=ALU.mult),
                                reads=[pob, B("bcs%d" % kk)], writes=[B("OTb")])
                    tilesB = [16, 17] if ctxq else list(range(18))
                    for h in range(4):
                        r0 = (h % 2) * 64
                        pos = [next_ps("a") for _ in range(2)]
                        for ti, t in enumerate(tilesB):
                            for m in range(2):
                                ps, psb_ = next_ps("s")
                                mm(ps[:, 0:N], KT[r0:r0 + 64, 1 + 2 * m + h // 2, t * 128:(t + 1) * 128], Q[r0:r0 + 64, 3 + h // 2, 0:N],
                                   True, True, [B("KT%d_%d" % (1 + 2 * m + h // 2, t // 4)), Qb], [psb_])
                                pt, ptb = new_pt()
                                fw.op("act", lambda ps=ps, pt=pt: nc.scalar.activation(out=pt[:, 0:N], in_=ps[:, 0:N], func=AF.Exp, scale=float(32 ** -0.5)),
                                      reads=[psb_], writes=[ptb])
                                mm(pos[m][0][0:65, 0:N], Vx[:, t, 2 + h, 0:65], pt[:, 0:N], ti == 0, ti == len(tilesB) - 1, [B("Vx"), ptb], [pos[m][1]])
                        ks = []
                        for m in range(2):
                            k = cnt["rb"] % 3
                            cnt["rb"] += 1
                            ks.append(k)
                            if m == 0:
                                fw.op("dve", lambda k=k: nc.vector.reciprocal(out=rb[k][64:65, 0:N], in_=pos[0][0][64:65, 0:N]),
                                      reads=[pos[0][1]], writes=[B("rb%d" % k)])
                            else:
                                kr = cnt["rsf"] % 2
                                cnt["rsf"] += 1
                                fw.op("dve", lambda kr=kr: nc.vector.reciprocal(out=rsf[kr][64:65, 0:N], in_=pos[1][0][64:65, 0:N]),
                                      reads=[pos[1][1]], writes=[B("rsf%d" % kr)])
                                fw.op("dve", lambda kr=kr, k=k: nc.vector.tensor_scalar(out=rb[k][64:65, 0:N], in0=rsf[kr][64:65, 0:N],
                                                                                       scalar1=lamt[64:65, 3:4], scalar2=None, op0=ALU.mult),
                                      reads=[B("rsf%d" % kr), B("lamt")], writes=[B("rb%d" % k)])
                        kks = normalise(None, None, N, [(ks[0],), (ks[1],)], None)
                        for m in range(2):
                            fw.op("dve", lambda m=m: nc.vector.tensor_tensor(out=tA[m][:, 0:N], in0=pos[m][0][0:64, 0:N], in1=bcs[kks[m]][:, 0:N], op=ALU.mult),
                                  reads=[pos[m][1], B("bcs%d" % kks[m])], writes=[B("tA%d" % m)])
                        fw.op("dve", lambda: nc.vector.tensor_tensor(out=tA[2][:, 0:N], in0=tA[0][:, 0:N], in1=tA[1][:, 0:N], op=ALU.subtract),
                              reads=[B("tA0"), B("tA1")], writes=[B("tA2")])
                        fw.op("dve", lambda: nc.vector.tensor_tensor(out=sq[:, 0:N], in0=tA[2][:, 0:N], in1=tA[2][:, 0:N], op=ALU.mult),
                              reads=[B("tA2")], writes=[B("sq")])
                        pms, pmsb = next_ps("s")
                        mm(pms[0:64, 0:N], o64[:, :], sq[:, 0:N], True, True, [B("o64"), B("sq")], [pmsb])
                        fw.op("act", lambda pms=pms: nc.scalar.activation(out=tA[0][:, 0:N], in_=pms[0:64, 0:N], func=AF.Sqrt, bias=EPS, scale=1.0),
                              reads=[pmsb], writes=[B("tA0")])
                        fw.op("dve", lambda: nc.vector.reciprocal(out=tA[0][:, 0:N], in_=tA[0][:, 0:N]), reads=[B("tA0")], writes=[B("tA0")])
                        fw.op("dve", lambda h=h: nc.vector.scalar_tensor_tensor(out=OTb[0:64, 6 + h, 0:N], in0=tA[2][:, 0:N], scalar=gs[:, 1:2],
                                                                               in1=tA[0][:, 0:N], op0=ALU.mult, op1=ALU.mult),
                              reads=[B("tA2"), B("tA0"), B("gs")], writes=[B("OTb")])
                    for h in range(6):
                        r0 = (h % 2) * 64
                        for jj in range(1 if ctxq else 2):
                            if ctxq:
                                tiles = [(16, None), (17, None)]
                            else:
                                jg = bi * 2 + jj
                                if jg == 0:
                                    tiles = [(u, 6 + u) for u in range(4)]
                                elif jg == 7:
                                    tiles = [(u, 10 + u - 12) for u in range(12, 16)]
                                else:
                                    tiles = [(2 * jg - 2 + i, i) for i in range(6)]
                                tiles += [(16, None), (17, None)]
                            po, pob = next_ps("a")
                            for ti, (t, ev) in enumerate(tiles):
                                ps, psb_ = next_ps("s")
                                mm(ps[:, 0:256], KT[r0:r0 + 64, 5 + h // 2, t * 128:(t + 1) * 128], Q[r0:r0 + 64, 5 + h // 2, jj * 256:(jj + 1) * 256],
                                   True, True, [B("KT%d_%d" % (5 + h // 2, t // 4)), Qb], [psb_])
                                pt, ptb = new_pt()
                                fw.op("act", lambda ps=ps, pt=pt: nc.scalar.activation(out=pt[:, 0:256], in_=ps[:, 0:256], func=AF.Exp, scale=0.125),
                                      reads=[psb_], writes=[ptb])
                                if ev is not None:
                                    fw.op("dve", lambda pt=pt, ev=ev, h=h: nc.vector.tensor_tensor(out=pt[:, 0:256], in0=pt[:, 0:256],
                                                                                                  in1=Et[:, h, ev * 256:(ev + 1) * 256], op=ALU.mult),
                                          reads=[ptb, B("Et%d" % h)], writes=[ptb])
                                mm(po[0:65, 0:256], Vx[:, t, 6 + h, 0:65], pt[:, 0:256], ti == 0, ti == len(tiles) - 1, [B("Vx"), ptb], [pob])
                            k = cnt["rb"] % 3
                            cnt["rb"] += 1
                            fw.op("dve", lambda po=po, k=k: nc.vector.reciprocal(out=rb[k][64:65, 0:256], in_=po[64:65, 0:256]),
                                  reads=[pob], writes=[B("rb%d" % k)])
                            (kk,) = normalise(None, None, 256, [(k,)], None)
                            fw.op("dve", lambda po=po, kk=kk, h=h, jj=jj: nc.vector.tensor_tensor(
                                out=OTb[0:64, 10 + h, jj * 256:(jj + 1) * 256], in0=po[0:64, 0:256], in1=bcs[kk][:, 0:256], op=ALU.mult),
                                reads=[pob, B("bcs%d" % kk)], writes=[B("OTb")])
                    fw.dma("pool", s_o[:, :, tok0:tok0 + N].rearrange("h p t -> p h t"), OTb[:, :, 0:N], reads=[B("OTb")], writes=[B("s_o%d" % bi)])
                fw.barrier()
            kvs.close()

            with ExitStack() as ph:
                alloc_psum(ph, 6, 2)
                Wo = sb(ph, "Wo", [64, 16, D], BF16)
                xsb = [sb(ph, "xsb%d" % s, [128, 4, D], F32) for s in range(2)]
                OT2 = sb(ph, "OT2", [64, 16, 512], BF16)
                ub = sb(ph, "ub", [128, 4, D], F32)
                xb = sb(ph, "xb", [128, 4, D], BF16)
                hT = sb(ph, "hT", [128, 8, 512], BF16)
                AT = sb(ph, "AT", [128, 22, 512], BF16)
                Wi = [sb(ph, "Wi%d" % s, [128, 8, 512], BF16) for s in range(2)]
                Wf = [sb(ph, "Wf%d" % s, [128, 11, 512], BF16) for s in range(2)]
                lnt = sb(ph, "lnt", [128, 4, D], F32)
                Gx = sb(ph, "Gx", [128, 2, D], F32)
                sgt = [sb(ph, "sgt%d" % s, [128, 512], F32) for s in range(2)]
                ph_ln = {"lnst": [sb(ph, "lnst%d" % s, [128, 12], F32) for s in range(4)],
                         "lnmv": [sb(ph, "lnmv%d" % s, [128, 2], F32) for s in range(4)],
                         "lnrs": [sb(ph, "lnrs%d" % s, [128, 1], F32) for s in range(4)],
                         "lnnb": [sb(ph, "lnnb%d" % s, [128, 1], F32) for s in range(4)]}
                lncnt = [0]
                wic = [0]
                wfc = [0]
                fw.dma("sp", Wo[:], s_wo[l], reads=[B("s_wo%d_%d" % (l, h)) for h in range(16)], writes=[B("Wo")])
                for k in range(4):
                    fw.dma("sp", lnt[:, k, :], lnp[l, k].partition_broadcast(128), writes=[B("lnp")])

                def load_blk(bi):
                    nt = 4 if bi < 4 else 2
                    s = bi % 2
                    fw.dma("sp", xsb[s][:, 0:nt, :], cur_in[bi * 512:bi * 512 + nt * 128, :].rearrange("(i p) d -> p i d", p=128),
                           reads=[B("xs_%d_%d" % (l, bi))], writes=[B("xsb%d_%d" % (s, i)) for i in range(4)])

                def load_wi(cp):
                    s = wic[0] % 2
                    wic[0] += 1
                    fw.dma("sp", Wi[s][:], s_wfi[l, :, :, cp * 512:(cp + 1) * 512], reads=[B("s_wfi%d_%d" % (l, kc)) for kc in range(8)], writes=[B("Wi%d" % s)])
                    return s

                def load_wf(half, cg):
                    s = wfc[0] % 2
                    wfc[0] += 1
                    fw.dma("sp", Wf[s][:], s_wfo[l, :, cg * 11:(cg + 1) * 11, half * 512:(half + 1) * 512], reads=[B("s_wfo%d_%d" % (l, cg * 11 + cl)) for cl in range(11)], writes=[B("Wf%d" % s)])
                    return s

                def gated_res_ln(ps, pb_, i, half, gi, s):
                    cs = slice(half * 512, (half + 1) * 512)
                    fw.op("dve", lambda: nc.vector.tensor_tensor(out=ub[:, i, cs], in0=ps[:, :], in1=Gx[:, gi, cs], op=ALU.mult),
                          reads=[pb_, B("Gx")], writes=[B("ub%d" % i)])
                    fw.op("dve", lambda: nc.vector.scalar_tensor_tensor(out=ub[:, i, cs], in0=xsb[s][:, i, cs], scalar=float(ALPHA), in1=ub[:, i, cs],
                                                                       op0=ALU.mult, op1=ALU.add),
                          reads=[B("ub%d" % i), B("xsb%d_%d" % (s, i))], writes=[B("ub%d" % i)])

                load_blk(0)
                for bi in range(nblk):
                    nt = 4 if bi < 4 else 2
                    N = nt * 128
                    tok0 = bi * 512
                    j = 0 if bi < 4 else 1
                    s = bi % 2
                    if bi == 0 or bi == 4:
                        for gi in range(2):
                            fw.dma("sp", Gx[:, gi, :], s_g[j, gi], reads=[B("s_g%d%d" % (j, gi))], writes=[B("Gx")])
                    fw.dma("sp", OT2[:, :, 0:N], s_o[:, :, tok0:tok0 + N].rearrange("h p t -> p h t"), reads=[B("s_o%d" % bi)], writes=[B("OT2")])
                    if bi + 1 < nblk:
                        load_blk(bi + 1)
                    w0 = load_wi(0)
                    for i in range(nt):
                        for half in range(2):
                            ps, pb_ = next_ps()
                            for h in range(16):
                                mm(ps[:, :], OT2[0:64, h, i * 128:(i + 1) * 128], Wo[0:64, h, half * 512:(half + 1) * 512], h == 0, h == 15,
                                   [B("OT2"), B("Wo")], [pb_])
                            gated_res_ln(ps, pb_, i, half, 0, s)
                        k = lncnt[0] % 4
                        lncnt[0] += 1
                        layer_norm(ub[:, i, :], xsb[s][:, i, :], lnt[:, 0, :], lnt[:, 1, :], k, [B("ub%d" % i)], [B("xsb%d_%d" % (s, i))], ph_ln)
                        fw.op("act", lambda i=i: nc.scalar.copy(out=xb[:, i, :], in_=xsb[s][:, i, :]), reads=[B("xsb%d_%d" % (s, i))], writes=[B("xb%d" % i)])
                    transpose_mod(xb, nt, hT, 2, j, "hT")
                    hTB = [B("hT%d" % c) for c in range(8)]
                    for cp in range(11):
                        ws = w0
                        if cp + 1 < 11:
                            w0 = load_wi(cp + 1)
                        for cc in range(2):
                            c = 2 * cp + cc
                            pg, pgb = next_ps()
                            for kc in range(8):
                                mm(pg[:, 0:N], Wi[ws][:, kc, cc * 256:cc * 256 + 128], hT[:, kc, 0:N], kc == 0, kc == 7, [B("Wi%d" % ws), hTB[kc]], [pgb])
                            pu, pub = next_ps()
                            for kc in range(8):
                                mm(pu[:, 0:N], Wi[ws][:, kc, cc * 256 + 128:cc * 256 + 256], hT[:, kc, 0:N], kc == 0, kc == 7, [B("Wi%d" % ws), hTB[kc]], [pub])
                            k = c % 2
                            fw.op("act", lambda pg=pg, k=k: nc.scalar.activation(out=sgt[k][:, 0:N], in_=pg[:, 0:N], func=AF.Silu), reads=[pgb], writes=[B("sgt%d" % k)])
                            fw.op("dve", lambda pu=pu, k=k, c=c: nc.vector.tensor_tensor(out=AT[:, c, 0:N], in0=pu[:, 0:N], in1=sgt[k][:, 0:N], op=ALU.mult),
                                  reads=[pub, B("sgt%d" % k)], writes=[B("AT%d" % c)])
                    f0 = load_wf(0, 0)
                    for half in range(2):
                        accs = [next_ps() for _ in range(nt)]
                        for cg in range(2):
                            fs = f0
                            if not (half == 1 and cg == 1):
                                f0 = load_wf(half if cg == 0 else half + 1, 1 - cg)
                            for i in range(nt):
                                for cl in range(11):
                                    c = cg * 11 + cl
                                    mm(accs[i][0][:, :], AT[:, c, i * 128:(i + 1) * 128], Wf[fs][:, cl, :], cg == 0 and cl == 0, cg == 1 and cl == 10,
                                       [B("AT%d" % c), B("Wf%d" % fs)], [accs[i][1]])
                        for i in range(nt):
                            gated_res_ln(accs[i][0], accs[i][1], i, half, 1, s)
                    for i in range(nt):
                        k = lncnt[0] % 4
                        lncnt[0] += 1
                        layer_norm(ub[:, i, :], xsb[s][:, i, :], lnt[:, 2, :], lnt[:, 3, :], k, [B("ub%d" % i)], [B("xsb%d_%d" % (s, i))], ph_ln)
                    fw.dma("pool", dst[tok0:tok0 + N, :].rearrange("(i p) d -> p i d", p=128), xsb[s][:, 0:nt, :],
                           reads=[B("xsb%d_%d" % (s, i)) for i in range(nt)], writes=[B("xs_%d_%d" % (l + 1, bi))])
                fw.barrier()
            cur_in = s_x1
    return nc


_NC_CACHE = {}


def _host_inputs(inputs):
    cst = _consts()
    f = np.float32
    g = {k: np.asarray(v) for k, v in inputs.items()}
    shared = {}
    shared["w_ada"] = np.ascontiguousarray(g["w_ada"], dtype=f)
    shared["b_ada"] = np.ascontiguousarray(g["b_ada"], dtype=f)
    shared["b_adaT"] = np.ascontiguousarray(g["b_ada"].reshape(DEPTH, 48, 128).transpose(0, 2, 1), dtype=f)
    shared["win"] = np.ascontiguousarray(g["w_in"][:, :, cst["cols"]], dtype=f)
    shared["w_o"] = np.ascontiguousarray(g["w_o"], dtype=f)
    shared["sink"] = np.ascontiguousarray(g["sink"], dtype=f)
    shared["lamv"] = np.ascontiguousarray(np.stack([g["lam_q1"], g["lam_k1"], g["lam_q2"], g["lam_k2"]], axis=1), dtype=f)
    shared["subg"] = np.ascontiguousarray(g["subln_g"], dtype=f)
    nb = g["na_bias"]
    gath = nb[:, :, cst["dr"], cst["dc"]]
    shared["nag"] = np.ascontiguousarray(gath.transpose(0, 1, 3, 2, 4).reshape(DEPTH, 6, 128, NV * 256), dtype=f)
    shared["nmask"] = cst["nmask"]
    shared["amask"] = cst["amask"]
    shared["lnp"] = np.ascontiguousarray(np.stack([g["ln1_g"], g["ln1_b"], g["ln2_g"], g["ln2_b"]], axis=1), dtype=f)
    shared["wfi"] = np.ascontiguousarray(g["w_ffn_in"][:, :, cst["ffperm"]], dtype=f)
    shared["wfo"] = np.ascontiguousarray(g["w_ffn_out"], dtype=f)
    shared["rope"] = cst["rope"]
    per = []
    for b in range(g["x"].shape[0]):
        m = dict(shared)
        m["xin"] = np.ascontiguousarray(np.concatenate([g["x"][b], g["ctx"][b]], axis=0), dtype=f)
        cc = np.stack([g["c"][b].reshape(8, 128).T, g["c_ctx"].reshape(8, 128).T], axis=-1)
        m["cT"] = np.ascontiguousarray(cc.reshape(128, 16), dtype=f)
        per.append(m)
    return per


def kernel(**inputs):
    per = _host_inputs(inputs)
    if "nc" not in _NC_CACHE:
        _NC_CACHE["nc"] = build()
    nc = _NC_CACHE["nc"]
    n = len(per)
    res = run_bass_kernel_spmd(nc, per, core_ids=list(range(n)))
    return np.stack([np.asarray(r["out"]) for r in res.results], axis=0).astype(np.float32)
```

```python
import math
from contextlib import ExitStack

import numpy as np
import concourse.bass as bass
import concourse.mybir as mybir
from concourse.bass_utils import run_bass_kernel_spmd

F32 = mybir.dt.float32
BF16 = mybir.dt.bfloat16
AF = mybir.ActivationFunctionType
ALU = mybir.AluOpType

D = 1024
L = 2048
C = 256
T = L + C
NT = T // 128
DFF = 2816
NCH = 22
NCOL = NCH * 128 + 768
DEPTH = 2
ALPHA = (2.0 * DEPTH) ** 0.25
EPS = 1e-5
NV = 14


class Buf:
    __slots__ = ("name", "w", "r")

    def __init__(self, name):
        self.name = name
        self.w = None
        self.r = {}


class FW:
    NDMA = 32

    def __init__(self, nc, stack):
        self.nc = nc
        self.eng = {"pe": nc.tensor, "act": nc.scalar, "dve": nc.vector, "pool": nc.gpsimd, "sp": nc.sync}
        self.sems = {}
        self.cnt = {}
        self.known = {e: {} for e in self.eng}
        for e in ("pe", "act", "dve", "pool"):
            self.sems[e] = stack.enter_context(nc.semaphore("s_" + e))
            self.cnt[e] = 0
        self.dsems = []
        for i in range(self.NDMA):
            k = "dma%d" % i
            self.sems[k] = stack.enter_context(nc.semaphore(k))
            self.cnt[k] = 0
            self.dsems.append(k)
        self.rr = {"hw": 0, "sw": 0}
        self.bufs = {}

    def B(self, name):
        b = self.bufs.get(name)
        if b is None:
            b = self.bufs[name] = Buf(name)
        return b

    def _wait(self, e, toks):
        need = {}
        for t in toks:
            if t is None:
                continue
            k, v = t
            if e == "pe" and k == "pe":
                continue
            if self.known[e].get(k, 0) >= v:
                continue
            if need.get(k, 0) < v:
                need[k] = v
        for k, v in need.items():
            self.eng[e].wait_ge(self.sems[k], v)
            self.known[e][k] = v

    @staticmethod
    def _deps(reads, writes):
        toks = []
        for b in reads:
            toks.append(b.w)
        for b in writes:
            toks.append(b.w)
            for k, v in b.r.items():
                toks.append((k, v))
        return toks

    @staticmethod
    def _mark(tok, reads, writes):
        k, v = tok
        for b in reads:
            if b.r.get(k, 0) < v:
                b.r[k] = v
        for b in writes:
            b.w = tok
            b.r = {}

    def op(self, e, fn, reads=(), writes=(), inc=True):
        self._wait(e, self._deps(reads, writes))
        ins = fn()
        if inc:
            self.cnt[e] += 1
            ins.then_inc(self.sems[e], 1)
            tok = (e, self.cnt[e])
        else:
            tok = (e, self.cnt[e] + 1)
        self._mark(tok, reads, writes)
        return ins

    def dma(self, q, out, in_, reads=(), writes=()):
        kind = "sw" if q == "pool" else "hw"
        half = self.NDMA // 2
        sem = self.dsems[(0 if kind == "hw" else half) + self.rr[kind]]
        self.rr[kind] = (self.rr[kind] + 1) % half
        toks = self._deps(reads, writes)
        if self.cnt[sem] > 0:
            toks.append((sem, self.cnt[sem]))
        self._wait(q, toks)
        ins = self.eng[q].dma_start(out=out, in_=in_)
        self.cnt[sem] += 16
        ins.then_inc(self.sems[sem], 16)
        self._mark((sem, self.cnt[sem]), reads, writes)
        return ins

    def barrier(self):
        toks = [(k, v) for k, v in self.cnt.items() if v > 0]
        for e in self.eng:
            self._wait(e, toks)


def _win_cols():
    aq0, ak0, av0, bq0, bk0, bv0, cq0, ck0, cv0 = 0, 384, 512, 640, 896, 1152, 1408, 1792, 2176
    pA = np.array([d + 16 if d % 32 < 16 else d - 16 for d in range(64)])
    pB32 = np.array([d + 8 if d % 16 < 8 else d - 8 for d in range(32)])
    pB = np.concatenate([pB32, 32 + pB32])
    ar = np.arange(64)

    def hc(base, h, perm=None):
        return base + h * 64 + (ar if perm is None else perm)

    ch = []
    for j in range(3):
        ch.append(np.concatenate([hc(aq0, j), hc(aq0, 3 + j)]))
    ch.append(np.concatenate([hc(ak0, 0), hc(ak0, 1)]))
    for j in range(3):
        ch.append(np.concatenate([hc(aq0, j, pA), hc(aq0, 3 + j, pA)]))
    ch.append(np.concatenate([hc(ak0, 0, pA), hc(ak0, 1, pA)]))
    for base, perm in ((bq0, None), (bk0, None), (bq0, pB), (bk0, pB)):
        for c in range(2):
            ch.append(np.concatenate([hc(base, 2 * c, perm), hc(base, 2 * c + 1, perm)]))
    for base in (cq0, ck0):
        for c in range(3):
            ch.append(np.concatenate([hc(base, 2 * c), hc(base, 2 * c + 1)]))
    ch.append(np.arange(av0, av0 + 128))
    ch.append(np.arange(bv0, bv0 + 256))
    ch.append(np.arange(cv0, cv0 + 384))
    cols = np.concatenate(ch)
    assert cols.shape[0] == NCOL
    return cols


def _rope_tables():
    f = np.float32
    pos = np.arange(L)
    rows = (pos // 64).astype(f)
    cols = (pos % 64).astype(f)
    tabs = np.zeros((8, 128, T), f)
    tabs[0::2, :, L:] = 1.0
    p = np.arange(128)
    inv16 = np.power(f(10000.0), -np.arange(16, dtype=f) / f(16)).astype(f)
    d = p % 64
    ang = np.where((d < 32)[:, None], rows[None, :], cols[None, :]).astype(f) * inv16[(d % 32) % 16][:, None]
    ang = ang.astype(f)
    sgn = np.where((d % 32) < 16, -1.0, 1.0).astype(f)
    tabs[0, :, :L] = np.cos(ang)
    tabs[1, :, :L] = np.sin(ang) * sgn[:, None]
    inv8 = np.power(f(10000.0), -np.arange(8, dtype=f) / f(8)).astype(f)
    d = p % 32
    ang = np.where((d < 16)[:, None], rows[None, :], cols[None, :]).astype(f) * inv8[(d % 16) % 8][:, None]
    ang = ang.astype(f)
    sgn = np.where((d % 16) < 8, -1.0, 1.0).astype(f)
    tabs[2, :, :L] = np.cos(ang)
    tabs[3, :, :L] = np.sin(ang) * sgn[:, None]
    m0 = ((p % 64) < 32).astype(f)[:, None]
    tabs[4] = tabs[2] * m0
    tabs[5] = tabs[3] * m0
    tabs[6] = tabs[2] * (1 - m0)
    tabs[7] = tabs[3] * (1 - m0)
    return tabs


def _na_variants():
    R, W, KH, KW = 32, 64, 8, 16
    var = []
    for i in range(6):
        var.append(([12, 13, 14, 15], [8 + 2 * i, 9 + 2 * i]))
    for u in range(4):
        var.append(([0, 1, 2, 3], [2 * u, 2 * u + 1]))
    for u in range(12, 16):
        var.append(([28, 29, 30, 31], [2 * u, 2 * u + 1]))
    dr = np.zeros((NV, 128, 256), np.int64)
    dc = np.zeros((NV, 128, 256), np.int64)
    ok = np.zeros((NV, 128, 256), np.float32)
    kc = np.arange(64)[:, None]
    qc = np.arange(64)[None, :]
    cs = np.clip(qc - KW // 2, 0, W - KW)
    colv = (kc >= cs) & (kc < cs + KW)
    dcm = np.clip(kc - qc, -(KW - 1), KW - 1) + (KW - 1)
    for v, (qrows, krows) in enumerate(var):
        for kk, kr in enumerate(krows):
            for qq, r in enumerate(qrows):
                rs = min(max(r - KH // 2, 0), R - KH)
                rowv = (rs <= kr <= rs + KH - 1)
                dr[v, kk * 64:(kk + 1) * 64, qq * 64:(qq + 1) * 64] = min(max(kr - r + 7, 0), 14)
                dc[v, kk * 64:(kk + 1) * 64, qq * 64:(qq + 1) * 64] = dcm
                ok[v, kk * 64:(kk + 1) * 64, qq * 64:(qq + 1) * 64] = (colv & rowv).astype(np.float32)
    return dr, dc, ok


_CONST = {}


def _consts():
    if not _CONST:
        _CONST["cols"] = _win_cols()
        _CONST["rope"] = _rope_tables()
        dr, dc, ok = _na_variants()
        _CONST["dr"], _CONST["dc"] = dr, dc
        _CONST["nmask"] = np.ascontiguousarray(ok.transpose(1, 0, 2).reshape(128, NV * 256))
        j = np.arange(128)[:, None]
        i = np.arange(128)[None, :]
        prev = (j >= i).astype(np.float32)
        nxt = (j <= i).astype(np.float32)
        _CONST["amask"] = np.concatenate([np.tile(prev, (1, 3)), np.tile(nxt, (1, 3))], axis=1)
        ffperm = np.concatenate([np.concatenate([np.arange(c * 128, (c + 1) * 128), DFF + np.arange(c * 128, (c + 1) * 128)])
                                 for c in range(22)])
        _CONST["ffperm"] = ffperm
    return _CONST


def build(nlayers=DEPTH, dbg=False):
    nc = bass.Bass("TRN2", target_bir_lowering=False)

    def din(name, shape, dt=F32):
        return nc.dram_tensor(name, list(shape), dt, kind="ExternalInput").ap()

    def dscr(name, shape, dt):
        return nc.dram_tensor(name, list(shape), dt, kind="ExternalOutput" if (dbg and name in ("s_q", "s_o", "s_x1", "s_g")) else "Internal").ap()

    xin = din("xin", [T, D])
    cT = din("cT", [128, 16])
    w_ada = din("w_ada", [DEPTH, D, 6 * D])
    b_ada = din("b_ada", [DEPTH, 6 * D])
    b_adaT = din("b_adaT", [DEPTH, 128, 48])
    win = din("win", [DEPTH, D, NCOL])
    w_o = din("w_o", [DEPTH, D, D])
    sink = din("sink", [DEPTH, 6])
    lamv = din("lamv", [DEPTH, 4, 32])
    subg = din("subg", [DEPTH, 64])
    nag = din("nag", [DEPTH, 6, 128, NV * 256])
    nmask = din("nmask", [128, NV * 256])
    amask = din("amask", [128, 768])
    lnp = din("lnp", [DEPTH, 4, D])
    wfi = din("wfi", [DEPTH, D, 2 * DFF])
    wfo = din("wfo", [DEPTH, DFF, D])
    rope = din("rope", [8, 128, T])
    out = nc.dram_tensor("out", [L, D], F32, kind="ExternalOutput").ap()

    s_ada = dscr("s_ada", [DEPTH, 128, 8, 6 * D], BF16)
    s_win = dscr("s_win", [DEPTH, 128, 8, NCOL], BF16)
    s_wo = dscr("s_wo", [DEPTH, 64, 16, D], BF16)
    s_wfi = dscr("s_wfi", [DEPTH, 128, 8, 2 * DFF], BF16)
    s_wfo = dscr("s_wfo", [DEPTH, 128, 22, D], BF16)
    s_q = dscr("s_q", [8, 128, T], BF16)
    s_o = dscr("s_o", [16, 64, T], BF16)
    s_x1 = dscr("s_x1", [T, D], F32)
    s_g = dscr("s_g", [2, 2, 128, D], F32)

    with ExitStack() as top:
        top.enter_context(nc.allow_low_precision(reason="bf16 matmul operands by design; fp32 accumulation"))
        fw = FW(nc, top)
        B = fw.B

        uid = [0]

        def sb(stack, name, shape, dt):
            uid[0] += 1
            return stack.enter_context(nc.sbuf_tensor("%s_u%d" % (name, uid[0]), list(shape), dt))

        psF = []
        psB = []
        pcnt = {"f": 0, "b": 0, "a": 0, "s": 0, "x": 0}

        def alloc_psum(stack, nf, nb):
            del psF[:]
            del psB[:]
            for i in range(nf):
                uid[0] += 1
                psF.append(stack.enter_context(nc.psum_tensor("psf%d_u%d" % (i, uid[0]), [128, 512], F32)))
            for i in range(nb):
                uid[0] += 1
                psB.append(stack.enter_context(nc.psum_tensor("psb%d_u%d" % (i, uid[0]), [128, 1024], BF16)))

        def next_ps(pool=None):
            if pool is None:
                i = pcnt["f"] % len(psF)
                pcnt["f"] += 1
            elif pool == "a":
                i = pcnt["a"] % 4
                pcnt["a"] += 1
            elif pool == "s":
                i = 4 + pcnt["s"] % 2
                pcnt["s"] += 1
            else:
                i = 6 + pcnt["x"] % 2
                pcnt["x"] += 1
            return psF[i], B("psf%d" % i)

        def next_psb():
            i = pcnt["b"] % 2
            pcnt["b"] += 1
            return psB[i], B("psb%d" % i)

        def mm(o, lhsT, rhs, start, stop, rd, wr):
            fw.op("pe", lambda: nc.tensor.matmul(o, lhsT=lhsT, rhs=rhs, start=start, stop=stop), reads=rd, writes=wr, inc=stop)

        ident = sb(top, "ident", [128, 128], BF16)
        onesb = sb(top, "onesb", [128, 128], BF16)
        sel = sb(top, "sel", [65, 64], BF16)
        o64 = sb(top, "o64", [64, 64], BF16)
        modT = sb(top, "modT", [128, 32, 2], F32)
        sinkE = sb(top, "sinkE", [128, 8], F32)
        lamt = sb(top, "lamt", [128, 4], F32)
        lamw = sb(top, "lamw", [128, 4, 32], F32)
        gs = sb(top, "gs", [64, 2], F32)

        fw.op("pool", lambda: nc.gpsimd.memset(ident[:], 1.0), writes=[B("ident")])
        fw.op("pool", lambda: nc.gpsimd.affine_select(out=ident[:], in_=ident[:], pattern=[[-1, 128]], compare_op=ALU.is_equal,
                                                     fill=0.0, base=0, channel_multiplier=1), reads=[B("ident")], writes=[B("ident")])
        fw.op("pool", lambda: nc.gpsimd.memset(onesb[:], 1.0), writes=[B("onesb")])
        fw.op("pool", lambda: nc.gpsimd.memset(sel[:], 0.0), writes=[B("sel")])
        fw.op("pool", lambda: nc.gpsimd.memset(sel[64:65, :], 1.0), reads=[B("sel")], writes=[B("sel")])
        fw.op("pool", lambda: nc.gpsimd.memset(o64[:], 1.0 / 64.0), writes=[B("o64")])

        def cast_weights(l):
            for kc in range(8):
                fw.dma("pool", s_ada[l, :, kc, :], w_ada[l, kc * 128:(kc + 1) * 128, :], writes=[B("s_ada%d_%d" % (l, kc))])
            for kc in range(8):
                fw.dma("pool", s_win[l, :, kc, :], win[l, kc * 128:(kc + 1) * 128, :], writes=[B("s_win%d_%d" % (l, kc))])
            for h in range(16):
                fw.dma("pool", s_wo[l, :, h, :], w_o[l, h * 64:(h + 1) * 64, :], writes=[B("s_wo%d_%d" % (l, h))])
            for kc in range(8):
                fw.dma("pool", s_wfi[l, :, kc, :], wfi[l, kc * 128:(kc + 1) * 128, :], writes=[B("s_wfi%d_%d" % (l, kc))])
            for c in range(22):
                fw.dma("pool", s_wfo[l, :, c, :], wfo[l, c * 128:(c + 1) * 128, :], writes=[B("s_wfo%d_%d" % (l, c))])

        cast_weights(0)

        def layer_norm(src, dst, lng, lnb, slot, rd, wr, ph):
            st_, mv, rs, nb = ph["lnst"][slot], ph["lnmv"][slot], ph["lnrs"][slot], ph["lnnb"][slot]
            bs = B("lnsm%d" % slot)
            fw.op("dve", lambda: nc.vector.bn_stats(out=st_[:, 0:6], in_=src[:, 0:512]), reads=rd, writes=[bs])
            fw.op("dve", lambda: nc.vector.bn_stats(out=st_[:, 6:12], in_=src[:, 512:1024]), reads=rd + [bs], writes=[bs])
            fw.op("dve", lambda: nc.vector.bn_aggr(out=mv[:, 0:2], in_=st_[:, 0:12]), reads=[bs], writes=[bs])
            fw.op("act", lambda: nc.scalar.activation(out=rs[:, 0:1], in_=mv[:, 1:2], func=AF.Sqrt, bias=EPS, scale=1.0), reads=[bs], writes=[bs])
            fw.op("dve", lambda: nc.vector.reciprocal(out=rs[:, 0:1], in_=rs[:, 0:1]), reads=[bs], writes=[bs])
            fw.op("dve", lambda: nc.vector.scalar_tensor_tensor(out=nb[:, 0:1], in0=mv[:, 0:1], scalar=-1.0, in1=rs[:, 0:1],
                                                               op0=ALU.mult, op1=ALU.mult), reads=[bs], writes=[bs])
            fw.op("act", lambda: nc.scalar.activation(out=src, in_=src, func=AF.Identity, bias=nb[:, 0:1], scale=rs[:, 0:1]),
                  reads=rd + [bs], writes=rd)
            fw.op("dve", lambda: nc.vector.tensor_tensor(out=dst, in0=src, in1=lng, op=ALU.mult), reads=rd + [B("lnp")], writes=wr)
            fw.op("dve", lambda: nc.vector.tensor_tensor(out=dst, in0=dst, in1=lnb, op=ALU.add), reads=wr + [B("lnp")], writes=wr)

        def transpose_mod(xb, nt, hT, mi, j, tag):
            N = nt * 128
            for c in range(8):
                pb, pbb = next_psb()
                for i in range(nt):
                    fw.op("pe", lambda i=i, c=c, pb=pb: nc.tensor.transpose(out=pb[:, i * 128:(i + 1) * 128],
                                                                          in_=xb[:, i, c * 128:(c + 1) * 128], identity=ident[:]),
                          reads=[B("xb%d" % i), B("ident")], writes=[pbb], inc=(i == nt - 1))
                fw.op("act", lambda c=c, pb=pb: nc.scalar.activation(out=hT[:, c, 0:N], in_=pb[:, 0:N], func=AF.Identity,
                                                                    bias=modT[:, mi * 8 + c, j:j + 1],
                                                                    scale=modT[:, (mi + 1) * 8 + c, j:j + 1]),
                      reads=[pbb, B("modT")], writes=[B("%s%d" % (tag, c))])

        cur_in = xin
        for l in range(nlayers):
            last = (l == nlayers - 1) and (nlayers == DEPTH)
            lam_init = 0.8 - 0.6 * math.exp(-0.3 * l)
            dst = out if last else s_x1
            nblk = 4 if last else 5

            with ExitStack() as ph:
                alloc_psum(ph, 6, 2)
                wada = sb(ph, "wada", [128, 8, 6 * D], BF16)
                cTt = sb(ph, "cTt", [128, 16], F32)
                scf = sb(ph, "scf", [128, 16], F32)
                scb = sb(ph, "scb", [128, 8, 2], BF16)
                rep = [sb(ph, "rep%d" % j, [128, 8, 128], BF16) for j in range(2)]
                bT = sb(ph, "bT", [128, 48], F32)
                bg = sb(ph, "bg", [128, 2, D], F32)
                Gt = [sb(ph, "Gt%d" % i, [128, D], F32) for i in range(2)]
                for kc in range(8):
                    fw.dma("sp", wada[:, kc, :], s_ada[l, :, kc, :], reads=[B("s_ada%d_%d" % (l, kc))], writes=[B("wada%d" % kc)])
                fw.dma("sp", cTt[:], cT, writes=[B("cTt")])
                fw.dma("sp", bT[:], b_adaT[l], writes=[B("bT")])
                for gi, m in enumerate((2, 5)):
                    fw.dma("sp", bg[:, gi, :], b_ada[l, m * D:(m + 1) * D].partition_broadcast(128), writes=[B("bg")])
                fw.dma("sp", sinkE[:, 0:6], sink[l].partition_broadcast(128), writes=[B("sinkE")])
                fw.dma("sp", lamw[:, :, :], lamv[l].partition_broadcast(128), writes=[B("lamw")])
                fw.dma("sp", gs[:, 0:1], subg[l].rearrange("(d o) -> d o", o=1), writes=[B("gs")])
                fw.op("act", lambda: nc.scalar.activation(out=sinkE[:, 0:6], in_=sinkE[:, 0:6], func=AF.Exp),
                      reads=[B("sinkE")], writes=[B("sinkE")])
                fw.op("dve", lambda: nc.vector.tensor_tensor(out=lamw[:, 0, :], in0=lamw[:, 0, :], in1=lamw[:, 1, :], op=ALU.mult),
                      reads=[B("lamw")], writes=[B("lamw")])
                fw.op("dve", lambda: nc.vector.tensor_tensor(out=lamw[:, 2, :], in0=lamw[:, 2, :], in1=lamw[:, 3, :], op=ALU.mult),
                      reads=[B("lamw")], writes=[B("lamw")])
                fw.op("dve", lambda: nc.vector.reduce_sum(out=lamt[:, 0:1], in_=lamw[:, 0, :], axis=mybir.AxisListType.X),
                      reads=[B("lamw")], writes=[B("lamt")])
                fw.op("dve", lambda: nc.vector.reduce_sum(out=lamt[:, 1:2], in_=lamw[:, 2, :], axis=mybir.AxisListType.X),
                      reads=[B("lamw"), B("lamt")], writes=[B("lamt")])
                fw.op("act", lambda: nc.scalar.activation(out=lamt[:, 0:2], in_=lamt[:, 0:2], func=AF.Exp),
                      reads=[B("lamt")], writes=[B("lamt")])
                fw.op("dve", lambda: nc.vector.tensor_tensor(out=lamt[:, 2:3], in0=lamt[:, 0:1], in1=lamt[:, 1:2], op=ALU.subtract),
                      reads=[B("lamt")], writes=[B("lamt")])
                fw.op("dve", lambda: nc.vector.tensor_scalar(out=lamt[:, 3:4], in0=lamt[:, 2:3], scalar1=float(lam_init), scalar2=None,
                                                            op0=ALU.add), reads=[B("lamt")], writes=[B("lamt")])
                fw.op("dve", lambda: nc.vector.tensor_scalar(out=gs[:, 1:2], in0=gs[:, 0:1], scalar1=float(1.0 - lam_init), scalar2=None,
                                                            op0=ALU.mult), reads=[B("gs")], writes=[B("gs")])
                fw.op("act", lambda: nc.scalar.activation(out=scf[:], in_=cTt[:], func=AF.Silu), reads=[B("cTt")], writes=[B("scf")])
                fw.op("dve", lambda: nc.vector.tensor_copy(out=scb[:].rearrange("p k j -> p (k j)"), in_=scf[:]), reads=[B("scf")], writes=[B("scb")])
                for j in range(2):
                    for kc in range(8):
                        fw.op("dve", lambda j=j, kc=kc: nc.vector.tensor_scalar(out=rep[j][:, kc, :], in0=onesb[:, :],
                                                                               scalar1=scf[:, kc * 2 + j:kc * 2 + j + 1], scalar2=None, op0=ALU.mult),
                              reads=[B("scf"), B("onesb")], writes=[B("rep%d" % j)])
                ps, psb_ = next_ps()
                for mi, m in enumerate((0, 1, 3, 4)):
                    for c in range(8):
                        col0 = m * D + c * 128
                        o = (mi * 8 + c) * 2
                        for kc in range(8):
                            mm(ps[:, o:o + 2], wada[:, kc, col0:col0 + 128], scb[:, kc, 0:2], kc == 0, kc == 7,
                               [B("wada%d" % kc), B("scb")], [psb_])
                psv = ps[:, 0:64].rearrange("p (a j) -> p a j", j=2)
                for j in range(2):
                    for (a0, b0) in ((0, 0), (16, 24)):
                        fw.op("dve", lambda j=j, a0=a0, b0=b0: nc.vector.tensor_tensor(out=modT[:, a0:a0 + 16, j], in0=psv[:, a0:a0 + 16, j],
                                                                                       in1=bT[:, b0:b0 + 16], op=ALU.add),
                              reads=[psb_, B("bT")], writes=[B("modT")])
                for a0 in (8, 24):
                    fw.op("dve", lambda a0=a0: nc.vector.tensor_scalar(out=modT[:, a0:a0 + 8, :], in0=modT[:, a0:a0 + 8, :], scalar1=1.0,
                                                                      scalar2=None, op0=ALU.add), reads=[B("modT")], writes=[B("modT")])
                for j in range(2 if not last else 1):
                    for gi, m in enumerate((2, 5)):
                        for half in range(2):
                            ps, psb_ = next_ps()
                            for kc in range(8):
                                mm(ps[:, :], rep[j][:, kc, :], wada[:, kc, m * D + half * 512:m * D + (half + 1) * 512], kc == 0, kc == 7,
                                   [B("rep%d" % j), B("wada%d" % kc)], [psb_])
                            fw.op("dve", lambda ps=ps, gi=gi, half=half: nc.vector.tensor_tensor(
                                out=Gt[gi][:, half * 512:(half + 1) * 512], in0=ps[:, :], in1=bg[:, gi, half * 512:(half + 1) * 512], op=ALU.add),
                                reads=[psb_, B("bg")], writes=[B("Gt%d" % gi)])
                        fw.dma("pool", s_g[j, gi], Gt[gi][:], reads=[B("Gt%d" % gi)], writes=[B("s_g%d%d" % (j, gi))])
                fw.barrier()

            kvs = ExitStack()
            KT = sb(kvs, "KT", [128, 8, T], BF16)
            Vx = sb(kvs, "Vx", [128, NT, 12, 66], BF16)
            fw.op("pool", lambda: nc.gpsimd.memset(Vx[:, :, :, 64:66], 1.0), writes=[B("Vx")])
            with ExitStack() as ph:
                alloc_psum(ph, 6, 2)
                Win = sb(ph, "Win", [128, 8, NCOL], BF16)
                xsb = [sb(ph, "xsb%d" % s, [128, 4, D], F32) for s in range(2)]
                xb = sb(ph, "xb", [128, 4, D], BF16)
                hT = sb(ph, "hT", [128, 8, 512], BF16)
                rt = sb(ph, "rt", [128, 8, 512], F32)
                t1 = [sb(ph, "t1_%d" % s, [128, 512], F32) for s in range(2)]
                t2 = [sb(ph, "t2_%d" % s, [128, 512], F32) for s in range(2)]
                qst = [sb(ph, "qst%d" % s, [128, 512], BF16) for s in range(2)]
                for kc in range(8):
                    fw.dma("sp", Win[:, kc, :], s_win[l, :, kc, :], reads=[B("s_win%d_%d" % (l, kc))], writes=[B("Win%d" % kc)])
                WinB = [B("Win%d" % kc) for kc in range(8)]
                tcnt = [0]

                def load_x(bi):
                    nt = 4 if bi < 4 else 2
                    s = bi % 2
                    fw.dma("sp", xsb[s][:, 0:nt, :], cur_in[bi * 512:bi * 512 + nt * 128, :].rearrange("(i p) d -> p i d", p=128),
                           reads=[B("xs_%d_%d" % (l, bi))], writes=[B("xsb%d" % s)])

                load_x(0)
                for bi in range(5):
                    nt = 4 if bi < 4 else 2
                    N = nt * 128
                    tok0 = bi * 512
                    j = 0 if bi < 4 else 1
                    s = bi % 2
                    if bi + 1 < 5:
                        load_x(bi + 1)
                    for k in range(8):
                        fw.dma("sp", rt[:, k, 0:N], rope[k, :, tok0:tok0 + N], writes=[B("rt%d" % k)])
                    for i in range(nt):
                        fw.op("act" if i % 2 else "dve",
                              (lambda i=i: nc.scalar.copy(out=xb[:, i, :], in_=xsb[s][:, i, :])) if i % 2 else
                              (lambda i=i: nc.vector.tensor_copy(out=xb[:, i, :], in_=xsb[s][:, i, :])),
                              reads=[B("xsb%d" % s)], writes=[B("xb%d" % i)])
                    transpose_mod(xb, nt, hT, 0, j, "hT")
                    hTB = [B("hT%d" % c) for c in range(8)]

                    def proj(ch):
                        ps, pb_ = next_ps()
                        for kc in range(8):
                            mm(ps[:, 0:N], Win[:, kc, ch * 128:(ch + 1) * 128], hT[:, kc, 0:N], kc == 0, kc == 7, [WinB[kc], hTB[kc]], [pb_])
                        return ps, pb_

                    def roped(ch, chp, tc, ts, dst_ap, dst_b):
                        pa, pab = proj(ch)
                        pp, ppb = proj(chp)
                        rope_comb(pa, pab, pp, ppb, tc, ts, dst_ap, dst_b)

                    def rope_comb(pa, pab, pp, ppb, tc, ts, dst_ap, dst_b):
                        k = tcnt[0] % 2
                        tcnt[0] += 1
                        fw.op("dve", lambda: nc.vector.tensor_tensor(out=t1[k][:, 0:N], in0=pa[:, 0:N], in1=rt[:, tc, 0:N], op=ALU.mult),
                              reads=[pab, B("rt%d" % tc)], writes=[B("t1_%d" % k)])
                        fw.op("dve", lambda: nc.vector.tensor_tensor(out=t2[k][:, 0:N], in0=pp[:, 0:N], in1=rt[:, ts, 0:N], op=ALU.mult),
                              reads=[ppb, B("rt%d" % ts)], writes=[B("t2_%d" % k)])
                        fw.op("pool", lambda: nc.gpsimd.tensor_tensor(out=dst_ap, in0=t1[k][:, 0:N], in1=t2[k][:, 0:N], op=ALU.add),
                              reads=[B("t1_%d" % k), B("t2_%d" % k)], writes=dst_b)

                    def q_out(qi, fn):
                        k = tcnt[0] % 2
                        fn(qst[k][:, 0:N], [B("qst%d" % k)])
                        fw.dma("sp", s_q[qi, :, tok0:tok0 + N], qst[k][:, 0:N], reads=[B("qst%d" % k)], writes=[B("s_q%d" % bi)])

                    need_q = (bi < 4) or (not last)
                    if need_q:
                        for jq in range(3):
                            q_out(jq, lambda ap, bb, jq=jq: roped(jq, 4 + jq, 0, 1, ap, bb))
                    roped(3, 7, 0, 1, KT[:, 0, tok0:tok0 + N], [B("KT0_%d" % bi)])
                    if need_q:
                        for c in range(2):
                            q_out(3 + c, lambda ap, bb, c=c: roped(8 + c, 12 + c, 2, 3, ap, bb))
                    for c in range(2):
                        pa, pab = proj(10 + c)
                        pp, ppb = proj(14 + c)
                        rope_comb(pa, pab, pp, ppb, 4, 5, KT[:, 1 + c, tok0:tok0 + N], [B("KT%d_%d" % (1 + c, bi))])
                        rope_comb(pa, pab, pp, ppb, 6, 7, KT[:, 3 + c, tok0:tok0 + N], [B("KT%d_%d" % (3 + c, bi))])
                    if need_q:
                        for c in range(3):
                            def cq(ap, bb, c=c):
                                ps, pb_ = proj(16 + c)
                                fw.op("act", lambda: nc.scalar.copy(out=ap, in_=ps[:, 0:N]), reads=[pb_], writes=bb)
                            q_out(5 + c, cq)
                            tcnt[0] += 1
                    for c in range(3):
                        ps, pb_ = proj(19 + c)
                        fw.op("act", lambda ps=ps, c=c: nc.scalar.copy(out=KT[:, 5 + c, tok0:tok0 + N], in_=ps[:, 0:N]),
                              reads=[pb_], writes=[B("KT%d_%d" % (5 + c, bi))])
                    for i in range(nt):
                        gt = bi * 4 + i
                        ps, pb_ = next_ps()
                        for kc in range(8):
                            mm(ps[:, :], hT[:, kc, i * 128:(i + 1) * 128], Win[:, kc, NCH * 128:NCH * 128 + 512], kc == 0, kc == 7,
                               [WinB[kc], hTB[kc]], [pb_])
                        fw.op("act", lambda ps=ps, gt=gt: nc.scalar.copy(out=Vx[:, gt, 0:8, 0:64], in_=ps[:, :].rearrange("p (h d) -> p h d", d=64)),
                              reads=[pb_], writes=[B("Vx")])
                        ps, pb_ = next_ps()
                        for kc in range(8):
                            mm(ps[:, 0:256], hT[:, kc, i * 128:(i + 1) * 128], Win[:, kc, NCH * 128 + 512:NCH * 128 + 768], kc == 0, kc == 7,
                               [WinB[kc], hTB[kc]], [pb_])
                        fw.op("dve", lambda ps=ps, gt=gt: nc.vector.tensor_copy(out=Vx[:, gt, 8:12, 0:64],
                                                                                in_=ps[:, 0:256].rearrange("p (h d) -> p h d", d=64)),
                              reads=[pb_], writes=[B("Vx")])
                fw.barrier()

            with ExitStack() as ph:
                alloc_psum(ph, 8, 0)
                Et = sb(ph, "Et", [128, 6, NV * 256], BF16)
                nmk = sb(ph, "nmk", [128, NV * 256], BF16)
                MA = sb(ph, "MA", [128, 768], BF16)
                QTb = [sb(ph, "QTb%d" % s, [128, 8, 512], BF16) for s in range(2)]
                OTb = sb(ph, "OTb", [64, 16, 512], BF16)
                NPT = 6
                PT = [sb(ph, "PT%d" % s, [128, 512], BF16) for s in range(NPT)]
                ssum = [sb(ph, "ssum%d" % s, [128, 384], F32) for s in range(2)]
                bcs = [sb(ph, "bcs%d" % s, [64, 512], F32) for s in range(4)]
                tA = [sb(ph, "tA%d" % s, [64, 512], F32) for s in range(4)]
                sq = sb(ph, "sq", [64, 512], BF16)
                cnt = {"pt": 0, "ss": 0, "bc": 0}
                if l + 1 < nlayers:
                    cast_weights(l + 1)
                with ExitStack() as sub:
                    gstg = sb(sub, "gstg", [128, NV * 256], F32)
                    fw.dma("pool", nmk[:], nmask, writes=[B("nmk")])
                    fw.dma("pool", MA[:], amask, writes=[B("MA")])
                    for h in range(6):
                        fw.dma("sp", gstg[:], nag[l, h], writes=[B("gstg")])
                        fw.op("act", lambda h=h: nc.scalar.activation(out=Et[:, h, :], in_=gstg[:], func=AF.Exp), reads=[B("gstg")], writes=[B("Et%d" % h)])
                        fw.op("dve", lambda h=h: nc.vector.tensor_tensor(out=Et[:, h, :], in0=Et[:, h, :], in1=nmk[:], op=ALU.mult),
                              reads=[B("Et%d" % h), B("nmk")], writes=[B("Et%d" % h)])
                    fw.barrier()

                rb = [sb(ph, "rb%d" % s_, [65, 512], BF16) for s_ in range(4)]
                for s_ in range(4):
                    fw.op("pool", lambda s_=s_: nc.gpsimd.memset(rb[s_][:], 0.0), writes=[B("rb%d" % s_)])
                cnt["rb"] = 0

                def v_lhsT(t, hh):
                    return Vx[:, t, hh, 0:65]

                def bcast_recip(rows, N_):
                    res = []
                    for wr in rows:
                        k = cnt["rb"] % 4
                        cnt["rb"] += 1
                        wr(rb[k], B("rb%d" % k))
                        pbc, pbcb = next_ps("x")
                        mm(pbc[0:64, 0:N_], sel[0:65, 0:64], rb[k][0:65, 0:N_], True, True, [B("sel"), B("rb%d" % k)], [pbcb])
                        c0, c0b = new_bc()
                        fw.op("dve", lambda pbc=pbc, c0=c0: nc.vector.reciprocal(out=c0[0:64, 0:N_], in_=pbc[0:64, 0:N_]), reads=[pbcb], writes=[c0b])
                        res.append((c0, c0b))
                    return res

                def load_q(bi):
                    nt = 4 if bi < 4 else 2
                    s = bi % 2
                    fw.dma("sp", QTb[s][:, :, 0:nt * 128], s_q[:, :, bi * 512:bi * 512 + nt * 128].rearrange("c p t -> p c t"),
                           reads=[B("s_q%d" % bi)], writes=[B("QTb%d" % s)])

                def new_pt():
                    k = cnt["pt"] % NPT
                    cnt["pt"] += 1
                    return PT[k], B("PT%d" % k)

                def new_bc():
                    k = cnt["bc"] % 4
                    cnt["bc"] += 1
                    return bcs[k], B("bcs%d" % k)

                live_acc = set()

                def new_acc():
                    i = pcnt["a"] % 4
                    assert i not in live_acc, "accumulator bank still live"
                    live_acc.add(i)
                    pcnt["a"] += 1
                    return psF[i], B("psf%d" % i), i

                def gen_jobs(Q, Qb, bi, ctxq, nt, N):
                    units = []
                    for n in range(nt):
                        gq = bi * 4 + n
                        for g in range(2):
                            if ctxq:
                                tiles = [(16, None), (17, None)]
                            else:
                                tiles = []
                                if gq - 1 >= 0:
                                    tiles.append((gq - 1, 0))
                                tiles.append((gq, None))
                                if gq + 1 < 16:
                                    tiles.append((gq + 1, 1))
                                tiles += [(16, None), (17, None)]
                            u = {"jobs": []}
                            for ti, (t, mk) in enumerate(tiles):
                                def S(u=u, ti=ti, t=t, g=g, n=n):
                                    if ti == 0:
                                        u["po"] = new_acc()
                                    ps, psb_ = next_ps("s")
                                    mm(ps[:, 0:384].rearrange("p (a b) -> p a b", a=3), KT[g * 64:(g + 1) * 64, 0, t * 128:(t + 1) * 128],
                                       Q[g * 64:(g + 1) * 64, 0:3, n * 128:(n + 1) * 128], True, True, [B("KT0_%d" % (t // 4)), Qb], [psb_])
                                    return ps, psb_

                                def rest(st, u=u, ti=ti, t=t, mk=mk, g=g, last=(ti == len(tiles) - 1)):
                                    ps, psb_ = st
                                    po, pob, _ = u["po"]
                                    pt, ptb = new_pt()
                                    fw.op("act", lambda: nc.scalar.activation(out=pt[:, 0:384], in_=ps[:, 0:384], func=AF.Exp, scale=0.125),
                                          reads=[psb_], writes=[ptb])
                                    if mk is not None:
                                        fw.op("pool", lambda: nc.gpsimd.tensor_tensor(out=pt[:, 0:384], in0=pt[:, 0:384],
                                                                                    in1=MA[:, mk * 384:(mk + 1) * 384], op=ALU.mult),
                                              reads=[ptb, B("MA")], writes=[ptb])
                                    mm(po[0:65, 0:384], v_lhsT(t, g), pt[:, 0:384], ti == 0, last, [B("Vx"), ptb], [pob])
                                u["jobs"].append((S, rest))

                            def fin(u=u, g=g, n=n):
                                po, pob, bank = u["po"]

                                def wr(rbt, rbb):
                                    for sg in range(3):
                                        h = 3 * g + sg
                                        fw.op("dve", lambda sg=sg, h=h: nc.vector.tensor_scalar(
                                            out=rbt[64:65, sg * 128:(sg + 1) * 128], in0=po[64:65, sg * 128:(sg + 1) * 128],
                                            scalar1=sinkE[64:65, h:h + 1], scalar2=None, op0=ALU.add), reads=[pob, B("sinkE")], writes=[rbb])
                                ((c0, c0b),) = bcast_recip([wr], 384)
                                fw.op("dve", lambda: nc.vector.tensor_tensor(
                                    out=OTb[0:64, 3 * g:3 * g + 3, n * 128:(n + 1) * 128], in0=po[0:64, 0:384].rearrange("p (a b) -> p a b", a=3),
                                    in1=c0[0:64, 0:384].rearrange("p (a b) -> p a b", a=3), op=ALU.mult),
                                    reads=[pob, c0b], writes=[B("OTb")])
                                live_acc.discard(bank)
                            u["fin"] = fin
                            units.append(u)
                    tilesB = [16, 17] if ctxq else list(range(18))
                    for h in range(4):
                        r0 = (h % 2) * 64
                        u = {"jobs": []}
                        for ti, t in enumerate(tilesB):
                            for m in range(2):
                                def S(u=u, ti=ti, t=t, m=m, h=h, r0=r0):
                                    if ti == 0 and m == 0:
                                        u["po"] = [new_acc(), new_acc()]
                                    ps, psb_ = next_ps("s")
                                    mm(ps[:, 0:N], KT[r0:r0 + 64, 1 + 2 * m + h // 2, t * 128:(t + 1) * 128], Q[r0:r0 + 64, 3 + h // 2, 0:N],
                                       True, True, [B("KT%d_%d" % (1 + 2 * m + h // 2, t // 4)), Qb], [psb_])
                                    return ps, psb_

                                def rest(st, u=u, ti=ti, t=t, m=m, h=h, last=(ti == len(tilesB) - 1)):
                                    ps, psb_ = st
                                    po, pob, _ = u["po"][m]
                                    pt, ptb = new_pt()
                                    fw.op("act", lambda: nc.scalar.activation(out=pt[:, 0:N], in_=ps[:, 0:N], func=AF.Exp, scale=float(32 ** -0.5)),
                                          reads=[psb_], writes=[ptb])
                                    mm(po[0:65, 0:N], v_lhsT(t, 2 + h), pt[:, 0:N], ti == 0, last, [B("Vx"), ptb], [pob])
                                u["jobs"].append((S, rest))

                        def fin(u=u, h=h):
                            (p0, p0b, b0), (p1, p1b, b1) = u["po"]
                            def wr0(rbt, rbb):
                                fw.op("dve", lambda: nc.vector.tensor_copy(out=rbt[64:65, 0:N], in_=p0[64:65, 0:N]), reads=[p0b], writes=[rbb])

                            def wr1(rbt, rbb):
                                fw.op("dve", lambda: nc.vector.tensor_copy(out=rbt[64:65, 0:N], in_=p1[64:65, 0:N]), reads=[p1b], writes=[rbb])
                            (c0, c0b), (c1, c1b) = bcast_recip([wr0, wr1], N)
                            fw.op("dve", lambda: nc.vector.tensor_tensor(out=tA[0][:, 0:N], in0=p0[0:64, 0:N], in1=c0[0:64, 0:N], op=ALU.mult),
                                  reads=[p0b, c0b], writes=[B("tA0")])
                            fw.op("dve", lambda: nc.vector.scalar_tensor_tensor(out=tA[1][:, 0:N], in0=p1[0:64, 0:N], scalar=lamt[0:64, 3:4],
                                                                               in1=c1[0:64, 0:N], op0=ALU.mult, op1=ALU.mult),
                                  reads=[p1b, c1b, B("lamt")], writes=[B("tA1")])
                            live_acc.discard(b0)
                            live_acc.discard(b1)
                            fw.op("pool", lambda: nc.gpsimd.tensor_tensor(out=tA[2][:, 0:N], in0=tA[0][:, 0:N], in1=tA[1][:, 0:N], op=ALU.subtract),
                                  reads=[B("tA0"), B("tA1")], writes=[B("tA2")])
                            fw.op("pool", lambda: nc.gpsimd.tensor_tensor(out=sq[:, 0:N], in0=tA[2][:, 0:N], in1=tA[2][:, 0:N], op=ALU.mult),
                                  reads=[B("tA2")], writes=[B("sq")])
                            pms, pmsb = next_ps("x")
                            mm(pms[0:64, 0:N], o64[:, :], sq[:, 0:N], True, True, [B("o64"), B("sq")], [pmsb])
                            fw.op("act", lambda: nc.scalar.activation(out=tA[3][:, 0:N], in_=pms[0:64, 0:N], func=AF.Sqrt, bias=EPS, scale=1.0),
                                  reads=[pmsb], writes=[B("tA3")])
                            fw.op("dve", lambda: nc.vector.reciprocal(out=tA[3][:, 0:N], in_=tA[3][:, 0:N]), reads=[B("tA3")], writes=[B("tA3")])
                            fw.op("dve", lambda: nc.vector.scalar_tensor_tensor(out=OTb[0:64, 6 + h, 0:N], in0=tA[2][:, 0:N], scalar=gs[:, 1:2],
                                                                               in1=tA[3][:, 0:N], op0=ALU.mult, op1=ALU.mult),
                                  reads=[B("tA2"), B("tA3"), B("gs")], writes=[B("OTb")])
                        u["fin"] = fin
                        units.append(u)
                    for h in range(6):
                        r0 = (h % 2) * 64
                        for jj in range(1 if ctxq else 2):
                            if ctxq:
                                tiles = [(16, None), (17, None)]
                            else:
                                jg = bi * 2 + jj
                                if jg == 0:
                                    tiles = [(uu, 6 + uu) for uu in range(4)]
                                elif jg == 7:
                                    tiles = [(uu, 10 + uu - 12) for uu in range(12, 16)]
                                else:
                                    tiles = [(2 * jg - 2 + i, i) for i in range(6)]
                                tiles += [(16, None), (17, None)]
                            u = {"jobs": []}
                            for ti, (t, ev) in enumerate(tiles):
                                def S(u=u, ti=ti, t=t, h=h, r0=r0, jj=jj):
                                    if ti == 0:
                                        u["po"] = new_acc()
                                    ps, psb_ = next_ps("s")
                                    mm(ps[:, 0:256], KT[r0:r0 + 64, 5 + h // 2, t * 128:(t + 1) * 128], Q[r0:r0 + 64, 5 + h // 2, jj * 256:(jj + 1) * 256],
                                       True, True, [B("KT%d_%d" % (5 + h // 2, t // 4)), Qb], [psb_])
                                    return ps, psb_

                                def rest(st, u=u, ti=ti, t=t, ev=ev, h=h, last=(ti == len(tiles) - 1)):
                                    ps, psb_ = st
                                    po, pob, _ = u["po"]
                                    pt, ptb = new_pt()
                                    fw.op("act", lambda: nc.scalar.activation(out=pt[:, 0:256], in_=ps[:, 0:256], func=AF.Exp, scale=0.125),
                                          reads=[psb_], writes=[ptb])
                                    if ev is not None:
                                        fw.op("pool", lambda: nc.gpsimd.tensor_tensor(out=pt[:, 0:256], in0=pt[:, 0:256],
                                                                                    in1=Et[:, h, ev * 256:(ev + 1) * 256], op=ALU.mult),
                                              reads=[ptb, B("Et%d" % h)], writes=[ptb])
                                    mm(po[0:65, 0:256], v_lhsT(t, 6 + h), pt[:, 0:256], ti == 0, last, [B("Vx"), ptb], [pob])
                                u["jobs"].append((S, rest))

                            def fin(u=u, h=h, jj=jj):
                                po, pob, bank = u["po"]
                                def wr(rbt, rbb):
                                    fw.op("dve", lambda: nc.vector.tensor_copy(out=rbt[64:65, 0:256], in_=po[64:65, 0:256]), reads=[pob], writes=[rbb])
                                ((c0, c0b),) = bcast_recip([wr], 256)
                                fw.op("dve", lambda: nc.vector.tensor_tensor(out=OTb[0:64, 10 + h, jj * 256:(jj + 1) * 256], in0=po[0:64, 0:256],
                                                                            in1=c0[0:64, 0:256], op=ALU.mult), reads=[pob, c0b], writes=[B("OTb")])
                                live_acc.discard(bank)
                            u["fin"] = fin
                            units.append(u)
                    return units

                nqb = 4 if last else 5
                load_q(0)
                for bi in range(nqb):
                    ctxq = bi == 4
                    nt = 4 if bi < 4 else 2
                    N = nt * 128
                    tok0 = bi * 512
                    s = bi % 2
                    if bi + 1 < nqb:
                        load_q(bi + 1)
                    units = gen_jobs(QTb[s], B("QTb%d" % s), bi, ctxq, nt, N)
                    jobs = []
                    for u in units:
                        for ji, (S, rest) in enumerate(u["jobs"]):
                            jobs.append((S, rest, u["fin"] if ji == len(u["jobs"]) - 1 else None))
                    pending = []
                    st_next = jobs[0][0]()
                    for k, (S, rest, fin) in enumerate(jobs):
                        st_cur = st_next
                        if k + 1 < len(jobs):
                            st_next = jobs[k + 1][0]()
                        rest(st_cur)
                        if fin is not None:
                            pending.append((k + 2, fin))
                        while pending and pending[0][0] <= k:
                            pending.pop(0)[1]()
                    while pending:
                        pending.pop(0)[1]()
                    fw.dma("pool", s_o[:, :, tok0:tok0 + N].rearrange("h p t -> p h t"), OTb[:, :, 0:N], reads=[B("OTb")], writes=[B("s_o%d" % bi)])
                fw.barrier()
            kvs.close()

            with ExitStack() as ph:
                alloc_psum(ph, 6, 2)
                Wo = sb(ph, "Wo", [64, 16, D], BF16)
                xsb = [sb(ph, "xsb%d" % s, [128, 4, D], F32) for s in range(2)]
                OT2 = sb(ph, "OT2", [64, 16, 512], BF16)
                ub = sb(ph, "ub", [128, 4, D], F32)
                xb = sb(ph, "xb", [128, 4, D], BF16)
                hT = sb(ph, "hT", [128, 8, 512], BF16)
                AT = sb(ph, "AT", [128, 22, 512], BF16)
                Wi = [sb(ph, "Wi%d" % s, [128, 8, 512], BF16) for s in range(2)]
                Wf = [sb(ph, "Wf%d" % s, [128, 11, 512], BF16) for s in range(2)]
                lnt = sb(ph, "lnt", [128, 4, D], F32)
                Gx = sb(ph, "Gx", [128, 2, D], F32)
                sgt = [sb(ph, "sgt%d" % s, [128, 512], F32) for s in range(2)]
                ph_ln = {"lnst": [sb(ph, "lnst%d" % s, [128, 12], F32) for s in range(4)],
                         "lnmv": [sb(ph, "lnmv%d" % s, [128, 2], F32) for s in range(4)],
                         "lnrs": [sb(ph, "lnrs%d" % s, [128, 1], F32) for s in range(4)],
                         "lnnb": [sb(ph, "lnnb%d" % s, [128, 1], F32) for s in range(4)]}
                lncnt = [0]
                wic = [0]
                wfc = [0]
                fw.dma("sp", Wo[:], s_wo[l], reads=[B("s_wo%d_%d" % (l, h)) for h in range(16)], writes=[B("Wo")])
                for k in range(4):
                    fw.dma("sp", lnt[:, k, :], lnp[l, k].partition_broadcast(128), writes=[B("lnp")])

                def load_blk(bi):
                    nt = 4 if bi < 4 else 2
                    s = bi % 2
                    fw.dma("sp", xsb[s][:, 0:nt, :], cur_in[bi * 512:bi * 512 + nt * 128, :].rearrange("(i p) d -> p i d", p=128),
                           reads=[B("xs_%d_%d" % (l, bi))], writes=[B("xsb%d_%d" % (s, i)) for i in range(4)])

                def load_wi(cp):
                    s = wic[0] % 2
                    wic[0] += 1
                    fw.dma("sp", Wi[s][:], s_wfi[l, :, :, cp * 512:(cp + 1) * 512], reads=[B("s_wfi%d_%d" % (l, kc)) for kc in range(8)], writes=[B("Wi%d" % s)])
                    return s

                def load_wf(half, cg):
                    s = wfc[0] % 2
                    wfc[0] += 1
                    fw.dma("sp", Wf[s][:], s_wfo[l, :, cg * 11:(cg + 1) * 11, half * 512:(half + 1) * 512], reads=[B("s_wfo%d_%d" % (l, cg * 11 + cl)) for cl in range(11)], writes=[B("Wf%d" % s)])
                    return s

                def gated_res_ln(ps, pb_, i, half, gi, s):
                    cs = slice(half * 512, (half + 1) * 512)
                    fw.op("dve", lambda: nc.vector.tensor_tensor(out=ub[:, i, cs], in0=ps[:, :], in1=Gx[:, gi, cs], op=ALU.mult),
                          reads=[pb_, B("Gx")], writes=[B("ub%d" % i)])
                    fw.op("dve", lambda: nc.vector.scalar_tensor_tensor(out=ub[:, i, cs], in0=xsb[s][:, i, cs], scalar=float(ALPHA), in1=ub[:, i, cs],
                                                                       op0=ALU.mult, op1=ALU.add),
                          reads=[B("ub%d" % i), B("xsb%d_%d" % (s, i))], writes=[B("ub%d" % i)])

                load_blk(0)
                for bi in range(nblk):
                    nt = 4 if bi < 4 else 2
                    N = nt * 128
                    tok0 = bi * 512
                    j = 0 if bi < 4 else 1
                    s = bi % 2
                    if bi == 0 or bi == 4:
                        for gi in range(2):
                            fw.dma("sp", Gx[:, gi, :], s_g[j, gi], reads=[B("s_g%d%d" % (j, gi))], writes=[B("Gx")])
                    fw.dma("sp", OT2[:, :, 0:N], s_o[:, :, tok0:tok0 + N].rearrange("h p t -> p h t"), reads=[B("s_o%d" % bi)], writes=[B("OT2")])
                    if bi + 1 < nblk:
                        load_blk(bi + 1)
                    w0 = load_wi(0)
                    for i in range(nt):
                        for half in range(2):
                            ps, pb_ = next_ps()
                            for h in range(16):
                                mm(ps[:, :], OT2[0:64, h, i * 128:(i + 1) * 128], Wo[0:64, h, half * 512:(half + 1) * 512], h == 0, h == 15,
                                   [B("OT2"), B("Wo")], [pb_])
                            gated_res_ln(ps, pb_, i, half, 0, s)
                        k = lncnt[0] % 4
                        lncnt[0] += 1
                        layer_norm(ub[:, i, :], xsb[s][:, i, :], lnt[:, 0, :], lnt[:, 1, :], k, [B("ub%d" % i)], [B("xsb%d_%d" % (s, i))], ph_ln)
                        fw.op("act", lambda i=i: nc.scalar.copy(out=xb[:, i, :], in_=xsb[s][:, i, :]), reads=[B("xsb%d_%d" % (s, i))], writes=[B("xb%d" % i)])
                    transpose_mod(xb, nt, hT, 2, j, "hT")
                    hTB = [B("hT%d" % c) for c in range(8)]
                    for cp in range(11):
                        ws = w0
                        if cp + 1 < 11:
                            w0 = load_wi(cp + 1)
                        for cc in range(2):
                            c = 2 * cp + cc
                            pg, pgb = next_ps()
                            for kc in range(8):
                                mm(pg[:, 0:N], Wi[ws][:, kc, cc * 256:cc * 256 + 128], hT[:, kc, 0:N], kc == 0, kc == 7, [B("Wi%d" % ws), hTB[kc]], [pgb])
                            pu, pub = next_ps()
                            for kc in range(8):
                                mm(pu[:, 0:N], Wi[ws][:, kc, cc * 256 + 128:cc * 256 + 256], hT[:, kc, 0:N], kc == 0, kc == 7, [B("Wi%d" % ws), hTB[kc]], [pub])
                            k = c % 2
                            fw.op("act", lambda pg=pg, k=k: nc.scalar.activation(out=sgt[k][:, 0:N], in_=pg[:, 0:N], func=AF.Silu), reads=[pgb], writes=[B("sgt%d" % k)])
                            fw.op("dve", lambda pu=pu, k=k, c=c: nc.vector.tensor_tensor(out=AT[:, c, 0:N], in0=pu[:, 0:N], in1=sgt[k][:, 0:N], op=ALU.mult),
                                  reads=[pub, B("sgt%d" % k)], writes=[B("AT%d" % c)])
                    f0 = load_wf(0, 0)
                    for half in range(2):
                        accs = [next_ps() for _ in range(nt)]
                        for cg in range(2):
                            fs = f0
                            if not (half == 1 and cg == 1):
                                f0 = load_wf(half if cg == 0 else half + 1, 1 - cg)
                            for i in range(nt):
                                for cl in range(11):
                                    c = cg * 11 + cl
                                    mm(accs[i][0][:, :], AT[:, c, i * 128:(i + 1) * 128], Wf[fs][:, cl, :], cg == 0 and cl == 0, cg == 1 and cl == 10,
                                       [B("AT%d" % c), B("Wf%d" % fs)], [accs[i][1]])
                        for i in range(nt):
                            gated_res_ln(accs[i][0], accs[i][1], i, half, 1, s)
                    for i in range(nt):
                        k = lncnt[0] % 4
                        lncnt[0] += 1
                        layer_norm(ub[:, i, :], xsb[s][:, i, :], lnt[:, 2, :], lnt[:, 3, :], k, [B("ub%d" % i)], [B("xsb%d_%d" % (s, i))], ph_ln)
                    fw.dma("pool", dst[tok0:tok0 + N, :].rearrange("(i p) d -> p i d", p=128), xsb[s][:, 0:nt, :],
                           reads=[B("xsb%d_%d" % (s, i)) for i in range(nt)], writes=[B("xs_%d_%d" % (l + 1, bi))])
                fw.barrier()
            cur_in = s_x1
    return nc


_NC_CACHE = {}


def _host_inputs(inputs):
    cst = _consts()
    f = np.float32
    g = {k: np.asarray(v) for k, v in inputs.items()}
    shared = {}
    shared["w_ada"] = np.ascontiguousarray(g["w_ada"], dtype=f)
    shared["b_ada"] = np.ascontiguousarray(g["b_ada"], dtype=f)
    shared["b_adaT"] = np.ascontiguousarray(g["b_ada"].reshape(DEPTH, 48, 128).transpose(0, 2, 1), dtype=f)
    shared["win"] = np.ascontiguousarray(g["w_in"][:, :, cst["cols"]], dtype=f)
    shared["w_o"] = np.ascontiguousarray(g["w_o"], dtype=f)
    shared["sink"] = np.ascontiguousarray(g["sink"], dtype=f)
    shared["lamv"] = np.ascontiguousarray(np.stack([g["lam_q1"], g["lam_k1"], g["lam_q2"], g["lam_k2"]], axis=1), dtype=f)
    shared["subg"] = np.ascontiguousarray(g["subln_g"], dtype=f)
    nb = g["na_bias"]
    gath = nb[:, :, cst["dr"], cst["dc"]]
    shared["nag"] = np.ascontiguousarray(gath.transpose(0, 1, 3, 2, 4).reshape(DEPTH, 6, 128, NV * 256), dtype=f)
    shared["nmask"] = cst["nmask"]
    shared["amask"] = cst["amask"]
    shared["lnp"] = np.ascontiguousarray(np.stack([g["ln1_g"], g["ln1_b"], g["ln2_g"], g["ln2_b"]], axis=1), dtype=f)
    shared["wfi"] = np.ascontiguousarray(g["w_ffn_in"][:, :, cst["ffperm"]], dtype=f)
    shared["wfo"] = np.ascontiguousarray(g["w_ffn_out"], dtype=f)
    shared["rope"] = cst["rope"]
    per = []
    for b in range(g["x"].shape[0]):
        m = dict(shared)
        m["xin"] = np.ascontiguousarray(np.concatenate([g["x"][b], g["ctx"][b]], axis=0), dtype=f)
        cc = np.stack([g["c"][b].reshape(8, 128).T, g["c_ctx"].reshape(8, 128).T], axis=-1)
        m["cT"] = np.ascontiguousarray(cc.reshape(128, 16), dtype=f)
        per.append(m)
    return per


def kernel(**inputs):
    per = _host_inputs(inputs)
    if "nc" not in _NC_CACHE:
        _NC_CACHE["nc"] = build()
    nc = _NC_CACHE["nc"]
    n = len(per)
    res = run_bass_kernel_spmd(nc, per, core_ids=list(range(n)))
    return np.stack([np.asarray(r["out"]) for r in res.results], axis=0).astype(np.float32)
```

```python
import math
from contextlib import ExitStack

import numpy as np
import concourse.bass as bass
import concourse.mybir as mybir
from concourse.bass_utils import run_bass_kernel_spmd

F32 = mybir.dt.float32
BF16 = mybir.dt.bfloat16
AF = mybir.ActivationFunctionType
ALU = mybir.AluOpType

D = 1024
L = 2048
C = 256
T = L + C
NT = T // 128
DFF = 2816
NCH = 22
NCOL = NCH * 128 + 768
DEPTH = 2
ALPHA = (2.0 * DEPTH) ** 0.25
EPS = 1e-5
NV = 14
import os
FILL_N = 0


class Buf:
    __slots__ = ("name", "w", "r")

    def __init__(self, name):
        self.name = name
        self.w = None
        self.r = {}


class FW:
    NDMA = 32

    def __init__(self, nc, stack):
        self.nc = nc
        self.eng = {"pe": nc.tensor, "act": nc.scalar, "dve": nc.vector, "pool": nc.gpsimd, "sp": nc.sync}
        self.sems = {}
        self.cnt = {}
        self.known = {e: {} for e in self.eng}
        for e in ("pe", "act", "dve", "pool"):
            self.sems[e] = stack.enter_context(nc.semaphore("s_" + e))
            self.cnt[e] = 0
        self.dsems = []
        for i in range(self.NDMA):
            k = "dma%d" % i
            self.sems[k] = stack.enter_context(nc.semaphore(k))
            self.cnt[k] = 0
            self.dsems.append(k)
        self.rr = {"hw": 0, "sw": 0}
        self.bufs = {}

    def B(self, name):
        b = self.bufs.get(name)
        if b is None:
            b = self.bufs[name] = Buf(name)
        return b

    def _wait(self, e, toks):
        need = {}
        for t in toks:
            if t is None:
                continue
            k, v = t
            if e == "pe" and k == "pe":
                continue
            if self.known[e].get(k, 0) >= v:
                continue
            if need.get(k, 0) < v:
                need[k] = v
        for k, v in need.items():
            self.eng[e].wait_ge(self.sems[k], v)
            self.known[e][k] = v

    @staticmethod
    def _deps(reads, writes):
        toks = []
        for b in reads:
            toks.append(b.w)
        for b in writes:
            toks.append(b.w)
            for k, v in b.r.items():
                toks.append((k, v))
        return toks

    @staticmethod
    def _mark(tok, reads, writes):
        k, v = tok
        for b in reads:
            if b.r.get(k, 0) < v:
                b.r[k] = v
        for b in writes:
            b.w = tok
            b.r = {}

    def op(self, e, fn, reads=(), writes=(), inc=True):
        self._wait(e, self._deps(reads, writes))
        ins = fn()
        if inc:
            self.cnt[e] += 1
            ins.then_inc(self.sems[e], 1)
            tok = (e, self.cnt[e])
        else:
            tok = (e, self.cnt[e] + 1)
        self._mark(tok, reads, writes)
        return ins

    def dma(self, q, out, in_, reads=(), writes=()):
        kind = "sw" if q == "pool" else "hw"
        half = self.NDMA // 2
        sem = self.dsems[(0 if kind == "hw" else half) + self.rr[kind]]
        self.rr[kind] = (self.rr[kind] + 1) % half
        toks = self._deps(reads, writes)
        if self.cnt[sem] > 0:
            toks.append((sem, self.cnt[sem]))
        self._wait(q, toks)
        ins = self.eng[q].dma_start(out=out, in_=in_)
        self.cnt[sem] += 16
        ins.then_inc(self.sems[sem], 16)
        self._mark((sem, self.cnt[sem]), reads, writes)
        return ins

    def barrier(self):
        toks = [(k, v) for k, v in self.cnt.items() if v > 0]
        for e in self.eng:
            self._wait(e, toks)


def _win_cols():
    aq0, ak0, av0, bq0, bk0, bv0, cq0, ck0, cv0 = 0, 384, 512, 640, 896, 1152, 1408, 1792, 2176
    pA = np.array([d + 16 if d % 32 < 16 else d - 16 for d in range(64)])
    pB32 = np.array([d + 8 if d % 16 < 8 else d - 8 for d in range(32)])
    pB = np.concatenate([pB32, 32 + pB32])
    ar = np.arange(64)

    def hc(base, h, perm=None):
        return base + h * 64 + (ar if perm is None else perm)

    ch = []
    for j in range(3):
        ch.append(np.concatenate([hc(aq0, j), hc(aq0, 3 + j)]))
    ch.append(np.concatenate([hc(ak0, 0), hc(ak0, 1)]))
    for j in range(3):
        ch.append(np.concatenate([hc(aq0, j, pA), hc(aq0, 3 + j, pA)]))
    ch.append(np.concatenate([hc(ak0, 0, pA), hc(ak0, 1, pA)]))
    for base, perm in ((bq0, None), (bk0, None), (bq0, pB), (bk0, pB)):
        for c in range(2):
            ch.append(np.concatenate([hc(base, 2 * c, perm), hc(base, 2 * c + 1, perm)]))
    for base in (cq0, ck0):
        for c in range(3):
            ch.append(np.concatenate([hc(base, 2 * c), hc(base, 2 * c + 1)]))
    ch.append(np.arange(av0, av0 + 128))
    ch.append(np.arange(bv0, bv0 + 256))
    ch.append(np.arange(cv0, cv0 + 384))
    cols = np.concatenate(ch)
    assert cols.shape[0] == NCOL
    return cols


def _rope_tables():
    f = np.float32
    pos = np.arange(L)
    rows = (pos // 64).astype(f)
    cols = (pos % 64).astype(f)
    p = np.arange(128)
    base = np.zeros((4, 128, T), f)
    base[0::2, :, L:] = 1.0
    inv16 = np.power(f(10000.0), -np.arange(16, dtype=f) / f(16)).astype(f)
    d = p % 64
    ang = (np.where((d < 32)[:, None], rows[None, :], cols[None, :]).astype(f) * inv16[(d % 32) % 16][:, None]).astype(f)
    sgn = np.where((d % 32) < 16, -1.0, 1.0).astype(f)
    base[0, :, :L] = np.cos(ang)
    base[1, :, :L] = np.sin(ang) * sgn[:, None]
    inv8 = np.power(f(10000.0), -np.arange(8, dtype=f) / f(8)).astype(f)
    d = p % 32
    ang = (np.where((d < 16)[:, None], rows[None, :], cols[None, :]).astype(f) * inv8[(d % 16) % 8][:, None]).astype(f)
    sgn = np.where((d % 16) < 8, -1.0, 1.0).astype(f)
    base[2, :, :L] = np.cos(ang)
    base[3, :, :L] = np.sin(ang) * sgn[:, None]
    tabs = np.zeros((16, 128, T), f)
    tabs[0:4] = base
    for g in range(2):
        m = ((p // 64) == g).astype(f)[:, None]
        tabs[4 + 2 * g] = base[0] * m
        tabs[5 + 2 * g] = base[1] * m
    for v in range(4):
        m = ((p // 32) == v).astype(f)[:, None]
        tabs[8 + 2 * v] = base[2] * m
        tabs[9 + 2 * v] = base[3] * m
    return tabs


def _na_variants():
    R, W, KH, KW = 32, 64, 8, 16
    var = []
    for i in range(6):
        var.append(([12, 13, 14, 15], [8 + 2 * i, 9 + 2 * i]))
    for u in range(4):
        var.append(([0, 1, 2, 3], [2 * u, 2 * u + 1]))
    for u in range(12, 16):
        var.append(([28, 29, 30, 31], [2 * u, 2 * u + 1]))
    dr = np.zeros((NV, 128, 256), np.int64)
    dc = np.zeros((NV, 128, 256), np.int64)
    ok = np.zeros((NV, 128, 256), np.float32)
    kc = np.arange(64)[:, None]
    qc = np.arange(64)[None, :]
    cs = np.clip(qc - KW // 2, 0, W - KW)
    colv = (kc >= cs) & (kc < cs + KW)
    dcm = np.clip(kc - qc, -(KW - 1), KW - 1) + (KW - 1)
    for v, (qrows, krows) in enumerate(var):
        for kk, kr in enumerate(krows):
            for qq, r in enumerate(qrows):
                rs = min(max(r - KH // 2, 0), R - KH)
                rowv = (rs <= kr <= rs + KH - 1)
                dr[v, kk * 64:(kk + 1) * 64, qq * 64:(qq + 1) * 64] = min(max(kr - r + 7, 0), 14)
                dc[v, kk * 64:(kk + 1) * 64, qq * 64:(qq + 1) * 64] = dcm
                ok[v, kk * 64:(kk + 1) * 64, qq * 64:(qq + 1) * 64] = (colv & rowv).astype(np.float32)
    return dr, dc, ok


_CONST = {}


def _consts():
    if not _CONST:
        _CONST["cols"] = _win_cols()
        _CONST["rope"] = _rope_tables()
        dr, dc, ok = _na_variants()
        _CONST["dr"], _CONST["dc"] = dr, dc
        _CONST["nmask"] = np.ascontiguousarray(ok.transpose(1, 0, 2).reshape(128, NV * 256))
        j = np.arange(128)[:, None]
        i = np.arange(128)[None, :]
        prev = (j >= i).astype(np.float32)
        nxt = (j <= i).astype(np.float32)
        _CONST["amask"] = np.concatenate([np.tile(prev, (1, 3)), np.tile(nxt, (1, 3))], axis=1)
        ffperm = np.concatenate([np.concatenate([np.arange(c * 128, (c + 1) * 128), DFF + np.arange(c * 128, (c + 1) * 128)])
                                 for c in range(22)])
        _CONST["ffperm"] = ffperm
    return _CONST


def build(nlayers=DEPTH, dbg=False):
    nc = bass.Bass("TRN2", target_bir_lowering=False)

    def din(name, shape, dt=F32):
        return nc.dram_tensor(name, list(shape), dt, kind="ExternalInput").ap()

    def dscr(name, shape, dt):
        return nc.dram_tensor(name, list(shape), dt, kind="ExternalOutput" if (dbg and name in ("s_q", "s_o", "s_x1", "s_g")) else "Internal").ap()

    xin = din("xin", [T, D])
    cT = din("cT", [128, 16])
    w_ada = din("w_ada", [DEPTH, D, 6 * D])
    b_ada = din("b_ada", [DEPTH, 6 * D])
    b_adaT = din("b_adaT", [DEPTH, 128, 48])
    win = din("win", [DEPTH, D, NCOL])
    w_o = din("w_o", [DEPTH, D, D])
    sink = din("sink", [DEPTH, 6])
    lamv = din("lamv", [DEPTH, 4, 32])
    subg = din("subg", [DEPTH, 64])
    nag = din("nag", [DEPTH, 6, 128, NV * 256])
    nmask = din("nmask", [128, NV * 256])
    amask = din("amask", [128, 768])
    lnp = din("lnp", [DEPTH, 4, D])
    wfi = din("wfi", [DEPTH, D, 2 * DFF])
    wfo = din("wfo", [DEPTH, DFF, D])
    rope = din("rope", [16, 128, T])
    out = nc.dram_tensor("out", [L, D], F32, kind="ExternalOutput").ap()

    s_ada = dscr("s_ada", [DEPTH, 128, 8, 6 * D], BF16)
    s_win = dscr("s_win", [DEPTH, 128, 8, NCOL], BF16)
    s_wo = dscr("s_wo", [DEPTH, 64, 16, D], BF16)
    s_wfi = dscr("s_wfi", [DEPTH, 128, 8, 2 * DFF], BF16)
    s_wfo = dscr("s_wfo", [DEPTH, 128, 22, D], BF16)
    s_q = dscr("s_q", [20, 128, T], BF16)
    s_o = dscr("s_o", [16, 64, T], BF16)
    s_x1 = dscr("s_x1", [T, D], F32)
    s_g = dscr("s_g", [2, 2, 128, D], F32)

    with ExitStack() as top:
        top.enter_context(nc.allow_low_precision(reason="bf16 matmul operands by design; fp32 accumulation"))
        fw = FW(nc, top)
        B = fw.B

        uid = [0]

        def sb(stack, name, shape, dt):
            uid[0] += 1
            return stack.enter_context(nc.sbuf_tensor("%s_u%d" % (name, uid[0]), list(shape), dt))

        psF = []
        psB = []
        pcnt = {"f": 0, "b": 0, "a": 0, "s": 0, "x": 0}

        def alloc_psum(stack, nf, nb):
            del psF[:]
            del psB[:]
            for i in range(nf):
                uid[0] += 1
                psF.append(stack.enter_context(nc.psum_tensor("psf%d_u%d" % (i, uid[0]), [128, 512], F32)))
            for i in range(nb):
                uid[0] += 1
                psB.append(stack.enter_context(nc.psum_tensor("psb%d_u%d" % (i, uid[0]), [128, 1024], BF16)))

        def next_ps(pool=None):
            if pool is None:
                i = pcnt["f"] % len(psF)
                pcnt["f"] += 1
            elif pool == "a":
                i = pcnt["a"] % 4
                pcnt["a"] += 1
            elif pool == "s":
                i = 4 + pcnt["s"] % 2
                pcnt["s"] += 1
            else:
                i = 6 + pcnt["x"] % 2
                pcnt["x"] += 1
            return psF[i], B("psf%d" % i)

        def next_psb():
            i = pcnt["b"] % 2
            pcnt["b"] += 1
            return psB[i], B("psb%d" % i)

        def mm(o, lhsT, rhs, start, stop, rd, wr):
            fw.op("pe", lambda: nc.tensor.matmul(o, lhsT=lhsT, rhs=rhs, start=start, stop=stop), reads=rd, writes=wr, inc=stop)

        ident = sb(top, "ident", [128, 128], BF16)
        onesb = sb(top, "onesb", [128, 128], BF16)
        sel = sb(top, "sel", [128, 128], BF16)
        o64 = sb(top, "o64", [128, 128], BF16)
        fillr = sb(top, "fillr", [128, 512], BF16)
        epsc = sb(top, "epsc", [128, 1], F32)
        maskcol = sb(top, "maskcol", [128, 2], F32)
        modT = sb(top, "modT", [128, 32, 2], F32)
        sinkE = sb(top, "sinkE", [128, 8], F32)
        lamt = sb(top, "lamt", [128, 4], F32)
        lamw = sb(top, "lamw", [128, 4, 32], F32)
        gs = sb(top, "gs", [64, 2], F32)

        fw.op("pool", lambda: nc.gpsimd.memset(ident[:], 1.0), writes=[B("ident")])
        fw.op("pool", lambda: nc.gpsimd.affine_select(out=ident[:], in_=ident[:], pattern=[[-1, 128]], compare_op=ALU.is_equal,
                                                     fill=0.0, base=0, channel_multiplier=1), reads=[B("ident")], writes=[B("ident")])
        fw.op("pool", lambda: nc.gpsimd.memset(onesb[:], 1.0), writes=[B("onesb")])
        fw.op("pool", lambda: nc.gpsimd.memset(sel[:], 0.0), writes=[B("sel")])
        fw.op("pool", lambda: nc.gpsimd.memset(sel[64:65, :], 1.0), reads=[B("sel")], writes=[B("sel")])
        fw.op("pool", lambda: nc.gpsimd.memset(o64[:], 0.0), writes=[B("o64")])
        fw.op("pool", lambda: nc.gpsimd.memset(o64[0:64, 0:64], 1.0 / 64.0), reads=[B("o64")], writes=[B("o64")])
        fw.op("pool", lambda: nc.gpsimd.memset(fillr[:], 1.0), writes=[B("fillr")])
        fw.op("pool", lambda: nc.gpsimd.memset(epsc[:], EPS), writes=[B("epsc")])
        fw.op("pool", lambda: nc.gpsimd.memset(maskcol[:], 0.0), writes=[B("maskcol")])
        fw.op("pool", lambda: nc.gpsimd.memset(maskcol[0:64, 0:1], 1.0), reads=[B("maskcol")], writes=[B("maskcol")])
        fw.op("pool", lambda: nc.gpsimd.memset(maskcol[64:128, 1:2], 1.0), reads=[B("maskcol")], writes=[B("maskcol")])

        def cast_weights(l):
            for kc in range(8):
                fw.dma("pool", s_ada[l, :, kc, :], w_ada[l, kc * 128:(kc + 1) * 128, :], writes=[B("s_ada%d_%d" % (l, kc))])
            for kc in range(8):
                fw.dma("pool", s_win[l, :, kc, :], win[l, kc * 128:(kc + 1) * 128, :], writes=[B("s_win%d_%d" % (l, kc))])
            for h in range(16):
                fw.dma("pool", s_wo[l, :, h, :], w_o[l, h * 64:(h + 1) * 64, :], writes=[B("s_wo%d_%d" % (l, h))])
            for kc in range(8):
                fw.dma("pool", s_wfi[l, :, kc, :], wfi[l, kc * 128:(kc + 1) * 128, :], writes=[B("s_wfi%d_%d" % (l, kc))])
            for c in range(22):
                fw.dma("pool", s_wfo[l, :, c, :], wfo[l, c * 128:(c + 1) * 128, :], writes=[B("s_wfo%d_%d" % (l, c))])

        cast_weights(0)

        def layer_norm(src, dst, lng, lnb, slot, rd, wr, ph):
            st_, mv, rs, nb = ph["lnst"][slot], ph["lnmv"][slot], ph["lnrs"][slot], ph["lnnb"][slot]
            bs = B("lnsm%d" % slot)
            fw.op("dve", lambda: nc.vector.bn_stats(out=st_[:, 0:6], in_=src[:, 0:512]), reads=rd, writes=[bs])
            fw.op("dve", lambda: nc.vector.bn_stats(out=st_[:, 6:12], in_=src[:, 512:1024]), reads=rd + [bs], writes=[bs])
            fw.op("dve", lambda: nc.vector.bn_aggr(out=mv[:, 0:2], in_=st_[:, 0:12]), reads=[bs], writes=[bs])
            fw.op("act", lambda: nc.scalar.activation(out=rs[:, 0:1], in_=mv[:, 1:2], func=AF.Sqrt, bias=EPS, scale=1.0), reads=[bs], writes=[bs])
            fw.op("dve", lambda: nc.vector.reciprocal(out=rs[:, 0:1], in_=rs[:, 0:1]), reads=[bs], writes=[bs])
            fw.op("dve", lambda: nc.vector.scalar_tensor_tensor(out=nb[:, 0:1], in0=mv[:, 0:1], scalar=-1.0, in1=rs[:, 0:1],
                                                               op0=ALU.mult, op1=ALU.mult), reads=[bs], writes=[bs])
            fw.op("act", lambda: nc.scalar.activation(out=src, in_=src, func=AF.Identity, bias=nb[:, 0:1], scale=rs[:, 0:1]),
                  reads=rd + [bs], writes=rd)
            fw.op("dve", lambda: nc.vector.tensor_tensor(out=dst, in0=src, in1=lng, op=ALU.mult), reads=rd + [B("lnp")], writes=wr)
            fw.op("dve", lambda: nc.vector.tensor_tensor(out=dst, in0=dst, in1=lnb, op=ALU.add), reads=wr + [B("lnp")], writes=wr)

        def transpose_mod(xb, nt, hT, mi, j, tag):
            N = nt * 128
            for c in range(8):
                pb, pbb = next_psb()
                for i in range(nt):
                    fw.op("pe", lambda i=i, c=c, pb=pb: nc.tensor.transpose(out=pb[:, i * 128:(i + 1) * 128],
                                                                          in_=xb[:, i, c * 128:(c + 1) * 128], identity=ident[:]),
                          reads=[B("xb%d" % i), B("ident")], writes=[pbb], inc=(i == nt - 1))
                fw.op("act", lambda c=c, pb=pb: nc.scalar.activation(out=hT[:, c, 0:N], in_=pb[:, 0:N], func=AF.Identity,
                                                                    bias=modT[:, mi * 8 + c, j:j + 1],
                                                                    scale=modT[:, (mi + 1) * 8 + c, j:j + 1]),
                      reads=[pbb, B("modT")], writes=[B("%s%d" % (tag, c))])

        cur_in = xin
        for l in range(nlayers):
            last = (l == nlayers - 1) and (nlayers == DEPTH)
            lam_init = 0.8 - 0.6 * math.exp(-0.3 * l)
            dst = out if last else s_x1
            nblk = 4 if last else 5

            with ExitStack() as ph:
                alloc_psum(ph, 6, 2)
                wada = sb(ph, "wada", [128, 8, 6 * D], BF16)
                cTt = sb(ph, "cTt", [128, 16], F32)
                scf = sb(ph, "scf", [128, 16], F32)
                scb = sb(ph, "scb", [128, 8, 2], BF16)
                rep = [sb(ph, "rep%d" % j, [128, 8, 128], BF16) for j in range(2)]
                bT = sb(ph, "bT", [128, 48], F32)
                bg = sb(ph, "bg", [128, 2, D], F32)
                Gt = [sb(ph, "Gt%d" % i, [128, D], F32) for i in range(2)]
                for kc in range(8):
                    fw.dma("sp", wada[:, kc, :], s_ada[l, :, kc, :], reads=[B("s_ada%d_%d" % (l, kc))], writes=[B("wada%d" % kc)])
                fw.dma("sp", cTt[:], cT, writes=[B("cTt")])
                fw.dma("sp", bT[:], b_adaT[l], writes=[B("bT")])
                for gi, m in enumerate((2, 5)):
                    fw.dma("sp", bg[:, gi, :], b_ada[l, m * D:(m + 1) * D].partition_broadcast(128), writes=[B("bg")])
                fw.dma("sp", sinkE[:, 0:6], sink[l].partition_broadcast(128), writes=[B("sinkE")])
                fw.dma("sp", lamw[:, :, :], lamv[l].partition_broadcast(128), writes=[B("lamw")])
                fw.dma("sp", gs[:, 0:1], subg[l].rearrange("(d o) -> d o", o=1), writes=[B("gs")])
                fw.op("act", lambda: nc.scalar.activation(out=sinkE[:, 0:6], in_=sinkE[:, 0:6], func=AF.Exp),
                      reads=[B("sinkE")], writes=[B("sinkE")])
                fw.op("dve", lambda: nc.vector.tensor_tensor(out=lamw[:, 0, :], in0=lamw[:, 0, :], in1=lamw[:, 1, :], op=ALU.mult),
                      reads=[B("lamw")], writes=[B("lamw")])
                fw.op("dve", lambda: nc.vector.tensor_tensor(out=lamw[:, 2, :], in0=lamw[:, 2, :], in1=lamw[:, 3, :], op=ALU.mult),
                      reads=[B("lamw")], writes=[B("lamw")])
                fw.op("dve", lambda: nc.vector.reduce_sum(out=lamt[:, 0:1], in_=lamw[:, 0, :], axis=mybir.AxisListType.X),
                      reads=[B("lamw")], writes=[B("lamt")])
                fw.op("dve", lambda: nc.vector.reduce_sum(out=lamt[:, 1:2], in_=lamw[:, 2, :], axis=mybir.AxisListType.X),
                      reads=[B("lamw"), B("lamt")], writes=[B("lamt")])
                fw.op("act", lambda: nc.scalar.activation(out=lamt[:, 0:2], in_=lamt[:, 0:2], func=AF.Exp),
                      reads=[B("lamt")], writes=[B("lamt")])
                fw.op("dve", lambda: nc.vector.tensor_tensor(out=lamt[:, 2:3], in0=lamt[:, 0:1], in1=lamt[:, 1:2], op=ALU.subtract),
                      reads=[B("lamt")], writes=[B("lamt")])
                fw.op("dve", lambda: nc.vector.tensor_scalar(out=lamt[:, 3:4], in0=lamt[:, 2:3], scalar1=float(lam_init), scalar2=None,
                                                            op0=ALU.add), reads=[B("lamt")], writes=[B("lamt")])
                fw.op("dve", lambda: nc.vector.tensor_scalar(out=gs[:, 1:2], in0=gs[:, 0:1], scalar1=float(1.0 - lam_init), scalar2=None,
                                                            op0=ALU.mult), reads=[B("gs")], writes=[B("gs")])
                fw.op("act", lambda: nc.scalar.activation(out=scf[:], in_=cTt[:], func=AF.Silu), reads=[B("cTt")], writes=[B("scf")])
                fw.op("dve", lambda: nc.vector.tensor_copy(out=scb[:].rearrange("p k j -> p (k j)"), in_=scf[:]), reads=[B("scf")], writes=[B("scb")])
                for j in range(2):
                    for kc in range(8):
                        fw.op("dve", lambda j=j, kc=kc: nc.vector.tensor_scalar(out=rep[j][:, kc, :], in0=onesb[:, :],
                                                                               scalar1=scf[:, kc * 2 + j:kc * 2 + j + 1], scalar2=None, op0=ALU.mult),
                              reads=[B("scf"), B("onesb")], writes=[B("rep%d" % j)])
                ps, psb_ = next_ps()
                for mi, m in enumerate((0, 1, 3, 4)):
                    for c in range(8):
                        col0 = m * D + c * 128
                        o = (mi * 8 + c) * 2
                        for kc in range(8):
                            mm(ps[:, o:o + 2], wada[:, kc, col0:col0 + 128], scb[:, kc, 0:2], kc == 0, kc == 7,
                               [B("wada%d" % kc), B("scb")], [psb_])
                psv = ps[:, 0:64].rearrange("p (a j) -> p a j", j=2)
                for j in range(2):
                    for (a0, b0) in ((0, 0), (16, 24)):
                        fw.op("dve", lambda j=j, a0=a0, b0=b0: nc.vector.tensor_tensor(out=modT[:, a0:a0 + 16, j], in0=psv[:, a0:a0 + 16, j],
                                                                                       in1=bT[:, b0:b0 + 16], op=ALU.add),
                              reads=[psb_, B("bT")], writes=[B("modT")])
                for a0 in (8, 24):
                    fw.op("dve", lambda a0=a0: nc.vector.tensor_scalar(out=modT[:, a0:a0 + 8, :], in0=modT[:, a0:a0 + 8, :], scalar1=1.0,
                                                                      scalar2=None, op0=ALU.add), reads=[B("modT")], writes=[B("modT")])
                for j in range(2 if not last else 1):
                    for gi, m in enumerate((2, 5)):
                        for half in range(2):
                            ps, psb_ = next_ps()
                            for kc in range(8):
                                mm(ps[:, :], rep[j][:, kc, :], wada[:, kc, m * D + half * 512:m * D + (half + 1) * 512], kc == 0, kc == 7,
                                   [B("rep%d" % j), B("wada%d" % kc)], [psb_])
                            fw.op("dve", lambda ps=ps, gi=gi, half=half: nc.vector.tensor_tensor(
                                out=Gt[gi][:, half * 512:(half + 1) * 512], in0=ps[:, :], in1=bg[:, gi, half * 512:(half + 1) * 512], op=ALU.add),
                                reads=[psb_, B("bg")], writes=[B("Gt%d" % gi)])
                        fw.dma("pool", s_g[j, gi], Gt[gi][:], reads=[B("Gt%d" % gi)], writes=[B("s_g%d%d" % (j, gi))])
                fw.barrier()

            kvs = ExitStack()
            KT = sb(kvs, "KT", [128, 6, T], BF16)
            Vx = sb(kvs, "Vx", [128, NT, 12, 66], BF16)
            fw.op("pool", lambda: nc.gpsimd.memset(Vx[:, :, :, 64:66], 1.0), writes=[B("Vx")])
            with ExitStack() as ph:
                alloc_psum(ph, 6, 2)
                Win = sb(ph, "Win", [128, 8, NCOL], BF16)
                xsb = [sb(ph, "xsb%d" % s, [128, 4, D], F32) for s in range(2)]
                xb = sb(ph, "xb", [128, 4, D], BF16)
                hT = sb(ph, "hT", [128, 8, 512], BF16)
                rt = sb(ph, "rt", [128, 16, 512], F32)
                t1 = [sb(ph, "t1_%d" % s, [128, 512], F32) for s in range(2)]
                t2 = [sb(ph, "t2_%d" % s, [128, 512], F32) for s in range(2)]
                qst = [sb(ph, "qst%d" % s, [128, 512], BF16) for s in range(2)]
                for kc in range(8):
                    fw.dma("sp", Win[:, kc, :], s_win[l, :, kc, :], reads=[B("s_win%d_%d" % (l, kc))], writes=[B("Win%d" % kc)])
                WinB = [B("Win%d" % kc) for kc in range(8)]
                tcnt = [0]

                def load_x(bi):
                    nt = 4 if bi < 4 else 2
                    s = bi % 2
                    fw.dma("sp", xsb[s][:, 0:nt, :], cur_in[bi * 512:bi * 512 + nt * 128, :].rearrange("(i p) d -> p i d", p=128),
                           reads=[B("xs_%d_%d" % (l, bi))], writes=[B("xsb%d" % s)])

                load_x(0)
                for bi in range(5):
                    nt = 4 if bi < 4 else 2
                    N = nt * 128
                    tok0 = bi * 512
                    j = 0 if bi < 4 else 1
                    s = bi % 2
                    if bi + 1 < 5:
                        load_x(bi + 1)
                    fw.dma("sp", rt[:, :, 0:N], rope[:, :, tok0:tok0 + N].rearrange("k p t -> p k t"), writes=[B("rt%d" % k) for k in range(16)])
                    for i in range(nt):
                        fw.op("act" if i % 2 else "dve",
                              (lambda i=i: nc.scalar.copy(out=xb[:, i, :], in_=xsb[s][:, i, :])) if i % 2 else
                              (lambda i=i: nc.vector.tensor_copy(out=xb[:, i, :], in_=xsb[s][:, i, :])),
                              reads=[B("xsb%d" % s)], writes=[B("xb%d" % i)])
                    transpose_mod(xb, nt, hT, 0, j, "hT")
                    hTB = [B("hT%d" % c) for c in range(8)]

                    def proj(ch):
                        ps, pb_ = next_ps()
                        for kc in range(8):
                            mm(ps[:, 0:N], Win[:, kc, ch * 128:(ch + 1) * 128], hT[:, kc, 0:N], kc == 0, kc == 7, [WinB[kc], hTB[kc]], [pb_])
                        return ps, pb_

                    def roped(ch, chp, tc, ts, dst_ap, dst_b):
                        pa, pab = proj(ch)
                        pp, ppb = proj(chp)
                        rope_comb(pa, pab, pp, ppb, tc, ts, dst_ap, dst_b)

                    def rope_comb(pa, pab, pp, ppb, tc, ts, dst_ap, dst_b):
                        k = tcnt[0] % 2
                        tcnt[0] += 1
                        fw.op("dve", lambda: nc.vector.tensor_tensor(out=t1[k][:, 0:N], in0=pa[:, 0:N], in1=rt[:, tc, 0:N], op=ALU.mult),
                              reads=[pab, B("rt%d" % tc)], writes=[B("t1_%d" % k)])
                        fw.op("dve", lambda: nc.vector.tensor_tensor(out=t2[k][:, 0:N], in0=pp[:, 0:N], in1=rt[:, ts, 0:N], op=ALU.mult),
                              reads=[ppb, B("rt%d" % ts)], writes=[B("t2_%d" % k)])
                        fw.op("pool", lambda: nc.gpsimd.tensor_tensor(out=dst_ap, in0=t1[k][:, 0:N], in1=t2[k][:, 0:N], op=ALU.add),
                              reads=[B("t1_%d" % k), B("t2_%d" % k)], writes=dst_b)

                    def q_out(qi, fn):
                        k = tcnt[0] % 2
                        fn(qst[k][:, 0:N], [B("qst%d" % k)])
                        fw.dma("sp", s_q[qi, :, tok0:tok0 + N], qst[k][:, 0:N], reads=[B("qst%d" % k)], writes=[B("s_q%d" % bi)])

                    need_q = (bi < 4) or (not last)
                    if need_q:
                        for jq in range(3):
                            pa, pab = proj(jq)
                            pp, ppb = proj(4 + jq)
                            for g in range(2):
                                q_out(g * 3 + jq, lambda ap, bb, g=g: rope_comb(pa, pab, pp, ppb, 4 + 2 * g, 5 + 2 * g, ap, bb))
                    roped(3, 7, 0, 1, KT[:, 0, tok0:tok0 + N], [B("KT0_%d" % bi)])
                    if need_q:
                        for c in range(2):
                            pa, pab = proj(8 + c)
                            pp, ppb = proj(12 + c)
                            for v in range(4):
                                q_out(6 + c * 4 + v, lambda ap, bb, v=v: rope_comb(pa, pab, pp, ppb, 8 + 2 * v, 9 + 2 * v, ap, bb))
                    for c in range(2):
                        roped(10 + c, 14 + c, 2, 3, KT[:, 1 + c, tok0:tok0 + N], [B("KT%d_%d" % (1 + c, bi))])
                    if need_q:
                        for c in range(3):
                            ps, pb_ = proj(16 + c)
                            for hb in range(2):
                                def cq(ap, bb, ps=ps, pb_=pb_, hb=hb):
                                    fw.op("act", lambda: nc.scalar.activation(out=ap, in_=ps[:, 0:N], func=AF.Copy, scale=maskcol[:, hb:hb + 1]),
                                          reads=[pb_, B("maskcol")], writes=bb)
                                q_out(14 + c * 2 + hb, cq)
                                tcnt[0] += 1
                    for c in range(3):
                        ps, pb_ = proj(19 + c)
                        fw.op("act", lambda ps=ps, c=c: nc.scalar.copy(out=KT[:, 3 + c, tok0:tok0 + N], in_=ps[:, 0:N]),
                              reads=[pb_], writes=[B("KT%d_%d" % (3 + c, bi))])
                    for i in range(nt):
                        gt = bi * 4 + i
                        ps, pb_ = next_ps()
                        for kc in range(8):
                            mm(ps[:, :], hT[:, kc, i * 128:(i + 1) * 128], Win[:, kc, NCH * 128:NCH * 128 + 512], kc == 0, kc == 7,
                               [WinB[kc], hTB[kc]], [pb_])
                        fw.op("act", lambda ps=ps, gt=gt: nc.scalar.copy(out=Vx[:, gt, 0:8, 0:64], in_=ps[:, :].rearrange("p (h d) -> p h d", d=64)),
                              reads=[pb_], writes=[B("Vx")])
                        ps, pb_ = next_ps()
                        for kc in range(8):
                            mm(ps[:, 0:256], hT[:, kc, i * 128:(i + 1) * 128], Win[:, kc, NCH * 128 + 512:NCH * 128 + 768], kc == 0, kc == 7,
                               [WinB[kc], hTB[kc]], [pb_])
                        fw.op("dve", lambda ps=ps, gt=gt: nc.vector.tensor_copy(out=Vx[:, gt, 8:12, 0:64],
                                                                                in_=ps[:, 0:256].rearrange("p (h d) -> p h d", d=64)),
                              reads=[pb_], writes=[B("Vx")])
                fw.barrier()

            with ExitStack() as ph:
                alloc_psum(ph, 8, 0)
                Et = sb(ph, "Et", [128, 6, NV * 256], BF16)
                nmk = sb(ph, "nmk", [128, NV * 256], BF16)
                MA = sb(ph, "MA", [128, 768], BF16)
                QTb = [sb(ph, "QTb%d" % s, [128, 20, 512], BF16) for s in range(2)]
                OTb = sb(ph, "OTb", [64, 16, 512], BF16)
                NPT = 6
                PT = [sb(ph, "PT%d" % s, [128, 512], BF16) for s in range(NPT)]
                bcs = [sb(ph, "bcs%d" % s, [64, 512], F32) for s in range(4)]
                tA = [sb(ph, "tA%d" % s, [64, 512], F32) for s in range(4)]
                sq = sb(ph, "sq", [128, 512], BF16)
                cnt = {"pt": 0, "ss": 0, "bc": 0}
                if l + 1 < nlayers:
                    cast_weights(l + 1)
                with ExitStack() as sub:
                    gstg = sb(sub, "gstg", [128, NV * 256], F32)
                    fw.dma("pool", nmk[:], nmask, writes=[B("nmk")])
                    fw.dma("pool", MA[:], amask, writes=[B("MA")])
                    for h in range(6):
                        fw.dma("sp", gstg[:], nag[l, h], writes=[B("gstg")])
                        fw.op("act", lambda h=h: nc.scalar.activation(out=Et[:, h, :], in_=gstg[:], func=AF.Exp), reads=[B("gstg")], writes=[B("Et%d" % h)])
                        fw.op("dve", lambda h=h: nc.vector.tensor_tensor(out=Et[:, h, :], in0=Et[:, h, :], in1=nmk[:], op=ALU.mult),
                              reads=[B("Et%d" % h), B("nmk")], writes=[B("Et%d" % h)])
                    fw.barrier()

                rb = [sb(ph, "rb%d" % s_, [128, 512], BF16) for s_ in range(4)]
                for s_ in range(4):
                    fw.op("pool", lambda s_=s_: nc.gpsimd.memset(rb[s_][:], 0.0), writes=[B("rb%d" % s_)])
                cnt["rb"] = 0
                fw.op("pool", lambda: nc.gpsimd.memset(sq[:], 0.0), writes=[B("sq")])

                def v_lhsT(t, hh):
                    return Vx[:, t, hh, 0:65]

                def bcast_rows(rows, N_):
                    res = []
                    for wr in rows:
                        k = cnt["rb"] % 4
                        cnt["rb"] += 1
                        wr(rb[k], B("rb%d" % k))
                        pbc, pbcb = next_ps("x")
                        mm(pbc[:, 0:N_], sel[:, :], rb[k][:, 0:N_], True, True, [B("sel"), B("rb%d" % k)], [pbcb])
                        res.append((pbc, pbcb))
                    return res

                def recips(pbcs, N_):
                    res = []
                    for pbc, pbcb in pbcs:
                        c0, c0b = new_bc()
                        fw.op("dve", lambda pbc=pbc, c0=c0: nc.vector.reciprocal(out=c0[0:64, 0:N_], in_=pbc[0:64, 0:N_]), reads=[pbcb], writes=[c0b])
                        res.append((c0, c0b))
                    return res

                def load_q(bi):
                    nt = 4 if bi < 4 else 2
                    s = bi % 2
                    fw.dma("sp", QTb[s][:, :, 0:nt * 128], s_q[:, :, bi * 512:bi * 512 + nt * 128].rearrange("c p t -> p c t"),
                           reads=[B("s_q%d" % bi)], writes=[B("QTb%d" % s)])

                def new_pt():
                    k = cnt["pt"] % NPT
                    cnt["pt"] += 1
                    return PT[k], B("PT%d" % k)

                def new_bc():
                    k = cnt["bc"] % 4
                    cnt["bc"] += 1
                    return bcs[k], B("bcs%d" % k)

                live_acc = set()
                flush_hook = [lambda: None]

                def new_acc():
                    i = pcnt["a"] % 4
                    if i in live_acc:
                        flush_hook[0]()
                    assert i not in live_acc, "accumulator bank still live"
                    live_acc.add(i)
                    pcnt["a"] += 1
                    return psF[i], B("psf%d" % i), i

                def gen_jobs(Q, Qb, bi, ctxq, nt, N):
                    units = []
                    for n in range(nt):
                        gq = bi * 4 + n
                        for g in range(2):
                            if ctxq:
                                tiles = [(16, None), (17, None)]
                            else:
                                tiles = []
                                if gq - 1 >= 0:
                                    tiles.append((gq - 1, 0))
                                tiles.append((gq, None))
                                if gq + 1 < 16:
                                    tiles.append((gq + 1, 1))
                                tiles += [(16, None), (17, None)]
                            u = {"jobs": []}
                            for ti, (t, mk) in enumerate(tiles):
                                def S(u=u, ti=ti, t=t, g=g, n=n):
                                    if ti == 0:
                                        u["po"] = new_acc()
                                    ps, psb_ = next_ps("s")
                                    mm(ps[:, 0:384].rearrange("p (a b) -> p a b", a=3), KT[:, 0, t * 128:(t + 1) * 128],
                                       Q[:, g * 3:g * 3 + 3, n * 128:(n + 1) * 128], True, True, [B("KT0_%d" % (t // 4)), Qb], [psb_])
                                    return ps, psb_

                                def rest(st, u=u, ti=ti, t=t, mk=mk, g=g, last=(ti == len(tiles) - 1)):
                                    ps, psb_ = st
                                    po, pob, _ = u["po"]
                                    pt, ptb = new_pt()
                                    fw.op("act", lambda: nc.scalar.activation(out=pt[:, 0:384], in_=ps[:, 0:384], func=AF.Exp, scale=0.125),
                                          reads=[psb_], writes=[ptb])
                                    if mk is not None:
                                        fw.op("pool", lambda: nc.gpsimd.tensor_tensor(out=pt[:, 0:384], in0=pt[:, 0:384],
                                                                                    in1=MA[:, mk * 384:(mk + 1) * 384], op=ALU.mult),
                                              reads=[ptb, B("MA")], writes=[ptb])
                                    mm(po[0:65, 0:384], v_lhsT(t, g), pt[:, 0:384], ti == 0, last, [B("Vx"), ptb], [pob])
                                u["jobs"].append((S, rest))

                            def fin(u=u, g=g, n=n):
                                po, pob, bank = u["po"]
                                stt = {}

                                def wr(rbt, rbb):
                                    for sg in range(3):
                                        h = 3 * g + sg
                                        fw.op("dve", lambda sg=sg, h=h: nc.vector.tensor_scalar(
                                            out=rbt[64:65, sg * 128:(sg + 1) * 128], in0=po[64:65, sg * 128:(sg + 1) * 128],
                                            scalar1=sinkE[64:65, h:h + 1], scalar2=None, op0=ALU.add), reads=[pob, B("sinkE")], writes=[rbb])

                                def s1():
                                    stt["p"] = bcast_rows([wr], 384)

                                def s2():
                                    stt["c"] = recips(stt["p"], 384)

                                def s3():
                                    ((c0, c0b),) = stt["c"]
                                    fw.op("dve", lambda: nc.vector.tensor_tensor(
                                        out=OTb[0:64, 3 * g:3 * g + 3, n * 128:(n + 1) * 128], in0=po[0:64, 0:384].rearrange("p (a b) -> p a b", a=3),
                                        in1=c0[0:64, 0:384].rearrange("p (a b) -> p a b", a=3), op=ALU.mult),
                                        reads=[pob, c0b], writes=[B("OTb")])
                                    live_acc.discard(bank)
                                return [s1, s2, s3]
                            u["fin"] = fin
                            units.append(u)
                    tilesB = [16, 17] if ctxq else list(range(18))
                    for h in range(4):
                        r0 = (h % 2) * 64
                        u = {"jobs": []}
                        for ti, t in enumerate(tilesB):
                            for m in range(2):
                                def S(u=u, ti=ti, t=t, m=m, h=h, r0=r0):
                                    if ti == 0 and m == 0:
                                        u["po"] = [new_acc(), new_acc()]
                                    ps, psb_ = next_ps("s")
                                    mm(ps[:, 0:N], KT[:, 1 + h // 2, t * 128:(t + 1) * 128], Q[:, 6 + (h // 2) * 4 + (h % 2) * 2 + m, 0:N],
                                       True, True, [B("KT%d_%d" % (1 + h // 2, t // 4)), Qb], [psb_])
                                    return ps, psb_

                                def rest(st, u=u, ti=ti, t=t, m=m, h=h, last=(ti == len(tilesB) - 1)):
                                    ps, psb_ = st
                                    po, pob, _ = u["po"][m]
                                    pt, ptb = new_pt()
                                    fw.op("act", lambda: nc.scalar.activation(out=pt[:, 0:N], in_=ps[:, 0:N], func=AF.Exp, scale=float(32 ** -0.5)),
                                          reads=[psb_], writes=[ptb])
                                    mm(po[0:65, 0:N], v_lhsT(t, 2 + h), pt[:, 0:N], ti == 0, last, [B("Vx"), ptb], [pob])
                                u["jobs"].append((S, rest))

                        def fin(u=u, h=h):
                            (p0, p0b, b0_), (p1, p1b, b1_) = u["po"]
                            stt = {}

                            def wr0(rbt, rbb):
                                fw.op("dve", lambda: nc.vector.tensor_copy(out=rbt[64:65, 0:N], in_=p0[64:65, 0:N]), reads=[p0b], writes=[rbb])

                            def wr1(rbt, rbb):
                                fw.op("dve", lambda: nc.vector.tensor_copy(out=rbt[64:65, 0:N], in_=p1[64:65, 0:N]), reads=[p1b], writes=[rbb])

                            def s1():
                                stt["p"] = bcast_rows([wr0, wr1], N)

                            def s2():
                                stt["c"] = recips(stt["p"], N)

                            def s3():
                                (c0, c0b), (c1, c1b) = stt["c"]
                                fw.op("dve", lambda: nc.vector.tensor_tensor(out=tA[0][:, 0:N], in0=p0[0:64, 0:N], in1=c0[0:64, 0:N], op=ALU.mult),
                                      reads=[p0b, c0b], writes=[B("tA0")])
                                fw.op("dve", lambda: nc.vector.scalar_tensor_tensor(out=tA[1][:, 0:N], in0=p1[0:64, 0:N], scalar=lamt[0:64, 3:4],
                                                                                   in1=c1[0:64, 0:N], op0=ALU.mult, op1=ALU.mult),
                                      reads=[p1b, c1b, B("lamt")], writes=[B("tA1")])
                                live_acc.discard(b0_)
                                live_acc.discard(b1_)
                                fw.op("pool", lambda: nc.gpsimd.tensor_tensor(out=tA[2][:, 0:N], in0=tA[0][:, 0:N], in1=tA[1][:, 0:N], op=ALU.subtract),
                                      reads=[B("tA0"), B("tA1")], writes=[B("tA2")])
                                fw.op("pool", lambda: nc.gpsimd.tensor_tensor(out=sq[0:64, 0:N], in0=tA[2][:, 0:N], in1=tA[2][:, 0:N], op=ALU.mult),
                                      reads=[B("tA2")], writes=[B("sq")])

                            def s4():
                                pms, pmsb = next_ps("x")
                                stt["pms"] = (pms, pmsb)
                                mm(pms[:, 0:N], o64[:, :], sq[:, 0:N], True, True, [B("o64"), B("sq")], [pmsb])

                            def s5():
                                pms, pmsb = stt["pms"]
                                fw.op("act", lambda: nc.scalar.activation(out=tA[3][:, 0:N], in_=pms[0:64, 0:N], func=AF.Ln, bias=epsc[0:64, 0:1], scale=1.0),
                                      reads=[pmsb, B("epsc")], writes=[B("tA3")])
                                fw.op("act", lambda: nc.scalar.activation(out=tA[3][:, 0:N], in_=tA[3][:, 0:N], func=AF.Exp, scale=-0.5),
                                      reads=[B("tA3")], writes=[B("tA3")])

                            def s6():
                                fw.op("dve", lambda: nc.vector.scalar_tensor_tensor(out=OTb[0:64, 6 + h, 0:N], in0=tA[2][:, 0:N], scalar=gs[:, 1:2],
                                                                                   in1=tA[3][:, 0:N], op0=ALU.mult, op1=ALU.mult),
                                      reads=[B("tA2"), B("tA3"), B("gs")], writes=[B("OTb")])
                            return [s1, s2, s3, s4, s5, s6]
                        u["fin"] = fin
                        units.append(u)
                    for h in range(6):
                        r0 = (h % 2) * 64
                        for jj in range(1 if ctxq else 2):
                            if ctxq:
                                tiles = [(16, None), (17, None)]
                            else:
                                jg = bi * 2 + jj
                                if jg == 0:
                                    tiles = [(uu, 6 + uu) for uu in range(4)]
                                elif jg == 7:
                                    tiles = [(uu, 10 + uu - 12) for uu in range(12, 16)]
                                else:
                                    tiles = [(2 * jg - 2 + i, i) for i in range(6)]
                                tiles += [(16, None), (17, None)]
                            u = {"jobs": []}
                            for ti, (t, ev) in enumerate(tiles):
                                def S(u=u, ti=ti, t=t, h=h, r0=r0, jj=jj):
                                    if ti == 0:
                                        u["po"] = new_acc()
                                    ps, psb_ = next_ps("s")
                                    mm(ps[:, 0:256], KT[:, 3 + h // 2, t * 128:(t + 1) * 128], Q[:, 14 + (h // 2) * 2 + (h % 2), jj * 256:(jj + 1) * 256],
                                       True, True, [B("KT%d_%d" % (3 + h // 2, t // 4)), Qb], [psb_])
                                    return ps, psb_

                                def rest(st, u=u, ti=ti, t=t, ev=ev, h=h, last=(ti == len(tiles) - 1)):
                                    ps, psb_ = st
                                    po, pob, _ = u["po"]
                                    pt, ptb = new_pt()
                                    fw.op("act", lambda: nc.scalar.activation(out=pt[:, 0:256], in_=ps[:, 0:256], func=AF.Exp, scale=0.125),
                                          reads=[psb_], writes=[ptb])
                                    if ev is not None:
                                        fw.op("dve", lambda: nc.vector.tensor_tensor(out=pt[:, 0:256], in0=pt[:, 0:256],
                                                                                    in1=Et[:, h, ev * 256:(ev + 1) * 256], op=ALU.mult),
                                              reads=[ptb, B("Et%d" % h)], writes=[ptb])
                                    mm(po[0:65, 0:256], v_lhsT(t, 6 + h), pt[:, 0:256], ti == 0, last, [B("Vx"), ptb], [pob])
                                u["jobs"].append((S, rest))

                            def fin(u=u, h=h, jj=jj):
                                po, pob, bank = u["po"]
                                stt = {}

                                def wr(rbt, rbb):
                                    fw.op("dve", lambda: nc.vector.tensor_copy(out=rbt[64:65, 0:256], in_=po[64:65, 0:256]), reads=[pob], writes=[rbb])

                                def s1():
                                    stt["p"] = bcast_rows([wr], 256)

                                def s2():
                                    stt["c"] = recips(stt["p"], 256)

                                def s3():
                                    ((c0, c0b),) = stt["c"]
                                    fw.op("dve", lambda: nc.vector.tensor_tensor(out=OTb[0:64, 10 + h, jj * 256:(jj + 1) * 256], in0=po[0:64, 0:256],
                                                                                in1=c0[0:64, 0:256], op=ALU.mult), reads=[pob, c0b], writes=[B("OTb")])
                                    live_acc.discard(bank)
                                return [s1, s2, s3]
                            u["fin"] = fin
                            units.append(u)
                    return units

                nqb = 4 if last else 5
                load_q(0)
                for bi in range(nqb):
                    ctxq = bi == 4
                    nt = 4 if bi < 4 else 2
                    N = nt * 128
                    tok0 = bi * 512
                    s = bi % 2
                    if bi + 1 < nqb:
                        load_q(bi + 1)
                    units = gen_jobs(QTb[s], B("QTb%d" % s), bi, ctxq, nt, N)
                    jobs = []
                    for u in units:
                        for ji, (S, rest) in enumerate(u["jobs"]):
                            jobs.append((S, rest, u["fin"] if ji == len(u["jobs"]) - 1 else None))
                    pending = []
                    flush_all = [None]

                    def flush():
                        while pending:
                            pending.pop(0)[1]()
                    flush_hook[0] = flush
                    st_next = jobs[0][0]()
                    for k, (S, rest, fin) in enumerate(jobs):
                        st_cur = st_next
                        if k + 1 < len(jobs):
                            st_next = jobs[k + 1][0]()
                        rest(st_cur)
                        if fin is not None:
                            for si, stg in enumerate(fin()):
                                pending.append((k + 2 + si, stg))
                            pending.sort(key=lambda x: x[0])
                        while pending and pending[0][0] <= k:
                            pending.pop(0)[1]()
                    flush()
                    fw.dma("pool", s_o[:, :, tok0:tok0 + N].rearrange("h p t -> p h t"), OTb[:, :, 0:N], reads=[B("OTb")], writes=[B("s_o%d" % bi)])
                fw.barrier()
            kvs.close()

            with ExitStack() as ph:
                alloc_psum(ph, 6, 2)
                Wo = sb(ph, "Wo", [64, 16, D], BF16)
                xsb = [sb(ph, "xsb%d" % s, [128, 4, D], F32) for s in range(2)]
                OT2 = sb(ph, "OT2", [64, 16, 512], BF16)
                ub = sb(ph, "ub", [128, 4, D], F32)
                xb = sb(ph, "xb", [128, 4, D], BF16)
                hT = sb(ph, "hT", [128, 8, 512], BF16)
                AT = sb(ph, "AT", [128, 22, 512], BF16)
                Wi = [sb(ph, "Wi%d" % s, [128, 8, 512], BF16) for s in range(2)]
                Wf = [sb(ph, "Wf%d" % s, [128, 11, 512], BF16) for s in range(2)]
                lnt = sb(ph, "lnt", [128, 4, D], F32)
                Gx = sb(ph, "Gx", [128, 2, D], F32)
                sgt = [sb(ph, "sgt%d" % s, [128, 512], F32) for s in range(2)]
                ph_ln = {"lnst": [sb(ph, "lnst%d" % s, [128, 12], F32) for s in range(4)],
                         "lnmv": [sb(ph, "lnmv%d" % s, [128, 2], F32) for s in range(4)],
                         "lnrs": [sb(ph, "lnrs%d" % s, [128, 1], F32) for s in range(4)],
                         "lnnb": [sb(ph, "lnnb%d" % s, [128, 1], F32) for s in range(4)]}
                lncnt = [0]
                wic = [0]
                wfc = [0]
                fw.dma("sp", Wo[:], s_wo[l], reads=[B("s_wo%d_%d" % (l, h)) for h in range(16)], writes=[B("Wo")])
                for k in range(4):
                    fw.dma("sp", lnt[:, k, :], lnp[l, k].partition_broadcast(128), writes=[B("lnp")])

                def load_blk(bi):
                    nt = 4 if bi < 4 else 2
                    s = bi % 2
                    fw.dma("sp", xsb[s][:, 0:nt, :], cur_in[bi * 512:bi * 512 + nt * 128, :].rearrange("(i p) d -> p i d", p=128),
                           reads=[B("xs_%d_%d" % (l, bi))], writes=[B("xsb%d_%d" % (s, i)) for i in range(4)])

                def load_wi(cp):
                    s = wic[0] % 2
                    wic[0] += 1
                    fw.dma("sp", Wi[s][:], s_wfi[l, :, :, cp * 512:(cp + 1) * 512], reads=[B("s_wfi%d_%d" % (l, kc)) for kc in range(8)], writes=[B("Wi%d" % s)])
                    return s

                def load_wf(half, cg):
                    s = wfc[0] % 2
                    wfc[0] += 1
                    fw.dma("sp", Wf[s][:], s_wfo[l, :, cg * 11:(cg + 1) * 11, half * 512:(half + 1) * 512], reads=[B("s_wfo%d_%d" % (l, cg * 11 + cl)) for cl in range(11)], writes=[B("Wf%d" % s)])
                    return s

                def gated_res_ln(ps, pb_, i, half, gi, s):
                    cs = slice(half * 512, (half + 1) * 512)
                    fw.op("dve", lambda: nc.vector.tensor_tensor(out=ub[:, i, cs], in0=ps[:, :], in1=Gx[:, gi, cs], op=ALU.mult),
                          reads=[pb_, B("Gx")], writes=[B("ub%d" % i)])
                    fw.op("dve", lambda: nc.vector.scalar_tensor_tensor(out=ub[:, i, cs], in0=xsb[s][:, i, cs], scalar=float(ALPHA), in1=ub[:, i, cs],
                                                                       op0=ALU.mult, op1=ALU.add),
                          reads=[B("ub%d" % i), B("xsb%d_%d" % (s, i))], writes=[B("ub%d" % i)])

                load_blk(0)
                for bi in range(nblk):
                    nt = 4 if bi < 4 else 2
                    N = nt * 128
                    tok0 = bi * 512
                    j = 0 if bi < 4 else 1
                    s = bi % 2
                    if bi == 0 or bi == 4:
                        for gi in range(2):
                            fw.dma("sp", Gx[:, gi, :], s_g[j, gi], reads=[B("s_g%d%d" % (j, gi))], writes=[B("Gx")])
                    fw.dma("sp", OT2[:, :, 0:N], s_o[:, :, tok0:tok0 + N].rearrange("h p t -> p h t"), reads=[B("s_o%d" % bi)], writes=[B("OT2")])
                    if bi + 1 < nblk:
                        load_blk(bi + 1)
                    w0 = load_wi(0)
                    for i in range(nt):
                        for half in range(2):
                            ps, pb_ = next_ps()
                            for h in range(16):
                                mm(ps[:, :], OT2[0:64, h, i * 128:(i + 1) * 128], Wo[0:64, h, half * 512:(half + 1) * 512], h == 0, h == 15,
                                   [B("OT2"), B("Wo")], [pb_])
                            gated_res_ln(ps, pb_, i, half, 0, s)
                        k = lncnt[0] % 4
                        lncnt[0] += 1
                        layer_norm(ub[:, i, :], xsb[s][:, i, :], lnt[:, 0, :], lnt[:, 1, :], k, [B("ub%d" % i)], [B("xsb%d_%d" % (s, i))], ph_ln)
                        fw.op("act", lambda i=i: nc.scalar.copy(out=xb[:, i, :], in_=xsb[s][:, i, :]), reads=[B("xsb%d_%d" % (s, i))], writes=[B("xb%d" % i)])
                    transpose_mod(xb, nt, hT, 2, j, "hT")
                    hTB = [B("hT%d" % c) for c in range(8)]
                    for cp in range(11):
                        ws = w0
                        if cp + 1 < 11:
                            w0 = load_wi(cp + 1)
                        for cc in range(2):
                            c = 2 * cp + cc
                            pg, pgb = next_ps()
                            for kc in range(8):
                                mm(pg[:, 0:N], Wi[ws][:, kc, cc * 256:cc * 256 + 128], hT[:, kc, 0:N], kc == 0, kc == 7, [B("Wi%d" % ws), hTB[kc]], [pgb])
                            pu, pub = next_ps()
                            for kc in range(8):
                                mm(pu[:, 0:N], Wi[ws][:, kc, cc * 256 + 128:cc * 256 + 256], hT[:, kc, 0:N], kc == 0, kc == 7, [B("Wi%d" % ws), hTB[kc]], [pub])
                            k = c % 2
                            fw.op("act", lambda pg=pg, k=k: nc.scalar.activation(out=sgt[k][:, 0:N], in_=pg[:, 0:N], func=AF.Silu), reads=[pgb], writes=[B("sgt%d" % k)])
                            fw.op("dve", lambda pu=pu, k=k, c=c: nc.vector.tensor_tensor(out=AT[:, c, 0:N], in0=pu[:, 0:N], in1=sgt[k][:, 0:N], op=ALU.mult),
                                  reads=[pub, B("sgt%d" % k)], writes=[B("AT%d" % c)])
                    f0 = load_wf(0, 0)
                    for half in range(2):
                        accs = [next_ps() for _ in range(nt)]
                        for cg in range(2):
                            fs = f0
                            if not (half == 1 and cg == 1):
                                f0 = load_wf(half if cg == 0 else half + 1, 1 - cg)
                            for i in range(nt):
                                for cl in range(11):
                                    c = cg * 11 + cl
                                    mm(accs[i][0][:, :], AT[:, c, i * 128:(i + 1) * 128], Wf[fs][:, cl, :], cg == 0 and cl == 0, cg == 1 and cl == 10,
                                       [B("AT%d" % c), B("Wf%d" % fs)], [accs[i][1]])
                        for i in range(nt):
                            gated_res_ln(accs[i][0], accs[i][1], i, half, 1, s)
                    for i in range(nt):
                        k = lncnt[0] % 4
                        lncnt[0] += 1
                        layer_norm(ub[:, i, :], xsb[s][:, i, :], lnt[:, 2, :], lnt[:, 3, :], k, [B("ub%d" % i)], [B("xsb%d_%d" % (s, i))], ph_ln)
                    fw.dma("pool", dst[tok0:tok0 + N, :].rearrange("(i p) d -> p i d", p=128), xsb[s][:, 0:nt, :],
                           reads=[B("xsb%d_%d" % (s, i)) for i in range(nt)], writes=[B("xs_%d_%d" % (l + 1, bi))])
                fw.barrier()
            cur_in = s_x1
    return nc


_NC_CACHE = {}


def _host_inputs(inputs):
    cst = _consts()
    f = np.float32
    g = {k: np.asarray(v) for k, v in inputs.items()}
    shared = {}
    shared["w_ada"] = np.ascontiguousarray(g["w_ada"], dtype=f)
    shared["b_ada"] = np.ascontiguousarray(g["b_ada"], dtype=f)
    shared["b_adaT"] = np.ascontiguousarray(g["b_ada"].reshape(DEPTH, 48, 128).transpose(0, 2, 1), dtype=f)
    shared["win"] = np.ascontiguousarray(g["w_in"][:, :, cst["cols"]], dtype=f)
    shared["w_o"] = np.ascontiguousarray(g["w_o"], dtype=f)
    shared["sink"] = np.ascontiguousarray(g["sink"], dtype=f)
    shared["lamv"] = np.ascontiguousarray(np.stack([g["lam_q1"], g["lam_k1"], g["lam_q2"], g["lam_k2"]], axis=1), dtype=f)
    shared["subg"] = np.ascontiguousarray(g["subln_g"], dtype=f)
    nb = g["na_bias"]
    gath = nb[:, :, cst["dr"], cst["dc"]]
    shared["nag"] = np.ascontiguousarray(gath.transpose(0, 1, 3, 2, 4).reshape(DEPTH, 6, 128, NV * 256), dtype=f)
    shared["nmask"] = cst["nmask"]
    shared["amask"] = cst["amask"]
    shared["lnp"] = np.ascontiguousarray(np.stack([g["ln1_g"], g["ln1_b"], g["ln2_g"], g["ln2_b"]], axis=1), dtype=f)
    shared["wfi"] = np.ascontiguousarray(g["w_ffn_in"][:, :, cst["ffperm"]], dtype=f)
    shared["wfo"] = np.ascontiguousarray(g["w_ffn_out"], dtype=f)
    shared["rope"] = cst["rope"]
    per = []
    for b in range(g["x"].shape[0]):
        m = dict(shared)
        m["xin"] = np.ascontiguousarray(np.concatenate([g["x"][b], g["ctx"][b]], axis=0), dtype=f)
        cc = np.stack([g["c"][b].reshape(8, 128).T, g["c_ctx"].reshape(8, 128).T], axis=-1)
        m["cT"] = np.ascontiguousarray(cc.reshape(128, 16), dtype=f)
        per.append(m)
    return per


def kernel(**inputs):
    per = _host_inputs(inputs)
    if "nc" not in _NC_CACHE:
        _NC_CACHE["nc"] = build()
    nc = _NC_CACHE["nc"]
    n = len(per)
    res = run_bass_kernel_spmd(nc, per, core_ids=list(range(n)))
    return np.stack([np.asarray(r["out"]) for r in res.results], axis=0).astype(np.float32)
```

```python
import math
from contextlib import ExitStack

import numpy as np
import concourse.bass as bass
import concourse.mybir as mybir
from concourse.bass_utils import run_bass_kernel_spmd

F32 = mybir.dt.float32
BF16 = mybir.dt.bfloat16
AF = mybir.ActivationFunctionType
ALU = mybir.AluOpType

D = 1024
L = 2048
C = 256
T = L + C
NT = T // 128
DFF = 2816
NCH = 22
NCOL = NCH * 128 + 768
DEPTH = 2
ALPHA = (2.0 * DEPTH) ** 0.25
EPS = 1e-5
NV = 14
import os
FILL_N = 0


class Buf:
    __slots__ = ("name", "w", "r")

    def __init__(self, name):
        self.name = name
        self.w = None
        self.r = {}


class FW:
    NDMA = 32

    def __init__(self, nc, stack):
        self.nc = nc
        self.eng = {"pe": nc.tensor, "act": nc.scalar, "dve": nc.vector, "pool": nc.gpsimd, "sp": nc.sync}
        self.sems = {}
        self.cnt = {}
        self.known = {e: {} for e in self.eng}
        for e in ("pe", "act", "dve", "pool"):
            self.sems[e] = stack.enter_context(nc.semaphore("s_" + e))
            self.cnt[e] = 0
        self.dsems = []
        for i in range(self.NDMA):
            k = "dma%d" % i
            self.sems[k] = stack.enter_context(nc.semaphore(k))
            self.cnt[k] = 0
            self.dsems.append(k)
        self.rr = {"hw": 0, "sw": 0}
        self.bufs = {}

    def B(self, name):
        b = self.bufs.get(name)
        if b is None:
            b = self.bufs[name] = Buf(name)
        return b

    def _wait(self, e, toks):
        need = {}
        for t in toks:
            if t is None:
                continue
            k, v = t
            if e == "pe" and k == "pe":
                continue
            if self.known[e].get(k, 0) >= v:
                continue
            if need.get(k, 0) < v:
                need[k] = v
        for k, v in need.items():
            self.eng[e].wait_ge(self.sems[k], v)
            self.known[e][k] = v

    @staticmethod
    def _deps(reads, writes):
        toks = []
        for b in reads:
            toks.append(b.w)
        for b in writes:
            toks.append(b.w)
            for k, v in b.r.items():
                toks.append((k, v))
        return toks

    @staticmethod
    def _mark(tok, reads, writes):
        k, v = tok
        for b in reads:
            if b.r.get(k, 0) < v:
                b.r[k] = v
        for b in writes:
            b.w = tok
            b.r = {}

    def op(self, e, fn, reads=(), writes=(), inc=True):
        self._wait(e, self._deps(reads, writes))
        ins = fn()
        if inc:
            self.cnt[e] += 1
            ins.then_inc(self.sems[e], 1)
            tok = (e, self.cnt[e])
        else:
            tok = (e, self.cnt[e] + 1)
        self._mark(tok, reads, writes)
        return ins

    def dma(self, q, out, in_, reads=(), writes=()):
        kind = "sw" if q == "pool" else "hw"
        half = self.NDMA // 2
        sem = self.dsems[(0 if kind == "hw" else half) + self.rr[kind]]
        self.rr[kind] = (self.rr[kind] + 1) % half
        toks = self._deps(reads, writes)
        if self.cnt[sem] > 0:
            toks.append((sem, self.cnt[sem]))
        self._wait(q, toks)
        ins = self.eng[q].dma_start(out=out, in_=in_)
        self.cnt[sem] += 16
        ins.then_inc(self.sems[sem], 16)
        self._mark((sem, self.cnt[sem]), reads, writes)
        return ins

    def barrier(self):
        toks = [(k, v) for k, v in self.cnt.items() if v > 0]
        for e in self.eng:
            self._wait(e, toks)


def _win_cols():
    aq0, ak0, av0, bq0, bk0, bv0, cq0, ck0, cv0 = 0, 384, 512, 640, 896, 1152, 1408, 1792, 2176
    pA = np.array([d + 16 if d % 32 < 16 else d - 16 for d in range(64)])
    pB32 = np.array([d + 8 if d % 16 < 8 else d - 8 for d in range(32)])
    pB = np.concatenate([pB32, 32 + pB32])
    ar = np.arange(64)

    def hc(base, h, perm=None):
        return base + h * 64 + (ar if perm is None else perm)

    ch = []
    for j in range(3):
        ch.append(np.concatenate([hc(aq0, j), hc(aq0, 3 + j)]))
    ch.append(np.concatenate([hc(ak0, 0), hc(ak0, 1)]))
    for j in range(3):
        ch.append(np.concatenate([hc(aq0, j, pA), hc(aq0, 3 + j, pA)]))
    ch.append(np.concatenate([hc(ak0, 0, pA), hc(ak0, 1, pA)]))
    for base, perm in ((bq0, None), (bk0, None), (bq0, pB), (bk0, pB)):
        for c in range(2):
            ch.append(np.concatenate([hc(base, 2 * c, perm), hc(base, 2 * c + 1, perm)]))
    for base in (cq0, ck0):
        for c in range(3):
            ch.append(np.concatenate([hc(base, 2 * c), hc(base, 2 * c + 1)]))
    ch.append(np.arange(av0, av0 + 128))
    ch.append(np.arange(bv0, bv0 + 256))
    ch.append(np.arange(cv0, cv0 + 384))
    cols = np.concatenate(ch)
    assert cols.shape[0] == NCOL
    return cols


def _rope_tables():
    f = np.float32
    pos = np.arange(L)
    rows = (pos // 64).astype(f)
    cols = (pos % 64).astype(f)
    p = np.arange(128)
    base = np.zeros((4, 128, T), f)
    base[0::2, :, L:] = 1.0
    inv16 = np.power(f(10000.0), -np.arange(16, dtype=f) / f(16)).astype(f)
    d = p % 64
    ang = (np.where((d < 32)[:, None], rows[None, :], cols[None, :]).astype(f) * inv16[(d % 32) % 16][:, None]).astype(f)
    sgn = np.where((d % 32) < 16, -1.0, 1.0).astype(f)
    base[0, :, :L] = np.cos(ang)
    base[1, :, :L] = np.sin(ang) * sgn[:, None]
    inv8 = np.power(f(10000.0), -np.arange(8, dtype=f) / f(8)).astype(f)
    d = p % 32
    ang = (np.where((d < 16)[:, None], rows[None, :], cols[None, :]).astype(f) * inv8[(d % 16) % 8][:, None]).astype(f)
    sgn = np.where((d % 16) < 8, -1.0, 1.0).astype(f)
    base[2, :, :L] = np.cos(ang)
    base[3, :, :L] = np.sin(ang) * sgn[:, None]
    tabs = np.zeros((16, 128, T), f)
    tabs[0:4] = base
    for g in range(2):
        m = ((p // 64) == g).astype(f)[:, None]
        tabs[4 + 2 * g] = base[0] * m
        tabs[5 + 2 * g] = base[1] * m
    for v in range(4):
        m = ((p // 32) == v).astype(f)[:, None]
        tabs[8 + 2 * v] = base[2] * m
        tabs[9 + 2 * v] = base[3] * m
    return tabs


def _na_variants():
    R, W, KH, KW = 32, 64, 8, 16
    var = []
    for i in range(6):
        var.append(([12, 13, 14, 15], [8 + 2 * i, 9 + 2 * i]))
    for u in range(4):
        var.append(([0, 1, 2, 3], [2 * u, 2 * u + 1]))
    for u in range(12, 16):
        var.append(([28, 29, 30, 31], [2 * u, 2 * u + 1]))
    dr = np.zeros((NV, 128, 256), np.int64)
    dc = np.zeros((NV, 128, 256), np.int64)
    ok = np.zeros((NV, 128, 256), np.float32)
    kc = np.arange(64)[:, None]
    qc = np.arange(64)[None, :]
    cs = np.clip(qc - KW // 2, 0, W - KW)
    colv = (kc >= cs) & (kc < cs + KW)
    dcm = np.clip(kc - qc, -(KW - 1), KW - 1) + (KW - 1)
    for v, (qrows, krows) in enumerate(var):
        for kk, kr in enumerate(krows):
            for qq, r in enumerate(qrows):
                rs = min(max(r - KH // 2, 0), R - KH)
                rowv = (rs <= kr <= rs + KH - 1)
                dr[v, kk * 64:(kk + 1) * 64, qq * 64:(qq + 1) * 64] = min(max(kr - r + 7, 0), 14)
                dc[v, kk * 64:(kk + 1) * 64, qq * 64:(qq + 1) * 64] = dcm
                ok[v, kk * 64:(kk + 1) * 64, qq * 64:(qq + 1) * 64] = (colv & rowv).astype(np.float32)
    return dr, dc, ok


_CONST = {}


def _consts():
    if not _CONST:
        _CONST["cols"] = _win_cols()
        _CONST["rope"] = _rope_tables()
        dr, dc, ok = _na_variants()
        _CONST["dr"], _CONST["dc"] = dr, dc
        _CONST["nmask"] = np.ascontiguousarray(ok.transpose(1, 0, 2).reshape(128, NV * 256))
        j = np.arange(128)[:, None]
        i = np.arange(128)[None, :]
        prev = (j >= i).astype(np.float32)
        nxt = (j <= i).astype(np.float32)
        _CONST["amask"] = np.concatenate([np.tile(prev, (1, 3)), np.tile(nxt, (1, 3))], axis=1)
        ffperm = np.concatenate([np.concatenate([np.arange(c * 128, (c + 1) * 128), DFF + np.arange(c * 128, (c + 1) * 128)])
                                 for c in range(22)])
        _CONST["ffperm"] = ffperm
    return _CONST


def build(nlayers=DEPTH, dbg=False):
    nc = bass.Bass("TRN2", target_bir_lowering=False)

    def din(name, shape, dt=F32):
        return nc.dram_tensor(name, list(shape), dt, kind="ExternalInput").ap()

    def dscr(name, shape, dt):
        return nc.dram_tensor(name, list(shape), dt, kind="ExternalOutput" if (dbg and name in ("s_q", "s_o", "s_x1", "s_g")) else "Internal").ap()

    xin = din("xin", [T, D])
    cT = din("cT", [128, 16])
    w_ada = din("w_ada", [DEPTH, D, 6 * D])
    b_ada = din("b_ada", [DEPTH, 6 * D])
    b_adaT = din("b_adaT", [DEPTH, 128, 48])
    win = din("win", [DEPTH, D, NCOL])
    w_o = din("w_o", [DEPTH, D, D])
    sink = din("sink", [DEPTH, 6])
    lamv = din("lamv", [DEPTH, 4, 32])
    subg = din("subg", [DEPTH, 64])
    nag = din("nag", [DEPTH, 6, 128, NV * 256])
    nmask = din("nmask", [128, NV * 256])
    amask = din("amask", [128, 768])
    lnp = din("lnp", [DEPTH, 4, D])
    wfi = din("wfi", [DEPTH, D, 2 * DFF])
    wfo = din("wfo", [DEPTH, DFF, D])
    rope = din("rope", [16, 128, T])
    out = nc.dram_tensor("out", [L, D], F32, kind="ExternalOutput").ap()

    s_ada = dscr("s_ada", [DEPTH, 128, 8, 6 * D], BF16)
    s_win = dscr("s_win", [DEPTH, 128, 8, NCOL], BF16)
    s_wo = dscr("s_wo", [DEPTH, 128, 8, D], BF16)
    s_wfi = dscr("s_wfi", [DEPTH, 128, 8, 2 * DFF], BF16)
    s_wfo = dscr("s_wfo", [DEPTH, 128, 22, D], BF16)
    s_q = dscr("s_q", [20, 128, T], BF16)
    s_o = dscr("s_o", [16, 64, T], BF16)
    s_x1 = dscr("s_x1", [T, D], F32)
    s_g = dscr("s_g", [2, 2, 128, D], F32)

    with ExitStack() as top:
        top.enter_context(nc.allow_low_precision(reason="bf16 matmul operands by design; fp32 accumulation"))
        fw = FW(nc, top)
        B = fw.B

        uid = [0]

        def sb(stack, name, shape, dt):
            uid[0] += 1
            return stack.enter_context(nc.sbuf_tensor("%s_u%d" % (name, uid[0]), list(shape), dt))

        psF = []
        psB = []
        pcnt = {"f": 0, "b": 0, "a": 0, "s": 0, "x": 0}

        def alloc_psum(stack, nf, nb):
            del psF[:]
            del psB[:]
            for i in range(nf):
                uid[0] += 1
                psF.append(stack.enter_context(nc.psum_tensor("psf%d_u%d" % (i, uid[0]), [128, 512], F32)))
            for i in range(nb):
                uid[0] += 1
                psB.append(stack.enter_context(nc.psum_tensor("psb%d_u%d" % (i, uid[0]), [128, 1024], BF16)))

        def next_ps(pool=None):
            if pool is None:
                i = pcnt["f"] % len(psF)
                pcnt["f"] += 1
            elif pool == "a":
                i = pcnt["a"] % 4
                pcnt["a"] += 1
            elif pool == "s":
                i = 4 + pcnt["s"] % 2
                pcnt["s"] += 1
            else:
                i = 6 + pcnt["x"] % 2
                pcnt["x"] += 1
            return psF[i], B("psf%d" % i)

        def next_psb():
            i = pcnt["b"] % 2
            pcnt["b"] += 1
            return psB[i], B("psb%d" % i)

        def mm(o, lhsT, rhs, start, stop, rd, wr):
            fw.op("pe", lambda: nc.tensor.matmul(o, lhsT=lhsT, rhs=rhs, start=start, stop=stop), reads=rd, writes=wr, inc=stop)

        ident = sb(top, "ident", [128, 128], BF16)
        onesb = sb(top, "onesb", [128, 128], BF16)
        sel = sb(top, "sel", [128, 128], BF16)
        o64 = sb(top, "o64", [128, 128], BF16)
        fillr = sb(top, "fillr", [128, 512], BF16)
        epsc = sb(top, "epsc", [128, 1], F32)
        maskcol = sb(top, "maskcol", [128, 2], F32)
        modT = sb(top, "modT", [128, 32, 2], F32)
        sinkE = sb(top, "sinkE", [128, 8], F32)
        lamt = sb(top, "lamt", [128, 4], F32)
        lamw = sb(top, "lamw", [128, 4, 32], F32)
        gs = sb(top, "gs", [64, 2], F32)

        fw.op("pool", lambda: nc.gpsimd.memset(ident[:], 1.0), writes=[B("ident")])
        fw.op("pool", lambda: nc.gpsimd.affine_select(out=ident[:], in_=ident[:], pattern=[[-1, 128]], compare_op=ALU.is_equal,
                                                     fill=0.0, base=0, channel_multiplier=1), reads=[B("ident")], writes=[B("ident")])
        fw.op("pool", lambda: nc.gpsimd.memset(onesb[:], 1.0), writes=[B("onesb")])
        fw.op("pool", lambda: nc.gpsimd.memset(sel[:], 0.0), writes=[B("sel")])
        fw.op("pool", lambda: nc.gpsimd.memset(sel[64:65, :], 1.0), reads=[B("sel")], writes=[B("sel")])
        fw.op("pool", lambda: nc.gpsimd.memset(o64[:], 0.0), writes=[B("o64")])
        fw.op("pool", lambda: nc.gpsimd.memset(o64[0:64, 0:64], 1.0 / 64.0), reads=[B("o64")], writes=[B("o64")])
        fw.op("pool", lambda: nc.gpsimd.memset(fillr[:], 1.0), writes=[B("fillr")])
        fw.op("pool", lambda: nc.gpsimd.memset(epsc[:], EPS), writes=[B("epsc")])
        fw.op("pool", lambda: nc.gpsimd.memset(maskcol[:], 0.0), writes=[B("maskcol")])
        fw.op("pool", lambda: nc.gpsimd.memset(maskcol[0:64, 0:1], 1.0), reads=[B("maskcol")], writes=[B("maskcol")])
        fw.op("pool", lambda: nc.gpsimd.memset(maskcol[64:128, 1:2], 1.0), reads=[B("maskcol")], writes=[B("maskcol")])

        def cast_weights(l):
            for kc in range(8):
                fw.dma("pool", s_ada[l, :, kc, :], w_ada[l, kc * 128:(kc + 1) * 128, :], writes=[B("s_ada%d_%d" % (l, kc))])
            for kc in range(8):
                fw.dma("pool", s_win[l, :, kc, :], win[l, kc * 128:(kc + 1) * 128, :], writes=[B("s_win%d_%d" % (l, kc))])
            for kc in range(8):
                fw.dma("pool", s_wo[l, :, kc, :], w_o[l, kc * 128:(kc + 1) * 128, :], writes=[B("s_wo%d_%d" % (l, kc))])
            for kc in range(8):
                fw.dma("pool", s_wfi[l, :, kc, :], wfi[l, kc * 128:(kc + 1) * 128, :], writes=[B("s_wfi%d_%d" % (l, kc))])
            for c in range(22):
                fw.dma("pool", s_wfo[l, :, c, :], wfo[l, c * 128:(c + 1) * 128, :], writes=[B("s_wfo%d_%d" % (l, c))])

        cast_weights(0)

        def layer_norm(src, dst, lng, lnb, slot, rd, wr, ph):
            st_, mv, rs, nb = ph["lnst"][slot], ph["lnmv"][slot], ph["lnrs"][slot], ph["lnnb"][slot]
            bs = B("lnsm%d" % slot)
            fw.op("dve", lambda: nc.vector.bn_stats(out=st_[:, 0:6], in_=src[:, 0:512]), reads=rd, writes=[bs])
            fw.op("dve", lambda: nc.vector.bn_stats(out=st_[:, 6:12], in_=src[:, 512:1024]), reads=rd + [bs], writes=[bs])
            fw.op("dve", lambda: nc.vector.bn_aggr(out=mv[:, 0:2], in_=st_[:, 0:12]), reads=[bs], writes=[bs])
            fw.op("act", lambda: nc.scalar.activation(out=rs[:, 0:1], in_=mv[:, 1:2], func=AF.Sqrt, bias=EPS, scale=1.0), reads=[bs], writes=[bs])
            fw.op("dve", lambda: nc.vector.reciprocal(out=rs[:, 0:1], in_=rs[:, 0:1]), reads=[bs], writes=[bs])
            fw.op("dve", lambda: nc.vector.scalar_tensor_tensor(out=nb[:, 0:1], in0=mv[:, 0:1], scalar=-1.0, in1=rs[:, 0:1],
                                                               op0=ALU.mult, op1=ALU.mult), reads=[bs], writes=[bs])
            fw.op("act", lambda: nc.scalar.activation(out=src, in_=src, func=AF.Identity, bias=nb[:, 0:1], scale=rs[:, 0:1]),
                  reads=rd + [bs], writes=rd)
            fw.op("pool", lambda: nc.gpsimd.tensor_tensor(out=dst, in0=src, in1=lng, op=ALU.mult), reads=rd + [B("lnp")], writes=wr)
            fw.op("pool", lambda: nc.gpsimd.tensor_tensor(out=dst, in0=dst, in1=lnb, op=ALU.add), reads=wr + [B("lnp")], writes=wr)

        def layer_norm_blk(nt, src_fn, dst_fn, lng, lnb, rd_fn, wr_fn, ph):
            st_, mv, rs, nb = ph["bst"], ph["bmv"], ph["brs"], ph["bnb"]
            bs = B("lnblk")
            for i in range(nt):
                src = src_fn(i)
                fw.op("dve", lambda i=i, src=src: nc.vector.bn_stats(out=st_[:, i, 0:6], in_=src[:, 0:512]), reads=rd_fn(i) + [bs], writes=[bs])
                fw.op("dve", lambda i=i, src=src: nc.vector.bn_stats(out=st_[:, i, 6:12], in_=src[:, 512:1024]), reads=rd_fn(i) + [bs], writes=[bs])
                fw.op("dve", lambda i=i: nc.vector.bn_aggr(out=mv[:, i, 0:2], in_=st_[:, i, 0:12]), reads=[bs], writes=[bs])
            fw.op("act", lambda: nc.scalar.activation(out=rs[:, 0:nt], in_=mv[:, 0:nt, 1], func=AF.Sqrt, bias=EPS, scale=1.0), reads=[bs], writes=[bs])
            fw.op("dve", lambda: nc.vector.reciprocal(out=rs[:, 0:nt], in_=rs[:, 0:nt]), reads=[bs], writes=[bs])
            fw.op("dve", lambda: nc.vector.scalar_tensor_tensor(out=nb[:, 0:nt], in0=mv[:, 0:nt, 0], scalar=-1.0, in1=rs[:, 0:nt],
                                                               op0=ALU.mult, op1=ALU.mult), reads=[bs], writes=[bs])
            for i in range(nt):
                src = src_fn(i)
                dst_ = dst_fn(i)
                fw.op("act", lambda i=i, src=src: nc.scalar.activation(out=src, in_=src, func=AF.Identity, bias=nb[:, i:i + 1], scale=rs[:, i:i + 1]),
                      reads=rd_fn(i) + [bs], writes=rd_fn(i))
                fw.op("pool", lambda src=src, dst_=dst_: nc.gpsimd.tensor_tensor(out=dst_, in0=src, in1=lng, op=ALU.mult), reads=rd_fn(i) + [B("lnp")], writes=wr_fn(i))
                fw.op("pool", lambda dst_=dst_: nc.gpsimd.tensor_tensor(out=dst_, in0=dst_, in1=lnb, op=ALU.add), reads=wr_fn(i) + [B("lnp")], writes=wr_fn(i))

        def transpose_mod(xb, nt, hT, mi, j, tag):
            N = nt * 128
            for c in range(8):
                pb, pbb = next_psb()
                for i in range(nt):
                    fw.op("pe", lambda i=i, c=c, pb=pb: nc.tensor.transpose(out=pb[:, i * 128:(i + 1) * 128],
                                                                          in_=xb[:, i, c * 128:(c + 1) * 128], identity=ident[:]),
                          reads=[B("xb%d" % i), B("ident")], writes=[pbb], inc=(i == nt - 1))
                fw.op("act", lambda c=c, pb=pb: nc.scalar.activation(out=hT[:, c, 0:N], in_=pb[:, 0:N], func=AF.Identity,
                                                                    bias=modT[:, mi * 8 + c, j:j + 1],
                                                                    scale=modT[:, (mi + 1) * 8 + c, j:j + 1]),
                      reads=[pbb, B("modT")], writes=[B("%s%d" % (tag, c))])

        cur_in = xin
        for l in range(nlayers):
            last = (l == nlayers - 1) and (nlayers == DEPTH)
            lam_init = 0.8 - 0.6 * math.exp(-0.3 * l)
            dst = out if last else s_x1
            nblk = 4 if last else 5

            with ExitStack() as ph:
                alloc_psum(ph, 6, 2)
                wada = sb(ph, "wada", [128, 8, 6 * D], BF16)
                cTt = sb(ph, "cTt", [128, 16], F32)
                scf = sb(ph, "scf", [128, 16], F32)
                scb = sb(ph, "scb", [128, 8, 2], BF16)
                rep = [sb(ph, "rep%d" % j, [128, 8, 128], BF16) for j in range(2)]
                bT = sb(ph, "bT", [128, 48], F32)
                bg = sb(ph, "bg", [128, 2, D], F32)
                Gt = [sb(ph, "Gt%d" % i, [128, D], F32) for i in range(2)]
                for kc in range(8):
                    fw.dma("sp", wada[:, kc, :], s_ada[l, :, kc, :], reads=[B("s_ada%d_%d" % (l, kc))], writes=[B("wada%d" % kc)])
                fw.dma("sp", cTt[:], cT, writes=[B("cTt")])
                fw.dma("sp", bT[:], b_adaT[l], writes=[B("bT")])
                for gi, m in enumerate((2, 5)):
                    fw.dma("sp", bg[:, gi, :], b_ada[l, m * D:(m + 1) * D].partition_broadcast(128), writes=[B("bg")])
                fw.dma("sp", sinkE[:, 0:6], sink[l].partition_broadcast(128), writes=[B("sinkE")])
                fw.dma("sp", lamw[:, :, :], lamv[l].partition_broadcast(128), writes=[B("lamw")])
                fw.dma("sp", gs[:, 0:1], subg[l].rearrange("(d o) -> d o", o=1), writes=[B("gs")])
                fw.op("act", lambda: nc.scalar.activation(out=sinkE[:, 0:6], in_=sinkE[:, 0:6], func=AF.Exp),
                      reads=[B("sinkE")], writes=[B("sinkE")])
                fw.op("dve", lambda: nc.vector.tensor_tensor(out=lamw[:, 0, :], in0=lamw[:, 0, :], in1=lamw[:, 1, :], op=ALU.mult),
                      reads=[B("lamw")], writes=[B("lamw")])
                fw.op("dve", lambda: nc.vector.tensor_tensor(out=lamw[:, 2, :], in0=lamw[:, 2, :], in1=lamw[:, 3, :], op=ALU.mult),
                      reads=[B("lamw")], writes=[B("lamw")])
                fw.op("dve", lambda: nc.vector.reduce_sum(out=lamt[:, 0:1], in_=lamw[:, 0, :], axis=mybir.AxisListType.X),
                      reads=[B("lamw")], writes=[B("lamt")])
                fw.op("dve", lambda: nc.vector.reduce_sum(out=lamt[:, 1:2], in_=lamw[:, 2, :], axis=mybir.AxisListType.X),
                      reads=[B("lamw"), B("lamt")], writes=[B("lamt")])
                fw.op("act", lambda: nc.scalar.activation(out=lamt[:, 0:2], in_=lamt[:, 0:2], func=AF.Exp),
                      reads=[B("lamt")], writes=[B("lamt")])
                fw.op("dve", lambda: nc.vector.tensor_tensor(out=lamt[:, 2:3], in0=lamt[:, 0:1], in1=lamt[:, 1:2], op=ALU.subtract),
                      reads=[B("lamt")], writes=[B("lamt")])
                fw.op("dve", lambda: nc.vector.tensor_scalar(out=lamt[:, 3:4], in0=lamt[:, 2:3], scalar1=float(lam_init), scalar2=None,
                                                            op0=ALU.add), reads=[B("lamt")], writes=[B("lamt")])
                fw.op("dve", lambda: nc.vector.tensor_scalar(out=gs[:, 1:2], in0=gs[:, 0:1], scalar1=float(1.0 - lam_init), scalar2=None,
                                                            op0=ALU.mult), reads=[B("gs")], writes=[B("gs")])
                fw.op("act", lambda: nc.scalar.activation(out=scf[:], in_=cTt[:], func=AF.Silu), reads=[B("cTt")], writes=[B("scf")])
                fw.op("dve", lambda: nc.vector.tensor_copy(out=scb[:].rearrange("p k j -> p (k j)"), in_=scf[:]), reads=[B("scf")], writes=[B("scb")])
                for j in range(2):
                    for kc in range(8):
                        fw.op("dve", lambda j=j, kc=kc: nc.vector.tensor_scalar(out=rep[j][:, kc, :], in0=onesb[:, :],
                                                                               scalar1=scf[:, kc * 2 + j:kc * 2 + j + 1], scalar2=None, op0=ALU.mult),
                              reads=[B("scf"), B("onesb")], writes=[B("rep%d" % j)])
                ps, psb_ = next_ps()
                for mi, m in enumerate((0, 1, 3, 4)):
                    for c in range(8):
                        col0 = m * D + c * 128
                        o = (mi * 8 + c) * 2
                        for kc in range(8):
                            mm(ps[:, o:o + 2], wada[:, kc, col0:col0 + 128], scb[:, kc, 0:2], kc == 0, kc == 7,
                               [B("wada%d" % kc), B("scb")], [psb_])
                psv = ps[:, 0:64].rearrange("p (a j) -> p a j", j=2)
                for j in range(2):
                    for (a0, b0) in ((0, 0), (16, 24)):
                        fw.op("dve", lambda j=j, a0=a0, b0=b0: nc.vector.tensor_tensor(out=modT[:, a0:a0 + 16, j], in0=psv[:, a0:a0 + 16, j],
                                                                                       in1=bT[:, b0:b0 + 16], op=ALU.add),
                              reads=[psb_, B("bT")], writes=[B("modT")])
                for a0 in (8, 24):
                    fw.op("dve", lambda a0=a0: nc.vector.tensor_scalar(out=modT[:, a0:a0 + 8, :], in0=modT[:, a0:a0 + 8, :], scalar1=1.0,
                                                                      scalar2=None, op0=ALU.add), reads=[B("modT")], writes=[B("modT")])
                for j in range(2 if not last else 1):
                    for gi, m in enumerate((2, 5)):
                        for half in range(2):
                            ps, psb_ = next_ps()
                            for kc in range(8):
                                mm(ps[:, :], rep[j][:, kc, :], wada[:, kc, m * D + half * 512:m * D + (half + 1) * 512], kc == 0, kc == 7,
                                   [B("rep%d" % j), B("wada%d" % kc)], [psb_])
                            fw.op("dve", lambda ps=ps, gi=gi, half=half: nc.vector.tensor_tensor(
                                out=Gt[gi][:, half * 512:(half + 1) * 512], in0=ps[:, :], in1=bg[:, gi, half * 512:(half + 1) * 512], op=ALU.add),
                                reads=[psb_, B("bg")], writes=[B("Gt%d" % gi)])
                        fw.dma("pool", s_g[j, gi], Gt[gi][:], reads=[B("Gt%d" % gi)], writes=[B("s_g%d%d" % (j, gi))])
                fw.barrier()

            kvs = ExitStack()
            KT = sb(kvs, "KT", [128, 6, T], BF16)
            VW = 12 * 66 + 62
            Vx = sb(kvs, "Vx", [128, NT, VW], BF16)
            fw.op("pool", lambda: nc.gpsimd.memset(Vx[:], 0.0), writes=[B("Vx")])
            fw.op("pool", lambda: nc.gpsimd.memset(Vx[:, :, 0:792].rearrange("p t (h c) -> p t h c", c=66)[:, :, :, 64:66], 1.0),
                  reads=[B("Vx")], writes=[B("Vx")])
            with ExitStack() as ph:
                alloc_psum(ph, 6, 2)
                Win = sb(ph, "Win", [128, 8, NCOL], BF16)
                xsb = [sb(ph, "xsb%d" % s, [128, 4, D], F32) for s in range(2)]
                xb = sb(ph, "xb", [128, 4, D], BF16)
                hT = sb(ph, "hT", [128, 8, 512], BF16)
                rt = sb(ph, "rt", [128, 16, 512], F32)
                t1 = [sb(ph, "t1_%d" % s, [128, 512], F32) for s in range(2)]
                t2 = [sb(ph, "t2_%d" % s, [128, 512], F32) for s in range(2)]
                qst = [sb(ph, "qst%d" % s, [128, 512], BF16) for s in range(2)]
                for kc in range(8):
                    fw.dma("sp", Win[:, kc, :], s_win[l, :, kc, :], reads=[B("s_win%d_%d" % (l, kc))], writes=[B("Win%d" % kc)])
                WinB = [B("Win%d" % kc) for kc in range(8)]
                tcnt = [0]

                def load_x(bi):
                    nt = 4 if bi < 4 else 2
                    s = bi % 2
                    fw.dma("sp", xsb[s][:, 0:nt, :], cur_in[bi * 512:bi * 512 + nt * 128, :].rearrange("(i p) d -> p i d", p=128),
                           reads=[B("xs_%d_%d" % (l, bi))], writes=[B("xsb%d" % s)])

                load_x(0)
                for bi in range(5):
                    nt = 4 if bi < 4 else 2
                    N = nt * 128
                    tok0 = bi * 512
                    j = 0 if bi < 4 else 1
                    s = bi % 2
                    if bi + 1 < 5:
                        load_x(bi + 1)
                    fw.dma("sp", rt[:, :, 0:N], rope[:, :, tok0:tok0 + N].rearrange("k p t -> p k t"), writes=[B("rt%d" % k) for k in range(16)])
                    for i in range(nt):
                        fw.op("act" if i % 2 else "dve",
                              (lambda i=i: nc.scalar.copy(out=xb[:, i, :], in_=xsb[s][:, i, :])) if i % 2 else
                              (lambda i=i: nc.vector.tensor_copy(out=xb[:, i, :], in_=xsb[s][:, i, :])),
                              reads=[B("xsb%d" % s)], writes=[B("xb%d" % i)])
                    transpose_mod(xb, nt, hT, 0, j, "hT")
                    hTB = [B("hT%d" % c) for c in range(8)]

                    def proj(ch):
                        ps, pb_ = next_ps()
                        for kc in range(8):
                            mm(ps[:, 0:N], Win[:, kc, ch * 128:(ch + 1) * 128], hT[:, kc, 0:N], kc == 0, kc == 7, [WinB[kc], hTB[kc]], [pb_])
                        return ps, pb_

                    def roped(ch, chp, tc, ts, dst_ap, dst_b):
                        pa, pab = proj(ch)
                        pp, ppb = proj(chp)
                        rope_comb(pa, pab, pp, ppb, tc, ts, dst_ap, dst_b)

                    def rope_comb(pa, pab, pp, ppb, tc, ts, dst_ap, dst_b):
                        k = tcnt[0] % 2
                        tcnt[0] += 1
                        fw.op("dve", lambda: nc.vector.tensor_tensor(out=t1[k][:, 0:N], in0=pa[:, 0:N], in1=rt[:, tc, 0:N], op=ALU.mult),
                              reads=[pab, B("rt%d" % tc)], writes=[B("t1_%d" % k)])
                        fw.op("dve", lambda: nc.vector.tensor_tensor(out=t2[k][:, 0:N], in0=pp[:, 0:N], in1=rt[:, ts, 0:N], op=ALU.mult),
                              reads=[ppb, B("rt%d" % ts)], writes=[B("t2_%d" % k)])
                        fw.op("pool", lambda: nc.gpsimd.tensor_tensor(out=dst_ap, in0=t1[k][:, 0:N], in1=t2[k][:, 0:N], op=ALU.add),
                              reads=[B("t1_%d" % k), B("t2_%d" % k)], writes=dst_b)

                    def q_out(qi, fn):
                        k = tcnt[0] % 2
                        fn(qst[k][:, 0:N], [B("qst%d" % k)])
                        fw.dma("sp", s_q[qi, :, tok0:tok0 + N], qst[k][:, 0:N], reads=[B("qst%d" % k)], writes=[B("s_q%d" % bi)])

                    need_q = (bi < 4) or (not last)
                    if need_q:
                        for jq in range(3):
                            pa, pab = proj(jq)
                            pp, ppb = proj(4 + jq)
                            for g in range(2):
                                q_out(g * 3 + jq, lambda ap, bb, g=g: rope_comb(pa, pab, pp, ppb, 4 + 2 * g, 5 + 2 * g, ap, bb))
                    roped(3, 7, 0, 1, KT[:, 0, tok0:tok0 + N], [B("KT0_%d" % bi)])
                    if need_q:
                        for c in range(2):
                            pa, pab = proj(8 + c)
                            pp, ppb = proj(12 + c)
                            for v in range(4):
                                q_out(6 + c * 4 + v, lambda ap, bb, v=v: rope_comb(pa, pab, pp, ppb, 8 + 2 * v, 9 + 2 * v, ap, bb))
                    for c in range(2):
                        roped(10 + c, 14 + c, 2, 3, KT[:, 1 + c, tok0:tok0 + N], [B("KT%d_%d" % (1 + c, bi))])
                    if need_q:
                        for c in range(3):
                            ps, pb_ = proj(16 + c)
                            for hb in range(2):
                                def cq(ap, bb, ps=ps, pb_=pb_, hb=hb):
                                    fw.op("act", lambda: nc.scalar.activation(out=ap, in_=ps[:, 0:N], func=AF.Copy, scale=maskcol[:, hb:hb + 1]),
                                          reads=[pb_, B("maskcol")], writes=bb)
                                q_out(14 + c * 2 + hb, cq)
                                tcnt[0] += 1
                    for c in range(3):
                        ps, pb_ = proj(19 + c)
                        fw.op("act", lambda ps=ps, c=c: nc.scalar.copy(out=KT[:, 3 + c, tok0:tok0 + N], in_=ps[:, 0:N]),
                              reads=[pb_], writes=[B("KT%d_%d" % (3 + c, bi))])
                    for i in range(nt):
                        gt = bi * 4 + i
                        ps, pb_ = next_ps()
                        for kc in range(8):
                            mm(ps[:, :], hT[:, kc, i * 128:(i + 1) * 128], Win[:, kc, NCH * 128:NCH * 128 + 512], kc == 0, kc == 7,
                               [WinB[kc], hTB[kc]], [pb_])
                        fw.op("act", lambda ps=ps, gt=gt: nc.scalar.copy(out=Vx[:, gt, 0:528].rearrange("p (h c) -> p h c", c=66)[:, :, 0:64], in_=ps[:, :].rearrange("p (h d) -> p h d", d=64)),
                              reads=[pb_], writes=[B("Vx")])
                        ps, pb_ = next_ps()
                        for kc in range(8):
                            mm(ps[:, 0:256], hT[:, kc, i * 128:(i + 1) * 128], Win[:, kc, NCH * 128 + 512:NCH * 128 + 768], kc == 0, kc == 7,
                               [WinB[kc], hTB[kc]], [pb_])
                        fw.op("dve", lambda ps=ps, gt=gt: nc.vector.tensor_copy(out=Vx[:, gt, 528:792].rearrange("p (h c) -> p h c", c=66)[:, :, 0:64],
                                                                                in_=ps[:, 0:256].rearrange("p (h d) -> p h d", d=64)),
                              reads=[pb_], writes=[B("Vx")])
                fw.barrier()

            with ExitStack() as ph:
                alloc_psum(ph, 8, 0)
                Et = sb(ph, "Et", [128, 6, NV * 256], BF16)
                nmk = sb(ph, "nmk", [128, NV * 256], BF16)
                MA = sb(ph, "MA", [128, 768], BF16)
                QTb = [sb(ph, "QTb%d" % s, [128, 20, 512], BF16) for s in range(2)]
                OTb = sb(ph, "OTb", [64, 16, 512], BF16)
                NPT = 6
                PT = [sb(ph, "PT%d" % s, [128, 512], BF16) for s in range(NPT)]
                bcs = [sb(ph, "bcs%d" % s, [64, 512], F32) for s in range(4)]
                tA = [sb(ph, "tA%d" % s, [64, 512], F32) for s in range(4)]
                sq = sb(ph, "sq", [128, 512], BF16)
                cnt = {"pt": 0, "ss": 0, "bc": 0}
                if l + 1 < nlayers:
                    cast_weights(l + 1)
                with ExitStack() as sub:
                    gstg = sb(sub, "gstg", [128, NV * 256], F32)
                    fw.dma("pool", nmk[:], nmask, writes=[B("nmk")])
                    fw.dma("pool", MA[:], amask, writes=[B("MA")])
                    for h in range(6):
                        fw.dma("sp", gstg[:], nag[l, h], writes=[B("gstg")])
                        fw.op("act", lambda h=h: nc.scalar.activation(out=Et[:, h, :], in_=gstg[:], func=AF.Exp), reads=[B("gstg")], writes=[B("Et%d" % h)])
                        fw.op("dve", lambda h=h: nc.vector.tensor_tensor(out=Et[:, h, :], in0=Et[:, h, :], in1=nmk[:], op=ALU.mult),
                              reads=[B("Et%d" % h), B("nmk")], writes=[B("Et%d" % h)])
                    fw.barrier()

                rb = [sb(ph, "rb%d" % s_, [128, 512], BF16) for s_ in range(4)]
                for s_ in range(4):
                    fw.op("pool", lambda s_=s_: nc.gpsimd.memset(rb[s_][:], 0.0), writes=[B("rb%d" % s_)])
                cnt["rb"] = 0
                fw.op("pool", lambda: nc.gpsimd.memset(sq[:], 0.0), writes=[B("sq")])

                def v_lhsT(t, hh):
                    return Vx[:, t, hh * 66:hh * 66 + 128]

                def bcast_rows(rows, N_):
                    res = []
                    for wr in rows:
                        k = cnt["rb"] % 4
                        cnt["rb"] += 1
                        wr(rb[k], B("rb%d" % k))
                        pbc, pbcb = next_ps("x")
                        mm(pbc[:, 0:N_], sel[:, :], rb[k][:, 0:N_], True, True, [B("sel"), B("rb%d" % k)], [pbcb])
                        res.append((pbc, pbcb))
                    return res

                def recips(pbcs, N_):
                    res = []
                    for pbc, pbcb in pbcs:
                        c0, c0b = new_bc()
                        fw.op("dve", lambda pbc=pbc, c0=c0: nc.vector.reciprocal(out=c0[0:64, 0:N_], in_=pbc[0:64, 0:N_]), reads=[pbcb], writes=[c0b])
                        res.append((c0, c0b))
                    return res

                def load_q(bi):
                    nt = 4 if bi < 4 else 2
                    s = bi % 2
                    fw.dma("sp", QTb[s][:, :, 0:nt * 128], s_q[:, :, bi * 512:bi * 512 + nt * 128].rearrange("c p t -> p c t"),
                           reads=[B("s_q%d" % bi)], writes=[B("QTb%d" % s)])

                def new_pt():
                    k = cnt["pt"] % NPT
                    cnt["pt"] += 1
                    return PT[k], B("PT%d" % k)

                def new_bc():
                    k = cnt["bc"] % 4
                    cnt["bc"] += 1
                    return bcs[k], B("bcs%d" % k)

                live_acc = set()
                flush_hook = [lambda: None]

                def new_acc():
                    i = pcnt["a"] % 4
                    if i in live_acc:
                        flush_hook[0]()
                    assert i not in live_acc, "accumulator bank still live"
                    live_acc.add(i)
                    pcnt["a"] += 1
                    return psF[i], B("psf%d" % i), i

                def gen_jobs(Q, Qb, bi, ctxq, nt, N):
                    units = []
                    for n in range(nt):
                        gq = bi * 4 + n
                        for g in range(2):
                            if ctxq:
                                tiles = [(16, None), (17, None)]
                            else:
                                tiles = []
                                if gq - 1 >= 0:
                                    tiles.append((gq - 1, 0))
                                tiles.append((gq, None))
                                if gq + 1 < 16:
                                    tiles.append((gq + 1, 1))
                                tiles += [(16, None), (17, None)]
                            u = {"jobs": []}
                            for ti, (t, mk) in enumerate(tiles):
                                def S(u=u, ti=ti, t=t, g=g, n=n):
                                    if ti == 0:
                                        u["po"] = new_acc()
                                    ps, psb_ = next_ps("s")
                                    mm(ps[:, 0:384].rearrange("p (a b) -> p a b", a=3), KT[:, 0, t * 128:(t + 1) * 128],
                                       Q[:, g * 3:g * 3 + 3, n * 128:(n + 1) * 128], True, True, [B("KT0_%d" % (t // 4)), Qb], [psb_])
                                    return ps, psb_

                                def rest(st, u=u, ti=ti, t=t, mk=mk, g=g, last=(ti == len(tiles) - 1)):
                                    ps, psb_ = st
                                    po, pob, _ = u["po"]
                                    pt, ptb = new_pt()
                                    fw.op("act", lambda: nc.scalar.activation(out=pt[:, 0:384], in_=ps[:, 0:384], func=AF.Exp, scale=0.125),
                                          reads=[psb_], writes=[ptb])
                                    if mk is not None:
                                        fw.op("pool", lambda: nc.gpsimd.tensor_tensor(out=pt[:, 0:384], in0=pt[:, 0:384],
                                                                                    in1=MA[:, mk * 384:(mk + 1) * 384], op=ALU.mult),
                                              reads=[ptb, B("MA")], writes=[ptb])
                                    mm(po[:, 0:384], v_lhsT(t, g), pt[:, 0:384], ti == 0, last, [B("Vx"), ptb], [pob])
                                u["jobs"].append((S, rest))

                            def fin(u=u, g=g, n=n):
                                po, pob, bank = u["po"]
                                stt = {}

                                def wr(rbt, rbb):
                                    for sg in range(3):
                                        h = 3 * g + sg
                                        fw.op("dve", lambda sg=sg, h=h: nc.vector.tensor_scalar(
                                            out=rbt[64:65, sg * 128:(sg + 1) * 128], in0=po[64:65, sg * 128:(sg + 1) * 128],
                                            scalar1=sinkE[64:65, h:h + 1], scalar2=None, op0=ALU.add), reads=[pob, B("sinkE")], writes=[rbb])

                                def s1():
                                    stt["p"] = bcast_rows([wr], 384)

                                def s2():
                                    stt["c"] = recips(stt["p"], 384)

                                def s3():
                                    ((c0, c0b),) = stt["c"]
                                    fw.op("dve", lambda: nc.vector.tensor_tensor(
                                        out=OTb[0:64, 3 * g:3 * g + 3, n * 128:(n + 1) * 128], in0=po[0:64, 0:384].rearrange("p (a b) -> p a b", a=3),
                                        in1=c0[0:64, 0:384].rearrange("p (a b) -> p a b", a=3), op=ALU.mult),
                                        reads=[pob, c0b], writes=[B("OTb")])
                                    live_acc.discard(bank)
                                return [s1, s2, s3]
                            u["fin"] = fin
                            units.append(u)
                    tilesB = [16, 17] if ctxq else list(range(18))
                    for h in range(4):
                        r0 = (h % 2) * 64
                        u = {"jobs": []}
                        for ti, t in enumerate(tilesB):
                            for m in range(2):
                                def S(u=u, ti=ti, t=t, m=m, h=h, r0=r0):
                                    if ti == 0 and m == 0:
                                        u["po"] = [new_acc(), new_acc()]
                                    ps, psb_ = next_ps("s")
                                    mm(ps[:, 0:N], KT[:, 1 + h // 2, t * 128:(t + 1) * 128], Q[:, 6 + (h // 2) * 4 + (h % 2) * 2 + m, 0:N],
                                       True, True, [B("KT%d_%d" % (1 + h // 2, t // 4)), Qb], [psb_])
                                    return ps, psb_

                                def rest(st, u=u, ti=ti, t=t, m=m, h=h, last=(ti == len(tilesB) - 1)):
                                    ps, psb_ = st
                                    po, pob, _ = u["po"][m]
                                    pt, ptb = new_pt()
                                    fw.op("act", lambda: nc.scalar.activation(out=pt[:, 0:N], in_=ps[:, 0:N], func=AF.Exp, scale=float(32 ** -0.5)),
                                          reads=[psb_], writes=[ptb])
                                    mm(po[:, 0:N], v_lhsT(t, 2 + h), pt[:, 0:N], ti == 0, last, [B("Vx"), ptb], [pob])
                                u["jobs"].append((S, rest))

                        def fin(u=u, h=h):
                            (p0, p0b, b0_), (p1, p1b, b1_) = u["po"]
                            stt = {}

                            def wr0(rbt, rbb):
                                fw.op("dve", lambda: nc.vector.tensor_copy(out=rbt[64:65, 0:N], in_=p0[64:65, 0:N]), reads=[p0b], writes=[rbb])

                            def wr1(rbt, rbb):
                                fw.op("dve", lambda: nc.vector.tensor_copy(out=rbt[64:65, 0:N], in_=p1[64:65, 0:N]), reads=[p1b], writes=[rbb])

                            def s1():
                                stt["p"] = bcast_rows([wr0, wr1], N)

                            def s2():
                                stt["c"] = recips(stt["p"], N)

                            def s3():
                                (c0, c0b), (c1, c1b) = stt["c"]
                                fw.op("dve", lambda: nc.vector.tensor_tensor(out=tA[0][:, 0:N], in0=p0[0:64, 0:N], in1=c0[0:64, 0:N], op=ALU.mult),
                                      reads=[p0b, c0b], writes=[B("tA0")])
                                fw.op("dve", lambda: nc.vector.scalar_tensor_tensor(out=tA[1][:, 0:N], in0=p1[0:64, 0:N], scalar=lamt[0:64, 3:4],
                                                                                   in1=c1[0:64, 0:N], op0=ALU.mult, op1=ALU.mult),
                                      reads=[p1b, c1b, B("lamt")], writes=[B("tA1")])
                                live_acc.discard(b0_)
                                live_acc.discard(b1_)
                                fw.op("pool", lambda: nc.gpsimd.tensor_tensor(out=tA[2][:, 0:N], in0=tA[0][:, 0:N], in1=tA[1][:, 0:N], op=ALU.subtract),
                                      reads=[B("tA0"), B("tA1")], writes=[B("tA2")])
                                fw.op("pool", lambda: nc.gpsimd.tensor_tensor(out=sq[0:64, 0:N], in0=tA[2][:, 0:N], in1=tA[2][:, 0:N], op=ALU.mult),
                                      reads=[B("tA2")], writes=[B("sq")])

                            def s4():
                                pms, pmsb = next_ps("x")
                                stt["pms"] = (pms, pmsb)
                                mm(pms[:, 0:N], o64[:, :], sq[:, 0:N], True, True, [B("o64"), B("sq")], [pmsb])

                            def s5():
                                pms, pmsb = stt["pms"]
                                fw.op("act", lambda: nc.scalar.activation(out=tA[3][:, 0:N], in_=pms[0:64, 0:N], func=AF.Ln, bias=epsc[0:64, 0:1], scale=1.0),
                                      reads=[pmsb, B("epsc")], writes=[B("tA3")])
                                fw.op("act", lambda: nc.scalar.activation(out=tA[3][:, 0:N], in_=tA[3][:, 0:N], func=AF.Exp, scale=-0.5),
                                      reads=[B("tA3")], writes=[B("tA3")])

                            def s6():
                                fw.op("dve", lambda: nc.vector.scalar_tensor_tensor(out=OTb[0:64, 6 + h, 0:N], in0=tA[2][:, 0:N], scalar=gs[:, 1:2],
                                                                                   in1=tA[3][:, 0:N], op0=ALU.mult, op1=ALU.mult),
                                      reads=[B("tA2"), B("tA3"), B("gs")], writes=[B("OTb")])
                            return [s1, s2, s3, s4, s5, s6]
                        u["fin"] = fin
                        units.append(u)
                    for h in range(6):
                        r0 = (h % 2) * 64
                        for jj in range(1 if ctxq else 2):
                            if ctxq:
                                tiles = [(16, None), (17, None)]
                            else:
                                jg = bi * 2 + jj
                                if jg == 0:
                                    tiles = [(uu, 6 + uu) for uu in range(4)]
                                elif jg == 7:
                                    tiles = [(uu, 10 + uu - 12) for uu in range(12, 16)]
                                else:
                                    tiles = [(2 * jg - 2 + i, i) for i in range(6)]
                                tiles += [(16, None), (17, None)]
                            u = {"jobs": []}
                            for ti, (t, ev) in enumerate(tiles):
                                def S(u=u, ti=ti, t=t, h=h, r0=r0, jj=jj):
                                    if ti == 0:
                                        u["po"] = new_acc()
                                    ps, psb_ = next_ps("s")
                                    mm(ps[:, 0:256], KT[:, 3 + h // 2, t * 128:(t + 1) * 128], Q[:, 14 + (h // 2) * 2 + (h % 2), jj * 256:(jj + 1) * 256],
                                       True, True, [B("KT%d_%d" % (3 + h // 2, t // 4)), Qb], [psb_])
                                    return ps, psb_

                                def rest(st, u=u, ti=ti, t=t, ev=ev, h=h, last=(ti == len(tiles) - 1)):
                                    ps, psb_ = st
                                    po, pob, _ = u["po"]
                                    pt, ptb = new_pt()
                                    fw.op("act", lambda: nc.scalar.activation(out=pt[:, 0:256], in_=ps[:, 0:256], func=AF.Exp, scale=0.125),
                                          reads=[psb_], writes=[ptb])
                                    if ev is not None:
                                        fw.op("dve", lambda: nc.vector.tensor_tensor(out=pt[:, 0:256], in0=pt[:, 0:256],
                                                                                    in1=Et[:, h, ev * 256:(ev + 1) * 256], op=ALU.mult),
                                              reads=[ptb, B("Et%d" % h)], writes=[ptb])
                                    mm(po[:, 0:256], v_lhsT(t, 6 + h), pt[:, 0:256], ti == 0, last, [B("Vx"), ptb], [pob])
                                u["jobs"].append((S, rest))

                            def fin(u=u, h=h, jj=jj):
                                po, pob, bank = u["po"]
                                stt = {}

                                def wr(rbt, rbb):
                                    fw.op("dve", lambda: nc.vector.tensor_copy(out=rbt[64:65, 0:256], in_=po[64:65, 0:256]), reads=[pob], writes=[rbb])

                                def s1():
                                    stt["p"] = bcast_rows([wr], 256)

                                def s2():
                                    stt["c"] = recips(stt["p"], 256)

                                def s3():
                                    ((c0, c0b),) = stt["c"]
                                    fw.op("dve", lambda: nc.vector.tensor_tensor(out=OTb[0:64, 10 + h, jj * 256:(jj + 1) * 256], in0=po[0:64, 0:256],
                                                                                in1=c0[0:64, 0:256], op=ALU.mult), reads=[pob, c0b], writes=[B("OTb")])
                                    live_acc.discard(bank)
                                return [s1, s2, s3]
                            u["fin"] = fin
                            units.append(u)
                    return units

                nqb = 4 if last else 5
                load_q(0)
                for bi in range(nqb):
                    ctxq = bi == 4
                    nt = 4 if bi < 4 else 2
                    N = nt * 128
                    tok0 = bi * 512
                    s = bi % 2
                    if bi + 1 < nqb:
                        load_q(bi + 1)
                    units = gen_jobs(QTb[s], B("QTb%d" % s), bi, ctxq, nt, N)
                    jobs = []
                    for u in units:
                        for ji, (S, rest) in enumerate(u["jobs"]):
                            jobs.append((S, rest, u["fin"] if ji == len(u["jobs"]) - 1 else None))
                    pending = []
                    flush_all = [None]

                    def flush():
                        while pending:
                            pending.pop(0)[1]()
                    flush_hook[0] = flush
                    st_next = jobs[0][0]()
                    for k, (S, rest, fin) in enumerate(jobs):
                        st_cur = st_next
                        if k + 1 < len(jobs):
                            st_next = jobs[k + 1][0]()
                        rest(st_cur)
                        if fin is not None:
                            for si, stg in enumerate(fin()):
                                pending.append((k + 2 + si, stg))
                            pending.sort(key=lambda x: x[0])
                        while pending and pending[0][0] <= k:
                            pending.pop(0)[1]()
                    flush()
                    fw.dma("pool", s_o[:, :, tok0:tok0 + N].rearrange("h p t -> p h t"), OTb[:, :, 0:N], reads=[B("OTb")], writes=[B("s_o%d" % bi)])
                fw.barrier()
            kvs.close()

            with ExitStack() as ph:
                alloc_psum(ph, 6, 2)
                Wo = sb(ph, "Wo", [128, 8, D], BF16)
                xsb = [sb(ph, "xsb%d" % s, [128, 4, D], F32) for s in range(2)]
                OT2 = [sb(ph, "OT2_%d" % s, [128, 8, 512], BF16) for s in range(2)]
                ubA = sb(ph, "ubA", [128, 4, D], F32)
                ubF = sb(ph, "ubF", [128, 4, D], F32)
                xb = sb(ph, "xb", [128, 4, D], BF16)
                hT = sb(ph, "hT", [128, 8, 512], BF16)
                AT = sb(ph, "AT", [128, 22, 512], BF16)
                Wi = [sb(ph, "Wi%d" % s, [128, 8, 512], BF16) for s in range(2)]
                Wf = [sb(ph, "Wf%d" % s, [128, 11, 512], BF16) for s in range(2)]
                lnt = sb(ph, "lnt", [128, 4, D], F32)
                Gx = sb(ph, "Gx", [128, 2, D], F32)
                sgt = [sb(ph, "sgt%d" % s, [128, 512], F32) for s in range(2)]
                ph_ln = {"lnst": [sb(ph, "lnst%d" % s, [128, 12], F32) for s in range(4)],
                         "lnmv": [sb(ph, "lnmv%d" % s, [128, 2], F32) for s in range(4)],
                         "lnrs": [sb(ph, "lnrs%d" % s, [128, 1], F32) for s in range(4)],
                         "lnnb": [sb(ph, "lnnb%d" % s, [128, 1], F32) for s in range(4)]}
                ph_ln["bst"] = sb(ph, "bst", [128, 4, 12], F32)
                ph_ln["bmv"] = sb(ph, "bmv", [128, 4, 2], F32)
                ph_ln["brs"] = sb(ph, "brs", [128, 4], F32)
                ph_ln["bnb"] = sb(ph, "bnb", [128, 4], F32)
                lncnt = [0]
                wic = [0]
                wfc = [0]
                for kc in range(8):
                    fw.dma("sp", Wo[:, kc, :], s_wo[l, :, kc, :], reads=[B("s_wo%d_%d" % (l, kc))], writes=[B("Wo")])
                for k in range(4):
                    fw.dma("sp", lnt[:, k, :], lnp[l, k].partition_broadcast(128), writes=[B("lnp")])
                s_o2 = s_o.rearrange("(c two) p t -> two p c t", two=2)

                def geo(bi):
                    nt = 4 if bi < 4 else 2
                    return nt, nt * 128, bi * 512, (0 if bi < 4 else 1), bi % 2

                def load_x(bi):
                    nt, N, tok0, j, s = geo(bi)
                    fw.dma("sp", xsb[s][:, 0:nt, :], cur_in[tok0:tok0 + N, :].rearrange("(i p) d -> p i d", p=128),
                           reads=[B("xs_%d_%d" % (l, bi))], writes=[B("xsb%d_%d" % (s, i)) for i in range(4)])

                def load_ot(bi):
                    nt, N, tok0, j, s = geo(bi)
                    for hp in range(2):
                        fw.dma("sp", OT2[s][hp * 64:(hp + 1) * 64, :, 0:N], s_o2[hp, :, :, tok0:tok0 + N], reads=[B("s_o%d" % bi)], writes=[B("OT2_%d" % s)])

                def load_wi(cp):
                    s = wic[0] % 2
                    wic[0] += 1
                    fw.dma("sp", Wi[s][:], s_wfi[l, :, :, cp * 512:(cp + 1) * 512], reads=[B("s_wfi%d_%d" % (l, kc)) for kc in range(8)], writes=[B("Wi%d" % s)])
                    return s

                def load_wf(half, cg):
                    s = wfc[0] % 2
                    wfc[0] += 1
                    fw.dma("sp", Wf[s][:], s_wfo[l, :, cg * 11:(cg + 1) * 11, half * 512:(half + 1) * 512],
                           reads=[B("s_wfo%d_%d" % (l, cg * 11 + cl)) for cl in range(11)], writes=[B("Wf%d" % s)])
                    return s

                def gated_res(ps, pb_, ub, ubn, i, half, gi, s):
                    cs = slice(half * 512, (half + 1) * 512)
                    fw.op("dve", lambda: nc.vector.tensor_tensor(out=ub[:, i, cs], in0=ps[:, :], in1=Gx[:, gi, cs], op=ALU.mult),
                          reads=[pb_, B("Gx")], writes=[B("%s%d" % (ubn, i))])
                    fw.op("dve", lambda: nc.vector.scalar_tensor_tensor(out=ub[:, i, cs], in0=xsb[s][:, i, cs], scalar=float(ALPHA), in1=ub[:, i, cs],
                                                                       op0=ALU.mult, op1=ALU.add),
                          reads=[B("%s%d" % (ubn, i)), B("xsb%d_%d" % (s, i))], writes=[B("%s%d" % (ubn, i))])

                def stage_O(bi):
                    nt, N, tok0, j, s = geo(bi)
                    for i in range(nt):
                        for half in range(2):
                            ps, pb_ = next_ps()
                            for c in range(8):
                                mm(ps[:, :], OT2[s][:, c, i * 128:(i + 1) * 128], Wo[:, c, half * 512:(half + 1) * 512], c == 0, c == 7,
                                   [B("OT2_%d" % s), B("Wo")], [pb_])
                            gated_res(ps, pb_, ubA, "ubA", i, half, 0, s)

                def stage_L1(bi):
                    nt, N, tok0, j, s = geo(bi)
                    layer_norm_blk(nt, lambda i: ubA[:, i, :], lambda i: xsb[s][:, i, :], lnt[:, 0, :], lnt[:, 1, :],
                                   lambda i: [B("ubA%d" % i)], lambda i: [B("xsb%d_%d" % (s, i))], ph_ln)

                def stage_xb(bi):
                    nt, N, tok0, j, s = geo(bi)
                    for i in range(nt):
                        fw.op("act", lambda i=i: nc.scalar.copy(out=xb[:, i, :], in_=xsb[s][:, i, :]), reads=[B("xsb%d_%d" % (s, i))], writes=[B("xb%d" % i)])

                def stage_rest(bi, w0):
                    nt, N, tok0, j, s = geo(bi)
                    transpose_mod(xb, nt, hT, 2, j, "hT")
                    hTB = [B("hT%d" % c) for c in range(8)]
                    for cp in range(11):
                        ws = w0
                        if cp + 1 < 11:
                            w0 = load_wi(cp + 1)
                        for cc in range(2):
                            c = 2 * cp + cc
                            pg, pgb = next_ps()
                            for kc in range(8):
                                mm(pg[:, 0:N], Wi[ws][:, kc, cc * 256:cc * 256 + 128], hT[:, kc, 0:N], kc == 0, kc == 7, [B("Wi%d" % ws), hTB[kc]], [pgb])
                            pu, pub = next_ps()
                            for kc in range(8):
                                mm(pu[:, 0:N], Wi[ws][:, kc, cc * 256 + 128:cc * 256 + 256], hT[:, kc, 0:N], kc == 0, kc == 7, [B("Wi%d" % ws), hTB[kc]], [pub])
                            k = c % 2
                            fw.op("act", lambda pg=pg, k=k: nc.scalar.activation(out=sgt[k][:, 0:N], in_=pg[:, 0:N], func=AF.Silu), reads=[pgb], writes=[B("sgt%d" % k)])
                            fw.op("dve", lambda pu=pu, k=k, c=c: nc.vector.tensor_tensor(out=AT[:, c, 0:N], in0=pu[:, 0:N], in1=sgt[k][:, 0:N], op=ALU.mult),
                                  reads=[pub, B("sgt%d" % k)], writes=[B("AT%d" % c)])
                    f0 = load_wf(0, 0)
                    for half in range(2):
                        accs = [next_ps() for _ in range(nt)]
                        for cg in range(2):
                            fs = f0
                            if not (half == 1 and cg == 1):
                                f0 = load_wf(half if cg == 0 else half + 1, 1 - cg)
                            for i in range(nt):
                                for cl in range(11):
                                    c = cg * 11 + cl
                                    mm(accs[i][0][:, :], AT[:, c, i * 128:(i + 1) * 128], Wf[fs][:, cl, :], cg == 0 and cl == 0, cg == 1 and cl == 10,
                                       [B("AT%d" % c), B("Wf%d" % fs)], [accs[i][1]])
                        for i in range(nt):
                            gated_res(accs[i][0], accs[i][1], ubF, "ubF", i, half, 1, s)

                def stage_L2(bi):
                    nt, N, tok0, j, s = geo(bi)
                    layer_norm_blk(nt, lambda i: ubF[:, i, :], lambda i: ubF[:, i, :], lnt[:, 2, :], lnt[:, 3, :],
                                   lambda i: [B("ubF%d" % i)], lambda i: [B("ubF%d" % i)], ph_ln)
                    fw.dma("pool", dst[tok0:tok0 + N, :].rearrange("(i p) d -> p i d", p=128), ubF[:, 0:nt, :],
                           reads=[B("ubF%d" % i) for i in range(nt)], writes=[B("xs_%d_%d" % (l + 1, bi))])

                def load_gates(j):
                    for gi in range(2):
                        fw.dma("sp", Gx[:, gi, :], s_g[j, gi], reads=[B("s_g%d%d" % (j, gi))], writes=[B("Gx")])

                load_gates(0)
                load_x(0)
                load_ot(0)
                if nblk > 1:
                    load_x(1)
                    load_ot(1)
                stage_O(0)
                stage_L1(0)
                for bi in range(nblk):
                    w0 = load_wi(0)
                    if bi + 2 < nblk:
                        load_ot(bi + 2)
                    if bi < 4:
                        stage_xb(bi)
                    if bi + 1 < min(nblk, 4):
                        stage_O(bi + 1)
                        stage_L1(bi + 1)
                    if bi == 4:
                        load_gates(1)
                        stage_O(4)
                        stage_L1(4)
                        stage_xb(4)
                    stage_rest(bi, w0)
                    if bi + 2 < nblk:
                        load_x(bi + 2)
                    stage_L2(bi)
                fw.barrier()
            cur_in = s_x1
    return nc


_NC_CACHE = {}


def _host_inputs(inputs):
    cst = _consts()
    f = np.float32
    g = {k: np.asarray(v) for k, v in inputs.items()}
    shared = {}
    shared["w_ada"] = np.ascontiguousarray(g["w_ada"], dtype=f)
    shared["b_ada"] = np.ascontiguousarray(g["b_ada"], dtype=f)
    shared["b_adaT"] = np.ascontiguousarray(g["b_ada"].reshape(DEPTH, 48, 128).transpose(0, 2, 1), dtype=f)
    shared["win"] = np.ascontiguousarray(g["w_in"][:, :, cst["cols"]], dtype=f)
    shared["w_o"] = np.ascontiguousarray(g["w_o"], dtype=f)
    shared["sink"] = np.ascontiguousarray(g["sink"], dtype=f)
    shared["lamv"] = np.ascontiguousarray(np.stack([g["lam_q1"], g["lam_k1"], g["lam_q2"], g["lam_k2"]], axis=1), dtype=f)
    shared["subg"] = np.ascontiguousarray(g["subln_g"], dtype=f)
    nb = g["na_bias"]
    gath = nb[:, :, cst["dr"], cst["dc"]]
    shared["nag"] = np.ascontiguousarray(gath.transpose(0, 1, 3, 2, 4).reshape(DEPTH, 6, 128, NV * 256), dtype=f)
    shared["nmask"] = cst["nmask"]
    shared["amask"] = cst["amask"]
    shared["lnp"] = np.ascontiguousarray(np.stack([g["ln1_g"], g["ln1_b"], g["ln2_g"], g["ln2_b"]], axis=1), dtype=f)
    shared["wfi"] = np.ascontiguousarray(g["w_ffn_in"][:, :, cst["ffperm"]], dtype=f)
    shared["wfo"] = np.ascontiguousarray(g["w_ffn_out"], dtype=f)
    shared["rope"] = cst["rope"]
    per = []
    for b in range(g["x"].shape[0]):
        m = dict(shared)
        m["xin"] = np.ascontiguousarray(np.concatenate([g["x"][b], g["ctx"][b]], axis=0), dtype=f)
        cc = np.stack([g["c"][b].reshape(8, 128).T, g["c_ctx"].reshape(8, 128).T], axis=-1)
        m["cT"] = np.ascontiguousarray(cc.reshape(128, 16), dtype=f)
        per.append(m)
    return per


def kernel(**inputs):
    per = _host_inputs(inputs)
    if "nc" not in _NC_CACHE:
        _NC_CACHE["nc"] = build()
    nc = _NC_CACHE["nc"]
    n = len(per)
    res = run_bass_kernel_spmd(nc, per, core_ids=list(range(n)))
    return np.stack([np.asarray(r["out"]) for r in res.results], axis=0).astype(np.float32)
```

```python
import math
from contextlib import ExitStack

import numpy as np
import concourse.bass as bass
import concourse.mybir as mybir
from concourse.bass_utils import run_bass_kernel_spmd

F32 = mybir.dt.float32
BF16 = mybir.dt.bfloat16
AF = mybir.ActivationFunctionType
ALU = mybir.AluOpType

D = 1024
L = 2048
C = 256
T = L + C
NT = T // 128
DFF = 2816
NCH = 22
NCOL = NCH * 128 + 768
DEPTH = 2
ALPHA = (2.0 * DEPTH) ** 0.25
EPS = 1e-5
NV = 14
import os
FILL_N = 0


class Buf:
    __slots__ = ("name", "w", "r")

    def __init__(self, name):
        self.name = name
        self.w = None
        self.r = {}


class FW:
    NDMA = 32

    def __init__(self, nc, stack):
        self.nc = nc
        self.eng = {"pe": nc.tensor, "act": nc.scalar, "dve": nc.vector, "pool": nc.gpsimd, "sp": nc.sync}
        self.sems = {}
        self.cnt = {}
        self.known = {e: {} for e in self.eng}
        for e in ("pe", "act", "dve", "pool"):
            self.sems[e] = stack.enter_context(nc.semaphore("s_" + e))
            self.cnt[e] = 0
        self.dsems = []
        for i in range(self.NDMA):
            k = "dma%d" % i
            self.sems[k] = stack.enter_context(nc.semaphore(k))
            self.cnt[k] = 0
            self.dsems.append(k)
        self.rr = {"hw": 0, "sw": 0}
        self.bufs = {}

    def B(self, name):
        b = self.bufs.get(name)
        if b is None:
            b = self.bufs[name] = Buf(name)
        return b

    def _wait(self, e, toks):
        need = {}
        for t in toks:
            if t is None:
                continue
            k, v = t
            if e == "pe" and k == "pe":
                continue
            if self.known[e].get(k, 0) >= v:
                continue
            if need.get(k, 0) < v:
                need[k] = v
        for k, v in need.items():
            self.eng[e].wait_ge(self.sems[k], v)
            self.known[e][k] = v

    @staticmethod
    def _deps(reads, writes):
        toks = []
        for b in reads:
            toks.append(b.w)
        for b in writes:
            toks.append(b.w)
            for k, v in b.r.items():
                toks.append((k, v))
        return toks

    @staticmethod
    def _mark(tok, reads, writes):
        k, v = tok
        for b in reads:
            if b.r.get(k, 0) < v:
                b.r[k] = v
        for b in writes:
            b.w = tok
            b.r = {}

    def op(self, e, fn, reads=(), writes=(), inc=True):
        self._wait(e, self._deps(reads, writes))
        ins = fn()
        if inc:
            self.cnt[e] += 1
            ins.then_inc(self.sems[e], 1)
            tok = (e, self.cnt[e])
        else:
            tok = (e, self.cnt[e] + 1)
        self._mark(tok, reads, writes)
        return ins

    def dma(self, q, out, in_, reads=(), writes=()):
        kind = "sw" if q == "pool" else "hw"
        half = self.NDMA // 2
        sem = self.dsems[(0 if kind == "hw" else half) + self.rr[kind]]
        self.rr[kind] = (self.rr[kind] + 1) % half
        toks = self._deps(reads, writes)
        if self.cnt[sem] > 0:
            toks.append((sem, self.cnt[sem]))
        self._wait(q, toks)
        ins = self.eng[q].dma_start(out=out, in_=in_)
        self.cnt[sem] += 16
        ins.then_inc(self.sems[sem], 16)
        self._mark((sem, self.cnt[sem]), reads, writes)
        return ins

    def barrier(self):
        toks = [(k, v) for k, v in self.cnt.items() if v > 0]
        for e in self.eng:
            self._wait(e, toks)


def _win_cols():
    aq0, ak0, av0, bq0, bk0, bv0, cq0, ck0, cv0 = 0, 384, 512, 640, 896, 1152, 1408, 1792, 2176
    pA = np.array([d + 16 if d % 32 < 16 else d - 16 for d in range(64)])
    pB32 = np.array([d + 8 if d % 16 < 8 else d - 8 for d in range(32)])
    pB = np.concatenate([pB32, 32 + pB32])
    ar = np.arange(64)

    def hc(base, h, perm=None):
        return base + h * 64 + (ar if perm is None else perm)

    ch = []
    for j in range(3):
        ch.append(np.concatenate([hc(aq0, j), hc(aq0, 3 + j)]))
    ch.append(np.concatenate([hc(ak0, 0), hc(ak0, 1)]))
    for j in range(3):
        ch.append(np.concatenate([hc(aq0, j, pA), hc(aq0, 3 + j, pA)]))
    ch.append(np.concatenate([hc(ak0, 0, pA), hc(ak0, 1, pA)]))
    for base, perm in ((bq0, None), (bk0, None), (bq0, pB), (bk0, pB)):
        for c in range(2):
            ch.append(np.concatenate([hc(base, 2 * c, perm), hc(base, 2 * c + 1, perm)]))
    for base in (cq0, ck0):
        for c in range(3):
            ch.append(np.concatenate([hc(base, 2 * c), hc(base, 2 * c + 1)]))
    ch.append(np.arange(av0, av0 + 128))
    ch.append(np.arange(bv0, bv0 + 256))
    ch.append(np.arange(cv0, cv0 + 384))
    cols = np.concatenate(ch)
    assert cols.shape[0] == NCOL
    return cols


def _rope_tables():
    f = np.float32
    pos = np.arange(L)
    rows = (pos // 64).astype(f)
    cols = (pos % 64).astype(f)
    p = np.arange(128)
    base = np.zeros((4, 128, T), f)
    base[0::2, :, L:] = 1.0
    inv16 = np.power(f(10000.0), -np.arange(16, dtype=f) / f(16)).astype(f)
    d = p % 64
    ang = (np.where((d < 32)[:, None], rows[None, :], cols[None, :]).astype(f) * inv16[(d % 32) % 16][:, None]).astype(f)
    sgn = np.where((d % 32) < 16, -1.0, 1.0).astype(f)
    base[0, :, :L] = np.cos(ang)
    base[1, :, :L] = np.sin(ang) * sgn[:, None]
    inv8 = np.power(f(10000.0), -np.arange(8, dtype=f) / f(8)).astype(f)
    d = p % 32
    ang = (np.where((d < 16)[:, None], rows[None, :], cols[None, :]).astype(f) * inv8[(d % 16) % 8][:, None]).astype(f)
    sgn = np.where((d % 16) < 8, -1.0, 1.0).astype(f)
    base[2, :, :L] = np.cos(ang)
    base[3, :, :L] = np.sin(ang) * sgn[:, None]
    tabs = np.zeros((16, 128, T), f)
    tabs[0:4] = base
    for g in range(2):
        m = ((p // 64) == g).astype(f)[:, None]
        tabs[4 + 2 * g] = base[0] * m
        tabs[5 + 2 * g] = base[1] * m
    for v in range(4):
        m = ((p // 32) == v).astype(f)[:, None]
        tabs[8 + 2 * v] = base[2] * m
        tabs[9 + 2 * v] = base[3] * m
    return tabs


def _na_variants():
    R, W, KH, KW = 32, 64, 8, 16
    var = []
    for i in range(6):
        var.append(([12, 13, 14, 15], [8 + 2 * i, 9 + 2 * i]))
    for u in range(4):
        var.append(([0, 1, 2, 3], [2 * u, 2 * u + 1]))
    for u in range(12, 16):
        var.append(([28, 29, 30, 31], [2 * u, 2 * u + 1]))
    dr = np.zeros((NV, 128, 256), np.int64)
    dc = np.zeros((NV, 128, 256), np.int64)
    ok = np.zeros((NV, 128, 256), np.float32)
    kc = np.arange(64)[:, None]
    qc = np.arange(64)[None, :]
    cs = np.clip(qc - KW // 2, 0, W - KW)
    colv = (kc >= cs) & (kc < cs + KW)
    dcm = np.clip(kc - qc, -(KW - 1), KW - 1) + (KW - 1)
    for v, (qrows, krows) in enumerate(var):
        for kk, kr in enumerate(krows):
            for qq, r in enumerate(qrows):
                rs = min(max(r - KH // 2, 0), R - KH)
                rowv = (rs <= kr <= rs + KH - 1)
                dr[v, kk * 64:(kk + 1) * 64, qq * 64:(qq + 1) * 64] = min(max(kr - r + 7, 0), 14)
                dc[v, kk * 64:(kk + 1) * 64, qq * 64:(qq + 1) * 64] = dcm
                ok[v, kk * 64:(kk + 1) * 64, qq * 64:(qq + 1) * 64] = (colv & rowv).astype(np.float32)
    return dr, dc, ok


_CONST = {}


def _consts():
    if not _CONST:
        _CONST["cols"] = _win_cols()
        _CONST["rope"] = _rope_tables()
        dr, dc, ok = _na_variants()
        _CONST["dr"], _CONST["dc"] = dr, dc
        _CONST["nmask"] = np.ascontiguousarray(ok.transpose(1, 0, 2).reshape(128, NV * 256))
        j = np.arange(128)[:, None]
        i = np.arange(128)[None, :]
        prev = (j >= i).astype(np.float32)
        nxt = (j <= i).astype(np.float32)
        _CONST["amask"] = np.concatenate([np.tile(prev, (1, 3)), np.tile(nxt, (1, 3))], axis=1)
        ffperm = np.concatenate([np.concatenate([np.arange(c * 128, (c + 1) * 128), DFF + np.arange(c * 128, (c + 1) * 128)])
                                 for c in range(22)])
        _CONST["ffperm"] = ffperm
    return _CONST


def build(nlayers=DEPTH, dbg=False):
    nc = bass.Bass("TRN2", target_bir_lowering=False)

    def din(name, shape, dt=F32):
        return nc.dram_tensor(name, list(shape), dt, kind="ExternalInput").ap()

    def dscr(name, shape, dt):
        return nc.dram_tensor(name, list(shape), dt, kind="ExternalOutput" if (dbg and name in ("s_q", "s_o", "s_x1", "s_g")) else "Internal").ap()

    xin = din("xin", [T, D])
    cT = din("cT", [128, 16])
    w_ada = din("w_ada", [DEPTH, D, 6 * D])
    b_ada = din("b_ada", [DEPTH, 6 * D])
    b_adaT = din("b_adaT", [DEPTH, 128, 48])
    win = din("win", [DEPTH, D, NCOL])
    w_o = din("w_o", [DEPTH, D, D])
    sink = din("sink", [DEPTH, 6])
    lamv = din("lamv", [DEPTH, 4, 32])
    subg = din("subg", [DEPTH, 64])
    nag = din("nag", [DEPTH, 6, 128, NV * 256])
    nmask = din("nmask", [128, NV * 256])
    amask = din("amask", [128, 768])
    lnp = din("lnp", [DEPTH, 4, D])
    wfi = din("wfi", [DEPTH, D, 2 * DFF])
    wfo = din("wfo", [DEPTH, DFF, D])
    rope = din("rope", [16, 128, T])
    out = nc.dram_tensor("out", [L, D], F32, kind="ExternalOutput").ap()

    s_ada = dscr("s_ada", [DEPTH, 128, 8, 6 * D], BF16)
    s_win = dscr("s_win", [DEPTH, 128, 8, NCOL], BF16)
    s_wo = dscr("s_wo", [DEPTH, 128, 8, D], BF16)
    s_wfi = dscr("s_wfi", [DEPTH, 128, 8, 2 * DFF], BF16)
    s_wfo = dscr("s_wfo", [DEPTH, 128, 22, D], BF16)
    s_q = dscr("s_q", [20, 128, T], BF16)
    s_o = dscr("s_o", [16, 64, T], BF16)
    s_x1 = dscr("s_x1", [T, D], F32)
    s_g = dscr("s_g", [2, 2, 128, D], F32)

    with ExitStack() as top:
        top.enter_context(nc.allow_low_precision(reason="bf16 matmul operands by design; fp32 accumulation"))
        fw = FW(nc, top)
        B = fw.B

        uid = [0]

        def sb(stack, name, shape, dt):
            uid[0] += 1
            return stack.enter_context(nc.sbuf_tensor("%s_u%d" % (name, uid[0]), list(shape), dt))

        psF = []
        psB = []
        pcnt = {"f": 0, "b": 0, "a": 0, "s": 0, "x": 0}

        def alloc_psum(stack, nf, nb):
            del psF[:]
            del psB[:]
            for i in range(nf):
                uid[0] += 1
                psF.append(stack.enter_context(nc.psum_tensor("psf%d_u%d" % (i, uid[0]), [128, 512], F32)))
            for i in range(nb):
                uid[0] += 1
                psB.append(stack.enter_context(nc.psum_tensor("psb%d_u%d" % (i, uid[0]), [128, 1024], BF16)))

        def next_ps(pool=None):
            if pool is None:
                i = pcnt["f"] % len(psF)
                pcnt["f"] += 1
            elif pool == "a":
                i = pcnt["a"] % 4
                pcnt["a"] += 1
            elif pool == "s":
                i = 4 + pcnt["s"] % 2
                pcnt["s"] += 1
            else:
                i = 6 + pcnt["x"] % 2
                pcnt["x"] += 1
            return psF[i], B("psf%d" % i)

        def next_psb():
            i = pcnt["b"] % 2
            pcnt["b"] += 1
            return psB[i], B("psb%d" % i)

        def mm(o, lhsT, rhs, start, stop, rd, wr):
            fw.op("pe", lambda: nc.tensor.matmul(o, lhsT=lhsT, rhs=rhs, start=start, stop=stop), reads=rd, writes=wr, inc=stop)

        ident = sb(top, "ident", [128, 128], BF16)
        onesb = sb(top, "onesb", [128, 128], BF16)
        sel = sb(top, "sel", [128, 128], BF16)
        o64 = sb(top, "o64", [128, 128], BF16)
        fillr = sb(top, "fillr", [128, 512], BF16)
        epsc = sb(top, "epsc", [128, 1], F32)
        maskcol = sb(top, "maskcol", [128, 2], F32)
        modT = sb(top, "modT", [128, 32, 2], F32)
        sinkE = sb(top, "sinkE", [128, 8], F32)
        lamt = sb(top, "lamt", [128, 4], F32)
        lamw = sb(top, "lamw", [128, 4, 32], F32)
        gs = sb(top, "gs", [64, 2], F32)

        fw.op("pool", lambda: nc.gpsimd.memset(ident[:], 1.0), writes=[B("ident")])
        fw.op("pool", lambda: nc.gpsimd.affine_select(out=ident[:], in_=ident[:], pattern=[[-1, 128]], compare_op=ALU.is_equal,
                                                     fill=0.0, base=0, channel_multiplier=1), reads=[B("ident")], writes=[B("ident")])
        fw.op("pool", lambda: nc.gpsimd.memset(onesb[:], 1.0), writes=[B("onesb")])
        fw.op("pool", lambda: nc.gpsimd.memset(sel[:], 0.0), writes=[B("sel")])
        fw.op("pool", lambda: nc.gpsimd.memset(sel[64:65, :], 1.0), reads=[B("sel")], writes=[B("sel")])
        fw.op("pool", lambda: nc.gpsimd.memset(o64[:], 0.0), writes=[B("o64")])
        fw.op("pool", lambda: nc.gpsimd.memset(o64[0:64, 0:64], 1.0 / 64.0), reads=[B("o64")], writes=[B("o64")])
        fw.op("pool", lambda: nc.gpsimd.memset(fillr[:], 1.0), writes=[B("fillr")])
        fw.op("pool", lambda: nc.gpsimd.memset(epsc[:], EPS), writes=[B("epsc")])
        fw.op("pool", lambda: nc.gpsimd.memset(maskcol[:], 0.0), writes=[B("maskcol")])
        fw.op("pool", lambda: nc.gpsimd.memset(maskcol[0:64, 0:1], 1.0), reads=[B("maskcol")], writes=[B("maskcol")])
        fw.op("pool", lambda: nc.gpsimd.memset(maskcol[64:128, 1:2], 1.0), reads=[B("maskcol")], writes=[B("maskcol")])

        def cast_weights(l):
            for kc in range(8):
                fw.dma("pool", s_ada[l, :, kc, :], w_ada[l, kc * 128:(kc + 1) * 128, :], writes=[B("s_ada%d_%d" % (l, kc))])
            for kc in range(8):
                fw.dma("pool", s_win[l, :, kc, :], win[l, kc * 128:(kc + 1) * 128, :], writes=[B("s_win%d_%d" % (l, kc))])
            for kc in range(8):
                fw.dma("pool", s_wo[l, :, kc, :], w_o[l, kc * 128:(kc + 1) * 128, :], writes=[B("s_wo%d_%d" % (l, kc))])
            for kc in range(8):
                fw.dma("pool", s_wfi[l, :, kc, :], wfi[l, kc * 128:(kc + 1) * 128, :], writes=[B("s_wfi%d_%d" % (l, kc))])
            for c in range(22):
                fw.dma("pool", s_wfo[l, :, c, :], wfo[l, c * 128:(c + 1) * 128, :], writes=[B("s_wfo%d_%d" % (l, c))])

        cast_weights(0)

        def layer_norm(src, dst, lng, lnb, slot, rd, wr, ph):
            st_, mv, rs, nb = ph["lnst"][slot], ph["lnmv"][slot], ph["lnrs"][slot], ph["lnnb"][slot]
            bs = B("lnsm%d" % slot)
            fw.op("dve", lambda: nc.vector.bn_stats(out=st_[:, 0:6], in_=src[:, 0:512]), reads=rd, writes=[bs])
            fw.op("dve", lambda: nc.vector.bn_stats(out=st_[:, 6:12], in_=src[:, 512:1024]), reads=rd + [bs], writes=[bs])
            fw.op("dve", lambda: nc.vector.bn_aggr(out=mv[:, 0:2], in_=st_[:, 0:12]), reads=[bs], writes=[bs])
            fw.op("act", lambda: nc.scalar.activation(out=rs[:, 0:1], in_=mv[:, 1:2], func=AF.Sqrt, bias=EPS, scale=1.0), reads=[bs], writes=[bs])
            fw.op("dve", lambda: nc.vector.reciprocal(out=rs[:, 0:1], in_=rs[:, 0:1]), reads=[bs], writes=[bs])
            fw.op("dve", lambda: nc.vector.scalar_tensor_tensor(out=nb[:, 0:1], in0=mv[:, 0:1], scalar=-1.0, in1=rs[:, 0:1],
                                                               op0=ALU.mult, op1=ALU.mult), reads=[bs], writes=[bs])
            fw.op("act", lambda: nc.scalar.activation(out=src, in_=src, func=AF.Identity, bias=nb[:, 0:1], scale=rs[:, 0:1]),
                  reads=rd + [bs], writes=rd)
            fw.op("pool", lambda: nc.gpsimd.tensor_tensor(out=dst, in0=src, in1=lng, op=ALU.mult), reads=rd + [B("lnp")], writes=wr)
            fw.op("pool", lambda: nc.gpsimd.tensor_tensor(out=dst, in0=dst, in1=lnb, op=ALU.add), reads=wr + [B("lnp")], writes=wr)

        def layer_norm_blk(nt, src_fn, dst_fn, lng, lnb, rd_fn, wr_fn, ph):
            st_, mv, rs, nb = ph["bst"], ph["bmv"], ph["brs"], ph["bnb"]
            bs = B("lnblk")
            for i in range(nt):
                src = src_fn(i)
                fw.op("dve", lambda i=i, src=src: nc.vector.bn_stats(out=st_[:, i, 0:6], in_=src[:, 0:512]), reads=rd_fn(i) + [bs], writes=[bs])
                fw.op("dve", lambda i=i, src=src: nc.vector.bn_stats(out=st_[:, i, 6:12], in_=src[:, 512:1024]), reads=rd_fn(i) + [bs], writes=[bs])
                fw.op("dve", lambda i=i: nc.vector.bn_aggr(out=mv[:, i, 0:2], in_=st_[:, i, 0:12]), reads=[bs], writes=[bs])
            fw.op("act", lambda: nc.scalar.activation(out=rs[:, 0:nt], in_=mv[:, 0:nt, 1], func=AF.Sqrt, bias=EPS, scale=1.0), reads=[bs], writes=[bs])
            fw.op("dve", lambda: nc.vector.reciprocal(out=rs[:, 0:nt], in_=rs[:, 0:nt]), reads=[bs], writes=[bs])
            fw.op("dve", lambda: nc.vector.scalar_tensor_tensor(out=nb[:, 0:nt], in0=mv[:, 0:nt, 0], scalar=-1.0, in1=rs[:, 0:nt],
                                                               op0=ALU.mult, op1=ALU.mult), reads=[bs], writes=[bs])
            for i in range(nt):
                src = src_fn(i)
                dst_ = dst_fn(i)
                fw.op("act", lambda i=i, src=src: nc.scalar.activation(out=src, in_=src, func=AF.Identity, bias=nb[:, i:i + 1], scale=rs[:, i:i + 1]),
                      reads=rd_fn(i) + [bs], writes=rd_fn(i))
                fw.op("pool", lambda src=src, dst_=dst_: nc.gpsimd.tensor_tensor(out=dst_, in0=src, in1=lng, op=ALU.mult), reads=rd_fn(i) + [B("lnp")], writes=wr_fn(i))
                fw.op("pool", lambda dst_=dst_: nc.gpsimd.tensor_tensor(out=dst_, in0=dst_, in1=lnb, op=ALU.add), reads=wr_fn(i) + [B("lnp")], writes=wr_fn(i))

        def transpose_mod(xb, nt, hT, mi, j, tag):
            N = nt * 128
            for c in range(8):
                pb, pbb = next_psb()
                for i in range(nt):
                    fw.op("pe", lambda i=i, c=c, pb=pb: nc.tensor.transpose(out=pb[:, i * 128:(i + 1) * 128],
                                                                          in_=xb[:, i, c * 128:(c + 1) * 128], identity=ident[:]),
                          reads=[B("xb%d" % i), B("ident")], writes=[pbb], inc=(i == nt - 1))
                fw.op("act", lambda c=c, pb=pb: nc.scalar.activation(out=hT[:, c, 0:N], in_=pb[:, 0:N], func=AF.Identity,
                                                                    bias=modT[:, mi * 8 + c, j:j + 1],
                                                                    scale=modT[:, (mi + 1) * 8 + c, j:j + 1]),
                      reads=[pbb, B("modT")], writes=[B("%s%d" % (tag, c))])

        cur_in = xin
        for l in range(nlayers):
            last = (l == nlayers - 1) and (nlayers == DEPTH)
            lam_init = 0.8 - 0.6 * math.exp(-0.3 * l)
            dst = out if last else s_x1
            nblk = 4 if last else 5

            with ExitStack() as ph:
                alloc_psum(ph, 6, 2)
                wada = sb(ph, "wada", [128, 8, 6 * D], BF16)
                cTt = sb(ph, "cTt", [128, 16], F32)
                scf = sb(ph, "scf", [128, 16], F32)
                scb = sb(ph, "scb", [128, 8, 2], BF16)
                rep = [sb(ph, "rep%d" % j, [128, 8, 128], BF16) for j in range(2)]
                bT = sb(ph, "bT", [128, 48], F32)
                bg = sb(ph, "bg", [128, 2, D], F32)
                Gt = [sb(ph, "Gt%d" % i, [128, D], F32) for i in range(2)]
                for kc in range(8):
                    fw.dma("sp", wada[:, kc, :], s_ada[l, :, kc, :], reads=[B("s_ada%d_%d" % (l, kc))], writes=[B("wada%d" % kc)])
                fw.dma("sp", cTt[:], cT, writes=[B("cTt")])
                fw.dma("sp", bT[:], b_adaT[l], writes=[B("bT")])
                for gi, m in enumerate((2, 5)):
                    fw.dma("sp", bg[:, gi, :], b_ada[l, m * D:(m + 1) * D].partition_broadcast(128), writes=[B("bg")])
                fw.dma("sp", sinkE[:, 0:6], sink[l].partition_broadcast(128), writes=[B("sinkE")])
                fw.dma("sp", lamw[:, :, :], lamv[l].partition_broadcast(128), writes=[B("lamw")])
                fw.dma("sp", gs[:, 0:1], subg[l].rearrange("(d o) -> d o", o=1), writes=[B("gs")])
                fw.op("act", lambda: nc.scalar.activation(out=sinkE[:, 0:6], in_=sinkE[:, 0:6], func=AF.Exp),
                      reads=[B("sinkE")], writes=[B("sinkE")])
                fw.op("dve", lambda: nc.vector.tensor_tensor(out=lamw[:, 0, :], in0=lamw[:, 0, :], in1=lamw[:, 1, :], op=ALU.mult),
                      reads=[B("lamw")], writes=[B("lamw")])
                fw.op("dve", lambda: nc.vector.tensor_tensor(out=lamw[:, 2, :], in0=lamw[:, 2, :], in1=lamw[:, 3, :], op=ALU.mult),
                      reads=[B("lamw")], writes=[B("lamw")])
                fw.op("dve", lambda: nc.vector.reduce_sum(out=lamt[:, 0:1], in_=lamw[:, 0, :], axis=mybir.AxisListType.X),
                      reads=[B("lamw")], writes=[B("lamt")])
                fw.op("dve", lambda: nc.vector.reduce_sum(out=lamt[:, 1:2], in_=lamw[:, 2, :], axis=mybir.AxisListType.X),
                      reads=[B("lamw"), B("lamt")], writes=[B("lamt")])
                fw.op("act", lambda: nc.scalar.activation(out=lamt[:, 0:2], in_=lamt[:, 0:2], func=AF.Exp),
                      reads=[B("lamt")], writes=[B("lamt")])
                fw.op("dve", lambda: nc.vector.tensor_tensor(out=lamt[:, 2:3], in0=lamt[:, 0:1], in1=lamt[:, 1:2], op=ALU.subtract),
                      reads=[B("lamt")], writes=[B("lamt")])
                fw.op("dve", lambda: nc.vector.tensor_scalar(out=lamt[:, 3:4], in0=lamt[:, 2:3], scalar1=float(lam_init), scalar2=None,
                                                            op0=ALU.add), reads=[B("lamt")], writes=[B("lamt")])
                fw.op("dve", lambda: nc.vector.tensor_scalar(out=gs[:, 1:2], in0=gs[:, 0:1], scalar1=float(1.0 - lam_init), scalar2=None,
                                                            op0=ALU.mult), reads=[B("gs")], writes=[B("gs")])
                fw.op("act", lambda: nc.scalar.activation(out=scf[:], in_=cTt[:], func=AF.Silu), reads=[B("cTt")], writes=[B("scf")])
                fw.op("dve", lambda: nc.vector.tensor_copy(out=scb[:].rearrange("p k j -> p (k j)"), in_=scf[:]), reads=[B("scf")], writes=[B("scb")])
                for j in range(2):
                    for kc in range(8):
                        fw.op("dve", lambda j=j, kc=kc: nc.vector.tensor_scalar(out=rep[j][:, kc, :], in0=onesb[:, :],
                                                                               scalar1=scf[:, kc * 2 + j:kc * 2 + j + 1], scalar2=None, op0=ALU.mult),
                              reads=[B("scf"), B("onesb")], writes=[B("rep%d" % j)])
                ps, psb_ = next_ps()
                for mi, m in enumerate((0, 1, 3, 4)):
                    for c in range(8):
                        col0 = m * D + c * 128
                        o = (mi * 8 + c) * 2
                        for kc in range(8):
                            mm(ps[:, o:o + 2], wada[:, kc, col0:col0 + 128], scb[:, kc, 0:2], kc == 0, kc == 7,
                               [B("wada%d" % kc), B("scb")], [psb_])
                psv = ps[:, 0:64].rearrange("p (a j) -> p a j", j=2)
                for j in range(2):
                    for (a0, b0) in ((0, 0), (16, 24)):
                        fw.op("dve", lambda j=j, a0=a0, b0=b0: nc.vector.tensor_tensor(out=modT[:, a0:a0 + 16, j], in0=psv[:, a0:a0 + 16, j],
                                                                                       in1=bT[:, b0:b0 + 16], op=ALU.add),
                              reads=[psb_, B("bT")], writes=[B("modT")])
                for a0 in (8, 24):
                    fw.op("dve", lambda a0=a0: nc.vector.tensor_scalar(out=modT[:, a0:a0 + 8, :], in0=modT[:, a0:a0 + 8, :], scalar1=1.0,
                                                                      scalar2=None, op0=ALU.add), reads=[B("modT")], writes=[B("modT")])
                for j in range(2 if not last else 1):
                    for gi, m in enumerate((2, 5)):
                        for half in range(2):
                            ps, psb_ = next_ps()
                            for kc in range(8):
                                mm(ps[:, :], rep[j][:, kc, :], wada[:, kc, m * D + half * 512:m * D + (half + 1) * 512], kc == 0, kc == 7,
                                   [B("rep%d" % j), B("wada%d" % kc)], [psb_])
                            fw.op("dve", lambda ps=ps, gi=gi, half=half: nc.vector.tensor_tensor(
                                out=Gt[gi][:, half * 512:(half + 1) * 512], in0=ps[:, :], in1=bg[:, gi, half * 512:(half + 1) * 512], op=ALU.add),
                                reads=[psb_, B("bg")], writes=[B("Gt%d" % gi)])
                        fw.dma("pool", s_g[j, gi], Gt[gi][:], reads=[B("Gt%d" % gi)], writes=[B("s_g%d%d" % (j, gi))])
                fw.barrier()

            kvs = ExitStack()
            KT = sb(kvs, "KT", [128, 6, T], BF16)
            VW = 12 * 66 + 62
            Vx = sb(kvs, "Vx", [128, NT, VW], BF16)
            fw.op("pool", lambda: nc.gpsimd.memset(Vx[:], 0.0), writes=[B("Vx")])
            fw.op("pool", lambda: nc.gpsimd.memset(Vx[:, :, 0:792].rearrange("p t (h c) -> p t h c", c=66)[:, :, :, 64:66], 1.0),
                  reads=[B("Vx")], writes=[B("Vx")])
            with ExitStack() as ph:
                alloc_psum(ph, 6, 2)
                Win = sb(ph, "Win", [128, 8, NCOL], BF16)
                xsb = [sb(ph, "xsb%d" % s, [128, 4, D], F32) for s in range(2)]
                xb = sb(ph, "xb", [128, 4, D], BF16)
                hT = sb(ph, "hT", [128, 8, 512], BF16)
                rt = sb(ph, "rt", [128, 16, 512], F32)
                t1 = [sb(ph, "t1_%d" % s, [128, 512], F32) for s in range(2)]
                t2 = [sb(ph, "t2_%d" % s, [128, 512], F32) for s in range(2)]
                qst = [sb(ph, "qst%d" % s, [128, 512], BF16) for s in range(2)]
                for kc in range(8):
                    fw.dma("sp", Win[:, kc, :], s_win[l, :, kc, :], reads=[B("s_win%d_%d" % (l, kc))], writes=[B("Win%d" % kc)])
                WinB = [B("Win%d" % kc) for kc in range(8)]
                tcnt = [0]

                def load_x(bi):
                    nt = 4 if bi < 4 else 2
                    s = bi % 2
                    fw.dma("sp", xsb[s][:, 0:nt, :], cur_in[bi * 512:bi * 512 + nt * 128, :].rearrange("(i p) d -> p i d", p=128),
                           reads=[B("xs_%d_%d" % (l, bi))], writes=[B("xsb%d" % s)])

                load_x(0)
                for bi in range(5):
                    nt = 4 if bi < 4 else 2
                    N = nt * 128
                    tok0 = bi * 512
                    j = 0 if bi < 4 else 1
                    s = bi % 2
                    if bi + 1 < 5:
                        load_x(bi + 1)
                    fw.dma("sp", rt[:, :, 0:N], rope[:, :, tok0:tok0 + N].rearrange("k p t -> p k t"), writes=[B("rt%d" % k) for k in range(16)])
                    for i in range(nt):
                        fw.op("act" if i % 2 else "dve",
                              (lambda i=i: nc.scalar.copy(out=xb[:, i, :], in_=xsb[s][:, i, :])) if i % 2 else
                              (lambda i=i: nc.vector.tensor_copy(out=xb[:, i, :], in_=xsb[s][:, i, :])),
                              reads=[B("xsb%d" % s)], writes=[B("xb%d" % i)])
                    transpose_mod(xb, nt, hT, 0, j, "hT")
                    hTB = [B("hT%d" % c) for c in range(8)]

                    def proj(ch):
                        ps, pb_ = next_ps()
                        for kc in range(8):
                            mm(ps[:, 0:N], Win[:, kc, ch * 128:(ch + 1) * 128], hT[:, kc, 0:N], kc == 0, kc == 7, [WinB[kc], hTB[kc]], [pb_])
                        return ps, pb_

                    def roped(ch, chp, tc, ts, dst_ap, dst_b):
                        pa, pab = proj(ch)
                        pp, ppb = proj(chp)
                        rope_comb(pa, pab, pp, ppb, tc, ts, dst_ap, dst_b)

                    def rope_comb(pa, pab, pp, ppb, tc, ts, dst_ap, dst_b):
                        k = tcnt[0] % 2
                        tcnt[0] += 1
                        fw.op("dve", lambda: nc.vector.tensor_tensor(out=t1[k][:, 0:N], in0=pa[:, 0:N], in1=rt[:, tc, 0:N], op=ALU.mult),
                              reads=[pab, B("rt%d" % tc)], writes=[B("t1_%d" % k)])
                        fw.op("dve", lambda: nc.vector.tensor_tensor(out=t2[k][:, 0:N], in0=pp[:, 0:N], in1=rt[:, ts, 0:N], op=ALU.mult),
                              reads=[ppb, B("rt%d" % ts)], writes=[B("t2_%d" % k)])
                        fw.op("pool", lambda: nc.gpsimd.tensor_tensor(out=dst_ap, in0=t1[k][:, 0:N], in1=t2[k][:, 0:N], op=ALU.add),
                              reads=[B("t1_%d" % k), B("t2_%d" % k)], writes=dst_b)

                    def q_out(qi, fn):
                        k = tcnt[0] % 2
                        fn(qst[k][:, 0:N], [B("qst%d" % k)])
                        fw.dma("sp", s_q[qi, :, tok0:tok0 + N], qst[k][:, 0:N], reads=[B("qst%d" % k)], writes=[B("s_q%d" % bi)])

                    need_q = (bi < 4) or (not last)
                    if need_q:
                        for jq in range(3):
                            pa, pab = proj(jq)
                            pp, ppb = proj(4 + jq)
                            for g in range(2):
                                q_out(g * 3 + jq, lambda ap, bb, g=g: rope_comb(pa, pab, pp, ppb, 4 + 2 * g, 5 + 2 * g, ap, bb))
                    roped(3, 7, 0, 1, KT[:, 0, tok0:tok0 + N], [B("KT0_%d" % bi)])
                    if need_q:
                        for c in range(2):
                            pa, pab = proj(8 + c)
                            pp, ppb = proj(12 + c)
                            for v in range(4):
                                q_out(6 + c * 4 + v, lambda ap, bb, v=v: rope_comb(pa, pab, pp, ppb, 8 + 2 * v, 9 + 2 * v, ap, bb))
                    for c in range(2):
                        roped(10 + c, 14 + c, 2, 3, KT[:, 1 + c, tok0:tok0 + N], [B("KT%d_%d" % (1 + c, bi))])
                    if need_q:
                        for c in range(3):
                            ps, pb_ = proj(16 + c)
                            for hb in range(2):
                                def cq(ap, bb, ps=ps, pb_=pb_, hb=hb):
                                    fw.op("act", lambda: nc.scalar.activation(out=ap, in_=ps[:, 0:N], func=AF.Copy, scale=maskcol[:, hb:hb + 1]),
                                          reads=[pb_, B("maskcol")], writes=bb)
                                q_out(14 + c * 2 + hb, cq)
                                tcnt[0] += 1
                    for c in range(3):
                        ps, pb_ = proj(19 + c)
                        fw.op("act", lambda ps=ps, c=c: nc.scalar.copy(out=KT[:, 3 + c, tok0:tok0 + N], in_=ps[:, 0:N]),
                              reads=[pb_], writes=[B("KT%d_%d" % (3 + c, bi))])
                    for i in range(nt):
                        gt = bi * 4 + i
                        ps, pb_ = next_ps()
                        for kc in range(8):
                            mm(ps[:, :], hT[:, kc, i * 128:(i + 1) * 128], Win[:, kc, NCH * 128:NCH * 128 + 512], kc == 0, kc == 7,
                               [WinB[kc], hTB[kc]], [pb_])
                        fw.op("act", lambda ps=ps, gt=gt: nc.scalar.copy(out=Vx[:, gt, 0:528].rearrange("p (h c) -> p h c", c=66)[:, :, 0:64], in_=ps[:, :].rearrange("p (h d) -> p h d", d=64)),
                              reads=[pb_], writes=[B("Vx")])
                        ps, pb_ = next_ps()
                        for kc in range(8):
                            mm(ps[:, 0:256], hT[:, kc, i * 128:(i + 1) * 128], Win[:, kc, NCH * 128 + 512:NCH * 128 + 768], kc == 0, kc == 7,
                               [WinB[kc], hTB[kc]], [pb_])
                        fw.op("dve", lambda ps=ps, gt=gt: nc.vector.tensor_copy(out=Vx[:, gt, 528:792].rearrange("p (h c) -> p h c", c=66)[:, :, 0:64],
                                                                                in_=ps[:, 0:256].rearrange("p (h d) -> p h d", d=64)),
                              reads=[pb_], writes=[B("Vx")])
                fw.barrier()

            with ExitStack() as ph:
                alloc_psum(ph, 8, 0)
                Et = sb(ph, "Et", [128, 6, NV * 256], BF16)
                nmk = sb(ph, "nmk", [128, NV * 256], BF16)
                MA = sb(ph, "MA", [128, 768], BF16)
                QTb = [sb(ph, "QTb%d" % s, [128, 20, 512], BF16) for s in range(2)]
                OTb = sb(ph, "OTb", [64, 16, 512], BF16)
                NPT = 6
                PT = [sb(ph, "PT%d" % s, [128, 512], BF16) for s in range(NPT)]
                bcs = [sb(ph, "bcs%d" % s, [64, 512], F32) for s in range(4)]
                tA = [sb(ph, "tA%d" % s, [64, 512], F32) for s in range(4)]
                sq = sb(ph, "sq", [128, 512], BF16)
                cnt = {"pt": 0, "ss": 0, "bc": 0}
                if l + 1 < nlayers:
                    cast_weights(l + 1)
                with ExitStack() as sub:
                    gstg = sb(sub, "gstg", [128, NV * 256], F32)
                    fw.dma("pool", nmk[:], nmask, writes=[B("nmk")])
                    fw.dma("pool", MA[:], amask, writes=[B("MA")])
                    for h in range(6):
                        fw.dma("sp", gstg[:], nag[l, h], writes=[B("gstg")])
                        fw.op("act", lambda h=h: nc.scalar.activation(out=Et[:, h, :], in_=gstg[:], func=AF.Exp), reads=[B("gstg")], writes=[B("Et%d" % h)])
                        fw.op("dve", lambda h=h: nc.vector.tensor_tensor(out=Et[:, h, :], in0=Et[:, h, :], in1=nmk[:], op=ALU.mult),
                              reads=[B("Et%d" % h), B("nmk")], writes=[B("Et%d" % h)])
                    fw.barrier()

                rb = [sb(ph, "rb%d" % s_, [128, 512], BF16) for s_ in range(4)]
                for s_ in range(4):
                    fw.op("pool", lambda s_=s_: nc.gpsimd.memset(rb[s_][:], 0.0), writes=[B("rb%d" % s_)])
                cnt["rb"] = 0
                fw.op("pool", lambda: nc.gpsimd.memset(sq[:], 0.0), writes=[B("sq")])

                def v_lhsT(t, hh):
                    return Vx[:, t, hh * 66:hh * 66 + 128]

                def bcast_rows(rows, N_):
                    res = []
                    for wr in rows:
                        k = cnt["rb"] % 4
                        cnt["rb"] += 1
                        wr(rb[k], B("rb%d" % k))
                        pbc, pbcb = next_ps("x")
                        mm(pbc[:, 0:N_], sel[:, :], rb[k][:, 0:N_], True, True, [B("sel"), B("rb%d" % k)], [pbcb])
                        res.append((pbc, pbcb))
                    return res

                def recips(pbcs, N_):
                    res = []
                    for pbc, pbcb in pbcs:
                        c0, c0b = new_bc()
                        fw.op("dve", lambda pbc=pbc, c0=c0: nc.vector.reciprocal(out=c0[0:64, 0:N_], in_=pbc[0:64, 0:N_]), reads=[pbcb], writes=[c0b])
                        res.append((c0, c0b))
                    return res

                def load_q(bi):
                    nt = 4 if bi < 4 else 2
                    s = bi % 2
                    fw.dma("sp", QTb[s][:, :, 0:nt * 128], s_q[:, :, bi * 512:bi * 512 + nt * 128].rearrange("c p t -> p c t"),
                           reads=[B("s_q%d" % bi)], writes=[B("QTb%d" % s)])

                def new_pt():
                    k = cnt["pt"] % NPT
                    cnt["pt"] += 1
                    return PT[k], B("PT%d" % k)

                def new_bc():
                    k = cnt["bc"] % 4
                    cnt["bc"] += 1
                    return bcs[k], B("bcs%d" % k)

                live_acc = set()
                flush_hook = [lambda: None]

                def new_acc():
                    i = pcnt["a"] % 4
                    if i in live_acc:
                        flush_hook[0]()
                    assert i not in live_acc, "accumulator bank still live"
                    live_acc.add(i)
                    pcnt["a"] += 1
                    return psF[i], B("psf%d" % i), i

                def gen_jobs(Q, Qb, bi, ctxq, nt, N):
                    units = []
                    for n in range(nt):
                        gq = bi * 4 + n
                        for g in range(2):
                            if ctxq:
                                tiles = [(16, None), (17, None)]
                            else:
                                tiles = []
                                if gq - 1 >= 0:
                                    tiles.append((gq - 1, 0))
                                tiles.append((gq, None))
                                if gq + 1 < 16:
                                    tiles.append((gq + 1, 1))
                                tiles += [(16, None), (17, None)]
                            u = {"jobs": []}
                            for ti, (t, mk) in enumerate(tiles):
                                def S(u=u, ti=ti, t=t, g=g, n=n):
                                    if ti == 0:
                                        u["po"] = new_acc()
                                    ps, psb_ = next_ps("s")
                                    mm(ps[:, 0:384].rearrange("p (a b) -> p a b", a=3), KT[:, 0, t * 128:(t + 1) * 128],
                                       Q[:, g * 3:g * 3 + 3, n * 128:(n + 1) * 128], True, True, [B("KT0_%d" % (t // 4)), Qb], [psb_])
                                    return ps, psb_

                                def rest(st, u=u, ti=ti, t=t, mk=mk, g=g, last=(ti == len(tiles) - 1)):
                                    ps, psb_ = st
                                    po, pob, _ = u["po"]
                                    pt, ptb = new_pt()
                                    fw.op("act", lambda: nc.scalar.activation(out=pt[:, 0:384], in_=ps[:, 0:384], func=AF.Exp, scale=0.125),
                                          reads=[psb_], writes=[ptb])
                                    if mk is not None:
                                        fw.op("pool", lambda: nc.gpsimd.tensor_tensor(out=pt[:, 0:384], in0=pt[:, 0:384],
                                                                                    in1=MA[:, mk * 384:(mk + 1) * 384], op=ALU.mult),
                                              reads=[ptb, B("MA")], writes=[ptb])
                                    mm(po[:, 0:384], v_lhsT(t, g), pt[:, 0:384], ti == 0, last, [B("Vx"), ptb], [pob])
                                u["jobs"].append((S, rest))

                            def fin(u=u, g=g, n=n):
                                po, pob, bank = u["po"]
                                stt = {}

                                def wr(rbt, rbb):
                                    for sg in range(3):
                                        h = 3 * g + sg
                                        fw.op("dve", lambda sg=sg, h=h: nc.vector.tensor_scalar(
                                            out=rbt[64:65, sg * 128:(sg + 1) * 128], in0=po[64:65, sg * 128:(sg + 1) * 128],
                                            scalar1=sinkE[64:65, h:h + 1], scalar2=None, op0=ALU.add), reads=[pob, B("sinkE")], writes=[rbb])

                                def s1():
                                    stt["p"] = bcast_rows([wr], 384)

                                def s2():
                                    stt["c"] = recips(stt["p"], 384)

                                def s3():
                                    ((c0, c0b),) = stt["c"]
                                    fw.op("dve", lambda: nc.vector.tensor_tensor(
                                        out=OTb[0:64, 3 * g:3 * g + 3, n * 128:(n + 1) * 128], in0=po[0:64, 0:384].rearrange("p (a b) -> p a b", a=3),
                                        in1=c0[0:64, 0:384].rearrange("p (a b) -> p a b", a=3), op=ALU.mult),
                                        reads=[pob, c0b], writes=[B("OTb")])
                                    live_acc.discard(bank)
                                return [(2, s1, False), (3, s2, False), (6, s3, False)]
                            u["fin"] = fin
                            units.append(u)
                    tilesB = [16, 17] if ctxq else list(range(18))
                    for h in range(4):
                        r0 = (h % 2) * 64
                        u = {"jobs": []}
                        for ti, t in enumerate(tilesB):
                            for m in range(2):
                                def S(u=u, ti=ti, t=t, m=m, h=h, r0=r0):
                                    if ti == 0 and m == 0:
                                        u["po"] = [new_acc(), new_acc()]
                                    ps, psb_ = next_ps("s")
                                    mm(ps[:, 0:N], KT[:, 1 + h // 2, t * 128:(t + 1) * 128], Q[:, 6 + (h // 2) * 4 + (h % 2) * 2 + m, 0:N],
                                       True, True, [B("KT%d_%d" % (1 + h // 2, t // 4)), Qb], [psb_])
                                    return ps, psb_

                                def rest(st, u=u, ti=ti, t=t, m=m, h=h, last=(ti == len(tilesB) - 1)):
                                    ps, psb_ = st
                                    po, pob, _ = u["po"][m]
                                    pt, ptb = new_pt()
                                    fw.op("act", lambda: nc.scalar.activation(out=pt[:, 0:N], in_=ps[:, 0:N], func=AF.Exp, scale=float(32 ** -0.5)),
                                          reads=[psb_], writes=[ptb])
                                    mm(po[:, 0:N], v_lhsT(t, 2 + h), pt[:, 0:N], ti == 0, last, [B("Vx"), ptb], [pob])
                                u["jobs"].append((S, rest))

                        def fin(u=u, h=h):
                            (p0, p0b, b0_), (p1, p1b, b1_) = u["po"]
                            stt = {}

                            def wr0(rbt, rbb):
                                fw.op("dve", lambda: nc.vector.tensor_copy(out=rbt[64:65, 0:N], in_=p0[64:65, 0:N]), reads=[p0b], writes=[rbb])

                            def wr1(rbt, rbb):
                                fw.op("dve", lambda: nc.vector.tensor_copy(out=rbt[64:65, 0:N], in_=p1[64:65, 0:N]), reads=[p1b], writes=[rbb])

                            def s1():
                                stt["p"] = bcast_rows([wr0, wr1], N)

                            def s2():
                                stt["c"] = recips(stt["p"], N)

                            def s3():
                                (c0, c0b), (c1, c1b) = stt["c"]
                                fw.op("dve", lambda: nc.vector.tensor_tensor(out=tA[0][:, 0:N], in0=p0[0:64, 0:N], in1=c0[0:64, 0:N], op=ALU.mult),
                                      reads=[p0b, c0b], writes=[B("tA0")])
                                fw.op("dve", lambda: nc.vector.scalar_tensor_tensor(out=tA[1][:, 0:N], in0=p1[0:64, 0:N], scalar=lamt[0:64, 3:4],
                                                                                   in1=c1[0:64, 0:N], op0=ALU.mult, op1=ALU.mult),
                                      reads=[p1b, c1b, B("lamt")], writes=[B("tA1")])
                                live_acc.discard(b0_)
                                live_acc.discard(b1_)
                                fw.op("pool", lambda: nc.gpsimd.tensor_tensor(out=tA[2][:, 0:N], in0=tA[0][:, 0:N], in1=tA[1][:, 0:N], op=ALU.subtract),
                                      reads=[B("tA0"), B("tA1")], writes=[B("tA2")])
                                fw.op("pool", lambda: nc.gpsimd.tensor_tensor(out=sq[0:64, 0:N], in0=tA[2][:, 0:N], in1=tA[2][:, 0:N], op=ALU.mult),
                                      reads=[B("tA2")], writes=[B("sq")])

                            def s4():
                                pms, pmsb = next_ps("x")
                                stt["pms"] = (pms, pmsb)
                                mm(pms[:, 0:N], o64[:, :], sq[:, 0:N], True, True, [B("o64"), B("sq")], [pmsb])

                            def s5():
                                pms, pmsb = stt["pms"]
                                fw.op("act", lambda: nc.scalar.activation(out=tA[3][:, 0:N], in_=pms[0:64, 0:N], func=AF.Ln, bias=epsc[0:64, 0:1], scale=1.0),
                                      reads=[pmsb, B("epsc")], writes=[B("tA3")])
                                fw.op("act", lambda: nc.scalar.activation(out=tA[3][:, 0:N], in_=tA[3][:, 0:N], func=AF.Exp, scale=-0.5),
                                      reads=[B("tA3")], writes=[B("tA3")])

                            def s6():
                                fw.op("dve", lambda: nc.vector.scalar_tensor_tensor(out=OTb[0:64, 6 + h, 0:N], in0=tA[2][:, 0:N], scalar=gs[:, 1:2],
                                                                                   in1=tA[3][:, 0:N], op0=ALU.mult, op1=ALU.mult),
                                      reads=[B("tA2"), B("tA3"), B("gs")], writes=[B("OTb")])
                            return [(2, s1, False), (3, s2, False), (10, s3, False), (14, s4, False), (18, s5, False), (22, s6, False)]
                        u["fin"] = fin
                        units.append(u)
                    for h in range(6):
                        r0 = (h % 2) * 64
                        for jj in range(1 if ctxq else 2):
                            if ctxq:
                                tiles = [(16, None), (17, None)]
                            else:
                                jg = bi * 2 + jj
                                if jg == 0:
                                    tiles = [(uu, 6 + uu) for uu in range(4)]
                                elif jg == 7:
                                    tiles = [(uu, 10 + uu - 12) for uu in range(12, 16)]
                                else:
                                    tiles = [(2 * jg - 2 + i, i) for i in range(6)]
                                tiles += [(16, None), (17, None)]
                            u = {"jobs": []}
                            for ti, (t, ev) in enumerate(tiles):
                                def S(u=u, ti=ti, t=t, h=h, r0=r0, jj=jj):
                                    if ti == 0:
                                        u["po"] = new_acc()
                                    ps, psb_ = next_ps("s")
                                    mm(ps[:, 0:256], KT[:, 3 + h // 2, t * 128:(t + 1) * 128], Q[:, 14 + (h // 2) * 2 + (h % 2), jj * 256:(jj + 1) * 256],
                                       True, True, [B("KT%d_%d" % (3 + h // 2, t // 4)), Qb], [psb_])
                                    return ps, psb_

                                def rest(st, u=u, ti=ti, t=t, ev=ev, h=h, last=(ti == len(tiles) - 1)):
                                    ps, psb_ = st
                                    po, pob, _ = u["po"]
                                    pt, ptb = new_pt()
                                    fw.op("act", lambda: nc.scalar.activation(out=pt[:, 0:256], in_=ps[:, 0:256], func=AF.Exp, scale=0.125),
                                          reads=[psb_], writes=[ptb])
                                    if ev is not None:
                                        fw.op("dve", lambda: nc.vector.tensor_tensor(out=pt[:, 0:256], in0=pt[:, 0:256],
                                                                                    in1=Et[:, h, ev * 256:(ev + 1) * 256], op=ALU.mult),
                                              reads=[ptb, B("Et%d" % h)], writes=[ptb])
                                    mm(po[:, 0:256], v_lhsT(t, 6 + h), pt[:, 0:256], ti == 0, last, [B("Vx"), ptb], [pob])
                                u["jobs"].append((S, rest, ev is not None))

                            def fin(u=u, h=h, jj=jj):
                                po, pob, bank = u["po"]
                                stt = {}

                                def wr(rbt, rbb):
                                    fw.op("dve", lambda: nc.vector.tensor_copy(out=rbt[64:65, 0:256], in_=po[64:65, 0:256]), reads=[pob], writes=[rbb])

                                def s1():
                                    stt["p"] = bcast_rows([wr], 256)

                                def s2():
                                    stt["c"] = recips(stt["p"], 256)

                                def s3():
                                    ((c0, c0b),) = stt["c"]
                                    fw.op("dve", lambda: nc.vector.tensor_tensor(out=OTb[0:64, 10 + h, jj * 256:(jj + 1) * 256], in0=po[0:64, 0:256],
                                                                                in1=c0[0:64, 0:256], op=ALU.mult), reads=[pob, c0b], writes=[B("OTb")])
                                    live_acc.discard(bank)
                                return [(2, s1, False), (4, s2, True), (6, s3, False)]
                            u["fin"] = fin
                            units.append(u)
                    return units

                nqb = 4 if last else 5
                load_q(0)
                for bi in range(nqb):
                    ctxq = bi == 4
                    nt = 4 if bi < 4 else 2
                    N = nt * 128
                    tok0 = bi * 512
                    s = bi % 2
                    if bi + 1 < nqb:
                        load_q(bi + 1)
                    units = gen_jobs(QTb[s], B("QTb%d" % s), bi, ctxq, nt, N)
                    jobs = []
                    for u in units:
                        for ji, jb in enumerate(u["jobs"]):
                            jobs.append((jb[0], jb[1], u["fin"] if ji == len(u["jobs"]) - 1 else None, (jb[2] if len(jb) > 2 else False)))
                    pending = []

                    def flush():
                        while pending:
                            pending.pop(0)[1]()
                    flush_hook[0] = flush
                    st_next = jobs[0][0]()
                    for k, (S, rest, fin, _) in enumerate(jobs):
                        st_cur = st_next
                        if k + 1 < len(jobs):
                            st_next = jobs[k + 1][0]()
                        rest(st_cur)
                        if fin is not None:
                            base = 0
                            for (dl, stg, heavy) in fin():
                                pending.append([k + dl, stg, heavy])
                            pending.sort(key=lambda x: x[0])
                        nxt_mask = (k + 1 < len(jobs)) and jobs[k + 1][3]
                        i = 0
                        blocked = False
                        while i < len(pending):
                            due, stg, heavy = pending[i]
                            if due > k:
                                break
                            if heavy and nxt_mask and k - due < 6:
                                blocked = True
                            if blocked:
                                i += 1
                                continue
                            pending.pop(i)
                            stg()
                    flush()
                    fw.dma("pool", s_o[:, :, tok0:tok0 + N].rearrange("h p t -> p h t"), OTb[:, :, 0:N], reads=[B("OTb")], writes=[B("s_o%d" % bi)])
                fw.barrier()
            kvs.close()

            with ExitStack() as ph:
                alloc_psum(ph, 6, 2)
                Wo = sb(ph, "Wo", [128, 8, D], BF16)
                xsb = [sb(ph, "xsb%d" % s, [128, 4, D], F32) for s in range(2)]
                OT2 = [sb(ph, "OT2_%d" % s, [128, 8, 512], BF16) for s in range(2)]
                ubA = sb(ph, "ubA", [128, 4, D], F32)
                ubF = sb(ph, "ubF", [128, 4, D], F32)
                xb = sb(ph, "xb", [128, 4, D], BF16)
                hT = sb(ph, "hT", [128, 8, 512], BF16)
                AT = sb(ph, "AT", [128, 22, 512], BF16)
                Wi = [sb(ph, "Wi%d" % s, [128, 8, 512], BF16) for s in range(2)]
                Wf = [sb(ph, "Wf%d" % s, [128, 11, 512], BF16) for s in range(2)]
                lnt = sb(ph, "lnt", [128, 4, D], F32)
                Gx = sb(ph, "Gx", [128, 2, D], F32)
                sgt = [sb(ph, "sgt%d" % s, [128, 512], F32) for s in range(2)]
                ph_ln = {"lnst": [sb(ph, "lnst%d" % s, [128, 12], F32) for s in range(4)],
                         "lnmv": [sb(ph, "lnmv%d" % s, [128, 2], F32) for s in range(4)],
                         "lnrs": [sb(ph, "lnrs%d" % s, [128, 1], F32) for s in range(4)],
                         "lnnb": [sb(ph, "lnnb%d" % s, [128, 1], F32) for s in range(4)]}
                ph_ln["bst"] = sb(ph, "bst", [128, 4, 12], F32)
                ph_ln["bmv"] = sb(ph, "bmv", [128, 4, 2], F32)
                ph_ln["brs"] = sb(ph, "brs", [128, 4], F32)
                ph_ln["bnb"] = sb(ph, "bnb", [128, 4], F32)
                lncnt = [0]
                wic = [0]
                wfc = [0]
                for kc in range(8):
                    fw.dma("sp", Wo[:, kc, :], s_wo[l, :, kc, :], reads=[B("s_wo%d_%d" % (l, kc))], writes=[B("Wo")])
                for k in range(4):
                    fw.dma("sp", lnt[:, k, :], lnp[l, k].partition_broadcast(128), writes=[B("lnp")])
                s_o2 = s_o.rearrange("(c two) p t -> two p c t", two=2)

                def geo(bi):
                    nt = 4 if bi < 4 else 2
                    return nt, nt * 128, bi * 512, (0 if bi < 4 else 1), bi % 2

                def load_x(bi):
                    nt, N, tok0, j, s = geo(bi)
                    fw.dma("sp", xsb[s][:, 0:nt, :], cur_in[tok0:tok0 + N, :].rearrange("(i p) d -> p i d", p=128),
                           reads=[B("xs_%d_%d" % (l, bi))], writes=[B("xsb%d_%d" % (s, i)) for i in range(4)])

                def load_ot(bi):
                    nt, N, tok0, j, s = geo(bi)
                    for hp in range(2):
                        fw.dma("sp", OT2[s][hp * 64:(hp + 1) * 64, :, 0:N], s_o2[hp, :, :, tok0:tok0 + N], reads=[B("s_o%d" % bi)], writes=[B("OT2_%d" % s)])

                def load_wi(cp):
                    s = wic[0] % 2
                    wic[0] += 1
                    fw.dma("sp", Wi[s][:], s_wfi[l, :, :, cp * 512:(cp + 1) * 512], reads=[B("s_wfi%d_%d" % (l, kc)) for kc in range(8)], writes=[B("Wi%d" % s)])
                    return s

                def load_wf(half, cg):
                    s = wfc[0] % 2
                    wfc[0] += 1
                    fw.dma("sp", Wf[s][:], s_wfo[l, :, cg * 11:(cg + 1) * 11, half * 512:(half + 1) * 512],
                           reads=[B("s_wfo%d_%d" % (l, cg * 11 + cl)) for cl in range(11)], writes=[B("Wf%d" % s)])
                    return s

                def gated_res(ps, pb_, ub, ubn, i, half, gi, s):
                    cs = slice(half * 512, (half + 1) * 512)
                    fw.op("dve", lambda: nc.vector.tensor_tensor(out=ub[:, i, cs], in0=ps[:, :], in1=Gx[:, gi, cs], op=ALU.mult),
                          reads=[pb_, B("Gx")], writes=[B("%s%d" % (ubn, i))])
                    fw.op("dve", lambda: nc.vector.scalar_tensor_tensor(out=ub[:, i, cs], in0=xsb[s][:, i, cs], scalar=float(ALPHA), in1=ub[:, i, cs],
                                                                       op0=ALU.mult, op1=ALU.add),
                          reads=[B("%s%d" % (ubn, i)), B("xsb%d_%d" % (s, i))], writes=[B("%s%d" % (ubn, i))])

                def stage_O(bi):
                    nt, N, tok0, j, s = geo(bi)
                    for i in range(nt):
                        for half in range(2):
                            ps, pb_ = next_ps()
                            for c in range(8):
                                mm(ps[:, :], OT2[s][:, c, i * 128:(i + 1) * 128], Wo[:, c, half * 512:(half + 1) * 512], c == 0, c == 7,
                                   [B("OT2_%d" % s), B("Wo")], [pb_])
                            gated_res(ps, pb_, ubA, "ubA", i, half, 0, s)

                def stage_L1(bi):
                    nt, N, tok0, j, s = geo(bi)
                    layer_norm_blk(nt, lambda i: ubA[:, i, :], lambda i: xsb[s][:, i, :], lnt[:, 0, :], lnt[:, 1, :],
                                   lambda i: [B("ubA%d" % i)], lambda i: [B("xsb%d_%d" % (s, i))], ph_ln)

                def stage_xb(bi):
                    nt, N, tok0, j, s = geo(bi)
                    for i in range(nt):
                        fw.op("act", lambda i=i: nc.scalar.copy(out=xb[:, i, :], in_=xsb[s][:, i, :]), reads=[B("xsb%d_%d" % (s, i))], writes=[B("xb%d" % i)])

                def stage_rest(bi, w0):
                    nt, N, tok0, j, s = geo(bi)
                    transpose_mod(xb, nt, hT, 2, j, "hT")
                    hTB = [B("hT%d" % c) for c in range(8)]
                    for cp in range(11):
                        ws = w0
                        if cp + 1 < 11:
                            w0 = load_wi(cp + 1)
                        for cc in range(2):
                            c = 2 * cp + cc
                            pg, pgb = next_ps()
                            for kc in range(8):
                                mm(pg[:, 0:N], Wi[ws][:, kc, cc * 256:cc * 256 + 128], hT[:, kc, 0:N], kc == 0, kc == 7, [B("Wi%d" % ws), hTB[kc]], [pgb])
                            pu, pub = next_ps()
                            for kc in range(8):
                                mm(pu[:, 0:N], Wi[ws][:, kc, cc * 256 + 128:cc * 256 + 256], hT[:, kc, 0:N], kc == 0, kc == 7, [B("Wi%d" % ws), hTB[kc]], [pub])
                            k = c % 2
                            fw.op("act", lambda pg=pg, k=k: nc.scalar.activation(out=sgt[k][:, 0:N], in_=pg[:, 0:N], func=AF.Silu), reads=[pgb], writes=[B("sgt%d" % k)])
                            fw.op("dve", lambda pu=pu, k=k, c=c: nc.vector.tensor_tensor(out=AT[:, c, 0:N], in0=pu[:, 0:N], in1=sgt[k][:, 0:N], op=ALU.mult),
                                  reads=[pub, B("sgt%d" % k)], writes=[B("AT%d" % c)])
                    f0 = load_wf(0, 0)
                    for half in range(2):
                        accs = [next_ps() for _ in range(nt)]
                        for cg in range(2):
                            fs = f0
                            if not (half == 1 and cg == 1):
                                f0 = load_wf(half if cg == 0 else half + 1, 1 - cg)
                            for i in range(nt):
                                for cl in range(11):
                                    c = cg * 11 + cl
                                    mm(accs[i][0][:, :], AT[:, c, i * 128:(i + 1) * 128], Wf[fs][:, cl, :], cg == 0 and cl == 0, cg == 1 and cl == 10,
                                       [B("AT%d" % c), B("Wf%d" % fs)], [accs[i][1]])
                        for i in range(nt):
                            gated_res(accs[i][0], accs[i][1], ubF, "ubF", i, half, 1, s)

                def stage_L2(bi):
                    nt, N, tok0, j, s = geo(bi)
                    layer_norm_blk(nt, lambda i: ubF[:, i, :], lambda i: ubF[:, i, :], lnt[:, 2, :], lnt[:, 3, :],
                                   lambda i: [B("ubF%d" % i)], lambda i: [B("ubF%d" % i)], ph_ln)
                    fw.dma("pool", dst[tok0:tok0 + N, :].rearrange("(i p) d -> p i d", p=128), ubF[:, 0:nt, :],
                           reads=[B("ubF%d" % i) for i in range(nt)], writes=[B("xs_%d_%d" % (l + 1, bi))])

                def load_gates(j):
                    for gi in range(2):
                        fw.dma("sp", Gx[:, gi, :], s_g[j, gi], reads=[B("s_g%d%d" % (j, gi))], writes=[B("Gx")])

                load_gates(0)
                load_x(0)
                load_ot(0)
                if nblk > 1:
                    load_x(1)
                    load_ot(1)
                stage_O(0)
                stage_L1(0)
                for bi in range(nblk):
                    w0 = load_wi(0)
                    if bi + 2 < nblk:
                        load_ot(bi + 2)
                    if bi < 4:
                        stage_xb(bi)
                    if bi + 1 < min(nblk, 4):
                        stage_O(bi + 1)
                        stage_L1(bi + 1)
                    if bi == 4:
                        load_gates(1)
                        stage_O(4)
                        stage_L1(4)
                        stage_xb(4)
                    stage_rest(bi, w0)
                    if bi + 2 < nblk:
                        load_x(bi + 2)
                    stage_L2(bi)
                fw.barrier()
            cur_in = s_x1
    return nc


_NC_CACHE = {}


def _host_inputs(inputs):
    cst = _consts()
    f = np.float32
    g = {k: np.asarray(v) for k, v in inputs.items()}
    shared = {}
    shared["w_ada"] = np.ascontiguousarray(g["w_ada"], dtype=f)
    shared["b_ada"] = np.ascontiguousarray(g["b_ada"], dtype=f)
    shared["b_adaT"] = np.ascontiguousarray(g["b_ada"].reshape(DEPTH, 48, 128).transpose(0, 2, 1), dtype=f)
    shared["win"] = np.ascontiguousarray(g["w_in"][:, :, cst["cols"]], dtype=f)
    shared["w_o"] = np.ascontiguousarray(g["w_o"], dtype=f)
    shared["sink"] = np.ascontiguousarray(g["sink"], dtype=f)
    shared["lamv"] = np.ascontiguousarray(np.stack([g["lam_q1"], g["lam_k1"], g["lam_q2"], g["lam_k2"]], axis=1), dtype=f)
    shared["subg"] = np.ascontiguousarray(g["subln_g"], dtype=f)
    nb = g["na_bias"]
    gath = nb[:, :, cst["dr"], cst["dc"]]
    shared["nag"] = np.ascontiguousarray(gath.transpose(0, 1, 3, 2, 4).reshape(DEPTH, 6, 128, NV * 256), dtype=f)
    shared["nmask"] = cst["nmask"]
    shared["amask"] = cst["amask"]
    shared["lnp"] = np.ascontiguousarray(np.stack([g["ln1_g"], g["ln1_b"], g["ln2_g"], g["ln2_b"]], axis=1), dtype=f)
    shared["wfi"] = np.ascontiguousarray(g["w_ffn_in"][:, :, cst["ffperm"]], dtype=f)
    shared["wfo"] = np.ascontiguousarray(g["w_ffn_out"], dtype=f)
    shared["rope"] = cst["rope"]
    per = []
    for b in range(g["x"].shape[0]):
        m = dict(shared)
        m["xin"] = np.ascontiguousarray(np.concatenate([g["x"][b], g["ctx"][b]], axis=0), dtype=f)
        cc = np.stack([g["c"][b].reshape(8, 128).T, g["c_ctx"].reshape(8, 128).T], axis=-1)
        m["cT"] = np.ascontiguousarray(cc.reshape(128, 16), dtype=f)
        per.append(m)
    return per


def kernel(**inputs):
    per = _host_inputs(inputs)
    if "nc" not in _NC_CACHE:
        _NC_CACHE["nc"] = build()
    nc = _NC_CACHE["nc"]
    n = len(per)
    res = run_bass_kernel_spmd(nc, per, core_ids=list(range(n)))
    return np.stack([np.asarray(r["out"]) for r in res.results], axis=0).astype(np.float32)
```

```python
import math
from contextlib import ExitStack

import numpy as np
import concourse.bass as bass
import concourse.mybir as mybir
from concourse.bass_utils import run_bass_kernel_spmd

F32 = mybir.dt.float32
BF16 = mybir.dt.bfloat16
AF = mybir.ActivationFunctionType
ALU = mybir.AluOpType

D = 1024
L = 2048
C = 256
T = L + C
NT = T // 128
DFF = 2816
NCH = 22
NCOL = NCH * 128 + 768
DEPTH = 2
ALPHA = (2.0 * DEPTH) ** 0.25
EPS = 1e-5
NV = 14
import os
FILL_N = 0


class Buf:
    __slots__ = ("name", "w", "r")

    def __init__(self, name):
        self.name = name
        self.w = None
        self.r = {}


class FW:
    NDMA = 32

    def __init__(self, nc, stack):
        self.nc = nc
        self.eng = {"pe": nc.tensor, "act": nc.scalar, "dve": nc.vector, "pool": nc.gpsimd, "sp": nc.sync}
        self.sems = {}
        self.cnt = {}
        self.known = {e: {} for e in self.eng}
        for e in ("pe", "act", "dve", "pool"):
            self.sems[e] = stack.enter_context(nc.semaphore("s_" + e))
            self.cnt[e] = 0
        self.dsems = []
        for i in range(self.NDMA):
            k = "dma%d" % i
            self.sems[k] = stack.enter_context(nc.semaphore(k))
            self.cnt[k] = 0
            self.dsems.append(k)
        self.rr = {"hw": 0, "sw": 0}
        self.bufs = {}

    def B(self, name):
        b = self.bufs.get(name)
        if b is None:
            b = self.bufs[name] = Buf(name)
        return b

    def _wait(self, e, toks):
        need = {}
        for t in toks:
            if t is None:
                continue
            k, v = t
            if e == "pe" and k == "pe":
                continue
            if self.known[e].get(k, 0) >= v:
                continue
            if need.get(k, 0) < v:
                need[k] = v
        for k, v in need.items():
            self.eng[e].wait_ge(self.sems[k], v)
            self.known[e][k] = v

    @staticmethod
    def _deps(reads, writes):
        toks = []
        for b in reads:
            toks.append(b.w)
        for b in writes:
            toks.append(b.w)
            for k, v in b.r.items():
                toks.append((k, v))
        return toks

    @staticmethod
    def _mark(tok, reads, writes):
        k, v = tok
        for b in reads:
            if b.r.get(k, 0) < v:
                b.r[k] = v
        for b in writes:
            b.w = tok
            b.r = {}

    def op(self, e, fn, reads=(), writes=(), inc=True):
        self._wait(e, self._deps(reads, writes))
        ins = fn()
        if inc:
            self.cnt[e] += 1
            ins.then_inc(self.sems[e], 1)
            tok = (e, self.cnt[e])
        else:
            tok = (e, self.cnt[e] + 1)
        self._mark(tok, reads, writes)
        return ins

    def dma(self, q, out, in_, reads=(), writes=()):
        kind = "sw" if q == "pool" else "hw"
        half = self.NDMA // 2
        sem = self.dsems[(0 if kind == "hw" else half) + self.rr[kind]]
        self.rr[kind] = (self.rr[kind] + 1) % half
        toks = self._deps(reads, writes)
        if self.cnt[sem] > 0:
            toks.append((sem, self.cnt[sem]))
        self._wait(q, toks)
        ins = self.eng[q].dma_start(out=out, in_=in_)
        self.cnt[sem] += 16
        ins.then_inc(self.sems[sem], 16)
        self._mark((sem, self.cnt[sem]), reads, writes)
        return ins

    def barrier(self):
        toks = [(k, v) for k, v in self.cnt.items() if v > 0]
        for e in self.eng:
            self._wait(e, toks)


def _win_cols():
    aq0, ak0, av0, bq0, bk0, bv0, cq0, ck0, cv0 = 0, 384, 512, 640, 896, 1152, 1408, 1792, 2176
    pA = np.array([d + 16 if d % 32 < 16 else d - 16 for d in range(64)])
    pB32 = np.array([d + 8 if d % 16 < 8 else d - 8 for d in range(32)])
    pB = np.concatenate([pB32, 32 + pB32])
    ar = np.arange(64)

    def hc(base, h, perm=None):
        return base + h * 64 + (ar if perm is None else perm)

    ch = []
    for j in range(3):
        ch.append(np.concatenate([hc(aq0, j), hc(aq0, 3 + j)]))
    ch.append(np.concatenate([hc(ak0, 0), hc(ak0, 1)]))
    for j in range(3):
        ch.append(np.concatenate([hc(aq0, j, pA), hc(aq0, 3 + j, pA)]))
    ch.append(np.concatenate([hc(ak0, 0, pA), hc(ak0, 1, pA)]))
    for base, perm in ((bq0, None), (bk0, None), (bq0, pB), (bk0, pB)):
        for c in range(2):
            ch.append(np.concatenate([hc(base, 2 * c, perm), hc(base, 2 * c + 1, perm)]))
    for base in (cq0, ck0):
        for c in range(3):
            ch.append(np.concatenate([hc(base, 2 * c), hc(base, 2 * c + 1)]))
    ch.append(np.arange(av0, av0 + 128))
    ch.append(np.arange(bv0, bv0 + 256))
    ch.append(np.arange(cv0, cv0 + 384))
    cols = np.concatenate(ch)
    assert cols.shape[0] == NCOL
    return cols


def _rope_tables():
    f = np.float32
    pos = np.arange(L)
    rows = (pos // 64).astype(f)
    cols = (pos % 64).astype(f)
    p = np.arange(128)
    base = np.zeros((4, 128, T), f)
    base[0::2, :, L:] = 1.0
    inv16 = np.power(f(10000.0), -np.arange(16, dtype=f) / f(16)).astype(f)
    d = p % 64
    ang = (np.where((d < 32)[:, None], rows[None, :], cols[None, :]).astype(f) * inv16[(d % 32) % 16][:, None]).astype(f)
    sgn = np.where((d % 32) < 16, -1.0, 1.0).astype(f)
    base[0, :, :L] = np.cos(ang)
    base[1, :, :L] = np.sin(ang) * sgn[:, None]
    inv8 = np.power(f(10000.0), -np.arange(8, dtype=f) / f(8)).astype(f)
    d = p % 32
    ang = (np.where((d < 16)[:, None], rows[None, :], cols[None, :]).astype(f) * inv8[(d % 16) % 8][:, None]).astype(f)
    sgn = np.where((d % 16) < 8, -1.0, 1.0).astype(f)
    base[2, :, :L] = np.cos(ang)
    base[3, :, :L] = np.sin(ang) * sgn[:, None]
    tabs = np.zeros((16, 128, T), f)
    tabs[0:4] = base
    for g in range(2):
        m = ((p // 64) == g).astype(f)[:, None]
        tabs[4 + 2 * g] = base[0] * m
        tabs[5 + 2 * g] = base[1] * m
    for v in range(4):
        m = ((p // 32) == v).astype(f)[:, None]
        tabs[8 + 2 * v] = base[2] * m
        tabs[9 + 2 * v] = base[3] * m
    return tabs


def _na_variants():
    R, W, KH, KW = 32, 64, 8, 16
    var = []
    for i in range(6):
        var.append(([12, 13, 14, 15], [8 + 2 * i, 9 + 2 * i]))
    for u in range(4):
        var.append(([0, 1, 2, 3], [2 * u, 2 * u + 1]))
    for u in range(12, 16):
        var.append(([28, 29, 30, 31], [2 * u, 2 * u + 1]))
    dr = np.zeros((NV, 128, 256), np.int64)
    dc = np.zeros((NV, 128, 256), np.int64)
    ok = np.zeros((NV, 128, 256), np.float32)
    kc = np.arange(64)[:, None]
    qc = np.arange(64)[None, :]
    cs = np.clip(qc - KW // 2, 0, W - KW)
    colv = (kc >= cs) & (kc < cs + KW)
    dcm = np.clip(kc - qc, -(KW - 1), KW - 1) + (KW - 1)
    for v, (qrows, krows) in enumerate(var):
        for kk, kr in enumerate(krows):
            for qq, r in enumerate(qrows):
                rs = min(max(r - KH // 2, 0), R - KH)
                rowv = (rs <= kr <= rs + KH - 1)
                dr[v, kk * 64:(kk + 1) * 64, qq * 64:(qq + 1) * 64] = min(max(kr - r + 7, 0), 14)
                dc[v, kk * 64:(kk + 1) * 64, qq * 64:(qq + 1) * 64] = dcm
                ok[v, kk * 64:(kk + 1) * 64, qq * 64:(qq + 1) * 64] = (colv & rowv).astype(np.float32)
    return dr, dc, ok


_CONST = {}


def _consts():
    if not _CONST:
        _CONST["cols"] = _win_cols()
        _CONST["rope"] = _rope_tables()
        dr, dc, ok = _na_variants()
        _CONST["dr"], _CONST["dc"] = dr, dc
        _CONST["nmask"] = np.ascontiguousarray(ok.transpose(1, 0, 2).reshape(128, NV * 256))
        j = np.arange(128)[:, None]
        i = np.arange(128)[None, :]
        prev = (j >= i).astype(np.float32)
        nxt = (j <= i).astype(np.float32)
        _CONST["amask"] = np.concatenate([np.tile(prev, (1, 3)), np.tile(nxt, (1, 3))], axis=1)
        ffperm = np.concatenate([np.concatenate([np.arange(c * 128, (c + 1) * 128), DFF + np.arange(c * 128, (c + 1) * 128)])
                                 for c in range(22)])
        _CONST["ffperm"] = ffperm
    return _CONST


def build(nlayers=DEPTH, dbg=False):
    nc = bass.Bass("TRN2", target_bir_lowering=False)

    def din(name, shape, dt=F32):
        return nc.dram_tensor(name, list(shape), dt, kind="ExternalInput").ap()

    def dscr(name, shape, dt):
        return nc.dram_tensor(name, list(shape), dt, kind="ExternalOutput" if (dbg and name in ("s_q", "s_o", "s_x1", "s_g")) else "Internal").ap()

    xin = din("xin", [T, D])
    cT = din("cT", [128, 16])
    w_ada = din("w_ada", [DEPTH, D, 6 * D])
    b_ada = din("b_ada", [DEPTH, 6 * D])
    b_adaT = din("b_adaT", [DEPTH, 128, 48])
    win = din("win", [DEPTH, D, NCOL])
    w_o = din("w_o", [DEPTH, D, D])
    sink = din("sink", [DEPTH, 6])
    lamv = din("lamv", [DEPTH, 4, 32])
    subg = din("subg", [DEPTH, 64])
    nag = din("nag", [DEPTH, 6, 128, NV * 256])
    nmask = din("nmask", [128, NV * 256])
    amask = din("amask", [128, 768])
    lnp = din("lnp", [DEPTH, 4, D])
    wfi = din("wfi", [DEPTH, D, 2 * DFF])
    wfo = din("wfo", [DEPTH, DFF, D])
    rope = din("rope", [16, 128, T])
    out = nc.dram_tensor("out", [L, D], F32, kind="ExternalOutput").ap()

    s_ada = dscr("s_ada", [DEPTH, 128, 8, 6 * D], BF16)
    s_win = dscr("s_win", [DEPTH, 128, 8, NCOL], BF16)
    s_wo = dscr("s_wo", [DEPTH, 128, 8, D], BF16)
    s_wfi = dscr("s_wfi", [DEPTH, 128, 8, 2 * DFF], BF16)
    s_wfo = dscr("s_wfo", [DEPTH, 128, 22, D], BF16)
    s_q = dscr("s_q", [20, 128, T], BF16)
    s_o = dscr("s_o", [16, 64, T], BF16)
    s_x1 = dscr("s_x1", [T, D], F32)
    s_g = dscr("s_g", [2, 2, 128, D], F32)

    with ExitStack() as top:
        top.enter_context(nc.allow_low_precision(reason="bf16 matmul operands by design; fp32 accumulation"))
        fw = FW(nc, top)
        B = fw.B

        uid = [0]

        def sb(stack, name, shape, dt):
            uid[0] += 1
            return stack.enter_context(nc.sbuf_tensor("%s_u%d" % (name, uid[0]), list(shape), dt))

        psF = []
        psB = []
        pcnt = {"f": 0, "b": 0, "a": 0, "s": 0, "x": 0}

        def alloc_psum(stack, nf, nb):
            del psF[:]
            del psB[:]
            for i in range(nf):
                uid[0] += 1
                psF.append(stack.enter_context(nc.psum_tensor("psf%d_u%d" % (i, uid[0]), [128, 512], F32)))
            for i in range(nb):
                uid[0] += 1
                psB.append(stack.enter_context(nc.psum_tensor("psb%d_u%d" % (i, uid[0]), [128, 1024], BF16)))

        def next_ps(pool=None):
            if pool is None:
                i = pcnt["f"] % len(psF)
                pcnt["f"] += 1
            elif pool == "a":
                i = pcnt["a"] % 4
                pcnt["a"] += 1
            elif pool == "s":
                i = 4 + pcnt["s"] % 2
                pcnt["s"] += 1
            else:
                i = 6 + pcnt["x"] % 2
                pcnt["x"] += 1
            return psF[i], B("psf%d" % i)

        def next_psb():
            i = pcnt["b"] % 2
            pcnt["b"] += 1
            return psB[i], B("psb%d" % i)

        def mm(o, lhsT, rhs, start, stop, rd, wr):
            fw.op("pe", lambda: nc.tensor.matmul(o, lhsT=lhsT, rhs=rhs, start=start, stop=stop), reads=rd, writes=wr, inc=stop)

        ident = sb(top, "ident", [128, 128], BF16)
        onesb = sb(top, "onesb", [128, 128], BF16)
        sel = sb(top, "sel", [128, 128], BF16)
        o64 = sb(top, "o64", [128, 128], BF16)
        fillr = sb(top, "fillr", [128, 512], BF16)
        epsc = sb(top, "epsc", [128, 1], F32)
        maskcol = sb(top, "maskcol", [128, 2], F32)
        modT = sb(top, "modT", [128, 32, 2], F32)
        sinkE = sb(top, "sinkE", [128, 8], F32)
        lamt = sb(top, "lamt", [128, 4], F32)
        lamw = sb(top, "lamw", [128, 4, 32], F32)
        gs = sb(top, "gs", [64, 2], F32)

        fw.op("pool", lambda: nc.gpsimd.memset(ident[:], 1.0), writes=[B("ident")])
        fw.op("pool", lambda: nc.gpsimd.affine_select(out=ident[:], in_=ident[:], pattern=[[-1, 128]], compare_op=ALU.is_equal,
                                                     fill=0.0, base=0, channel_multiplier=1), reads=[B("ident")], writes=[B("ident")])
        fw.op("pool", lambda: nc.gpsimd.memset(onesb[:], 1.0), writes=[B("onesb")])
        fw.op("pool", lambda: nc.gpsimd.memset(sel[:], 0.0), writes=[B("sel")])
        fw.op("pool", lambda: nc.gpsimd.memset(sel[64:65, :], 1.0), reads=[B("sel")], writes=[B("sel")])
        fw.op("pool", lambda: nc.gpsimd.memset(o64[:], 0.0), writes=[B("o64")])
        fw.op("pool", lambda: nc.gpsimd.memset(o64[0:64, 0:64], 1.0 / 64.0), reads=[B("o64")], writes=[B("o64")])
        fw.op("pool", lambda: nc.gpsimd.memset(fillr[:], 1.0), writes=[B("fillr")])
        fw.op("pool", lambda: nc.gpsimd.memset(epsc[:], EPS), writes=[B("epsc")])
        fw.op("pool", lambda: nc.gpsimd.memset(maskcol[:], 0.0), writes=[B("maskcol")])
        fw.op("pool", lambda: nc.gpsimd.memset(maskcol[0:64, 0:1], 1.0), reads=[B("maskcol")], writes=[B("maskcol")])
        fw.op("pool", lambda: nc.gpsimd.memset(maskcol[64:128, 1:2], 1.0), reads=[B("maskcol")], writes=[B("maskcol")])

        def cast_list(l):
            lst = []
            for kc in range(8):
                lst.append((s_ada[l, :, kc, :], w_ada[l, kc * 128:(kc + 1) * 128, :], "s_ada%d_%d" % (l, kc)))
            for kc in range(8):
                lst.append((s_win[l, :, kc, :], win[l, kc * 128:(kc + 1) * 128, :], "s_win%d_%d" % (l, kc)))
            for kc in range(8):
                lst.append((s_wo[l, :, kc, :], w_o[l, kc * 128:(kc + 1) * 128, :], "s_wo%d_%d" % (l, kc)))
            for kc in range(8):
                lst.append((s_wfi[l, :, kc, :], wfi[l, kc * 128:(kc + 1) * 128, :], "s_wfi%d_%d" % (l, kc)))
            for c in range(22):
                lst.append((s_wfo[l, :, c, :], wfo[l, c * 128:(c + 1) * 128, :], "s_wfo%d_%d" % (l, c)))
            return lst

        def cast_some(lst, n):
            for _ in range(n):
                if lst:
                    o_, i_, bn = lst.pop(0)
                    fw.dma("pool", o_, i_, writes=[B(bn)])

        def cast_weights(l):
            lst = cast_list(l)
            cast_some(lst, len(lst))

        cast_weights(0)

        def layer_norm(src, dst, lng, lnb, slot, rd, wr, ph):
            st_, mv, rs, nb = ph["lnst"][slot], ph["lnmv"][slot], ph["lnrs"][slot], ph["lnnb"][slot]
            bs = B("lnsm%d" % slot)
            fw.op("dve", lambda: nc.vector.bn_stats(out=st_[:, 0:6], in_=src[:, 0:512]), reads=rd, writes=[bs])
            fw.op("dve", lambda: nc.vector.bn_stats(out=st_[:, 6:12], in_=src[:, 512:1024]), reads=rd + [bs], writes=[bs])
            fw.op("dve", lambda: nc.vector.bn_aggr(out=mv[:, 0:2], in_=st_[:, 0:12]), reads=[bs], writes=[bs])
            fw.op("act", lambda: nc.scalar.activation(out=rs[:, 0:1], in_=mv[:, 1:2], func=AF.Sqrt, bias=EPS, scale=1.0), reads=[bs], writes=[bs])
            fw.op("dve", lambda: nc.vector.reciprocal(out=rs[:, 0:1], in_=rs[:, 0:1]), reads=[bs], writes=[bs])
            fw.op("dve", lambda: nc.vector.scalar_tensor_tensor(out=nb[:, 0:1], in0=mv[:, 0:1], scalar=-1.0, in1=rs[:, 0:1],
                                                               op0=ALU.mult, op1=ALU.mult), reads=[bs], writes=[bs])
            fw.op("act", lambda: nc.scalar.activation(out=src, in_=src, func=AF.Identity, bias=nb[:, 0:1], scale=rs[:, 0:1]),
                  reads=rd + [bs], writes=rd)
            fw.op("pool", lambda: nc.gpsimd.tensor_tensor(out=dst, in0=src, in1=lng, op=ALU.mult), reads=rd + [B("lnp")], writes=wr)
            fw.op("pool", lambda: nc.gpsimd.tensor_tensor(out=dst, in0=dst, in1=lnb, op=ALU.add), reads=wr + [B("lnp")], writes=wr)

        def layer_norm_blk(nt, src_fn, dst_fn, lng, lnb, rd_fn, wr_fn, ph):
            st_, mv, rs, nb = ph["bst"], ph["bmv"], ph["brs"], ph["bnb"]
            bs = B("lnblk")
            for i in range(nt):
                src = src_fn(i)
                fw.op("dve", lambda i=i, src=src: nc.vector.bn_stats(out=st_[:, i, 0:6], in_=src[:, 0:512]), reads=rd_fn(i) + [bs], writes=[bs])
                fw.op("dve", lambda i=i, src=src: nc.vector.bn_stats(out=st_[:, i, 6:12], in_=src[:, 512:1024]), reads=rd_fn(i) + [bs], writes=[bs])
                fw.op("dve", lambda i=i: nc.vector.bn_aggr(out=mv[:, i, 0:2], in_=st_[:, i, 0:12]), reads=[bs], writes=[bs])
            fw.op("act", lambda: nc.scalar.activation(out=rs[:, 0:nt], in_=mv[:, 0:nt, 1], func=AF.Sqrt, bias=EPS, scale=1.0), reads=[bs], writes=[bs])
            fw.op("dve", lambda: nc.vector.reciprocal(out=rs[:, 0:nt], in_=rs[:, 0:nt]), reads=[bs], writes=[bs])
            fw.op("dve", lambda: nc.vector.scalar_tensor_tensor(out=nb[:, 0:nt], in0=mv[:, 0:nt, 0], scalar=-1.0, in1=rs[:, 0:nt],
                                                               op0=ALU.mult, op1=ALU.mult), reads=[bs], writes=[bs])
            for i in range(nt):
                src = src_fn(i)
                dst_ = dst_fn(i)
                fw.op("act", lambda i=i, src=src: nc.scalar.activation(out=src, in_=src, func=AF.Identity, bias=nb[:, i:i + 1], scale=rs[:, i:i + 1]),
                      reads=rd_fn(i) + [bs], writes=rd_fn(i))
                fw.op("pool", lambda src=src, dst_=dst_: nc.gpsimd.tensor_tensor(out=dst_, in0=src, in1=lng, op=ALU.mult), reads=rd_fn(i) + [B("lnp")], writes=wr_fn(i))
                fw.op("pool", lambda dst_=dst_: nc.gpsimd.tensor_tensor(out=dst_, in0=dst_, in1=lnb, op=ALU.add), reads=wr_fn(i) + [B("lnp")], writes=wr_fn(i))

        def transpose_mod(xb, nt, hT, mi, j, tag):
            N = nt * 128
            for c in range(8):
                pb, pbb = next_psb()
                for i in range(nt):
                    fw.op("pe", lambda i=i, c=c, pb=pb: nc.tensor.transpose(out=pb[:, i * 128:(i + 1) * 128],
                                                                          in_=xb[:, i, c * 128:(c + 1) * 128], identity=ident[:]),
                          reads=[B("xb%d" % i), B("ident")], writes=[pbb], inc=(i == nt - 1))
                fw.op("act", lambda c=c, pb=pb: nc.scalar.activation(out=hT[:, c, 0:N], in_=pb[:, 0:N], func=AF.Identity,
                                                                    bias=modT[:, mi * 8 + c, j:j + 1],
                                                                    scale=modT[:, (mi + 1) * 8 + c, j:j + 1]),
                      reads=[pbb, B("modT")], writes=[B("%s%d" % (tag, c))])

        cur_in = xin
        for l in range(nlayers):
            last = (l == nlayers - 1) and (nlayers == DEPTH)
            lam_init = 0.8 - 0.6 * math.exp(-0.3 * l)
            dst = out if last else s_x1
            nblk = 4 if last else 5

            with ExitStack() as ph:
                alloc_psum(ph, 6, 2)
                wada = sb(ph, "wada", [128, 8, 6 * D], BF16)
                cTt = sb(ph, "cTt", [128, 16], F32)
                scf = sb(ph, "scf", [128, 16], F32)
                scb = sb(ph, "scb", [128, 8, 2], BF16)
                rep = [sb(ph, "rep%d" % j, [128, 8, 128], BF16) for j in range(2)]
                bT = sb(ph, "bT", [128, 48], F32)
                bg = sb(ph, "bg", [128, 2, D], F32)
                Gt = [sb(ph, "Gt%d" % i, [128, D], F32) for i in range(2)]
                for kc in range(8):
                    fw.dma("sp", wada[:, kc, :], s_ada[l, :, kc, :], reads=[B("s_ada%d_%d" % (l, kc))], writes=[B("wada%d" % kc)])
                fw.dma("sp", cTt[:], cT, writes=[B("cTt")])
                fw.dma("sp", bT[:], b_adaT[l], writes=[B("bT")])
                for gi, m in enumerate((2, 5)):
                    fw.dma("sp", bg[:, gi, :], b_ada[l, m * D:(m + 1) * D].partition_broadcast(128), writes=[B("bg")])
                fw.dma("sp", sinkE[:, 0:6], sink[l].partition_broadcast(128), writes=[B("sinkE")])
                fw.dma("sp", lamw[:, :, :], lamv[l].partition_broadcast(128), writes=[B("lamw")])
                fw.dma("sp", gs[:, 0:1], subg[l].rearrange("(d o) -> d o", o=1), writes=[B("gs")])
                fw.op("act", lambda: nc.scalar.activation(out=sinkE[:, 0:6], in_=sinkE[:, 0:6], func=AF.Exp),
                      reads=[B("sinkE")], writes=[B("sinkE")])
                fw.op("dve", lambda: nc.vector.tensor_tensor(out=lamw[:, 0, :], in0=lamw[:, 0, :], in1=lamw[:, 1, :], op=ALU.mult),
                      reads=[B("lamw")], writes=[B("lamw")])
                fw.op("dve", lambda: nc.vector.tensor_tensor(out=lamw[:, 2, :], in0=lamw[:, 2, :], in1=lamw[:, 3, :], op=ALU.mult),
                      reads=[B("lamw")], writes=[B("lamw")])
                fw.op("dve", lambda: nc.vector.reduce_sum(out=lamt[:, 0:1], in_=lamw[:, 0, :], axis=mybir.AxisListType.X),
                      reads=[B("lamw")], writes=[B("lamt")])
                fw.op("dve", lambda: nc.vector.reduce_sum(out=lamt[:, 1:2], in_=lamw[:, 2, :], axis=mybir.AxisListType.X),
                      reads=[B("lamw"), B("lamt")], writes=[B("lamt")])
                fw.op("act", lambda: nc.scalar.activation(out=lamt[:, 0:2], in_=lamt[:, 0:2], func=AF.Exp),
                      reads=[B("lamt")], writes=[B("lamt")])
                fw.op("dve", lambda: nc.vector.tensor_tensor(out=lamt[:, 2:3], in0=lamt[:, 0:1], in1=lamt[:, 1:2], op=ALU.subtract),
                      reads=[B("lamt")], writes=[B("lamt")])
                fw.op("dve", lambda: nc.vector.tensor_scalar(out=lamt[:, 3:4], in0=lamt[:, 2:3], scalar1=float(lam_init), scalar2=None,
                                                            op0=ALU.add), reads=[B("lamt")], writes=[B("lamt")])
                fw.op("dve", lambda: nc.vector.tensor_scalar(out=gs[:, 1:2], in0=gs[:, 0:1], scalar1=float(1.0 - lam_init), scalar2=None,
                                                            op0=ALU.mult), reads=[B("gs")], writes=[B("gs")])
                fw.op("act", lambda: nc.scalar.activation(out=scf[:], in_=cTt[:], func=AF.Silu), reads=[B("cTt")], writes=[B("scf")])
                fw.op("dve", lambda: nc.vector.tensor_copy(out=scb[:].rearrange("p k j -> p (k j)"), in_=scf[:]), reads=[B("scf")], writes=[B("scb")])
                for j in range(2):
                    for kc in range(8):
                        fw.op("dve", lambda j=j, kc=kc: nc.vector.tensor_scalar(out=rep[j][:, kc, :], in0=onesb[:, :],
                                                                               scalar1=scf[:, kc * 2 + j:kc * 2 + j + 1], scalar2=None, op0=ALU.mult),
                              reads=[B("scf"), B("onesb")], writes=[B("rep%d" % j)])
                ps, psb_ = next_ps()
                for mi, m in enumerate((0, 1, 3, 4)):
                    for c in range(8):
                        col0 = m * D + c * 128
                        o = (mi * 8 + c) * 2
                        for kc in range(8):
                            mm(ps[:, o:o + 2], wada[:, kc, col0:col0 + 128], scb[:, kc, 0:2], kc == 0, kc == 7,
                               [B("wada%d" % kc), B("scb")], [psb_])
                psv = ps[:, 0:64].rearrange("p (a j) -> p a j", j=2)
                for j in range(2):
                    for (a0, b0) in ((0, 0), (16, 24)):
                        fw.op("dve", lambda j=j, a0=a0, b0=b0: nc.vector.tensor_tensor(out=modT[:, a0:a0 + 16, j], in0=psv[:, a0:a0 + 16, j],
                                                                                       in1=bT[:, b0:b0 + 16], op=ALU.add),
                              reads=[psb_, B("bT")], writes=[B("modT")])
                for a0 in (8, 24):
                    fw.op("dve", lambda a0=a0: nc.vector.tensor_scalar(out=modT[:, a0:a0 + 8, :], in0=modT[:, a0:a0 + 8, :], scalar1=1.0,
                                                                      scalar2=None, op0=ALU.add), reads=[B("modT")], writes=[B("modT")])
                for j in range(2 if not last else 1):
                    for gi, m in enumerate((2, 5)):
                        for half in range(2):
                            ps, psb_ = next_ps()
                            for kc in range(8):
                                mm(ps[:, :], rep[j][:, kc, :], wada[:, kc, m * D + half * 512:m * D + (half + 1) * 512], kc == 0, kc == 7,
                                   [B("rep%d" % j), B("wada%d" % kc)], [psb_])
                            fw.op("dve", lambda ps=ps, gi=gi, half=half: nc.vector.tensor_tensor(
                                out=Gt[gi][:, half * 512:(half + 1) * 512], in0=ps[:, :], in1=bg[:, gi, half * 512:(half + 1) * 512], op=ALU.add),
                                reads=[psb_, B("bg")], writes=[B("Gt%d" % gi)])
                        fw.dma("pool", s_g[j, gi], Gt[gi][:], reads=[B("Gt%d" % gi)], writes=[B("s_g%d%d" % (j, gi))])
                fw.barrier()

            kvs = ExitStack()
            KT = sb(kvs, "KT", [128, 6, T], BF16)
            VW = 12 * 66 + 62
            Vx = sb(kvs, "Vx", [128, NT, VW], BF16)
            fw.op("pool", lambda: nc.gpsimd.memset(Vx[:], 0.0), writes=[B("Vx")])
            fw.op("pool", lambda: nc.gpsimd.memset(Vx[:, :, 0:792].rearrange("p t (h c) -> p t h c", c=66)[:, :, :, 64:66], 1.0),
                  reads=[B("Vx")], writes=[B("Vx")])
            with ExitStack() as ph:
                alloc_psum(ph, 6, 2)
                Win = sb(ph, "Win", [128, 8, NCOL], BF16)
                xsb = [sb(ph, "xsb%d" % s, [128, 4, D], F32) for s in range(2)]
                xb = sb(ph, "xb", [128, 4, D], BF16)
                hT = sb(ph, "hT", [128, 8, 512], BF16)
                rt = sb(ph, "rt", [128, 16, 512], F32)
                t1 = [sb(ph, "t1_%d" % s, [128, 512], F32) for s in range(2)]
                t2 = [sb(ph, "t2_%d" % s, [128, 512], F32) for s in range(2)]
                qst = [sb(ph, "qst%d" % s, [128, 512], BF16) for s in range(2)]
                for kc in range(8):
                    fw.dma("sp", Win[:, kc, :], s_win[l, :, kc, :], reads=[B("s_win%d_%d" % (l, kc))], writes=[B("Win%d" % kc)])
                WinB = [B("Win%d" % kc) for kc in range(8)]
                tcnt = [0]

                def load_x(bi):
                    nt = 4 if bi < 4 else 2
                    s = bi % 2
                    fw.dma("sp", xsb[s][:, 0:nt, :], cur_in[bi * 512:bi * 512 + nt * 128, :].rearrange("(i p) d -> p i d", p=128),
                           reads=[B("xs_%d_%d" % (l, bi))], writes=[B("xsb%d" % s)])

                load_x(0)
                for bi in range(5):
                    nt = 4 if bi < 4 else 2
                    N = nt * 128
                    tok0 = bi * 512
                    j = 0 if bi < 4 else 1
                    s = bi % 2
                    if bi + 1 < 5:
                        load_x(bi + 1)
                    fw.dma("sp", rt[:, :, 0:N], rope[:, :, tok0:tok0 + N].rearrange("k p t -> p k t"), writes=[B("rt%d" % k) for k in range(16)])
                    for i in range(nt):
                        fw.op("act" if i % 2 else "dve",
                              (lambda i=i: nc.scalar.copy(out=xb[:, i, :], in_=xsb[s][:, i, :])) if i % 2 else
                              (lambda i=i: nc.vector.tensor_copy(out=xb[:, i, :], in_=xsb[s][:, i, :])),
                              reads=[B("xsb%d" % s)], writes=[B("xb%d" % i)])
                    transpose_mod(xb, nt, hT, 0, j, "hT")
                    hTB = [B("hT%d" % c) for c in range(8)]

                    def proj(ch):
                        ps, pb_ = next_ps()
                        for kc in range(8):
                            mm(ps[:, 0:N], Win[:, kc, ch * 128:(ch + 1) * 128], hT[:, kc, 0:N], kc == 0, kc == 7, [WinB[kc], hTB[kc]], [pb_])
                        return ps, pb_

                    def roped(ch, chp, tc, ts, dst_ap, dst_b):
                        pa, pab = proj(ch)
                        pp, ppb = proj(chp)
                        rope_comb(pa, pab, pp, ppb, tc, ts, dst_ap, dst_b)

                    def rope_comb(pa, pab, pp, ppb, tc, ts, dst_ap, dst_b):
                        k = tcnt[0] % 2
                        tcnt[0] += 1
                        fw.op("dve", lambda: nc.vector.tensor_tensor(out=t1[k][:, 0:N], in0=pa[:, 0:N], in1=rt[:, tc, 0:N], op=ALU.mult),
                              reads=[pab, B("rt%d" % tc)], writes=[B("t1_%d" % k)])
                        fw.op("dve", lambda: nc.vector.tensor_tensor(out=t2[k][:, 0:N], in0=pp[:, 0:N], in1=rt[:, ts, 0:N], op=ALU.mult),
                              reads=[ppb, B("rt%d" % ts)], writes=[B("t2_%d" % k)])
                        fw.op("pool", lambda: nc.gpsimd.tensor_tensor(out=dst_ap, in0=t1[k][:, 0:N], in1=t2[k][:, 0:N], op=ALU.add),
                              reads=[B("t1_%d" % k), B("t2_%d" % k)], writes=dst_b)

                    def q_out(qi, fn):
                        k = tcnt[0] % 2
                        fn(qst[k][:, 0:N], [B("qst%d" % k)])
                        fw.dma("sp", s_q[qi, :, tok0:tok0 + N], qst[k][:, 0:N], reads=[B("qst%d" % k)], writes=[B("s_q%d" % bi)])

                    need_q = (bi < 4) or (not last)
                    if need_q:
                        for jq in range(3):
                            pa, pab = proj(jq)
                            pp, ppb = proj(4 + jq)
                            for g in range(2):
                                q_out(g * 3 + jq, lambda ap, bb, g=g: rope_comb(pa, pab, pp, ppb, 4 + 2 * g, 5 + 2 * g, ap, bb))
                    roped(3, 7, 0, 1, KT[:, 0, tok0:tok0 + N], [B("KT0_%d" % bi)])
                    if need_q:
                        for c in range(2):
                            pa, pab = proj(8 + c)
                            pp, ppb = proj(12 + c)
                            for v in range(4):
                                q_out(6 + c * 4 + v, lambda ap, bb, v=v: rope_comb(pa, pab, pp, ppb, 8 + 2 * v, 9 + 2 * v, ap, bb))
                    for c in range(2):
                        roped(10 + c, 14 + c, 2, 3, KT[:, 1 + c, tok0:tok0 + N], [B("KT%d_%d" % (1 + c, bi))])
                    if need_q:
                        for c in range(3):
                            ps, pb_ = proj(16 + c)
                            for hb in range(2):
                                def cq(ap, bb, ps=ps, pb_=pb_, hb=hb):
                                    fw.op("act", lambda: nc.scalar.activation(out=ap, in_=ps[:, 0:N], func=AF.Copy, scale=maskcol[:, hb:hb + 1]),
                                          reads=[pb_, B("maskcol")], writes=bb)
                                q_out(14 + c * 2 + hb, cq)
                                tcnt[0] += 1
                    for c in range(3):
                        ps, pb_ = proj(19 + c)
                        fw.op("act", lambda ps=ps, c=c: nc.scalar.copy(out=KT[:, 3 + c, tok0:tok0 + N], in_=ps[:, 0:N]),
                              reads=[pb_], writes=[B("KT%d_%d" % (3 + c, bi))])
                    for i in range(nt):
                        gt = bi * 4 + i
                        ps, pb_ = next_ps()
                        for kc in range(8):
                            mm(ps[:, :], hT[:, kc, i * 128:(i + 1) * 128], Win[:, kc, NCH * 128:NCH * 128 + 512], kc == 0, kc == 7,
                               [WinB[kc], hTB[kc]], [pb_])
                        fw.op("act", lambda ps=ps, gt=gt: nc.scalar.copy(out=Vx[:, gt, 0:528].rearrange("p (h c) -> p h c", c=66)[:, :, 0:64], in_=ps[:, :].rearrange("p (h d) -> p h d", d=64)),
                              reads=[pb_], writes=[B("Vx")])
                        ps, pb_ = next_ps()
                        for kc in range(8):
                            mm(ps[:, 0:256], hT[:, kc, i * 128:(i + 1) * 128], Win[:, kc, NCH * 128 + 512:NCH * 128 + 768], kc == 0, kc == 7,
                               [WinB[kc], hTB[kc]], [pb_])
                        fw.op("dve", lambda ps=ps, gt=gt: nc.vector.tensor_copy(out=Vx[:, gt, 528:792].rearrange("p (h c) -> p h c", c=66)[:, :, 0:64],
                                                                                in_=ps[:, 0:256].rearrange("p (h d) -> p h d", d=64)),
                              reads=[pb_], writes=[B("Vx")])
                fw.barrier()

            with ExitStack() as ph:
                alloc_psum(ph, 8, 0)
                Et = sb(ph, "Et", [128, 6, NV * 256], BF16)
                nmk = sb(ph, "nmk", [128, NV * 256], BF16)
                MA = sb(ph, "MA", [128, 768], BF16)
                QTb = [sb(ph, "QTb%d" % s, [128, 20, 512], BF16) for s in range(2)]
                OTb = sb(ph, "OTb", [64, 16, 512], BF16)
                NPT = 6
                PT = [sb(ph, "PT%d" % s, [128, 512], BF16) for s in range(NPT)]
                bcs = [sb(ph, "bcs%d" % s, [64, 512], F32) for s in range(4)]
                tA = [sb(ph, "tA%d" % s, [64, 512], F32) for s in range(4)]
                sq = sb(ph, "sq", [128, 512], BF16)
                cnt = {"pt": 0, "ss": 0, "bc": 0}
                next_casts = cast_list(l + 1) if l + 1 < nlayers else []
                with ExitStack() as sub:
                    HV = NV * 128
                    gstg = [sb(sub, "gstg%d" % i_, [128, HV], F32) for i_ in range(2)]
                    fw.dma("pool", nmk[:], nmask, writes=[B("nmk")])
                    fw.dma("pool", MA[:], amask, writes=[B("MA")])
                    for hh in range(12):
                        h, part = hh // 2, hh % 2
                        gi_ = hh % 2
                        cs = slice(part * HV, (part + 1) * HV)
                        fw.dma("sp", gstg[gi_][:], nag[l, h, :, cs], writes=[B("gstg%d" % gi_)])
                        fw.op("act", lambda h=h, cs=cs, gi_=gi_: nc.scalar.activation(out=Et[:, h, cs], in_=gstg[gi_][:], func=AF.Exp),
                              reads=[B("gstg%d" % gi_)], writes=[B("Et%d_%d" % (h, part))])
                        fw.op("dve", lambda h=h, cs=cs: nc.vector.tensor_tensor(out=Et[:, h, cs], in0=Et[:, h, cs], in1=nmk[:, cs], op=ALU.mult),
                              reads=[B("Et%d_%d" % (h, part)), B("nmk")], writes=[B("Et%d_%d" % (h, part))])
                    fw.barrier()

                rb = [sb(ph, "rb%d" % s_, [128, 512], BF16) for s_ in range(4)]
                for s_ in range(4):
                    fw.op("pool", lambda s_=s_: nc.gpsimd.memset(rb[s_][:], 0.0), writes=[B("rb%d" % s_)])
                cnt["rb"] = 0
                fw.op("pool", lambda: nc.gpsimd.memset(sq[:], 0.0), writes=[B("sq")])

                def v_lhsT(t, hh):
                    return Vx[:, t, hh * 66:hh * 66 + 128]

                def bcast_rows(rows, N_):
                    res = []
                    for wr in rows:
                        k = cnt["rb"] % 4
                        cnt["rb"] += 1
                        wr(rb[k], B("rb%d" % k))
                        pbc, pbcb = next_ps("x")
                        mm(pbc[:, 0:N_], sel[:, :], rb[k][:, 0:N_], True, True, [B("sel"), B("rb%d" % k)], [pbcb])
                        res.append((pbc, pbcb))
                    return res

                def recips(pbcs, N_):
                    res = []
                    for pbc, pbcb in pbcs:
                        c0, c0b = new_bc()
                        fw.op("dve", lambda pbc=pbc, c0=c0: nc.vector.reciprocal(out=c0[0:64, 0:N_], in_=pbc[0:64, 0:N_]), reads=[pbcb], writes=[c0b])
                        res.append((c0, c0b))
                    return res

                def load_q(bi):
                    nt = 4 if bi < 4 else 2
                    s = bi % 2
                    fw.dma("sp", QTb[s][:, :, 0:nt * 128], s_q[:, :, bi * 512:bi * 512 + nt * 128].rearrange("c p t -> p c t"),
                           reads=[B("s_q%d" % bi)], writes=[B("QTb%d" % s)])

                def new_pt():
                    k = cnt["pt"] % NPT
                    cnt["pt"] += 1
                    return PT[k], B("PT%d" % k)

                def new_bc():
                    k = cnt["bc"] % 4
                    cnt["bc"] += 1
                    return bcs[k], B("bcs%d" % k)

                live_acc = set()
                flush_hook = [lambda: None]

                def new_acc():
                    i = pcnt["a"] % 4
                    if i in live_acc:
                        flush_hook[0]()
                    assert i not in live_acc, "accumulator bank still live"
                    live_acc.add(i)
                    pcnt["a"] += 1
                    return psF[i], B("psf%d" % i), i

                def gen_jobs(Q, Qb, bi, ctxq, nt, N):
                    units = []
                    for n in range(nt):
                        gq = bi * 4 + n
                        for g in range(2):
                            if ctxq:
                                tiles = [(16, None), (17, None)]
                            else:
                                tiles = []
                                if gq - 1 >= 0:
                                    tiles.append((gq - 1, 0))
                                tiles.append((gq, None))
                                if gq + 1 < 16:
                                    tiles.append((gq + 1, 1))
                                tiles += [(16, None), (17, None)]
                            u = {"jobs": []}
                            for ti, (t, mk) in enumerate(tiles):
                                def S(u=u, ti=ti, t=t, g=g, n=n):
                                    if ti == 0:
                                        u["po"] = new_acc()
                                    ps, psb_ = next_ps("s")
                                    mm(ps[:, 0:384].rearrange("p (a b) -> p a b", a=3), KT[:, 0, t * 128:(t + 1) * 128],
                                       Q[:, g * 3:g * 3 + 3, n * 128:(n + 1) * 128], True, True, [B("KT0_%d" % (t // 4)), Qb], [psb_])
                                    return ps, psb_

                                def rest(st, u=u, ti=ti, t=t, mk=mk, g=g, last=(ti == len(tiles) - 1)):
                                    ps, psb_ = st
                                    po, pob, _ = u["po"]
                                    pt, ptb = new_pt()
                                    fw.op("act", lambda: nc.scalar.activation(out=pt[:, 0:384], in_=ps[:, 0:384], func=AF.Exp, scale=0.125),
                                          reads=[psb_], writes=[ptb])
                                    if mk is not None:
                                        fw.op("pool", lambda: nc.gpsimd.tensor_tensor(out=pt[:, 0:384], in0=pt[:, 0:384],
                                                                                    in1=MA[:, mk * 384:(mk + 1) * 384], op=ALU.mult),
                                              reads=[ptb, B("MA")], writes=[ptb])
                                    mm(po[:, 0:384], v_lhsT(t, g), pt[:, 0:384], ti == 0, last, [B("Vx"), ptb], [pob])
                                u["jobs"].append((S, rest))

                            def fin(u=u, g=g, n=n):
                                po, pob, bank = u["po"]
                                stt = {}

                                def wr(rbt, rbb):
                                    for sg in range(3):
                                        h = 3 * g + sg
                                        fw.op("dve", lambda sg=sg, h=h: nc.vector.tensor_scalar(
                                            out=rbt[64:65, sg * 128:(sg + 1) * 128], in0=po[64:65, sg * 128:(sg + 1) * 128],
                                            scalar1=sinkE[64:65, h:h + 1], scalar2=None, op0=ALU.add), reads=[pob, B("sinkE")], writes=[rbb])

                                def s1():
                                    stt["p"] = bcast_rows([wr], 384)

                                def s2():
                                    stt["c"] = recips(stt["p"], 384)

                                def s3():
                                    ((c0, c0b),) = stt["c"]
                                    fw.op("dve", lambda: nc.vector.tensor_tensor(
                                        out=OTb[0:64, 3 * g:3 * g + 3, n * 128:(n + 1) * 128], in0=po[0:64, 0:384].rearrange("p (a b) -> p a b", a=3),
                                        in1=c0[0:64, 0:384].rearrange("p (a b) -> p a b", a=3), op=ALU.mult),
                                        reads=[pob, c0b], writes=[B("OTb")])
                                    live_acc.discard(bank)
                                return [(2, s1, False), (3, s2, False), (6, s3, False)]
                            u["fin"] = fin
                            units.append(u)
                    tilesB = [16, 17] if ctxq else list(range(18))
                    for h in range(4):
                        r0 = (h % 2) * 64
                        u = {"jobs": []}
                        for ti, t in enumerate(tilesB):
                            for m in range(2):
                                def S(u=u, ti=ti, t=t, m=m, h=h, r0=r0):
                                    if ti == 0 and m == 0:
                                        u["po"] = [new_acc(), new_acc()]
                                    ps, psb_ = next_ps("s")
                                    mm(ps[:, 0:N], KT[:, 1 + h // 2, t * 128:(t + 1) * 128], Q[:, 6 + (h // 2) * 4 + (h % 2) * 2 + m, 0:N],
                                       True, True, [B("KT%d_%d" % (1 + h // 2, t // 4)), Qb], [psb_])
                                    return ps, psb_

                                def rest(st, u=u, ti=ti, t=t, m=m, h=h, last=(ti == len(tilesB) - 1)):
                                    ps, psb_ = st
                                    po, pob, _ = u["po"][m]
                                    pt, ptb = new_pt()
                                    fw.op("act", lambda: nc.scalar.activation(out=pt[:, 0:N], in_=ps[:, 0:N], func=AF.Exp, scale=float(32 ** -0.5)),
                                          reads=[psb_], writes=[ptb])
                                    mm(po[:, 0:N], v_lhsT(t, 2 + h), pt[:, 0:N], ti == 0, last, [B("Vx"), ptb], [pob])
                                u["jobs"].append((S, rest))

                        def fin(u=u, h=h):
                            (p0, p0b, b0_), (p1, p1b, b1_) = u["po"]
                            stt = {}

                            def wr0(rbt, rbb):
                                fw.op("dve", lambda: nc.vector.tensor_copy(out=rbt[64:65, 0:N], in_=p0[64:65, 0:N]), reads=[p0b], writes=[rbb])

                            def wr1(rbt, rbb):
                                fw.op("dve", lambda: nc.vector.tensor_copy(out=rbt[64:65, 0:N], in_=p1[64:65, 0:N]), reads=[p1b], writes=[rbb])

                            def s1():
                                stt["p"] = bcast_rows([wr0, wr1], N)

                            def s2():
                                stt["c"] = recips(stt["p"], N)

                            def s3():
                                (c0, c0b), (c1, c1b) = stt["c"]
                                fw.op("dve", lambda: nc.vector.tensor_tensor(out=tA[0][:, 0:N], in0=p0[0:64, 0:N], in1=c0[0:64, 0:N], op=ALU.mult),
                                      reads=[p0b, c0b], writes=[B("tA0")])
                                fw.op("dve", lambda: nc.vector.scalar_tensor_tensor(out=tA[1][:, 0:N], in0=p1[0:64, 0:N], scalar=lamt[0:64, 3:4],
                                                                                   in1=c1[0:64, 0:N], op0=ALU.mult, op1=ALU.mult),
                                      reads=[p1b, c1b, B("lamt")], writes=[B("tA1")])
                                live_acc.discard(b0_)
                                live_acc.discard(b1_)
                                fw.op("pool", lambda: nc.gpsimd.tensor_tensor(out=tA[2][:, 0:N], in0=tA[0][:, 0:N], in1=tA[1][:, 0:N], op=ALU.subtract),
                                      reads=[B("tA0"), B("tA1")], writes=[B("tA2")])
                                fw.op("pool", lambda: nc.gpsimd.tensor_tensor(out=sq[0:64, 0:N], in0=tA[2][:, 0:N], in1=tA[2][:, 0:N], op=ALU.mult),
                                      reads=[B("tA2")], writes=[B("sq")])

                            def s4():
                                pms, pmsb = next_ps("x")
                                stt["pms"] = (pms, pmsb)
                                mm(pms[:, 0:N], o64[:, :], sq[:, 0:N], True, True, [B("o64"), B("sq")], [pmsb])

                            def s5():
                                pms, pmsb = stt["pms"]
                                fw.op("act", lambda: nc.scalar.activation(out=tA[3][:, 0:N], in_=pms[0:64, 0:N], func=AF.Ln, bias=epsc[0:64, 0:1], scale=1.0),
                                      reads=[pmsb, B("epsc")], writes=[B("tA3")])
                                fw.op("act", lambda: nc.scalar.activation(out=tA[3][:, 0:N], in_=tA[3][:, 0:N], func=AF.Exp, scale=-0.5),
                                      reads=[B("tA3")], writes=[B("tA3")])

                            def s6():
                                fw.op("dve", lambda: nc.vector.scalar_tensor_tensor(out=OTb[0:64, 6 + h, 0:N], in0=tA[2][:, 0:N], scalar=gs[:, 1:2],
                                                                                   in1=tA[3][:, 0:N], op0=ALU.mult, op1=ALU.mult),
                                      reads=[B("tA2"), B("tA3"), B("gs")], writes=[B("OTb")])
                            return [(2, s1, False), (3, s2, False), (10, s3, False), (14, s4, False), (18, s5, False), (22, s6, False)]
                        u["fin"] = fin
                        units.append(u)
                    for h in range(6):
                        r0 = (h % 2) * 64
                        for jj in range(1 if ctxq else 2):
                            if ctxq:
                                tiles = [(16, None), (17, None)]
                            else:
                                jg = bi * 2 + jj
                                if jg == 0:
                                    tiles = [(uu, 6 + uu) for uu in range(4)]
                                elif jg == 7:
                                    tiles = [(uu, 10 + uu - 12) for uu in range(12, 16)]
                                else:
                                    tiles = [(2 * jg - 2 + i, i) for i in range(6)]
                                tiles += [(16, None), (17, None)]
                            u = {"jobs": []}
                            for ti, (t, ev) in enumerate(tiles):
                                def S(u=u, ti=ti, t=t, h=h, r0=r0, jj=jj):
                                    if ti == 0:
                                        u["po"] = new_acc()
                                    ps, psb_ = next_ps("s")
                                    mm(ps[:, 0:256], KT[:, 3 + h // 2, t * 128:(t + 1) * 128], Q[:, 14 + (h // 2) * 2 + (h % 2), jj * 256:(jj + 1) * 256],
                                       True, True, [B("KT%d_%d" % (3 + h // 2, t // 4)), Qb], [psb_])
                                    return ps, psb_

                                def rest(st, u=u, ti=ti, t=t, ev=ev, h=h, last=(ti == len(tiles) - 1)):
                                    ps, psb_ = st
                                    po, pob, _ = u["po"]
                                    pt, ptb = new_pt()
                                    fw.op("act", lambda: nc.scalar.activation(out=pt[:, 0:256], in_=ps[:, 0:256], func=AF.Exp, scale=0.125),
                                          reads=[psb_], writes=[ptb])
                                    if ev is not None:
                                        fw.op("dve", lambda: nc.vector.tensor_tensor(out=pt[:, 0:256], in0=pt[:, 0:256],
                                                                                    in1=Et[:, h, ev * 256:(ev + 1) * 256], op=ALU.mult),
                                              reads=[ptb, B("Et%d" % h)], writes=[ptb])
                                    mm(po[:, 0:256], v_lhsT(t, 6 + h), pt[:, 0:256], ti == 0, last, [B("Vx"), ptb], [pob])
                                u["jobs"].append((S, rest, ev is not None))

                            def fin(u=u, h=h, jj=jj):
                                po, pob, bank = u["po"]
                                stt = {}

                                def wr(rbt, rbb):
                                    fw.op("dve", lambda: nc.vector.tensor_copy(out=rbt[64:65, 0:256], in_=po[64:65, 0:256]), reads=[pob], writes=[rbb])

                                def s1():
                                    stt["p"] = bcast_rows([wr], 256)

                                def s2():
                                    stt["c"] = recips(stt["p"], 256)

                                def s3():
                                    ((c0, c0b),) = stt["c"]
                                    fw.op("dve", lambda: nc.vector.tensor_tensor(out=OTb[0:64, 10 + h, jj * 256:(jj + 1) * 256], in0=po[0:64, 0:256],
                                                                                in1=c0[0:64, 0:256], op=ALU.mult), reads=[pob, c0b], writes=[B("OTb")])
                                    live_acc.discard(bank)
                                return [(2, s1, False), (4, s2, True), (6, s3, False)]
                            u["fin"] = fin
                            units.append(u)
                    return units

                nqb = 4 if last else 5
                load_q(0)
                for bi in range(nqb):
                    ctxq = bi == 4
                    nt = 4 if bi < 4 else 2
                    N = nt * 128
                    tok0 = bi * 512
                    s = bi % 2
                    if bi + 1 < nqb:
                        load_q(bi + 1)
                    units = gen_jobs(QTb[s], B("QTb%d" % s), bi, ctxq, nt, N)
                    jobs = []
                    for u in units:
                        for ji, jb in enumerate(u["jobs"]):
                            jobs.append((jb[0], jb[1], u["fin"] if ji == len(u["jobs"]) - 1 else None, (jb[2] if len(jb) > 2 else False)))
                    pending = []

                    def flush():
                        while pending:
                            pending.pop(0)[1]()
                    flush_hook[0] = flush
                    st_next = jobs[0][0]()
                    for k, (S, rest, fin, _) in enumerate(jobs):
                        st_cur = st_next
                        if k + 1 < len(jobs):
                            st_next = jobs[k + 1][0]()
                        rest(st_cur)
                        if fin is not None:
                            cast_some(next_casts, 1)
                            base = 0
                            for (dl, stg, heavy) in fin():
                                pending.append([k + dl, stg, heavy])
                            pending.sort(key=lambda x: x[0])
                        nxt_mask = (k + 1 < len(jobs)) and jobs[k + 1][3]
                        i = 0
                        blocked = False
                        while i < len(pending):
                            due, stg, heavy = pending[i]
                            if due > k:
                                break
                            if heavy and nxt_mask and k - due < 6:
                                blocked = True
                            if blocked:
                                i += 1
                                continue
                            pending.pop(i)
                            stg()
                    flush()
                    fw.dma("pool", s_o[:, :, tok0:tok0 + N].rearrange("h p t -> p h t"), OTb[:, :, 0:N], reads=[B("OTb")], writes=[B("s_o%d" % bi)])
                cast_some(next_casts, len(next_casts))
                fw.barrier()
            kvs.close()

            with ExitStack() as ph:
                alloc_psum(ph, 6, 2)
                Wo = sb(ph, "Wo", [128, 8, D], BF16)
                xsb = [sb(ph, "xsb%d" % s, [128, 4, D], F32) for s in range(2)]
                OT2 = [sb(ph, "OT2_%d" % s, [128, 8, 512], BF16) for s in range(2)]
                ubA = sb(ph, "ubA", [128, 4, D], F32)
                ubF = sb(ph, "ubF", [128, 4, D], F32)
                xb = sb(ph, "xb", [128, 4, D], BF16)
                hT = sb(ph, "hT", [128, 8, 512], BF16)
                AT = sb(ph, "AT", [128, 22, 512], BF16)
                Wi = [sb(ph, "Wi%d" % s, [128, 8, 512], BF16) for s in range(2)]
                Wf = [sb(ph, "Wf%d" % s, [128, 11, 512], BF16) for s in range(2)]
                lnt = sb(ph, "lnt", [128, 4, D], F32)
                Gx = sb(ph, "Gx", [128, 2, D], F32)
                sgt = [sb(ph, "sgt%d" % s, [128, 512], F32) for s in range(2)]
                ph_ln = {"lnst": [sb(ph, "lnst%d" % s, [128, 12], F32) for s in range(4)],
                         "lnmv": [sb(ph, "lnmv%d" % s, [128, 2], F32) for s in range(4)],
                         "lnrs": [sb(ph, "lnrs%d" % s, [128, 1], F32) for s in range(4)],
                         "lnnb": [sb(ph, "lnnb%d" % s, [128, 1], F32) for s in range(4)]}
                ph_ln["bst"] = sb(ph, "bst", [128, 4, 12], F32)
                ph_ln["bmv"] = sb(ph, "bmv", [128, 4, 2], F32)
                ph_ln["brs"] = sb(ph, "brs", [128, 4], F32)
                ph_ln["bnb"] = sb(ph, "bnb", [128, 4], F32)
                lncnt = [0]
                wic = [0]
                wfc = [0]
                for kc in range(8):
                    fw.dma("sp", Wo[:, kc, :], s_wo[l, :, kc, :], reads=[B("s_wo%d_%d" % (l, kc))], writes=[B("Wo")])
                for k in range(4):
                    fw.dma("sp", lnt[:, k, :], lnp[l, k].partition_broadcast(128), writes=[B("lnp")])
                s_o2 = s_o.rearrange("(c two) p t -> two p c t", two=2)

                def geo(bi):
                    nt = 4 if bi < 4 else 2
                    return nt, nt * 128, bi * 512, (0 if bi < 4 else 1), bi % 2

                def load_x(bi):
                    nt, N, tok0, j, s = geo(bi)
                    fw.dma("sp", xsb[s][:, 0:nt, :], cur_in[tok0:tok0 + N, :].rearrange("(i p) d -> p i d", p=128),
                           reads=[B("xs_%d_%d" % (l, bi))], writes=[B("xsb%d_%d" % (s, i)) for i in range(4)])

                def load_ot(bi):
                    nt, N, tok0, j, s = geo(bi)
                    for hp in range(2):
                        fw.dma("sp", OT2[s][hp * 64:(hp + 1) * 64, :, 0:N], s_o2[hp, :, :, tok0:tok0 + N], reads=[B("s_o%d" % bi)], writes=[B("OT2_%d" % s)])

                def load_wi(cp):
                    s = wic[0] % 2
                    wic[0] += 1
                    fw.dma("sp", Wi[s][:], s_wfi[l, :, :, cp * 512:(cp + 1) * 512], reads=[B("s_wfi%d_%d" % (l, kc)) for kc in range(8)], writes=[B("Wi%d" % s)])
                    return s

                def load_wf(half, cg):
                    s = wfc[0] % 2
                    wfc[0] += 1
                    fw.dma("sp", Wf[s][:], s_wfo[l, :, cg * 11:(cg + 1) * 11, half * 512:(half + 1) * 512],
                           reads=[B("s_wfo%d_%d" % (l, cg * 11 + cl)) for cl in range(11)], writes=[B("Wf%d" % s)])
                    return s

                def gated_res(ps, pb_, ub, ubn, i, half, gi, s):
                    cs = slice(half * 512, (half + 1) * 512)
                    fw.op("dve", lambda: nc.vector.tensor_tensor(out=ub[:, i, cs], in0=ps[:, :], in1=Gx[:, gi, cs], op=ALU.mult),
                          reads=[pb_, B("Gx")], writes=[B("%s%d" % (ubn, i))])
                    fw.op("dve", lambda: nc.vector.scalar_tensor_tensor(out=ub[:, i, cs], in0=xsb[s][:, i, cs], scalar=float(ALPHA), in1=ub[:, i, cs],
                                                                       op0=ALU.mult, op1=ALU.add),
                          reads=[B("%s%d" % (ubn, i)), B("xsb%d_%d" % (s, i))], writes=[B("%s%d" % (ubn, i))])

                def stage_O(bi):
                    nt, N, tok0, j, s = geo(bi)
                    for i in range(nt):
                        for half in range(2):
                            ps, pb_ = next_ps()
                            for c in range(8):
                                mm(ps[:, :], OT2[s][:, c, i * 128:(i + 1) * 128], Wo[:, c, half * 512:(half + 1) * 512], c == 0, c == 7,
                                   [B("OT2_%d" % s), B("Wo")], [pb_])
                            gated_res(ps, pb_, ubA, "ubA", i, half, 0, s)

                def stage_L1(bi):
                    nt, N, tok0, j, s = geo(bi)
                    layer_norm_blk(nt, lambda i: ubA[:, i, :], lambda i: xsb[s][:, i, :], lnt[:, 0, :], lnt[:, 1, :],
                                   lambda i: [B("ubA%d" % i)], lambda i: [B("xsb%d_%d" % (s, i))], ph_ln)

                def stage_xb(bi):
                    nt, N, tok0, j, s = geo(bi)
                    for i in range(nt):
                        fw.op("act", lambda i=i: nc.scalar.copy(out=xb[:, i, :], in_=xsb[s][:, i, :]), reads=[B("xsb%d_%d" % (s, i))], writes=[B("xb%d" % i)])

                def stage_rest(bi, w0):
                    nt, N, tok0, j, s = geo(bi)
                    transpose_mod(xb, nt, hT, 2, j, "hT")
                    hTB = [B("hT%d" % c) for c in range(8)]
                    for cp in range(11):
                        ws = w0
                        if cp + 1 < 11:
                            w0 = load_wi(cp + 1)
                        for cc in range(2):
                            c = 2 * cp + cc
                            pg, pgb = next_ps()
                            for kc in range(8):
                                mm(pg[:, 0:N], Wi[ws][:, kc, cc * 256:cc * 256 + 128], hT[:, kc, 0:N], kc == 0, kc == 7, [B("Wi%d" % ws), hTB[kc]], [pgb])
                            pu, pub = next_ps()
                            for kc in range(8):
                                mm(pu[:, 0:N], Wi[ws][:, kc, cc * 256 + 128:cc * 256 + 256], hT[:, kc, 0:N], kc == 0, kc == 7, [B("Wi%d" % ws), hTB[kc]], [pub])
                            k = c % 2
                            fw.op("act", lambda pg=pg, k=k: nc.scalar.activation(out=sgt[k][:, 0:N], in_=pg[:, 0:N], func=AF.Silu), reads=[pgb], writes=[B("sgt%d" % k)])
                            fw.op("dve", lambda pu=pu, k=k, c=c: nc.vector.tensor_tensor(out=AT[:, c, 0:N], in0=pu[:, 0:N], in1=sgt[k][:, 0:N], op=ALU.mult),
                                  reads=[pub, B("sgt%d" % k)], writes=[B("AT%d" % c)])
                    f0 = load_wf(0, 0)
                    for half in range(2):
                        accs = [next_ps() for _ in range(nt)]
                        for cg in range(2):
                            fs = f0
                            if not (half == 1 and cg == 1):
                                f0 = load_wf(half if cg == 0 else half + 1, 1 - cg)
                            for i in range(nt):
                                for cl in range(11):
                                    c = cg * 11 + cl
                                    mm(accs[i][0][:, :], AT[:, c, i * 128:(i + 1) * 128], Wf[fs][:, cl, :], cg == 0 and cl == 0, cg == 1 and cl == 10,
                                       [B("AT%d" % c), B("Wf%d" % fs)], [accs[i][1]])
                        for i in range(nt):
                            gated_res(accs[i][0], accs[i][1], ubF, "ubF", i, half, 1, s)

                def stage_L2(bi):
                    nt, N, tok0, j, s = geo(bi)
                    layer_norm_blk(nt, lambda i: ubF[:, i, :], lambda i: ubF[:, i, :], lnt[:, 2, :], lnt[:, 3, :],
                                   lambda i: [B("ubF%d" % i)], lambda i: [B("ubF%d" % i)], ph_ln)
                    fw.dma("pool", dst[tok0:tok0 + N, :].rearrange("(i p) d -> p i d", p=128), ubF[:, 0:nt, :],
                           reads=[B("ubF%d" % i) for i in range(nt)], writes=[B("xs_%d_%d" % (l + 1, bi))])

                def load_gates(j):
                    for gi in range(2):
                        fw.dma("sp", Gx[:, gi, :], s_g[j, gi], reads=[B("s_g%d%d" % (j, gi))], writes=[B("Gx")])

                load_gates(0)
                load_x(0)
                load_ot(0)
                if nblk > 1:
                    load_x(1)
                    load_ot(1)
                stage_O(0)
                stage_L1(0)
                for bi in range(nblk):
                    w0 = load_wi(0)
                    if bi + 2 < nblk:
                        load_ot(bi + 2)
                    if bi < 4:
                        stage_xb(bi)
                    if bi + 1 < min(nblk, 4):
                        stage_O(bi + 1)
                        stage_L1(bi + 1)
                    if bi == 4:
                        load_gates(1)
                        stage_O(4)
                        stage_L1(4)
                        stage_xb(4)
                    stage_rest(bi, w0)
                    if bi + 2 < nblk:
                        load_x(bi + 2)
                    stage_L2(bi)
                fw.barrier()
            cur_in = s_x1
    return nc


_NC_CACHE = {}


def _host_inputs(inputs):
    cst = _consts()
    f = np.float32
    g = {k: np.asarray(v) for k, v in inputs.items()}
    shared = {}
    shared["w_ada"] = np.ascontiguousarray(g["w_ada"], dtype=f)
    shared["b_ada"] = np.ascontiguousarray(g["b_ada"], dtype=f)
    shared["b_adaT"] = np.ascontiguousarray(g["b_ada"].reshape(DEPTH, 48, 128).transpose(0, 2, 1), dtype=f)
    shared["win"] = np.ascontiguousarray(g["w_in"][:, :, cst["cols"]], dtype=f)
    shared["w_o"] = np.ascontiguousarray(g["w_o"], dtype=f)
    shared["sink"] = np.ascontiguousarray(g["sink"], dtype=f)
    shared["lamv"] = np.ascontiguousarray(np.stack([g["lam_q1"], g["lam_k1"], g["lam_q2"], g["lam_k2"]], axis=1), dtype=f)
    shared["subg"] = np.ascontiguousarray(g["subln_g"], dtype=f)
    nb = g["na_bias"]
    gath = nb[:, :, cst["dr"], cst["dc"]]
    shared["nag"] = np.ascontiguousarray(gath.transpose(0, 1, 3, 2, 4).reshape(DEPTH, 6, 128, NV * 256), dtype=f)
    shared["nmask"] = cst["nmask"]
    shared["amask"] = cst["amask"]
    shared["lnp"] = np.ascontiguousarray(np.stack([g["ln1_g"], g["ln1_b"], g["ln2_g"], g["ln2_b"]], axis=1), dtype=f)
    shared["wfi"] = np.ascontiguousarray(g["w_ffn_in"][:, :, cst["ffperm"]], dtype=f)
    shared["wfo"] = np.ascontiguousarray(g["w_ffn_out"], dtype=f)
    shared["rope"] = cst["rope"]
    per = []
    for b in range(g["x"].shape[0]):
        m = dict(shared)
        m["xin"] = np.ascontiguousarray(np.concatenate([g["x"][b], g["ctx"][b]], axis=0), dtype=f)
        cc = np.stack([g["c"][b].reshape(8, 128).T, g["c_ctx"].reshape(8, 128).T], axis=-1)
        m["cT"] = np.ascontiguousarray(cc.reshape(128, 16), dtype=f)
        per.append(m)
    return per


def kernel(**inputs):
    per = _host_inputs(inputs)
    if "nc" not in _NC_CACHE:
        _NC_CACHE["nc"] = build()
    nc = _NC_CACHE["nc"]
    n = len(per)
    res = run_bass_kernel_spmd(nc, per, core_ids=list(range(n)))
    return np.stack([np.asarray(r["out"]) for r in res.results], axis=0).astype(np.float32)
```

```python
import math
from contextlib import ExitStack

import numpy as np
import concourse.bass as bass
import concourse.mybir as mybir
from concourse.bass_utils import run_bass_kernel_spmd

F32 = mybir.dt.float32
BF16 = mybir.dt.bfloat16
AF = mybir.ActivationFunctionType
ALU = mybir.AluOpType

D = 1024
L = 2048
C = 256
T = L + C
NT = T // 128
DFF = 2816
NCH = 22
NCOL = NCH * 128 + 768
DEPTH = 2
ALPHA = (2.0 * DEPTH) ** 0.25
EPS = 1e-5
NV = 14
import os
FILL_N = 0


class Buf:
    __slots__ = ("name", "w", "r")

    def __init__(self, name):
        self.name = name
        self.w = None
        self.r = {}


class FW:
    NDMA = 32

    def __init__(self, nc, stack):
        self.nc = nc
        self.eng = {"pe": nc.tensor, "act": nc.scalar, "dve": nc.vector, "pool": nc.gpsimd, "sp": nc.sync}
        self.sems = {}
        self.cnt = {}
        self.known = {e: {} for e in self.eng}
        for e in ("pe", "act", "dve", "pool"):
            self.sems[e] = stack.enter_context(nc.semaphore("s_" + e))
            self.cnt[e] = 0
        self.dsems = []
        for i in range(self.NDMA):
            k = "dma%d" % i
            self.sems[k] = stack.enter_context(nc.semaphore(k))
            self.cnt[k] = 0
            self.dsems.append(k)
        self.rr = {"hw": 0, "sw": 0}
        self.bufs = {}

    def B(self, name):
        b = self.bufs.get(name)
        if b is None:
            b = self.bufs[name] = Buf(name)
        return b

    def _wait(self, e, toks):
        need = {}
        for t in toks:
            if t is None:
                continue
            k, v = t
            if e == "pe" and k == "pe":
                continue
            if self.known[e].get(k, 0) >= v:
                continue
            if need.get(k, 0) < v:
                need[k] = v
        for k, v in need.items():
            self.eng[e].wait_ge(self.sems[k], v)
            self.known[e][k] = v

    @staticmethod
    def _deps(reads, writes):
        toks = []
        for b in reads:
            toks.append(b.w)
        for b in writes:
            toks.append(b.w)
            for k, v in b.r.items():
                toks.append((k, v))
        return toks

    @staticmethod
    def _mark(tok, reads, writes):
        k, v = tok
        for b in reads:
            if b.r.get(k, 0) < v:
                b.r[k] = v
        for b in writes:
            b.w = tok
            b.r = {}

    def op(self, e, fn, reads=(), writes=(), inc=True):
        self._wait(e, self._deps(reads, writes))
        ins = fn()
        if inc:
            self.cnt[e] += 1
            ins.then_inc(self.sems[e], 1)
            tok = (e, self.cnt[e])
        else:
            tok = (e, self.cnt[e] + 1)
        self._mark(tok, reads, writes)
        return ins

    def dma(self, q, out, in_, reads=(), writes=()):
        kind = "sw" if q == "pool" else "hw"
        half = self.NDMA // 2
        sem = self.dsems[(0 if kind == "hw" else half) + self.rr[kind]]
        self.rr[kind] = (self.rr[kind] + 1) % half
        toks = self._deps(reads, writes)
        if self.cnt[sem] > 0:
            toks.append((sem, self.cnt[sem]))
        self._wait(q, toks)
        ins = self.eng[q].dma_start(out=out, in_=in_)
        self.cnt[sem] += 16
        ins.then_inc(self.sems[sem], 16)
        self._mark((sem, self.cnt[sem]), reads, writes)
        return ins

    def barrier(self):
        toks = [(k, v) for k, v in self.cnt.items() if v > 0]
        for e in self.eng:
            self._wait(e, toks)


def _win_cols():
    aq0, ak0, av0, bq0, bk0, bv0, cq0, ck0, cv0 = 0, 384, 512, 640, 896, 1152, 1408, 1792, 2176
    pA = np.array([d + 16 if d % 32 < 16 else d - 16 for d in range(64)])
    pB32 = np.array([d + 8 if d % 16 < 8 else d - 8 for d in range(32)])
    pB = np.concatenate([pB32, 32 + pB32])
    ar = np.arange(64)

    def hc(base, h, perm=None):
        return base + h * 64 + (ar if perm is None else perm)

    ch = []
    for j in range(3):
        ch.append(np.concatenate([hc(aq0, j), hc(aq0, 3 + j)]))
    ch.append(np.concatenate([hc(ak0, 0), hc(ak0, 1)]))
    for j in range(3):
        ch.append(np.concatenate([hc(aq0, j, pA), hc(aq0, 3 + j, pA)]))
    ch.append(np.concatenate([hc(ak0, 0, pA), hc(ak0, 1, pA)]))
    for base, perm in ((bq0, None), (bk0, None), (bq0, pB), (bk0, pB)):
        for c in range(2):
            ch.append(np.concatenate([hc(base, 2 * c, perm), hc(base, 2 * c + 1, perm)]))
    for base in (cq0, ck0):
        for c in range(3):
            ch.append(np.concatenate([hc(base, 2 * c), hc(base, 2 * c + 1)]))
    ch.append(np.arange(av0, av0 + 128))
    ch.append(np.arange(bv0, bv0 + 256))
    ch.append(np.arange(cv0, cv0 + 384))
    cols = np.concatenate(ch)
    assert cols.shape[0] == NCOL
    return cols


def _rope_tables():
    f = np.float32
    pos = np.arange(L)
    rows = (pos // 64).astype(f)
    cols = (pos % 64).astype(f)
    p = np.arange(128)
    base = np.zeros((4, 128, T), f)
    base[0::2, :, L:] = 1.0
    inv16 = np.power(f(10000.0), -np.arange(16, dtype=f) / f(16)).astype(f)
    d = p % 64
    ang = (np.where((d < 32)[:, None], rows[None, :], cols[None, :]).astype(f) * inv16[(d % 32) % 16][:, None]).astype(f)
    sgn = np.where((d % 32) < 16, -1.0, 1.0).astype(f)
    base[0, :, :L] = np.cos(ang)
    base[1, :, :L] = np.sin(ang) * sgn[:, None]
    inv8 = np.power(f(10000.0), -np.arange(8, dtype=f) / f(8)).astype(f)
    d = p % 32
    ang = (np.where((d < 16)[:, None], rows[None, :], cols[None, :]).astype(f) * inv8[(d % 16) % 8][:, None]).astype(f)
    sgn = np.where((d % 16) < 8, -1.0, 1.0).astype(f)
    base[2, :, :L] = np.cos(ang)
    base[3, :, :L] = np.sin(ang) * sgn[:, None]
    tabs = np.zeros((16, 128, T), f)
    tabs[0:4] = base
    for g in range(2):
        m = ((p // 64) == g).astype(f)[:, None]
        tabs[4 + 2 * g] = base[0] * m
        tabs[5 + 2 * g] = base[1] * m
    for v in range(4):
        m = ((p // 32) == v).astype(f)[:, None]
        tabs[8 + 2 * v] = base[2] * m
        tabs[9 + 2 * v] = base[3] * m
    return tabs


def _na_variants():
    R, W, KH, KW = 32, 64, 8, 16
    var = []
    for i in range(6):
        var.append(([12, 13, 14, 15], [8 + 2 * i, 9 + 2 * i]))
    for u in range(4):
        var.append(([0, 1, 2, 3], [2 * u, 2 * u + 1]))
    for u in range(12, 16):
        var.append(([28, 29, 30, 31], [2 * u, 2 * u + 1]))
    dr = np.zeros((NV, 128, 256), np.int64)
    dc = np.zeros((NV, 128, 256), np.int64)
    ok = np.zeros((NV, 128, 256), np.float32)
    kc = np.arange(64)[:, None]
    qc = np.arange(64)[None, :]
    cs = np.clip(qc - KW // 2, 0, W - KW)
    colv = (kc >= cs) & (kc < cs + KW)
    dcm = np.clip(kc - qc, -(KW - 1), KW - 1) + (KW - 1)
    for v, (qrows, krows) in enumerate(var):
        for kk, kr in enumerate(krows):
            for qq, r in enumerate(qrows):
                rs = min(max(r - KH // 2, 0), R - KH)
                rowv = (rs <= kr <= rs + KH - 1)
                dr[v, kk * 64:(kk + 1) * 64, qq * 64:(qq + 1) * 64] = min(max(kr - r + 7, 0), 14)
                dc[v, kk * 64:(kk + 1) * 64, qq * 64:(qq + 1) * 64] = dcm
                ok[v, kk * 64:(kk + 1) * 64, qq * 64:(qq + 1) * 64] = (colv & rowv).astype(np.float32)
    return dr, dc, ok


_CONST = {}


def _consts():
    if not _CONST:
        _CONST["cols"] = _win_cols()
        _CONST["rope"] = _rope_tables()
        dr, dc, ok = _na_variants()
        _CONST["dr"], _CONST["dc"] = dr, dc
        _CONST["nmask"] = np.ascontiguousarray(ok.transpose(1, 0, 2).reshape(128, NV * 256))
        j = np.arange(128)[:, None]
        i = np.arange(128)[None, :]
        prev = (j >= i).astype(np.float32)
        nxt = (j <= i).astype(np.float32)
        _CONST["amask"] = np.concatenate([np.tile(prev, (1, 3)), np.tile(nxt, (1, 3))], axis=1)
        ffperm = np.concatenate([np.concatenate([np.arange(c * 128, (c + 1) * 128), DFF + np.arange(c * 128, (c + 1) * 128)])
                                 for c in range(22)])
        _CONST["ffperm"] = ffperm
    return _CONST


def build(nlayers=DEPTH, dbg=False):
    nc = bass.Bass("TRN2", target_bir_lowering=False)

    def din(name, shape, dt=F32):
        return nc.dram_tensor(name, list(shape), dt, kind="ExternalInput").ap()

    def dscr(name, shape, dt):
        return nc.dram_tensor(name, list(shape), dt, kind="ExternalOutput" if (dbg and name in ("s_q", "s_o", "s_x1", "s_g")) else "Internal").ap()

    xin = din("xin", [T, D])
    cT = din("cT", [128, 16])
    w_ada = din("w_ada", [DEPTH, D, 6 * D])
    b_ada = din("b_ada", [DEPTH, 6 * D])
    b_adaT = din("b_adaT", [DEPTH, 128, 48])
    win = din("win", [DEPTH, D, NCOL])
    w_o = din("w_o", [DEPTH, D, D])
    sink = din("sink", [DEPTH, 6])
    lamv = din("lamv", [DEPTH, 4, 32])
    subg = din("subg", [DEPTH, 64])
    nag = din("nag", [DEPTH, 6, 128, NV * 256])
    nmask = din("nmask", [128, NV * 256])
    amask = din("amask", [128, 768])
    lnp = din("lnp", [DEPTH, 4, D])
    wfi = din("wfi", [DEPTH, D, 2 * DFF])
    wfo = din("wfo", [DEPTH, DFF, D])
    rope = din("rope", [16, 128, T])
    out = nc.dram_tensor("out", [L, D], F32, kind="ExternalOutput").ap()

    s_ada = dscr("s_ada", [DEPTH, 128, 8, 6 * D], BF16)
    s_win = dscr("s_win", [DEPTH, 128, 8, NCOL], BF16)
    s_wo = dscr("s_wo", [DEPTH, 128, 8, D], BF16)
    s_wfi = dscr("s_wfi", [DEPTH, 128, 8, 2 * DFF], BF16)
    s_wfo = dscr("s_wfo", [DEPTH, 128, 22, D], BF16)
    s_q = dscr("s_q", [20, 128, T], BF16)
    s_o = dscr("s_o", [16, 64, T], BF16)
    s_x1 = dscr("s_x1", [T, D], F32)
    s_g = dscr("s_g", [2, 2, 128, D], F32)

    with ExitStack() as top:
        top.enter_context(nc.allow_low_precision(reason="bf16 matmul operands by design; fp32 accumulation"))
        fw = FW(nc, top)
        B = fw.B

        uid = [0]

        def sb(stack, name, shape, dt):
            uid[0] += 1
            return stack.enter_context(nc.sbuf_tensor("%s_u%d" % (name, uid[0]), list(shape), dt))

        psF = []
        psB = []
        pcnt = {"f": 0, "b": 0, "a": 0, "s": 0, "x": 0}

        def alloc_psum(stack, nf, nb):
            del psF[:]
            del psB[:]
            for i in range(nf):
                uid[0] += 1
                psF.append(stack.enter_context(nc.psum_tensor("psf%d_u%d" % (i, uid[0]), [128, 512], F32)))
            for i in range(nb):
                uid[0] += 1
                psB.append(stack.enter_context(nc.psum_tensor("psb%d_u%d" % (i, uid[0]), [128, 1024], BF16)))

        def next_ps(pool=None):
            if pool is None:
                i = pcnt["f"] % len(psF)
                pcnt["f"] += 1
            elif pool == "a":
                i = pcnt["a"] % 4
                pcnt["a"] += 1
            elif pool == "s":
                i = 4 + pcnt["s"] % 2
                pcnt["s"] += 1
            else:
                i = 6 + pcnt["x"] % 2
                pcnt["x"] += 1
            return psF[i], B("psf%d" % i)

        def next_psb():
            i = pcnt["b"] % 2
            pcnt["b"] += 1
            return psB[i], B("psb%d" % i)

        def mm(o, lhsT, rhs, start, stop, rd, wr):
            fw.op("pe", lambda: nc.tensor.matmul(o, lhsT=lhsT, rhs=rhs, start=start, stop=stop), reads=rd, writes=wr, inc=stop)

        ident = sb(top, "ident", [128, 128], BF16)
        onesb = sb(top, "onesb", [128, 128], BF16)
        sel = sb(top, "sel", [128, 128], BF16)
        o64 = sb(top, "o64", [128, 128], BF16)
        fillr = sb(top, "fillr", [128, 512], BF16)
        epsc = sb(top, "epsc", [128, 1], F32)
        maskcol = sb(top, "maskcol", [128, 2], F32)
        modT = sb(top, "modT", [128, 32, 2], F32)
        sinkE = sb(top, "sinkE", [128, 8], F32)
        lamt = sb(top, "lamt", [128, 4], F32)
        lamw = sb(top, "lamw", [128, 4, 32], F32)
        gs = sb(top, "gs", [64, 2], F32)

        fw.op("pool", lambda: nc.gpsimd.memset(ident[:], 1.0), writes=[B("ident")])
        fw.op("pool", lambda: nc.gpsimd.affine_select(out=ident[:], in_=ident[:], pattern=[[-1, 128]], compare_op=ALU.is_equal,
                                                     fill=0.0, base=0, channel_multiplier=1), reads=[B("ident")], writes=[B("ident")])
        fw.op("pool", lambda: nc.gpsimd.memset(onesb[:], 1.0), writes=[B("onesb")])
        fw.op("pool", lambda: nc.gpsimd.memset(sel[:], 0.0), writes=[B("sel")])
        fw.op("pool", lambda: nc.gpsimd.memset(sel[64:65, :], 1.0), reads=[B("sel")], writes=[B("sel")])
        fw.op("pool", lambda: nc.gpsimd.memset(o64[:], 0.0), writes=[B("o64")])
        fw.op("pool", lambda: nc.gpsimd.memset(o64[0:64, 0:64], 1.0 / 64.0), reads=[B("o64")], writes=[B("o64")])
        fw.op("pool", lambda: nc.gpsimd.memset(fillr[:], 1.0), writes=[B("fillr")])
        fw.op("pool", lambda: nc.gpsimd.memset(epsc[:], EPS), writes=[B("epsc")])
        fw.op("pool", lambda: nc.gpsimd.memset(maskcol[:], 0.0), writes=[B("maskcol")])
        fw.op("pool", lambda: nc.gpsimd.memset(maskcol[0:64, 0:1], 1.0), reads=[B("maskcol")], writes=[B("maskcol")])
        fw.op("pool", lambda: nc.gpsimd.memset(maskcol[64:128, 1:2], 1.0), reads=[B("maskcol")], writes=[B("maskcol")])

        def cast_list(l):
            lst = []
            for kc in range(8):
                lst.append((s_ada[l, :, kc, :], w_ada[l, kc * 128:(kc + 1) * 128, :], "s_ada%d_%d" % (l, kc)))
            for kc in range(8):
                lst.append((s_win[l, :, kc, :], win[l, kc * 128:(kc + 1) * 128, :], "s_win%d_%d" % (l, kc)))
            for kc in range(8):
                lst.append((s_wo[l, :, kc, :], w_o[l, kc * 128:(kc + 1) * 128, :], "s_wo%d_%d" % (l, kc)))
            for kc in range(8):
                lst.append((s_wfi[l, :, kc, :], wfi[l, kc * 128:(kc + 1) * 128, :], "s_wfi%d_%d" % (l, kc)))
            for c in range(22):
                lst.append((s_wfo[l, :, c, :], wfo[l, c * 128:(c + 1) * 128, :], "s_wfo%d_%d" % (l, c)))
            return lst

        def cast_some(lst, n):
            for _ in range(n):
                if lst:
                    o_, i_, bn = lst.pop(0)
                    fw.dma("pool", o_, i_, writes=[B(bn)])

        def cast_weights(l):
            lst = cast_list(l)
            cast_some(lst, len(lst))


        def layer_norm(src, dst, lng, lnb, slot, rd, wr, ph):
            st_, mv, rs, nb = ph["lnst"][slot], ph["lnmv"][slot], ph["lnrs"][slot], ph["lnnb"][slot]
            bs = B("lnsm%d" % slot)
            fw.op("dve", lambda: nc.vector.bn_stats(out=st_[:, 0:6], in_=src[:, 0:512]), reads=rd, writes=[bs])
            fw.op("dve", lambda: nc.vector.bn_stats(out=st_[:, 6:12], in_=src[:, 512:1024]), reads=rd + [bs], writes=[bs])
            fw.op("dve", lambda: nc.vector.bn_aggr(out=mv[:, 0:2], in_=st_[:, 0:12]), reads=[bs], writes=[bs])
            fw.op("act", lambda: nc.scalar.activation(out=rs[:, 0:1], in_=mv[:, 1:2], func=AF.Sqrt, bias=EPS, scale=1.0), reads=[bs], writes=[bs])
            fw.op("dve", lambda: nc.vector.reciprocal(out=rs[:, 0:1], in_=rs[:, 0:1]), reads=[bs], writes=[bs])
            fw.op("dve", lambda: nc.vector.scalar_tensor_tensor(out=nb[:, 0:1], in0=mv[:, 0:1], scalar=-1.0, in1=rs[:, 0:1],
                                                               op0=ALU.mult, op1=ALU.mult), reads=[bs], writes=[bs])
            fw.op("act", lambda: nc.scalar.activation(out=src, in_=src, func=AF.Identity, bias=nb[:, 0:1], scale=rs[:, 0:1]),
                  reads=rd + [bs], writes=rd)
            fw.op("pool", lambda: nc.gpsimd.tensor_tensor(out=dst, in0=src, in1=lng, op=ALU.mult), reads=rd + [B("lnp")], writes=wr)
            fw.op("pool", lambda: nc.gpsimd.tensor_tensor(out=dst, in0=dst, in1=lnb, op=ALU.add), reads=wr + [B("lnp")], writes=wr)

        def layer_norm_blk(nt, src_fn, dst_fn, lng, lnb, rd_fn, wr_fn, ph):
            st_, mv, rs, nb = ph["bst"], ph["bmv"], ph["brs"], ph["bnb"]
            bs = B("lnblk")
            for i in range(nt):
                src = src_fn(i)
                fw.op("dve", lambda i=i, src=src: nc.vector.bn_stats(out=st_[:, i, 0:6], in_=src[:, 0:512]), reads=rd_fn(i) + [bs], writes=[bs])
                fw.op("dve", lambda i=i, src=src: nc.vector.bn_stats(out=st_[:, i, 6:12], in_=src[:, 512:1024]), reads=rd_fn(i) + [bs], writes=[bs])
                fw.op("dve", lambda i=i: nc.vector.bn_aggr(out=mv[:, i, 0:2], in_=st_[:, i, 0:12]), reads=[bs], writes=[bs])
            fw.op("act", lambda: nc.scalar.activation(out=rs[:, 0:nt], in_=mv[:, 0:nt, 1], func=AF.Sqrt, bias=EPS, scale=1.0), reads=[bs], writes=[bs])
            fw.op("dve", lambda: nc.vector.reciprocal(out=rs[:, 0:nt], in_=rs[:, 0:nt]), reads=[bs], writes=[bs])
            fw.op("dve", lambda: nc.vector.scalar_tensor_tensor(out=nb[:, 0:nt], in0=mv[:, 0:nt, 0], scalar=-1.0, in1=rs[:, 0:nt],
                                                               op0=ALU.mult, op1=ALU.mult), reads=[bs], writes=[bs])
            for i in range(nt):
                src = src_fn(i)
                dst_ = dst_fn(i)
                fw.op("act", lambda i=i, src=src: nc.scalar.activation(out=src, in_=src, func=AF.Identity, bias=nb[:, i:i + 1], scale=rs[:, i:i + 1]),
                      reads=rd_fn(i) + [bs], writes=rd_fn(i))
                fw.op("pool", lambda src=src, dst_=dst_: nc.gpsimd.tensor_tensor(out=dst_, in0=src, in1=lng, op=ALU.mult), reads=rd_fn(i) + [B("lnp")], writes=wr_fn(i))
                fw.op("pool", lambda dst_=dst_: nc.gpsimd.tensor_tensor(out=dst_, in0=dst_, in1=lnb, op=ALU.add), reads=wr_fn(i) + [B("lnp")], writes=wr_fn(i))

        def transpose_mod(xb, nt, hT, mi, j, tag):
            N = nt * 128
            for c in range(8):
                pb, pbb = next_psb()
                for i in range(nt):
                    fw.op("pe", lambda i=i, c=c, pb=pb: nc.tensor.transpose(out=pb[:, i * 128:(i + 1) * 128],
                                                                          in_=xb[:, i, c * 128:(c + 1) * 128], identity=ident[:]),
                          reads=[B("xb%d" % i), B("ident")], writes=[pbb], inc=(i == nt - 1))
                fw.op("act", lambda c=c, pb=pb: nc.scalar.activation(out=hT[:, c, 0:N], in_=pb[:, 0:N], func=AF.Identity,
                                                                    bias=modT[:, mi * 8 + c, j:j + 1],
                                                                    scale=modT[:, (mi + 1) * 8 + c, j:j + 1]),
                      reads=[pbb, B("modT")], writes=[B("%s%d" % (tag, c))])

        cur_in = xin
        for l in range(nlayers):
            last = (l == nlayers - 1) and (nlayers == DEPTH)
            lam_init = 0.8 - 0.6 * math.exp(-0.3 * l)
            dst = out if last else s_x1
            nblk = 4 if last else 5

            with ExitStack() as ph:
                alloc_psum(ph, 6, 2)
                wada = sb(ph, "wada", [128, 8, 6 * D], BF16)
                cTt = sb(ph, "cTt", [128, 16], F32)
                scf = sb(ph, "scf", [128, 16], F32)
                scb = sb(ph, "scb", [128, 8, 2], BF16)
                rep = [sb(ph, "rep%d" % j, [128, 8, 128], BF16) for j in range(2)]
                bT = sb(ph, "bT", [128, 48], F32)
                bg = sb(ph, "bg", [128, 2, D], F32)
                Gt = [sb(ph, "Gt%d" % i, [128, D], F32) for i in range(2)]
                if l == 0:
                    for kc in range(8):
                        fw.dma("pool", wada[:, kc, :], w_ada[0, kc * 128:(kc + 1) * 128, :], writes=[B("wada%d" % kc)])
                    lst0 = cast_list(0)[8:]
                    cast_some(lst0, len(lst0))
                else:
                    for kc in range(8):
                        fw.dma("sp", wada[:, kc, :], s_ada[l, :, kc, :], reads=[B("s_ada%d_%d" % (l, kc))], writes=[B("wada%d" % kc)])
                fw.dma("sp", cTt[:], cT, writes=[B("cTt")])
                fw.dma("sp", bT[:], b_adaT[l], writes=[B("bT")])
                for gi, m in enumerate((2, 5)):
                    fw.dma("sp", bg[:, gi, :], b_ada[l, m * D:(m + 1) * D].partition_broadcast(128), writes=[B("bg")])
                fw.dma("sp", sinkE[:, 0:6], sink[l].partition_broadcast(128), writes=[B("sinkE")])
                fw.dma("sp", lamw[:, :, :], lamv[l].partition_broadcast(128), writes=[B("lamw")])
                fw.dma("sp", gs[:, 0:1], subg[l].rearrange("(d o) -> d o", o=1), writes=[B("gs")])
                fw.op("act", lambda: nc.scalar.activation(out=sinkE[:, 0:6], in_=sinkE[:, 0:6], func=AF.Exp),
                      reads=[B("sinkE")], writes=[B("sinkE")])
                fw.op("dve", lambda: nc.vector.tensor_tensor(out=lamw[:, 0, :], in0=lamw[:, 0, :], in1=lamw[:, 1, :], op=ALU.mult),
                      reads=[B("lamw")], writes=[B("lamw")])
                fw.op("dve", lambda: nc.vector.tensor_tensor(out=lamw[:, 2, :], in0=lamw[:, 2, :], in1=lamw[:, 3, :], op=ALU.mult),
                      reads=[B("lamw")], writes=[B("lamw")])
                fw.op("dve", lambda: nc.vector.reduce_sum(out=lamt[:, 0:1], in_=lamw[:, 0, :], axis=mybir.AxisListType.X),
                      reads=[B("lamw")], writes=[B("lamt")])
                fw.op("dve", lambda: nc.vector.reduce_sum(out=lamt[:, 1:2], in_=lamw[:, 2, :], axis=mybir.AxisListType.X),
                      reads=[B("lamw"), B("lamt")], writes=[B("lamt")])
                fw.op("act", lambda: nc.scalar.activation(out=lamt[:, 0:2], in_=lamt[:, 0:2], func=AF.Exp),
                      reads=[B("lamt")], writes=[B("lamt")])
                fw.op("dve", lambda: nc.vector.tensor_tensor(out=lamt[:, 2:3], in0=lamt[:, 0:1], in1=lamt[:, 1:2], op=ALU.subtract),
                      reads=[B("lamt")], writes=[B("lamt")])
                fw.op("dve", lambda: nc.vector.tensor_scalar(out=lamt[:, 3:4], in0=lamt[:, 2:3], scalar1=float(lam_init), scalar2=None,
                                                            op0=ALU.add), reads=[B("lamt")], writes=[B("lamt")])
                fw.op("dve", lambda: nc.vector.tensor_scalar(out=gs[:, 1:2], in0=gs[:, 0:1], scalar1=float(1.0 - lam_init), scalar2=None,
                                                            op0=ALU.mult), reads=[B("gs")], writes=[B("gs")])
                fw.op("act", lambda: nc.scalar.activation(out=scf[:], in_=cTt[:], func=AF.Silu), reads=[B("cTt")], writes=[B("scf")])
                fw.op("dve", lambda: nc.vector.tensor_copy(out=scb[:].rearrange("p k j -> p (k j)"), in_=scf[:]), reads=[B("scf")], writes=[B("scb")])
                for j in range(2):
                    for kc in range(8):
                        fw.op("dve", lambda j=j, kc=kc: nc.vector.tensor_scalar(out=rep[j][:, kc, :], in0=onesb[:, :],
                                                                               scalar1=scf[:, kc * 2 + j:kc * 2 + j + 1], scalar2=None, op0=ALU.mult),
                              reads=[B("scf"), B("onesb")], writes=[B("rep%d" % j)])
                ps, psb_ = next_ps()
                for mi, m in enumerate((0, 1, 3, 4)):
                    for c in range(8):
                        col0 = m * D + c * 128
                        o = (mi * 8 + c) * 2
                        for kc in range(8):
                            mm(ps[:, o:o + 2], wada[:, kc, col0:col0 + 128], scb[:, kc, 0:2], kc == 0, kc == 7,
                               [B("wada%d" % kc), B("scb")], [psb_])
                psv = ps[:, 0:64].rearrange("p (a j) -> p a j", j=2)
                for j in range(2):
                    for (a0, b0) in ((0, 0), (16, 24)):
                        fw.op("dve", lambda j=j, a0=a0, b0=b0: nc.vector.tensor_tensor(out=modT[:, a0:a0 + 16, j], in0=psv[:, a0:a0 + 16, j],
                                                                                       in1=bT[:, b0:b0 + 16], op=ALU.add),
                              reads=[psb_, B("bT")], writes=[B("modT")])
                for a0 in (8, 24):
                    fw.op("dve", lambda a0=a0: nc.vector.tensor_scalar(out=modT[:, a0:a0 + 8, :], in0=modT[:, a0:a0 + 8, :], scalar1=1.0,
                                                                      scalar2=None, op0=ALU.add), reads=[B("modT")], writes=[B("modT")])
                for j in range(2 if not last else 1):
                    for gi, m in enumerate((2, 5)):
                        for half in range(2):
                            ps, psb_ = next_ps()
                            for kc in range(8):
                                mm(ps[:, :], rep[j][:, kc, :], wada[:, kc, m * D + half * 512:m * D + (half + 1) * 512], kc == 0, kc == 7,
                                   [B("rep%d" % j), B("wada%d" % kc)], [psb_])
                            fw.op("dve", lambda ps=ps, gi=gi, half=half: nc.vector.tensor_tensor(
                                out=Gt[gi][:, half * 512:(half + 1) * 512], in0=ps[:, :], in1=bg[:, gi, half * 512:(half + 1) * 512], op=ALU.add),
                                reads=[psb_, B("bg")], writes=[B("Gt%d" % gi)])
                        fw.dma("pool", s_g[j, gi], Gt[gi][:], reads=[B("Gt%d" % gi)], writes=[B("s_g%d%d" % (j, gi))])
                fw.barrier()

            kvs = ExitStack()
            KT = sb(kvs, "KT", [128, 6, T], BF16)
            VW = 12 * 66 + 62
            Vx = sb(kvs, "Vx", [128, NT, VW], BF16)
            fw.op("pool", lambda: nc.gpsimd.memset(Vx[:], 0.0), writes=[B("Vx")])
            fw.op("pool", lambda: nc.gpsimd.memset(Vx[:, :, 0:792].rearrange("p t (h c) -> p t h c", c=66)[:, :, :, 64:66], 1.0),
                  reads=[B("Vx")], writes=[B("Vx")])
            with ExitStack() as ph:
                alloc_psum(ph, 6, 2)
                Win = sb(ph, "Win", [128, 8, NCOL], BF16)
                xsb = [sb(ph, "xsb%d" % s, [128, 4, D], F32) for s in range(2)]
                xb = sb(ph, "xb", [128, 4, D], BF16)
                hT = sb(ph, "hT", [128, 8, 512], BF16)
                rt = sb(ph, "rt", [128, 16, 512], F32)
                t1 = [sb(ph, "t1_%d" % s, [128, 512], F32) for s in range(2)]
                t2 = [sb(ph, "t2_%d" % s, [128, 512], F32) for s in range(2)]
                qst = [sb(ph, "qst%d" % s, [128, 512], BF16) for s in range(2)]
                for kc in range(8):
                    fw.dma("sp", Win[:, kc, :], s_win[l, :, kc, :], reads=[B("s_win%d_%d" % (l, kc))], writes=[B("Win%d" % kc)])
                WinB = [B("Win%d" % kc) for kc in range(8)]
                tcnt = [0]

                def load_x(bi):
                    nt = 4 if bi < 4 else 2
                    s = bi % 2
                    fw.dma("sp", xsb[s][:, 0:nt, :], cur_in[bi * 512:bi * 512 + nt * 128, :].rearrange("(i p) d -> p i d", p=128),
                           reads=[B("xs_%d_%d" % (l, bi))], writes=[B("xsb%d" % s)])

                load_x(0)
                for bi in range(5):
                    nt = 4 if bi < 4 else 2
                    N = nt * 128
                    tok0 = bi * 512
                    j = 0 if bi < 4 else 1
                    s = bi % 2
                    if bi + 1 < 5:
                        load_x(bi + 1)
                    fw.dma("sp", rt[:, :, 0:N], rope[:, :, tok0:tok0 + N].rearrange("k p t -> p k t"), writes=[B("rt%d" % k) for k in range(16)])
                    for i in range(nt):
                        fw.op("act" if i % 2 else "dve",
                              (lambda i=i: nc.scalar.copy(out=xb[:, i, :], in_=xsb[s][:, i, :])) if i % 2 else
                              (lambda i=i: nc.vector.tensor_copy(out=xb[:, i, :], in_=xsb[s][:, i, :])),
                              reads=[B("xsb%d" % s)], writes=[B("xb%d" % i)])
                    transpose_mod(xb, nt, hT, 0, j, "hT")
                    hTB = [B("hT%d" % c) for c in range(8)]

                    def proj(ch):
                        ps, pb_ = next_ps()
                        for kc in range(8):
                            mm(ps[:, 0:N], Win[:, kc, ch * 128:(ch + 1) * 128], hT[:, kc, 0:N], kc == 0, kc == 7, [WinB[kc], hTB[kc]], [pb_])
                        return ps, pb_

                    def roped(ch, chp, tc, ts, dst_ap, dst_b):
                        pa, pab = proj(ch)
                        pp, ppb = proj(chp)
                        rope_comb(pa, pab, pp, ppb, tc, ts, dst_ap, dst_b)

                    def rope_comb(pa, pab, pp, ppb, tc, ts, dst_ap, dst_b):
                        k = tcnt[0] % 2
                        tcnt[0] += 1
                        fw.op("dve", lambda: nc.vector.tensor_tensor(out=t1[k][:, 0:N], in0=pa[:, 0:N], in1=rt[:, tc, 0:N], op=ALU.mult),
                              reads=[pab, B("rt%d" % tc)], writes=[B("t1_%d" % k)])
                        fw.op("dve", lambda: nc.vector.tensor_tensor(out=t2[k][:, 0:N], in0=pp[:, 0:N], in1=rt[:, ts, 0:N], op=ALU.mult),
                              reads=[ppb, B("rt%d" % ts)], writes=[B("t2_%d" % k)])
                        fw.op("pool", lambda: nc.gpsimd.tensor_tensor(out=dst_ap, in0=t1[k][:, 0:N], in1=t2[k][:, 0:N], op=ALU.add),
                              reads=[B("t1_%d" % k), B("t2_%d" % k)], writes=dst_b)

                    def q_out(qi, fn):
                        k = tcnt[0] % 2
                        fn(qst[k][:, 0:N], [B("qst%d" % k)])
                        fw.dma("sp", s_q[qi, :, tok0:tok0 + N], qst[k][:, 0:N], reads=[B("qst%d" % k)], writes=[B("s_q%d" % bi)])

                    need_q = (bi < 4) or (not last)
                    if need_q:
                        for jq in range(3):
                            pa, pab = proj(jq)
                            pp, ppb = proj(4 + jq)
                            for g in range(2):
                                q_out(g * 3 + jq, lambda ap, bb, g=g: rope_comb(pa, pab, pp, ppb, 4 + 2 * g, 5 + 2 * g, ap, bb))
                    roped(3, 7, 0, 1, KT[:, 0, tok0:tok0 + N], [B("KT0_%d" % bi)])
                    if need_q:
                        for c in range(2):
                            pa, pab = proj(8 + c)
                            pp, ppb = proj(12 + c)
                            for v in range(4):
                                q_out(6 + c * 4 + v, lambda ap, bb, v=v: rope_comb(pa, pab, pp, ppb, 8 + 2 * v, 9 + 2 * v, ap, bb))
                    for c in range(2):
                        roped(10 + c, 14 + c, 2, 3, KT[:, 1 + c, tok0:tok0 + N], [B("KT%d_%d" % (1 + c, bi))])
                    if need_q:
                        for c in range(3):
                            ps, pb_ = proj(16 + c)
                            for hb in range(2):
                                def cq(ap, bb, ps=ps, pb_=pb_, hb=hb):
                                    fw.op("act", lambda: nc.scalar.activation(out=ap, in_=ps[:, 0:N], func=AF.Copy, scale=maskcol[:, hb:hb + 1]),
                                          reads=[pb_, B("maskcol")], writes=bb)
                                q_out(14 + c * 2 + hb, cq)
                                tcnt[0] += 1
                    for c in range(3):
                        ps, pb_ = proj(19 + c)
                        fw.op("act", lambda ps=ps, c=c: nc.scalar.copy(out=KT[:, 3 + c, tok0:tok0 + N], in_=ps[:, 0:N]),
                              reads=[pb_], writes=[B("KT%d_%d" % (3 + c, bi))])
                    for i in range(nt):
                        gt = bi * 4 + i
                        ps, pb_ = next_ps()
                        for kc in range(8):
                            mm(ps[:, :], hT[:, kc, i * 128:(i + 1) * 128], Win[:, kc, NCH * 128:NCH * 128 + 512], kc == 0, kc == 7,
                               [WinB[kc], hTB[kc]], [pb_])
                        fw.op("act", lambda ps=ps, gt=gt: nc.scalar.copy(out=Vx[:, gt, 0:528].rearrange("p (h c) -> p h c", c=66)[:, :, 0:64], in_=ps[:, :].rearrange("p (h d) -> p h d", d=64)),
                              reads=[pb_], writes=[B("Vx")])
                        ps, pb_ = next_ps()
                        for kc in range(8):
                            mm(ps[:, 0:256], hT[:, kc, i * 128:(i + 1) * 128], Win[:, kc, NCH * 128 + 512:NCH * 128 + 768], kc == 0, kc == 7,
                               [WinB[kc], hTB[kc]], [pb_])
                        fw.op("dve", lambda ps=ps, gt=gt: nc.vector.tensor_copy(out=Vx[:, gt, 528:792].rearrange("p (h c) -> p h c", c=66)[:, :, 0:64],
                                                                                in_=ps[:, 0:256].rearrange("p (h d) -> p h d", d=64)),
                              reads=[pb_], writes=[B("Vx")])
                fw.barrier()

            with ExitStack() as ph:
                alloc_psum(ph, 8, 0)
                Et = sb(ph, "Et", [128, 6, NV * 256], BF16)
                nmk = sb(ph, "nmk", [128, NV * 256], BF16)
                MA = sb(ph, "MA", [128, 768], BF16)
                QTb = [sb(ph, "QTb%d" % s, [128, 20, 512], BF16) for s in range(2)]
                OTb = sb(ph, "OTb", [64, 16, 512], BF16)
                NPT = 6
                PT = [sb(ph, "PT%d" % s, [128, 512], BF16) for s in range(NPT)]
                bcs = [sb(ph, "bcs%d" % s, [64, 512], F32) for s in range(4)]
                tA = [sb(ph, "tA%d" % s, [64, 512], F32) for s in range(4)]
                sq = sb(ph, "sq", [128, 512], BF16)
                cnt = {"pt": 0, "ss": 0, "bc": 0}
                next_casts = cast_list(l + 1) if l + 1 < nlayers else []
                with ExitStack() as sub:
                    HV = NV * 128
                    gstg = [sb(sub, "gstg%d" % i_, [128, HV], F32) for i_ in range(2)]
                    fw.dma("pool", nmk[:], nmask, writes=[B("nmk")])
                    fw.dma("pool", MA[:], amask, writes=[B("MA")])
                    for hh in range(12):
                        h, part = hh // 2, hh % 2
                        gi_ = hh % 2
                        cs = slice(part * HV, (part + 1) * HV)
                        fw.dma("sp", gstg[gi_][:], nag[l, h, :, cs], writes=[B("gstg%d" % gi_)])
                        fw.op("act", lambda h=h, cs=cs, gi_=gi_: nc.scalar.activation(out=Et[:, h, cs], in_=gstg[gi_][:], func=AF.Exp),
                              reads=[B("gstg%d" % gi_)], writes=[B("Et%d_%d" % (h, part))])
                        fw.op("dve", lambda h=h, cs=cs: nc.vector.tensor_tensor(out=Et[:, h, cs], in0=Et[:, h, cs], in1=nmk[:, cs], op=ALU.mult),
                              reads=[B("Et%d_%d" % (h, part)), B("nmk")], writes=[B("Et%d_%d" % (h, part))])
                    fw.barrier()

                rb = [sb(ph, "rb%d" % s_, [128, 512], BF16) for s_ in range(4)]
                for s_ in range(4):
                    fw.op("pool", lambda s_=s_: nc.gpsimd.memset(rb[s_][:], 0.0), writes=[B("rb%d" % s_)])
                cnt["rb"] = 0
                fw.op("pool", lambda: nc.gpsimd.memset(sq[:], 0.0), writes=[B("sq")])

                def v_lhsT(t, hh):
                    return Vx[:, t, hh * 66:hh * 66 + 128]

                def bcast_rows(rows, N_):
                    res = []
                    for wr in rows:
                        k = cnt["rb"] % 4
                        cnt["rb"] += 1
                        wr(rb[k], B("rb%d" % k))
                        pbc, pbcb = next_ps("x")
                        mm(pbc[:, 0:N_], sel[:, :], rb[k][:, 0:N_], True, True, [B("sel"), B("rb%d" % k)], [pbcb])
                        res.append((pbc, pbcb))
                    return res

                def recips(pbcs, N_):
                    res = []
                    for pbc, pbcb in pbcs:
                        c0, c0b = new_bc()
                        fw.op("dve", lambda pbc=pbc, c0=c0: nc.vector.reciprocal(out=c0[0:64, 0:N_], in_=pbc[0:64, 0:N_]), reads=[pbcb], writes=[c0b])
                        res.append((c0, c0b))
                    return res

                def load_q(bi):
                    nt = 4 if bi < 4 else 2
                    s = bi % 2
                    fw.dma("sp", QTb[s][:, :, 0:nt * 128], s_q[:, :, bi * 512:bi * 512 + nt * 128].rearrange("c p t -> p c t"),
                           reads=[B("s_q%d" % bi)], writes=[B("QTb%d" % s)])

                def new_pt():
                    k = cnt["pt"] % NPT
                    cnt["pt"] += 1
                    return PT[k], B("PT%d" % k)

                def new_bc():
                    k = cnt["bc"] % 4
                    cnt["bc"] += 1
                    return bcs[k], B("bcs%d" % k)

                live_acc = set()
                flush_hook = [lambda: None]

                def new_acc():
                    i = pcnt["a"] % 4
                    if i in live_acc:
                        flush_hook[0]()
                    assert i not in live_acc, "accumulator bank still live"
                    live_acc.add(i)
                    pcnt["a"] += 1
                    return psF[i], B("psf%d" % i), i

                def gen_jobs(Q, Qb, bi, ctxq, nt, N):
                    units = []
                    for n in range(nt):
                        gq = bi * 4 + n
                        for g in range(2):
                            if ctxq:
                                tiles = [(16, None), (17, None)]
                            else:
                                tiles = []
                                if gq - 1 >= 0:
                                    tiles.append((gq - 1, 0))
                                tiles.append((gq, None))
                                if gq + 1 < 16:
                                    tiles.append((gq + 1, 1))
                                tiles += [(16, None), (17, None)]
                            u = {"jobs": []}
                            for ti, (t, mk) in enumerate(tiles):
                                def S(u=u, ti=ti, t=t, g=g, n=n):
                                    if ti == 0:
                                        u["po"] = new_acc()
                                    ps, psb_ = next_ps("s")
                                    mm(ps[:, 0:384].rearrange("p (a b) -> p a b", a=3), KT[:, 0, t * 128:(t + 1) * 128],
                                       Q[:, g * 3:g * 3 + 3, n * 128:(n + 1) * 128], True, True, [B("KT0_%d" % (t // 4)), Qb], [psb_])
                                    return ps, psb_

                                def rest(st, u=u, ti=ti, t=t, mk=mk, g=g, last=(ti == len(tiles) - 1)):
                                    ps, psb_ = st
                                    po, pob, _ = u["po"]
                                    pt, ptb = new_pt()
                                    fw.op("act", lambda: nc.scalar.activation(out=pt[:, 0:384], in_=ps[:, 0:384], func=AF.Exp, scale=0.125),
                                          reads=[psb_], writes=[ptb])
                                    if mk is not None:
                                        fw.op("pool", lambda: nc.gpsimd.tensor_tensor(out=pt[:, 0:384], in0=pt[:, 0:384],
                                                                                    in1=MA[:, mk * 384:(mk + 1) * 384], op=ALU.mult),
                                              reads=[ptb, B("MA")], writes=[ptb])
                                    mm(po[:, 0:384], v_lhsT(t, g), pt[:, 0:384], ti == 0, last, [B("Vx"), ptb], [pob])
                                u["jobs"].append((S, rest))

                            def fin(u=u, g=g, n=n):
                                po, pob, bank = u["po"]
                                stt = {}

                                def wr(rbt, rbb):
                                    for sg in range(3):
                                        h = 3 * g + sg
                                        fw.op("dve", lambda sg=sg, h=h: nc.vector.tensor_scalar(
                                            out=rbt[64:65, sg * 128:(sg + 1) * 128], in0=po[64:65, sg * 128:(sg + 1) * 128],
                                            scalar1=sinkE[64:65, h:h + 1], scalar2=None, op0=ALU.add), reads=[pob, B("sinkE")], writes=[rbb])

                                def s1():
                                    stt["p"] = bcast_rows([wr], 384)

                                def s2():
                                    stt["c"] = recips(stt["p"], 384)

                                def s3():
                                    ((c0, c0b),) = stt["c"]
                                    fw.op("dve", lambda: nc.vector.tensor_tensor(
                                        out=OTb[0:64, 3 * g:3 * g + 3, n * 128:(n + 1) * 128], in0=po[0:64, 0:384].rearrange("p (a b) -> p a b", a=3),
                                        in1=c0[0:64, 0:384].rearrange("p (a b) -> p a b", a=3), op=ALU.mult),
                                        reads=[pob, c0b], writes=[B("OTb")])
                                    live_acc.discard(bank)
                                return [(2, s1, False), (3, s2, False), (6, s3, False)]
                            u["fin"] = fin
                            units.append(u)
                    tilesB = [16, 17] if ctxq else list(range(18))
                    for h in range(4):
                        r0 = (h % 2) * 64
                        u = {"jobs": []}
                        for ti, t in enumerate(tilesB):
                            for m in range(2):
                                def S(u=u, ti=ti, t=t, m=m, h=h, r0=r0):
                                    if ti == 0 and m == 0:
                                        u["po"] = [new_acc(), new_acc()]
                                    ps, psb_ = next_ps("s")
                                    mm(ps[:, 0:N], KT[:, 1 + h // 2, t * 128:(t + 1) * 128], Q[:, 6 + (h // 2) * 4 + (h % 2) * 2 + m, 0:N],
                                       True, True, [B("KT%d_%d" % (1 + h // 2, t // 4)), Qb], [psb_])
                                    return ps, psb_

                                def rest(st, u=u, ti=ti, t=t, m=m, h=h, last=(ti == len(tilesB) - 1)):
                                    ps, psb_ = st
                                    po, pob, _ = u["po"][m]
                                    pt, ptb = new_pt()
                                    fw.op("act", lambda: nc.scalar.activation(out=pt[:, 0:N], in_=ps[:, 0:N], func=AF.Exp, scale=float(32 ** -0.5)),
                                          reads=[psb_], writes=[ptb])
                                    mm(po[:, 0:N], v_lhsT(t, 2 + h), pt[:, 0:N], ti == 0, last, [B("Vx"), ptb], [pob])
                                u["jobs"].append((S, rest))

                        def fin(u=u, h=h):
                            (p0, p0b, b0_), (p1, p1b, b1_) = u["po"]
                            stt = {}

                            def wr0(rbt, rbb):
                                fw.op("dve", lambda: nc.vector.tensor_copy(out=rbt[64:65, 0:N], in_=p0[64:65, 0:N]), reads=[p0b], writes=[rbb])

                            def wr1(rbt, rbb):
                                fw.op("dve", lambda: nc.vector.tensor_copy(out=rbt[64:65, 0:N], in_=p1[64:65, 0:N]), reads=[p1b], writes=[rbb])

                            def s1():
                                stt["p"] = bcast_rows([wr0, wr1], N)

                            def s2():
                                stt["c"] = recips(stt["p"], N)

                            def s3():
                                (c0, c0b), (c1, c1b) = stt["c"]
                                fw.op("dve", lambda: nc.vector.tensor_tensor(out=tA[0][:, 0:N], in0=p0[0:64, 0:N], in1=c0[0:64, 0:N], op=ALU.mult),
                                      reads=[p0b, c0b], writes=[B("tA0")])
                                fw.op("dve", lambda: nc.vector.scalar_tensor_tensor(out=tA[1][:, 0:N], in0=p1[0:64, 0:N], scalar=lamt[0:64, 3:4],
                                                                                   in1=c1[0:64, 0:N], op0=ALU.mult, op1=ALU.mult),
                                      reads=[p1b, c1b, B("lamt")], writes=[B("tA1")])
                                live_acc.discard(b0_)
                                live_acc.discard(b1_)
                                fw.op("pool", lambda: nc.gpsimd.tensor_tensor(out=tA[2][:, 0:N], in0=tA[0][:, 0:N], in1=tA[1][:, 0:N], op=ALU.subtract),
                                      reads=[B("tA0"), B("tA1")], writes=[B("tA2")])
                                fw.op("pool", lambda: nc.gpsimd.tensor_tensor(out=sq[0:64, 0:N], in0=tA[2][:, 0:N], in1=tA[2][:, 0:N], op=ALU.mult),
                                      reads=[B("tA2")], writes=[B("sq")])

                            def s4():
                                pms, pmsb = next_ps("x")
                                stt["pms"] = (pms, pmsb)
                                mm(pms[:, 0:N], o64[:, :], sq[:, 0:N], True, True, [B("o64"), B("sq")], [pmsb])

                            def s5():
                                pms, pmsb = stt["pms"]
                                fw.op("act", lambda: nc.scalar.activation(out=tA[3][:, 0:N], in_=pms[0:64, 0:N], func=AF.Ln, bias=epsc[0:64, 0:1], scale=1.0),
                                      reads=[pmsb, B("epsc")], writes=[B("tA3")])
                                fw.op("act", lambda: nc.scalar.activation(out=tA[3][:, 0:N], in_=tA[3][:, 0:N], func=AF.Exp, scale=-0.5),
                                      reads=[B("tA3")], writes=[B("tA3")])

                            def s6():
                                fw.op("dve", lambda: nc.vector.scalar_tensor_tensor(out=OTb[0:64, 6 + h, 0:N], in0=tA[2][:, 0:N], scalar=gs[:, 1:2],
                                                                                   in1=tA[3][:, 0:N], op0=ALU.mult, op1=ALU.mult),
                                      reads=[B("tA2"), B("tA3"), B("gs")], writes=[B("OTb")])
                            return [(2, s1, False), (3, s2, False), (10, s3, False), (14, s4, False), (18, s5, False), (22, s6, False)]
                        u["fin"] = fin
                        units.append(u)
                    for h in range(6):
                        r0 = (h % 2) * 64
                        for jj in range(1 if ctxq else 2):
                            if ctxq:
                                tiles = [(16, None), (17, None)]
                            else:
                                jg = bi * 2 + jj
                                if jg == 0:
                                    tiles = [(uu, 6 + uu) for uu in range(4)]
                                elif jg == 7:
                                    tiles = [(uu, 10 + uu - 12) for uu in range(12, 16)]
                                else:
                                    tiles = [(2 * jg - 2 + i, i) for i in range(6)]
                                tiles += [(16, None), (17, None)]
                            u = {"jobs": []}
                            for ti, (t, ev) in enumerate(tiles):
                                def S(u=u, ti=ti, t=t, h=h, r0=r0, jj=jj):
                                    if ti == 0:
                                        u["po"] = new_acc()
                                    ps, psb_ = next_ps("s")
                                    mm(ps[:, 0:256], KT[:, 3 + h // 2, t * 128:(t + 1) * 128], Q[:, 14 + (h // 2) * 2 + (h % 2), jj * 256:(jj + 1) * 256],
                                       True, True, [B("KT%d_%d" % (3 + h // 2, t // 4)), Qb], [psb_])
                                    return ps, psb_

                                def rest(st, u=u, ti=ti, t=t, ev=ev, h=h, last=(ti == len(tiles) - 1)):
                                    ps, psb_ = st
                                    po, pob, _ = u["po"]
                                    pt, ptb = new_pt()
                                    fw.op("act", lambda: nc.scalar.activation(out=pt[:, 0:256], in_=ps[:, 0:256], func=AF.Exp, scale=0.125),
                                          reads=[psb_], writes=[ptb])
                                    if ev is not None:
                                        fw.op("dve", lambda: nc.vector.tensor_tensor(out=pt[:, 0:256], in0=pt[:, 0:256],
                                                                                    in1=Et[:, h, ev * 256:(ev + 1) * 256], op=ALU.mult),
                                              reads=[ptb, B("Et%d" % h)], writes=[ptb])
                                    mm(po[:, 0:256], v_lhsT(t, 6 + h), pt[:, 0:256], ti == 0, last, [B("Vx"), ptb], [pob])
                                u["jobs"].append((S, rest, ev is not None))

                            def fin(u=u, h=h, jj=jj):
                                po, pob, bank = u["po"]
                                stt = {}

                                def wr(rbt, rbb):
                                    fw.op("dve", lambda: nc.vector.tensor_copy(out=rbt[64:65, 0:256], in_=po[64:65, 0:256]), reads=[pob], writes=[rbb])

                                def s1():
                                    stt["p"] = bcast_rows([wr], 256)

                                def s2():
                                    stt["c"] = recips(stt["p"], 256)

                                def s3():
                                    ((c0, c0b),) = stt["c"]
                                    fw.op("dve", lambda: nc.vector.tensor_tensor(out=OTb[0:64, 10 + h, jj * 256:(jj + 1) * 256], in0=po[0:64, 0:256],
                                                                                in1=c0[0:64, 0:256], op=ALU.mult), reads=[pob, c0b], writes=[B("OTb")])
                                    live_acc.discard(bank)
                                return [(2, s1, False), (4, s2, True), (6, s3, False)]
                            u["fin"] = fin
                            units.append(u)
                    return units

                nqb = 4 if last else 5
                load_q(0)
                for bi in range(nqb):
                    ctxq = bi == 4
                    nt = 4 if bi < 4 else 2
                    N = nt * 128
                    tok0 = bi * 512
                    s = bi % 2
                    if bi + 1 < nqb:
                        load_q(bi + 1)
                    units = gen_jobs(QTb[s], B("QTb%d" % s), bi, ctxq, nt, N)
                    jobs = []
                    for u in units:
                        for ji, jb in enumerate(u["jobs"]):
                            jobs.append((jb[0], jb[1], u["fin"] if ji == len(u["jobs"]) - 1 else None, (jb[2] if len(jb) > 2 else False)))
                    pending = []

                    def flush():
                        while pending:
                            pending.pop(0)[1]()
                    flush_hook[0] = flush
                    st_next = jobs[0][0]()
                    for k, (S, rest, fin, _) in enumerate(jobs):
                        st_cur = st_next
                        if k + 1 < len(jobs):
                            st_next = jobs[k + 1][0]()
                        rest(st_cur)
                        if fin is not None:
                            cast_some(next_casts, 1)
                            base = 0
                            for (dl, stg, heavy) in fin():
                                pending.append([k + dl, stg, heavy])
                            pending.sort(key=lambda x: x[0])
                        nxt_mask = (k + 1 < len(jobs)) and jobs[k + 1][3]
                        i = 0
                        blocked = False
                        while i < len(pending):
                            due, stg, heavy = pending[i]
                            if due > k:
                                break
                            if heavy and nxt_mask and k - due < 6:
                                blocked = True
                            if blocked:
                                i += 1
                                continue
                            pending.pop(i)
                            stg()
                    flush()
                    fw.dma("pool", s_o[:, :, tok0:tok0 + N].rearrange("h p t -> p h t"), OTb[:, :, 0:N], reads=[B("OTb")], writes=[B("s_o%d" % bi)])
                cast_some(next_casts, len(next_casts))
                fw.barrier()
            kvs.close()

            with ExitStack() as ph:
                alloc_psum(ph, 6, 2)
                Wo = sb(ph, "Wo", [128, 8, D], BF16)
                xsb = [sb(ph, "xsb%d" % s, [128, 4, D], F32) for s in range(2)]
                OT2 = [sb(ph, "OT2_%d" % s, [128, 8, 512], BF16) for s in range(2)]
                ubA = sb(ph, "ubA", [128, 4, D], F32)
                ubF = sb(ph, "ubF", [128, 4, D], F32)
                xb = sb(ph, "xb", [128, 4, D], BF16)
                hT = sb(ph, "hT", [128, 8, 512], BF16)
                AT = sb(ph, "AT", [128, 22, 512], BF16)
                Wi = [sb(ph, "Wi%d" % s, [128, 8, 512], BF16) for s in range(2)]
                Wf = [sb(ph, "Wf%d" % s, [128, 11, 512], BF16) for s in range(2)]
                lnt = sb(ph, "lnt", [128, 4, D], F32)
                Gx = sb(ph, "Gx", [128, 2, D], F32)
                sgt = [sb(ph, "sgt%d" % s, [128, 512], F32) for s in range(2)]
                ph_ln = {"lnst": [sb(ph, "lnst%d" % s, [128, 12], F32) for s in range(4)],
                         "lnmv": [sb(ph, "lnmv%d" % s, [128, 2], F32) for s in range(4)],
                         "lnrs": [sb(ph, "lnrs%d" % s, [128, 1], F32) for s in range(4)],
                         "lnnb": [sb(ph, "lnnb%d" % s, [128, 1], F32) for s in range(4)]}
                ph_ln["bst"] = sb(ph, "bst", [128, 4, 12], F32)
                ph_ln["bmv"] = sb(ph, "bmv", [128, 4, 2], F32)
                ph_ln["brs"] = sb(ph, "brs", [128, 4], F32)
                ph_ln["bnb"] = sb(ph, "bnb", [128, 4], F32)
                lncnt = [0]
                wic = [0]
                wfc = [0]
                for kc in range(8):
                    fw.dma("sp", Wo[:, kc, :], s_wo[l, :, kc, :], reads=[B("s_wo%d_%d" % (l, kc))], writes=[B("Wo")])
                for k in range(4):
                    fw.dma("sp", lnt[:, k, :], lnp[l, k].partition_broadcast(128), writes=[B("lnp")])
                s_o2 = s_o.rearrange("(c two) p t -> two p c t", two=2)

                def geo(bi):
                    nt = 4 if bi < 4 else 2
                    return nt, nt * 128, bi * 512, (0 if bi < 4 else 1), bi % 2

                def load_x(bi):
                    nt, N, tok0, j, s = geo(bi)
                    fw.dma("sp", xsb[s][:, 0:nt, :], cur_in[tok0:tok0 + N, :].rearrange("(i p) d -> p i d", p=128),
                           reads=[B("xs_%d_%d" % (l, bi))], writes=[B("xsb%d_%d" % (s, i)) for i in range(4)])

                def load_ot(bi):
                    nt, N, tok0, j, s = geo(bi)
                    for hp in range(2):
                        fw.dma("sp", OT2[s][hp * 64:(hp + 1) * 64, :, 0:N], s_o2[hp, :, :, tok0:tok0 + N], reads=[B("s_o%d" % bi)], writes=[B("OT2_%d" % s)])

                def load_wi(cp):
                    s = wic[0] % 2
                    wic[0] += 1
                    fw.dma("sp", Wi[s][:], s_wfi[l, :, :, cp * 512:(cp + 1) * 512], reads=[B("s_wfi%d_%d" % (l, kc)) for kc in range(8)], writes=[B("Wi%d" % s)])
                    return s

                def load_wf(half, cg):
                    s = wfc[0] % 2
                    wfc[0] += 1
                    fw.dma("sp", Wf[s][:], s_wfo[l, :, cg * 11:(cg + 1) * 11, half * 512:(half + 1) * 512],
                           reads=[B("s_wfo%d_%d" % (l, cg * 11 + cl)) for cl in range(11)], writes=[B("Wf%d" % s)])
                    return s

                def gated_res(ps, pb_, ub, ubn, i, half, gi, s):
                    cs = slice(half * 512, (half + 1) * 512)
                    fw.op("dve", lambda: nc.vector.tensor_tensor(out=ub[:, i, cs], in0=ps[:, :], in1=Gx[:, gi, cs], op=ALU.mult),
                          reads=[pb_, B("Gx")], writes=[B("%s%d" % (ubn, i))])
                    fw.op("dve", lambda: nc.vector.scalar_tensor_tensor(out=ub[:, i, cs], in0=xsb[s][:, i, cs], scalar=float(ALPHA), in1=ub[:, i, cs],
                                                                       op0=ALU.mult, op1=ALU.add),
                          reads=[B("%s%d" % (ubn, i)), B("xsb%d_%d" % (s, i))], writes=[B("%s%d" % (ubn, i))])

                def stage_O(bi):
                    nt, N, tok0, j, s = geo(bi)
                    for i in range(nt):
                        for half in range(2):
                            ps, pb_ = next_ps()
                            for c in range(8):
                                mm(ps[:, :], OT2[s][:, c, i * 128:(i + 1) * 128], Wo[:, c, half * 512:(half + 1) * 512], c == 0, c == 7,
                                   [B("OT2_%d" % s), B("Wo")], [pb_])
                            gated_res(ps, pb_, ubA, "ubA", i, half, 0, s)

                def stage_L1(bi):
                    nt, N, tok0, j, s = geo(bi)
                    layer_norm_blk(nt, lambda i: ubA[:, i, :], lambda i: xsb[s][:, i, :], lnt[:, 0, :], lnt[:, 1, :],
                                   lambda i: [B("ubA%d" % i)], lambda i: [B("xsb%d_%d" % (s, i))], ph_ln)

                def stage_xb(bi):
                    nt, N, tok0, j, s = geo(bi)
                    for i in range(nt):
                        fw.op("act", lambda i=i: nc.scalar.copy(out=xb[:, i, :], in_=xsb[s][:, i, :]), reads=[B("xsb%d_%d" % (s, i))], writes=[B("xb%d" % i)])

                def stage_rest(bi, w0):
                    nt, N, tok0, j, s = geo(bi)
                    transpose_mod(xb, nt, hT, 2, j, "hT")
                    hTB = [B("hT%d" % c) for c in range(8)]
                    for cp in range(11):
                        ws = w0
                        if cp + 1 < 11:
                            w0 = load_wi(cp + 1)
                        for cc in range(2):
                            c = 2 * cp + cc
                            pg, pgb = next_ps()
                            for kc in range(8):
                                mm(pg[:, 0:N], Wi[ws][:, kc, cc * 256:cc * 256 + 128], hT[:, kc, 0:N], kc == 0, kc == 7, [B("Wi%d" % ws), hTB[kc]], [pgb])
                            pu, pub = next_ps()
                            for kc in range(8):
                                mm(pu[:, 0:N], Wi[ws][:, kc, cc * 256 + 128:cc * 256 + 256], hT[:, kc, 0:N], kc == 0, kc == 7, [B("Wi%d" % ws), hTB[kc]], [pub])
                            k = c % 2
                            fw.op("act", lambda pg=pg, k=k: nc.scalar.activation(out=sgt[k][:, 0:N], in_=pg[:, 0:N], func=AF.Silu), reads=[pgb], writes=[B("sgt%d" % k)])
                            fw.op("dve", lambda pu=pu, k=k, c=c: nc.vector.tensor_tensor(out=AT[:, c, 0:N], in0=pu[:, 0:N], in1=sgt[k][:, 0:N], op=ALU.mult),
                                  reads=[pub, B("sgt%d" % k)], writes=[B("AT%d" % c)])
                    f0 = load_wf(0, 0)
                    for half in range(2):
                        accs = [next_ps() for _ in range(nt)]
                        for cg in range(2):
                            fs = f0
                            if not (half == 1 and cg == 1):
                                f0 = load_wf(half if cg == 0 else half + 1, 1 - cg)
                            for i in range(nt):
                                for cl in range(11):
                                    c = cg * 11 + cl
                                    mm(accs[i][0][:, :], AT[:, c, i * 128:(i + 1) * 128], Wf[fs][:, cl, :], cg == 0 and cl == 0, cg == 1 and cl == 10,
                                       [B("AT%d" % c), B("Wf%d" % fs)], [accs[i][1]])
                        for i in range(nt):
                            gated_res(accs[i][0], accs[i][1], ubF, "ubF", i, half, 1, s)

                def stage_L2(bi):
                    nt, N, tok0, j, s = geo(bi)
                    layer_norm_blk(nt, lambda i: ubF[:, i, :], lambda i: ubF[:, i, :], lnt[:, 2, :], lnt[:, 3, :],
                                   lambda i: [B("ubF%d" % i)], lambda i: [B("ubF%d" % i)], ph_ln)
                    fw.dma("pool", dst[tok0:tok0 + N, :].rearrange("(i p) d -> p i d", p=128), ubF[:, 0:nt, :],
                           reads=[B("ubF%d" % i) for i in range(nt)], writes=[B("xs_%d_%d" % (l + 1, bi))])

                def load_gates(j):
                    for gi in range(2):
                        fw.dma("sp", Gx[:, gi, :], s_g[j, gi], reads=[B("s_g%d%d" % (j, gi))], writes=[B("Gx")])

                load_gates(0)
                load_x(0)
                load_ot(0)
                if nblk > 1:
                    load_x(1)
                    load_ot(1)
                stage_O(0)
                stage_L1(0)
                for bi in range(nblk):
                    w0 = load_wi(0)
                    if bi + 2 < nblk:
                        load_ot(bi + 2)
                    if bi < 4:
                        stage_xb(bi)
                    if bi + 1 < min(nblk, 4):
                        stage_O(bi + 1)
                        stage_L1(bi + 1)
                    if bi == 4:
                        load_gates(1)
                        stage_O(4)
                        stage_L1(4)
                        stage_xb(4)
                    stage_rest(bi, w0)
                    if bi + 2 < nblk:
                        load_x(bi + 2)
                    stage_L2(bi)
                fw.barrier()
            cur_in = s_x1
    return nc


_NC_CACHE = {}


def _host_inputs(inputs):
    cst = _consts()
    f = np.float32
    g = {k: np.asarray(v) for k, v in inputs.items()}
    shared = {}
    shared["w_ada"] = np.ascontiguousarray(g["w_ada"], dtype=f)
    shared["b_ada"] = np.ascontiguousarray(g["b_ada"], dtype=f)
    shared["b_adaT"] = np.ascontiguousarray(g["b_ada"].reshape(DEPTH, 48, 128).transpose(0, 2, 1), dtype=f)
    shared["win"] = np.ascontiguousarray(g["w_in"][:, :, cst["cols"]], dtype=f)
    shared["w_o"] = np.ascontiguousarray(g["w_o"], dtype=f)
    shared["sink"] = np.ascontiguousarray(g["sink"], dtype=f)
    shared["lamv"] = np.ascontiguousarray(np.stack([g["lam_q1"], g["lam_k1"], g["lam_q2"], g["lam_k2"]], axis=1), dtype=f)
    shared["subg"] = np.ascontiguousarray(g["subln_g"], dtype=f)
    nb = g["na_bias"]
    gath = nb[:, :, cst["dr"], cst["dc"]]
    shared["nag"] = np.ascontiguousarray(gath.transpose(0, 1, 3, 2, 4).reshape(DEPTH, 6, 128, NV * 256), dtype=f)
    shared["nmask"] = cst["nmask"]
    shared["amask"] = cst["amask"]
    shared["lnp"] = np.ascontiguousarray(np.stack([g["ln1_g"], g["ln1_b"], g["ln2_g"], g["ln2_b"]], axis=1), dtype=f)
    shared["wfi"] = np.ascontiguousarray(g["w_ffn_in"][:, :, cst["ffperm"]], dtype=f)
    shared["wfo"] = np.ascontiguousarray(g["w_ffn_out"], dtype=f)
    shared["rope"] = cst["rope"]
    per = []
    for b in range(g["x"].shape[0]):
        m = dict(shared)
        m["xin"] = np.ascontiguousarray(np.concatenate([g["x"][b], g["ctx"][b]], axis=0), dtype=f)
        cc = np.stack([g["c"][b].reshape(8, 128).T, g["c_ctx"].reshape(8, 128).T], axis=-1)
        m["cT"] = np.ascontiguousarray(cc.reshape(128, 16), dtype=f)
        per.append(m)
    return per


def kernel(**inputs):
    per = _host_inputs(inputs)
    if "nc" not in _NC_CACHE:
        _NC_CACHE["nc"] = build()
    nc = _NC_CACHE["nc"]
    n = len(per)
    res = run_bass_kernel_spmd(nc, per, core_ids=list(range(n)))
    return np.stack([np.asarray(r["out"]) for r in res.results], axis=0).astype(np.float32)
```
